# Optimizing a Trainium2 kernel written in Bass

```python
import jax
import jax.numpy as jnp
from jax import lax
import numpy as np

D_MODEL = 1024
BATCH = 16
SEQ = 2048
DEPTH = 4

N_MIXERS = 4
N_REPEAT = DEPTH // N_MIXERS
RMS_EPS = 1e-6
ROPE_THETA = 10000.0

DSA_HEADS = 16
DSA_KV_HEADS = 4
DSA_HEAD_DIM = 64
DSA_IDX_HEADS = 8
DSA_IDX_DIM = 128
DSA_TOPK = 256
DSA_QBLOCK = 128
DSA_Q = DSA_HEADS * DSA_HEAD_DIM
DSA_KV = DSA_KV_HEADS * DSA_HEAD_DIM
DSA_SPLITS = (DSA_Q, DSA_Q + DSA_KV, DSA_Q + 2 * DSA_KV, 2 * DSA_Q + 2 * DSA_KV,
              2 * DSA_Q + 2 * DSA_KV + DSA_IDX_HEADS * DSA_IDX_DIM,
              2 * DSA_Q + 2 * DSA_KV + DSA_IDX_HEADS * DSA_IDX_DIM + DSA_IDX_HEADS)
DSA_IN = DSA_SPLITS[-1] + DSA_IDX_DIM
DSA_IDX_SCALE = (DSA_IDX_HEADS * DSA_IDX_DIM) ** -0.5

LRU_WIDTH = D_MODEL
LRU_BLOCKS = 16
LRU_BLOCK_DIM = LRU_WIDTH // LRU_BLOCKS
LRU_CONV = 4
LRU_C = 8.0

RWKV_HEAD_DIM = 64
RWKV_HEADS = D_MODEL // RWKV_HEAD_DIM
RWKV_DECAY_LORA = 64
RWKV_AAA_LORA = 64
RWKV_GN_EPS = 64e-5

GLA_HEADS = 4
GLA_KEY_DIM = D_MODEL // 2
GLA_VAL_DIM = D_MODEL
GLA_DK = GLA_KEY_DIM // GLA_HEADS
GLA_DV = GLA_VAL_DIM // GLA_HEADS
GLA_GATE_RANK = 16
GLA_GATE_NORM = 16.0
GLA_CHUNK = 64
GLA_SPLITS = (GLA_KEY_DIM, 2 * GLA_KEY_DIM, 2 * GLA_KEY_DIM + GLA_VAL_DIM,
              2 * GLA_KEY_DIM + 2 * GLA_VAL_DIM)
GLA_IN = GLA_SPLITS[-1] + GLA_GATE_RANK

kernel_name = 'hybrid_dsa_rglru_rwkv7_gla_trunk'


def rms_norm(x, gain, eps=RMS_EPS):
    xf = x.astype(jnp.float32)
    y = xf * lax.rsqrt(jnp.mean(xf * xf, axis=-1, keepdims=True) + eps)
    return (y * gain.astype(jnp.float32)).astype(x.dtype)


def rope(x):
    seq, d = x.shape[1], x.shape[-1]
    half = d // 2
    inv_freq = ROPE_THETA ** (-jnp.arange(half, dtype=jnp.float32) / half)
    ang = jnp.arange(seq, dtype=jnp.float32)[:, None] * inv_freq[None, :]
    cos = jnp.cos(ang)[None, :, None, :]
    sin = jnp.sin(ang)[None, :, None, :]
    xf = x.astype(jnp.float32)
    x1, x2 = xf[..., :half], xf[..., half:]
    return jnp.concatenate([x1 * cos - x2 * sin, x2 * cos + x1 * sin], axis=-1).astype(x.dtype)


def dsa_mixer(h, w_in, q_gain, k_gain, w_out):
    f32 = jnp.float32
    bsz, seq, _ = h.shape
    q, k, v, g, qi, wi, ki = jnp.split(h @ w_in, DSA_SPLITS, axis=-1)
    q = rope(rms_norm(q.reshape(bsz, seq, DSA_HEADS, DSA_HEAD_DIM), q_gain))
    k = rope(rms_norm(k.reshape(bsz, seq, DSA_KV_HEADS, DSA_HEAD_DIM), k_gain))
    v = v.reshape(bsz, seq, DSA_KV_HEADS, DSA_HEAD_DIM)
    qi = rope(qi.reshape(bsz, seq, DSA_IDX_HEADS, DSA_IDX_DIM)).astype(f32)
    ki = rope(ki.reshape(bsz, seq, 1, DSA_IDX_DIM))[:, :, 0].astype(f32)
    wi = wi.astype(f32) * DSA_IDX_SCALE
    n_sel = min(DSA_TOPK, seq // 4)
    n_blk = seq // DSA_QBLOCK
    rep = DSA_HEADS // DSA_KV_HEADS
    scale = DSA_HEAD_DIM ** -0.5
    starts = jnp.arange(n_blk, dtype=jnp.int32) * DSA_QBLOCK
    key_pos = jnp.arange(seq, dtype=jnp.int32)

    def one_sequence(args):
        q_s, k_s, v_s, qi_s, wi_s, ki_s = args

        def one_block(blk):
            q_b, qi_b, wi_b, t0 = blk
            q_pos = t0 + jnp.arange(DSA_QBLOCK, dtype=jnp.int32)
            idx_logits = jnp.einsum('qhd,sd->qhs', qi_b, ki_s)
            score = jnp.einsum('qh,qhs->qs', wi_b, jax.nn.relu(idx_logits))
            score = jnp.where(key_pos[None, :] <= q_pos[:, None], score, -jnp.inf)
            _, sel = lax.top_k(score, n_sel)
            valid = sel <= q_pos[:, None]
            k_sel = k_s[sel]
            v_sel = v_s[sel]
            q_g = q_b.reshape(DSA_QBLOCK, DSA_KV_HEADS, rep, DSA_HEAD_DIM)
            s = jnp.einsum('qgrd,qkgd->qgrk', q_g, k_sel).astype(f32) * scale
            s = jnp.where(valid[:, None, None, :], s, -jnp.inf)
            p = jax.nn.softmax(s, axis=-1).astype(v_sel.dtype)
            o = jnp.einsum('qgrk,qkgd->qgrd', p, v_sel)
            return o.reshape(DSA_QBLOCK, DSA_Q)

        blocks = (q_s.reshape(n_blk, DSA_QBLOCK, DSA_HEADS, DSA_HEAD_DIM),
                  qi_s.reshape(n_blk, DSA_QBLOCK, DSA_IDX_HEADS, DSA_IDX_DIM),
                  wi_s.reshape(n_blk, DSA_QBLOCK, DSA_IDX_HEADS), starts)
        return lax.map(one_block, blocks).reshape(seq, DSA_Q)

    o = lax.map(one_sequence, (q, k, v, qi, wi, ki))
    return (o * jax.nn.silu(g)) @ w_out


def causal_depthwise_conv(u, w, b):
    width = w.shape[0]
    out = lax.conv_general_dilated(u, w[:, None, :].astype(u.dtype), window_strides=(1,),
                                   padding=[(width - 1, 0)],
                                   dimension_numbers=('NWC', 'WIO', 'NWC'),
                                   feature_group_count=u.shape[-1])
    return out + b


def rglru_mixer(h, w_in, conv_w, conv_b, gate_a_w, gate_a_b, gate_x_w, gate_x_b, lam, w_out):
    f32 = jnp.float32
    bsz, seq, _ = h.shape
    u, g = jnp.split(h @ w_in, 2, axis=-1)
    u = causal_depthwise_conv(u, conv_w, conv_b)
    u_blk = u.reshape(bsz, seq, LRU_BLOCKS, LRU_BLOCK_DIM)
    gr = jnp.einsum('bsnc,ncd->bsnd', u_blk, gate_a_w).reshape(bsz, seq, LRU_WIDTH) + gate_a_b
    gi = jnp.einsum('bsnc,ncd->bsnd', u_blk, gate_x_w).reshape(bsz, seq, LRU_WIDTH) + gate_x_b
    r = jax.nn.sigmoid(gr.astype(f32))
    i = jax.nn.sigmoid(gi.astype(f32))
    log_a = -LRU_C * r * jax.nn.softplus(-lam.astype(f32))
    a = jnp.exp(log_a)
    b = jnp.sqrt(-jnp.expm1(2.0 * log_a)) * (i * u.astype(f32))

    def combine(left, right):
        a_l, b_l = left
        a_r, b_r = right
        return a_l * a_r, a_r * b_l + b_r

    _, hs = lax.associative_scan(combine, (a, b), axis=1)
    return (hs.astype(h.dtype) * jax.nn.silu(g)) @ w_out


def rwkv7_mixer(h, mu, w_in, w0, w1, w2, a0, a1, a2, k_k, k_a, r_k, ln_w, ln_b, w_out):
    f32 = jnp.float32
    bsz, seq, dm = h.shape
    hshape = (RWKV_HEADS, RWKV_HEAD_DIM)
    h_prev = jnp.pad(h, ((0, 0), (1, 0), (0, 0)))[:, :-1]
    xs = h[None] + (h_prev - h)[None] * mu[:, None, None, :]
    r, k, v, g = jnp.einsum('nbsd,nde->nbse', xs[:4], w_in)
    w_log = -jax.nn.softplus(-(w0 + jnp.tanh(xs[4] @ w1) @ w2).astype(f32)) - 0.5
    decay = jnp.exp(-jnp.exp(w_log))
    a = jax.nn.sigmoid((a0 + (xs[5] @ a1) @ a2).astype(f32))

    def heads(t):
        return t.astype(f32).reshape(bsz, seq, *hshape)

    r, k, v, decay, a = heads(r), heads(k), heads(v), heads(decay), heads(a)
    kk = k * k_k.reshape(hshape)
    kk = kk / jnp.maximum(jnp.sqrt(jnp.sum(kk * kk, axis=-1, keepdims=True)), 1e-12)
    k = k * (1.0 + (a - 1.0) * k_a.reshape(hshape))

    def step(state, inp):
        r_t, w_t, k_t, v_t, a_t, b_t = inp
        sa = jnp.einsum('bhij,bhj->bhi', state, a_t)
        state = (state * w_t[:, :, None, :] + sa[..., None] * b_t[:, :, None, :]
                 + v_t[..., None] * k_t[:, :, None, :])
        return state, jnp.einsum('bhij,bhj->bhi', state, r_t)

    s0 = jnp.zeros((bsz, RWKV_HEADS, RWKV_HEAD_DIM, RWKV_HEAD_DIM), f32)
    seq_major = tuple(jnp.moveaxis(t, 1, 0) for t in (r, decay, k, v, -kk, kk * a))
    _, y = lax.scan(step, s0, seq_major)
    y = jnp.moveaxis(y, 0, 1)
    mean = jnp.mean(y, axis=-1, keepdims=True)
    var = jnp.mean(jnp.square(y - mean), axis=-1, keepdims=True)
    y = (y - mean) * lax.rsqrt(var + RWKV_GN_EPS) * ln_w.reshape(hshape) + ln_b.reshape(hshape)
    y = y + jnp.sum(r * k * r_k, axis=-1, keepdims=True) * v
    y = y.reshape(bsz, seq, dm).astype(h.dtype) * jax.nn.silu(g)
    return y @ w_out


def gla_mixer(h, w_in, alpha_w2, alpha_b, norm_gain, w_out):
    f32 = jnp.float32
    bsz, seq, _ = h.shape
    q, k, v, g, a_low = jnp.split(h @ w_in, GLA_SPLITS, axis=-1)
    log_alpha = jax.nn.log_sigmoid((a_low @ alpha_w2 + alpha_b).astype(f32)) / GLA_GATE_NORM
    n_chunk = seq // GLA_CHUNK

    def chunks(t, d):
        return t.astype(f32).reshape(bsz, n_chunk, GLA_CHUNK, GLA_HEADS, d)

    q = chunks(q, GLA_DK) * GLA_DK ** -0.5
    k = chunks(k, GLA_DK)
    v = chunks(v, GLA_DV)
    cum = lax.cumsum(chunks(log_alpha, GLA_DK), axis=2)
    last = cum[:, :, -1]
    q_dec = q * jnp.exp(cum)
    k_inv = k * jnp.exp(-cum)
    k_end = k * jnp.exp(last[:, :, None] - cum)
    causal = jnp.tril(jnp.ones((GLA_CHUNK, GLA_CHUNK), dtype=bool))
    att = jnp.where(causal, jnp.einsum('bnihd,bnjhd->bnhij', q_dec, k_inv), 0.0)
    o_intra = jnp.einsum('bnhij,bnjhe->bnihe', att, v)
    chunk_kv = jnp.einsum('bnjhd,bnjhe->bnhde', k_end, v)

    def carry_state(state, inp):
        kv_c, last_c = inp
        return state * jnp.exp(last_c)[..., None] + kv_c, state

    s0 = jnp.zeros((bsz, GLA_HEADS, GLA_DK, GLA_DV), f32)
    _, s_prev = lax.scan(carry_state, s0, (jnp.moveaxis(chunk_kv, 1, 0), jnp.moveaxis(last, 1, 0)))
    o_inter = jnp.einsum('bnihd,nbhde->bnihe', q_dec, s_prev)
    o = (o_intra + o_inter).reshape(bsz, seq, GLA_HEADS, GLA_DV)
    o = rms_norm(o, norm_gain).reshape(bsz, seq, GLA_VAL_DIM).astype(h.dtype)
    return (o * jax.nn.silu(g)) @ w_out


def setup_inputs(seed: int = 0) -> dict:
    key = jax.random.key(seed)
    keys = iter(jax.random.split(key, 64))
    f32 = jnp.float32
    D, R = D_MODEL, N_REPEAT

    def nrm(shape, scale):
        return jax.random.normal(next(keys), shape, f32) * scale

    def unif(shape, lo, hi):
        return jax.random.uniform(next(keys), shape, f32, lo, hi)

    lru_u = unif((R, LRU_WIDTH), 0.9, 0.999)
    lru_base = lru_u ** (1.0 / LRU_C)
    return {
        'x': nrm((BATCH, SEQ, D), 1.0),
        'c': nrm((BATCH, D), 1.0),
        'ln_gain': 1.0 + nrm((DEPTH, D), 0.02),
        'mod_w': nrm((DEPTH, D, 3 * D), 0.5 * D ** -0.5),
        'mod_b': nrm((DEPTH, 3 * D), 0.02),
        'dsa_w_in': nrm((R, D, DSA_IN), D ** -0.5),
        'dsa_q_gain': 1.0 + nrm((R, DSA_HEAD_DIM), 0.02),
        'dsa_k_gain': 1.0 + nrm((R, DSA_HEAD_DIM), 0.02),
        'dsa_w_out': nrm((R, DSA_Q, D), DSA_Q ** -0.5),
        'lru_w_in': nrm((R, D, 2 * LRU_WIDTH), D ** -0.5),
        'lru_conv_w': nrm((R, LRU_CONV, LRU_WIDTH), LRU_CONV ** -0.5),
        'lru_conv_b': nrm((R, LRU_WIDTH), 0.02),
        'lru_gate_a_w': nrm((R, LRU_BLOCKS, LRU_BLOCK_DIM, LRU_BLOCK_DIM), LRU_BLOCK_DIM ** -0.5),
        'lru_gate_a_b': nrm((R, LRU_WIDTH), 0.02),
        'lru_gate_x_w': nrm((R, LRU_BLOCKS, LRU_BLOCK_DIM, LRU_BLOCK_DIM), LRU_BLOCK_DIM ** -0.5),
        'lru_gate_x_b': nrm((R, LRU_WIDTH), 0.02),
        'lru_lambda': jnp.log(lru_base) - jnp.log1p(-lru_base),
        'lru_w_out': nrm((R, LRU_WIDTH, D), LRU_WIDTH ** -0.5),
        'rwkv_mu': unif((R, 6, D), 0.0, 1.0),
        'rwkv_w_in': nrm((R, 4, D, D), D ** -0.5),
        'rwkv_w0': unif((R, D), -6.0, -1.0),
        'rwkv_w1': nrm((R, D, RWKV_DECAY_LORA), D ** -0.5),
        'rwkv_w2': nrm((R, RWKV_DECAY_LORA, D), 0.1 * RWKV_DECAY_LORA ** -0.5),
        'rwkv_a0': nrm((R, D), 0.1),
        'rwkv_a1': nrm((R, D, RWKV_AAA_LORA), D ** -0.5),
        'rwkv_a2': nrm((R, RWKV_AAA_LORA, D), 0.1 * RWKV_AAA_LORA ** -0.5),
        'rwkv_k_k': 0.85 + nrm((R, D), 0.02),
        'rwkv_k_a': 1.0 + nrm((R, D), 0.02),
        'rwkv_r_k': nrm((R, RWKV_HEADS, RWKV_HEAD_DIM), 0.1),
        'rwkv_ln_w': 1.0 + nrm((R, D), 0.02),
        'rwkv_ln_b': nrm((R, D), 0.02),
        'rwkv_w_out': nrm((R, D, D), D ** -0.5),
        'gla_w_in': nrm((R, D, GLA_IN), D ** -0.5),
        'gla_alpha_w2': nrm((R, GLA_GATE_RANK, GLA_KEY_DIM), GLA_GATE_RANK ** -0.5),
        'gla_alpha_b': nrm((R, GLA_KEY_DIM), 0.1),
        'gla_norm_gain': 1.0 + nrm((R, GLA_DV), 0.02),
        'gla_w_out': nrm((R, GLA_VAL_DIM, D), GLA_VAL_DIM ** -0.5),
    }


def reference(x, c, ln_gain, mod_w, mod_b,
              dsa_w_in, dsa_q_gain, dsa_k_gain, dsa_w_out,
              lru_w_in, lru_conv_w, lru_conv_b, lru_gate_a_w, lru_gate_a_b,
              lru_gate_x_w, lru_gate_x_b, lru_lambda, lru_w_out,
              rwkv_mu, rwkv_w_in, rwkv_w0, rwkv_w1, rwkv_w2, rwkv_a0, rwkv_a1, rwkv_a2,
              rwkv_k_k, rwkv_k_a, rwkv_r_k, rwkv_ln_w, rwkv_ln_b, rwkv_w_out,
              gla_w_in, gla_alpha_w2, gla_alpha_b, gla_norm_gain, gla_w_out):
    c_act = jax.nn.silu(c)
    for layer in range(DEPTH):
        mixer, r = layer % N_MIXERS, layer // N_MIXERS
        mod = c_act @ mod_w[layer] + mod_b[layer]
        shift, scale, gate = jnp.split(mod, 3, axis=-1)
        h = rms_norm(x, ln_gain[layer]) * (1.0 + scale[:, None, :]) + shift[:, None, :]
        if mixer == 0:
            y = dsa_mixer(h, dsa_w_in[r], dsa_q_gain[r], dsa_k_gain[r], dsa_w_out[r])
        elif mixer == 1:
            y = rglru_mixer(h, lru_w_in[r], lru_conv_w[r], lru_conv_b[r], lru_gate_a_w[r],
                            lru_gate_a_b[r], lru_gate_x_w[r], lru_gate_x_b[r], lru_lambda[r],
                            lru_w_out[r])
        elif mixer == 2:
            y = rwkv7_mixer(h, rwkv_mu[r], rwkv_w_in[r], rwkv_w0[r], rwkv_w1[r], rwkv_w2[r],
                            rwkv_a0[r], rwkv_a1[r], rwkv_a2[r], rwkv_k_k[r], rwkv_k_a[r],
                            rwkv_r_k[r], rwkv_ln_w[r], rwkv_ln_b[r], rwkv_w_out[r])
        else:
            y = gla_mixer(h, gla_w_in[r], gla_alpha_w2[r], gla_alpha_b[r], gla_norm_gain[r],
                          gla_w_out[r])
        x = x + gate[:, None, :] * y
    return x
```

```python
from contextlib import ExitStack
import numpy as np
import concourse.bass as bass
import concourse.mybir as mybir
from concourse.bass_utils import run_bass_kernel_spmd

F32 = mybir.dt.float32
BF16 = mybir.dt.bfloat16
AF = mybir.ActivationFunctionType
ALU = mybir.AluOpType
AX = mybir.AxisListType

SAME_ENGINE_SYNC = True
LAZY_SIGNAL = ("pe",)
MAX_PENDING = 8
NO_SELF_SYNC = ("pe",)
N_DMA_SEMS = 8


class Buf:
    __slots__ = ("name", "w", "r")

    def __init__(self, name):
        self.name = name
        self.w = None
        self.r = {}


class Prog:
    ENG = ("pe", "act", "dve", "pool", "sp")

    def __init__(self, nc):
        self.nc = nc
        self.ops = {e: [] for e in self.ENG}
        self.count = {e: 0 for e in self.ENG}
        self.known = {e: {} for e in self.ENG}
        self.sems = {}
        self.dma_n = {e: 0 for e in self.ENG}
        self._stack = []
        self._last_pos = None
        self.recs = {e: [] for e in self.ENG}

    def alloc_sems(self, stack):
        for e in self.ENG:
            self.sems[("e", e)] = stack.enter_context(self.nc.semaphore("s_" + e))
        for e in ("sp", "act", "pool"):
            for i in range(N_DMA_SEMS):
                self.sems[("d", e, i)] = stack.enter_context(
                    self.nc.semaphore("d_%s%d" % (e, i)))

    def _deps(self, eng, reads, writes):
        need = {}
        def add(tok):
            if tok is None:
                return
            k, v = tok
            if need.get(k, 0) < v:
                need[k] = v
        for b in reads:
            add(b.w)
        for b in writes:
            add(b.w)
            for k, v in b.r.items():
                add((k, v))
        waits = []
        kn = self.known[eng]
        for k, v in need.items():
            if k == ("e", eng) and (not SAME_ENGINE_SYNC or eng in NO_SELF_SYNC):
                continue
            if kn.get(k, 0) < v:
                kn[k] = v
                waits.append((k, v))
                if k[0] == "e":
                    self.recs[k[1]][v - 1]["signal"] = True
        return waits

    def _commit(self, tok, reads, writes):
        k, v = tok
        for b in writes:
            b.w = tok
            b.r = {}
        for b in reads:
            if b.r.get(k, 0) < v:
                b.r[k] = v

    def op(self, eng, fn, reads=(), writes=(), pos=None):
        waits = self._deps(eng, reads, writes)
        if eng == "pe":
            if (pos is not None or self._last_pos is not None) and self.count["pe"] > 0:
                k, v = ("e", "pe"), self.count["pe"]
                if self.known["pe"].get(k, 0) < v:
                    self.known["pe"][k] = v
                    waits.append((k, v))
                    self.recs["pe"][v - 1]["signal"] = True
            self._last_pos = pos
        self.count[eng] += 1
        tok = (("e", eng), self.count[eng])
        rec = {"fn": fn, "waits": waits, "signal": eng not in LAZY_SIGNAL}
        self.ops[eng].append(rec)
        self.recs[eng].append(rec)
        self._commit(tok, reads, writes)

    def dma(self, q, out, in_, reads=(), writes=(), **kw):
        n = self.dma_n[q]
        self.dma_n[q] += 1
        slot = n % N_DMA_SEMS
        key = ("d", q, slot)
        val = 16 * (n // N_DMA_SEMS + 1)
        waits = self._deps(q, reads, writes)
        if val > 16:
            kn = self.known[q]
            if kn.get(key, 0) < val - 16:
                kn[key] = val - 16
                waits.append((key, val - 16))
        sems = self.sems

        def emit(e, waits=waits, key=key, out=out, in_=in_, kw=kw):
            for k, v in waits:
                e.wait_ge(sems[k], v)
            e.dma_start(out=out, in_=in_, **kw).then_inc(sems[key], 16)
        self.ops[q].append(emit)
        self._commit((key, val), reads, writes)

    def finish(self, final_bufs):
        waits = self._deps("sp", final_bufs, [])
        sems = self.sems

        def emit(e, waits=waits):
            for k, v in waits:
                e.wait_ge(sems[k], v)
        self.ops["sp"].append(emit)

    def _run(self, eng, e):
        sems = self.sems
        pending = 0
        n = len(self.ops[eng])
        last_rec = None
        for it in self.ops[eng]:
            if isinstance(it, dict):
                last_rec = it
        for it in self.ops[eng]:
            if not isinstance(it, dict):
                it(e)
                continue
            for k, v in it["waits"]:
                e.wait_ge(sems[k], v)
            ins = it["fn"](e)
            pending += 1
            if it["signal"] or it is last_rec or pending >= MAX_PENDING:
                ins.then_inc(sems[("e", eng)], pending)
                pending = 0

    def emit(self):
        nc = self.nc
        with nc.Block() as block:
            @block.tensor
            def _(e):
                self._run("pe", e)

            @block.scalar
            def _(e):
                self._run("act", e)

            @block.vector
            def _(e):
                self._run("dve", e)

            @block.gpsimd
            def _(e):
                self._run("pool", e)

            @block.sync
            def _(e):
                self._run("sp", e)


def _barrier(self):
    targets = {}
    for e in self.ENG:
        if self.count[e]:
            targets[("e", e)] = self.count[e]
            self.recs[e][self.count[e] - 1]["signal"] = True
    for q in ("sp", "act", "pool"):
        n = self.dma_n[q]
        for s in range(min(n, N_DMA_SEMS)):
            last = ((n - 1 - s) // N_DMA_SEMS) * N_DMA_SEMS + s
            targets[("d", q, s)] = 16 * (last // N_DMA_SEMS + 1)
    sems = self.sems
    for e in self.ENG:
        waits = []
        kn = self.known[e]
        for k, v in targets.items():
            if kn.get(k, 0) < v:
                kn[k] = v
                waits.append((k, v))

        def emit(eo, waits=waits):
            for k, v in waits:
                eo.wait_ge(sems[k], v)
        self.ops[e].append(emit)


Prog.barrier = _barrier


DEBUG_OUT = False
S = 2048
D = 1024
NB = 2
NT = S // 128
EPS = 1e-6


class T:
    def __init__(self, t, name, nslots=0):
        self.t = t
        self.b = Buf(name)
        self.bs = [Buf("%s_%d" % (name, i)) for i in range(nslots)]

    def __getitem__(self, k):
        return self.t[k]


class KB:
    def __init__(self, nc, P, st):
        self.nc, self.P, self.st = nc, P, st
        self.scopes = [st]
        self.rr = 0
        self.n = 0

    def push(self):
        s = ExitStack()
        self.scopes.append(s)
        return s

    def pop(self):
        self.P.barrier()
        s = self.scopes.pop()
        s.close()

    def sb(self, name, shape, dt=F32, nslots=0):
        self.n += 1
        nm = "%s_%d" % (name, self.n)
        t = self.scopes[-1].enter_context(self.nc.sbuf_tensor(nm, list(shape), dt))
        return T(t, nm, nslots)

    def psum(self, name, shape, dt=F32):
        t = self.scopes[-1].enter_context(self.nc.psum_tensor(name, list(shape), dt))
        return T(t, name)

    def dram(self, name, shape, dt=F32):
        t = self.nc.dram_tensor(name, list(shape), dt, kind=("ExternalOutput" if DEBUG_OUT else "Internal"))
        o = T(t.ap(), name)
        return o


def bufs(objs):
    out = []
    for o in objs:
        out.append(o.b if isinstance(o, T) else o)
    return out


def build_common(kb, consts):
    nc, P = kb.nc, kb.P
    g = {}
    g["ident"] = kb.sb("ident", [128, 128], F32)
    P.dma("sp", g["ident"][:], consts["ident"][:, :], writes=[g["ident"].b])
    g["identb"] = kb.sb("identb", [128, 128], BF16)
    P.op("dve", lambda e: e.tensor_copy(g["identb"][:], g["ident"][:]),
         reads=[g["ident"].b], writes=[g["identb"].b])
    g["ones"] = kb.sb("ones", [128, 128], F32)
    P.op("dve", lambda e: e.memset(g["ones"][:], 1.0), writes=[g["ones"].b])
    g["ps"] = [kb.psum("ps%d" % i, [128, 512], F32) for i in range(6)]
    g["stage_n"] = 0
    g["xt"] = [kb.sb("xt", [128, 1024], F32) for i in range(2)]
    g["xn"] = [kb.sb("xn", [128, 1024], F32) for i in range(1)]
    g["junk"] = kb.sb("junk", [128, 1024], F32)
    g["st"] = [kb.sb("stt", [128, 4], F32) for i in range(2)]
    g["gatebc"] = [kb.sb("gatebc", [128, 1024], F32) for b in range(NB)]
    g["AB"] = kb.sb("AB", [128, NB, 2, 8], F32)
    g["cT"] = kb.sb("cT", [128, 8, NB], F32)
    g["gainT"] = kb.sb("gainT", [128, 8], F32)
    g["mbT"] = kb.sb("mbT", [128, 16], F32)
    g["xo"] = [kb.sb("xo", [128, 1024], F32) for i in range(2)]
    g["cnt"] = 0
    return g


def begin_load(kb, g):
    kb.push()
    g["stage"] = [kb.sb("stage", [128, 8, 256], F32) for i in range(2)]
    g["crep"] = kb.sb("crep", [128, 8, 128], F32)
    g["mbrow"] = kb.sb("mbrow", [1, 256], F32)


def end_load(kb, g):
    kb.pop()


def load_w(kb, g, dst, dcol0, src, ncols, krows=1024):
    P = kb.P
    kc = krows // 128
    c0 = 0
    engs = ("dve", "pool")
    while c0 < ncols:
        n = min(256, ncols - c0)
        stg = g["stage"][g["stage_n"] % 2]
        g["stage_n"] += 1
        q = ("sp", "act")[g["stage_n"] % 2]
        P.dma(q, stg[:, 0:kc, 0:n], src[:, c0:c0 + n].rearrange("(k p) n -> p k n", p=128),
              writes=[stg.b])
        eng = engs[g["stage_n"] % 2]
        d_ap = dst[:, 0:kc, dcol0 + c0:dcol0 + c0 + n]
        s_ap = stg[:, 0:kc, 0:n]
        P.op(eng, lambda e, d_ap=d_ap, s_ap=s_ap: e.tensor_copy(d_ap, s_ap),
             reads=[stg.b], writes=[dst.b])
        c0 += n


def compute_mod(kb, g, layer, W):
    nc, P = kb.nc, kb.P
    ps = g["ps"]
    crep, mbrow = g["crep"], g["mbrow"]
    if not g.get("c_done"):
        g["c_done"] = True
        for b in range(NB):
            P.dma("sp", g["cT"][:, :, b], W["c"][b].rearrange("(k p) -> p k", p=128),
                  writes=[g["cT"].b], allow_slow_non_contiguous=True)
        P.op("act", lambda e: e.activation(g["cT"][:], g["cT"][:], AF.Silu),
             reads=[g["cT"].b], writes=[g["cT"].b])
    P.dma("sp", g["gainT"][:], W["ln_gain"][layer].rearrange("(k p) -> p k", p=128),
          writes=[g["gainT"].b], allow_slow_non_contiguous=True)
    P.dma("sp", g["mbT"][:], W["mod_b"][layer, 0:2048].rearrange("(k p) -> p k", p=128),
          writes=[g["mbT"].b], allow_slow_non_contiguous=True)
    pf = ps[2]
    for nb in range(12):
        stg = g["stage"][g["stage_n"] % 2]
        g["stage_n"] += 1
        P.dma("sp", stg[:], W["mod_w"][layer][:, nb * 256:(nb + 1) * 256].rearrange(
            "(k p) n -> p k n", p=128), writes=[stg.b])
        if nb < 8:
            for jj in range(2):
                j = nb * 2 + jj
                for k in range(8):
                    P.op("pe", lambda e, j=j, jj=jj, k=k, stg=stg: e.matmul(
                        pf[:, j * 2:j * 2 + 2], stg[:, k, jj * 128:(jj + 1) * 128], g["cT"][:, k, :],
                        start=(k == 0), stop=(k == 7)),
                        reads=[g["cT"].b, stg.b], writes=[pf.b])
        else:
            P.dma("act", mbrow[:], W["mod_b"][layer:layer + 1, nb * 256:(nb + 1) * 256],
                  writes=[mbrow.b])
            for b in range(NB):
                pt = ps[b]
                for k in range(8):
                    P.op("dve", lambda e, b=b, k=k: e.tensor_copy(
                        crep[:, k, :], g["cT"][:, k, b:b + 1].to_broadcast([128, 128])),
                        reads=[g["cT"].b], writes=[crep.b])
                for k in range(8):
                    P.op("pe", lambda e, pt=pt, k=k, stg=stg: e.matmul(
                        pt[:, 0:256], crep[:, k, :], stg[:, k, :], start=(k == 0), stop=False),
                        reads=[crep.b, stg.b], writes=[pt.b])
                P.op("pe", lambda e, pt=pt: e.matmul(
                    pt[:, 0:256], g["ones"][0:1, :], mbrow[0:1, :], start=False, stop=True),
                    reads=[g["ones"].b, mbrow.b], writes=[pt.b])
                P.op("act", lambda e, pt=pt, b=b, nb=nb: e.copy(
                    g["gatebc"][b][:, (nb - 8) * 256:(nb - 7) * 256], pt[:, 0:256]),
                    reads=[pt.b], writes=[g["gatebc"][b].b])
    pfv = pf[:, 0:32].rearrange("p (j b) -> p j b", b=2)
    for b in range(NB):
        P.op("dve", lambda e, b=b: e.tensor_tensor(
            g["AB"][:, b, 1, :], pfv[:, 0:8, b], g["mbT"][:, 0:8], ALU.add),
            reads=[pf.b, g["mbT"].b], writes=[g["AB"].b])
        P.op("dve", lambda e, b=b: e.tensor_tensor(
            g["AB"][:, b, 0, :], pfv[:, 8:16, b], g["mbT"][:, 8:16], ALU.add),
            reads=[pf.b, g["mbT"].b], writes=[g["AB"].b])
        P.op("dve", lambda e, b=b: e.scalar_tensor_tensor(
            g["AB"][:, b, 0, :], g["AB"][:, b, 0, :], 1.0, g["gainT"][:], ALU.add, ALU.mult),
            reads=[g["AB"].b, g["gainT"].b], writes=[g["AB"].b])


def front_tile(kb, g, b, tt, x_src, hT, hT_buf, pbanks=None, xi=None):
    nc, P = kb.nc, kb.P
    i = g["cnt"] % 2
    g["cnt"] += 1
    if xi is not None:
        i = xi
    xt, xn, stt = g["xt"][i], g["xn"][0], g["st"][i]
    P.dma("sp", xt[:], x_src[b, tt * 128:(tt + 1) * 128, :], reads=[x_src.b], writes=[xt.b])
    P.op("dve", lambda e: e.memset(stt[:], 0.0), writes=[stt.b])
    P.op("act", lambda e: e.activation(g["junk"][:], xt[:], AF.Square, accum_out=stt[:, 0:1]),
         reads=[xt.b, stt.b], writes=[g["junk"].b, stt.b])
    P.op("dve", lambda e: e.tensor_scalar(stt[:, 1:2], stt[:, 0:1], 1.0 / D, EPS, ALU.mult, ALU.add),
         reads=[stt.b], writes=[stt.b])
    P.op("act", lambda e: e.activation(stt[:, 1:2], stt[:, 1:2], AF.Sqrt),
         reads=[stt.b], writes=[stt.b])
    P.op("dve", lambda e: e.reciprocal(stt[:, 2:3], stt[:, 1:2]),
         reads=[stt.b], writes=[stt.b])
    P.op("dve", lambda e: e.tensor_scalar(xn[:], xt[:], stt[:, 2:3], None, ALU.mult),
         reads=[xt.b, stt.b], writes=[xn.b])
    pa, pb = pbanks if pbanks is not None else (g["ps"][0], g["ps"][1])
    for c in range(8):
        pt = pa if c < 4 else pb
        P.op("pe", lambda e, pt=pt, c=c: e.transpose(
            pt[:, (c % 4) * 128:(c % 4 + 1) * 128], xn[:, c * 128:(c + 1) * 128], g["ident"][:]),
            reads=[xn.b, g["ident"].b], writes=[pt.b])
    for c in range(8):
        pt = pa if c < 4 else pb
        src = pt[:, (c % 4) * 128:(c % 4 + 1) * 128]
        dst = hT[:, c, tt * 128:(tt + 1) * 128]
        A = g["AB"][:, b, 0, c:c + 1]
        Bv = g["AB"][:, b, 1, c:c + 1]
        if c % 2 == 0:
            P.op("act", lambda e, src=src, dst=dst, A=A, Bv=Bv: e.activation(
                dst, src, AF.Identity, bias=Bv, scale=A),
                reads=[pt.b, g["AB"].b], writes=[hT_buf])
        else:
            P.op("dve", lambda e, src=src, dst=dst, A=A, Bv=Bv: e.tensor_scalar(
                dst, src, A, Bv, ALU.mult, ALU.add),
                reads=[pt.b, g["AB"].b], writes=[hT_buf])


def back_tile(kb, g, b, tt, zT, zT_bufs, wout, x_src, x_dst, pbanks=None, xi=None):
    nc, P = kb.nc, kb.P
    i = g["cnt"] % 2
    g["cnt"] += 1
    if xi is not None:
        i = xi
    xt, xo = g["xt"][i], g["xo"][i]
    P.dma("act", xt[:], x_src[b, tt * 128:(tt + 1) * 128, :], reads=[x_src.b], writes=[xt.b])
    for half in range(2):
        pt = g["ps"][2 + half] if pbanks is None else pbanks[half]
        for c in range(8):
            P.op("pe", lambda e, pt=pt, c=c, half=half: e.matmul(
                pt[:], zT[:, c, tt * 128:(tt + 1) * 128], wout[:, c, half * 512:(half + 1) * 512],
                start=(c == 0), stop=(c == 7)),
                reads=list(zT_bufs) + [wout.b], writes=[pt.b])
        P.op("dve", lambda e, pt=pt, half=half: e.tensor_tensor(
            xo[:, half * 512:(half + 1) * 512], pt[:],
            g["gatebc"][b][:, half * 512:(half + 1) * 512], ALU.mult),
            reads=[pt.b, g["gatebc"][b].b], writes=[xo.b])
    P.op("pool", lambda e: e.tensor_tensor(xo[:], xo[:], xt[:], ALU.add),
         reads=[xo.b, xt.b], writes=[xo.b])
    P.dma("sp", x_dst[b, tt * 128:(tt + 1) * 128, :], xo[:], reads=[xo.b], writes=[x_dst.b])


def layer_lru(kb, g, W, x_src, x_dst):
    nc, P = kb.nc, kb.P
    kb.push()
    win = kb.sb("lru_win", [128, 8, 2048], BF16)
    wout = kb.sb("lru_wout", [128, 8, 1024], BF16)
    gwb = [kb.sb("lru_gwb", [128, 8, 128], BF16) for _ in range(2)]
    vec = kb.sb("lru_vec", [128, 10, 8], F32)
    begin_load(kb, g)
    compute_mod(kb, g, 1, W)
    load_w(kb, g, win, 0, W["lru_w_in"][0], 2048)
    load_w(kb, g, wout, 0, W["lru_w_out"][0], 1024)
    for i, nm in enumerate(("lru_gate_a_w", "lru_gate_x_w")):
        stg = g["stage"][g["stage_n"] % 2]
        g["stage_n"] += 1
        P.op("pool", lambda e, stg=stg: e.memset(stg[:], 0.0), writes=[stg.b])
        src = W[nm][0]
        for hh in range(2):
            P.dma("sp", stg[hh * 64:(hh + 1) * 64, :, hh * 64:(hh + 1) * 64],
                  src.rearrange("(j n2) c d -> n2 c j d", n2=2)[hh],
                  writes=[stg.b])
        P.op("dve", lambda e, i=i, stg=stg: e.tensor_copy(gwb[i][:], stg[:, :, 0:128]),
             reads=[stg.b], writes=[gwb[i].b])
    names = ["lru_conv_b", "lru_gate_a_b", "lru_gate_x_b", "lru_lambda"]
    for i, nm in enumerate(names):
        P.dma("sp", vec[:, i, :], W[nm][0].rearrange("(k p) -> p k", p=128),
              writes=[vec.b], allow_slow_non_contiguous=True)
    for j in range(4):
        P.dma("sp", vec[:, 4 + j, :], W["lru_conv_w"][0, j].rearrange("(k p) -> p k", p=128),
              writes=[vec.b], allow_slow_non_contiguous=True)
    P.op("act", lambda e: e.activation(vec[:, 8, :], vec[:, 3, :], AF.Exp, scale=-1.0),
         reads=[vec.b], writes=[vec.b])
    P.op("act", lambda e: e.activation(vec[:, 8, :], vec[:, 8, :], AF.Ln, bias=1.0),
         reads=[vec.b], writes=[vec.b])
    P.op("dve", lambda e: e.tensor_scalar(vec[:, 8, :], vec[:, 8, :], -8.0, None, ALU.mult),
         reads=[vec.b], writes=[vec.b])

    end_load(kb, g)
    TH = 1024
    hT = kb.sb("hT", [128, 8, S], BF16)
    zT = kb.sb("zT", [128, 8, S], BF16)
    upad = kb.sb("upad", [128, 3 + TH], F32)
    uc = kb.sb("uc", [128, TH], F32)
    ucb = kb.sb("ucb", [128, TH], BF16)
    rr = kb.sb("rr", [128, TH], F32)
    ii = kb.sb("ii", [128, TH], F32)
    aa = kb.sb("aa", [128, TH], F32)
    bb = kb.sb("bb", [128, TH], F32)
    carry = kb.sb("carry", [128, 4], F32)
    ps = g["ps"]
    for b in range(NB):
        for tt in range(NT):
            front_tile(kb, g, b, tt, x_src, hT, hT.b)
        for j in range(8):
          for hf in range(S // TH):
            t0 = hf * TH
            if hf == 0:
                P.op("dve", lambda e: e.memset(upad[:, 0:3], 0.0), writes=[upad.b])
            else:
                P.op("dve", lambda e: e.tensor_copy(carry[:, 0:3], upad[:, TH:TH + 3]),
                     reads=[upad.b], writes=[carry.b])
                P.op("dve", lambda e: e.tensor_copy(upad[:, 0:3], carry[:, 0:3]),
                     reads=[carry.b], writes=[upad.b])
                P.op("dve", lambda e: e.tensor_copy(carry[:, 3:4], rr[:, TH - 1:TH]),
                     reads=[rr.b], writes=[carry.b])
            for q in range(TH // 512):
                pt = ps[2 + q]
                for c in range(8):
                    P.op("pe", lambda e, pt=pt, c=c, q=q, j=j, t0=t0: e.matmul(
                        pt[:], win[:, c, j * 128:(j + 1) * 128], hT[:, c, t0 + q * 512:t0 + (q + 1) * 512],
                        start=(c == 0), stop=(c == 7)),
                        reads=[win.b, hT.b], writes=[pt.b])
                P.op("act", lambda e, pt=pt, q=q: e.copy(upad[:, 3 + q * 512:3 + (q + 1) * 512], pt[:]),
                     reads=[pt.b], writes=[upad.b])
            P.op("dve", lambda e, j=j: e.tensor_scalar(
                uc[:], upad[:, 0:TH], vec[:, 4, j:j + 1], vec[:, 0, j:j + 1], ALU.mult, ALU.add),
                reads=[upad.b, vec.b], writes=[uc.b])
            for k in range(1, 4):
                P.op("dve", lambda e, j=j, k=k: e.scalar_tensor_tensor(
                    uc[:], upad[:, k:k + TH], vec[:, 4 + k, j:j + 1], uc[:], ALU.mult, ALU.add),
                    reads=[upad.b, vec.b, uc.b], writes=[uc.b])
            P.op("pool", lambda e: e.tensor_copy(ucb[:], uc[:]), reads=[uc.b], writes=[ucb.b])
            for gi, dst, bi in ((0, aa, 1), (1, ii, 2)):
                for q in range(TH // 512):
                    pt = ps[4 + q]
                    P.op("pe", lambda e, pt=pt, q=q, gi=gi, j=j: e.matmul(
                        pt[:], gwb[gi][:, j, :], ucb[:, q * 512:(q + 1) * 512], start=True, stop=True),
                        reads=[gwb[gi].b, ucb.b], writes=[pt.b])
                    P.op("act", lambda e, pt=pt, q=q, dst=dst, bi=bi, j=j: e.activation(
                        dst[:, q * 512:(q + 1) * 512], pt[:], AF.Sigmoid, bias=vec[:, bi, j:j + 1]),
                        reads=[pt.b, vec.b], writes=[dst.b])
            P.op("act", lambda e, j=j: e.activation(aa[:], aa[:], AF.Exp, scale=vec[:, 8, j:j + 1]),
                 reads=[aa.b, vec.b], writes=[aa.b])
            P.op("dve", lambda e: e.tensor_tensor(bb[:], aa[:], aa[:], ALU.mult),
                 reads=[aa.b], writes=[bb.b])
            P.op("act", lambda e: e.activation(bb[:], bb[:], AF.Sqrt, bias=1.0, scale=-1.0),
                 reads=[bb.b], writes=[bb.b])
            P.op("pool", lambda e: e.tensor_tensor(ii[:], ii[:], uc[:], ALU.mult),
                 reads=[ii.b, uc.b], writes=[ii.b])
            P.op("dve", lambda e: e.tensor_tensor(bb[:], bb[:], ii[:], ALU.mult),
                 reads=[bb.b, ii.b], writes=[bb.b])
            init = 0.0 if hf == 0 else carry[:, 3:4]
            P.op("dve", lambda e, init=init: e.tensor_tensor_scan(rr[:], aa[:], bb[:], init, ALU.mult, ALU.add),
                 reads=[aa.b, bb.b, carry.b], writes=[rr.b])
            for q in range(TH // 512):
                pt = ps[2 + q]
                for c in range(8):
                    P.op("pe", lambda e, pt=pt, c=c, q=q, j=j, t0=t0: e.matmul(
                        pt[:], win[:, c, 1024 + j * 128:1024 + (j + 1) * 128],
                        hT[:, c, t0 + q * 512:t0 + (q + 1) * 512], start=(c == 0), stop=(c == 7)),
                        reads=[win.b, hT.b], writes=[pt.b])
                P.op("act", lambda e, pt=pt, q=q: e.activation(
                    ii[:, q * 512:(q + 1) * 512], pt[:], AF.Silu),
                    reads=[pt.b], writes=[ii.b])
            P.op("dve", lambda e, j=j, t0=t0: e.tensor_tensor(zT[:, j, t0:t0 + TH], rr[:], ii[:], ALU.mult),
                 reads=[rr.b, ii.b], writes=[zT.b])
        for tt in range(NT):
            back_tile(kb, g, b, tt, zT, [zT.b], wout, x_src, x_dst)
    kb.pop()


WSHAPES = None


def make_consts():
    c = {}
    c["ident"] = np.eye(128, dtype=np.float32)
    c.update(gla_consts())
    c.update(dsa_consts())
    c.update(rwkv_consts())
    return c


def build(layers, wshapes, consts_np):
    nc = bass.Bass("TRN2", target_bir_lowering=False)
    W = {}
    for k, shp in wshapes.items():
        if k == "x":
            continue
        W[k] = nc.dram_tensor(k, list(shp), F32, kind="ExternalInput").ap()
    consts = {k: nc.dram_tensor("k_" + k, list(v.shape), F32, kind="ExternalInput").ap()
              for k, v in consts_np.items()}
    x_in = T(nc.dram_tensor("x", [NB, S, D], F32, kind="ExternalInput").ap(), "x_in")
    y_out = T(nc.dram_tensor("y", [NB, S, D], F32, kind="ExternalOutput").ap(), "y_out")
    with ExitStack() as st:
        P = Prog(nc)
        P.alloc_sems(st)
        kb = KB(nc, P, st)
        g = build_common(kb, consts)
        g["consts"] = consts
        scr = [kb.dram("xs%d" % i, [NB, S, D]) for i in range(2)]
        fns = {0: None, 1: layer_lru, 2: None, 3: None}
        fns.update(LAYER_FNS)
        src = x_in
        for li, layer in enumerate(layers):
            dst = y_out if li == len(layers) - 1 else scr[li % 2]
            fns[layer](kb, g, W, src, dst)
            src = dst
        P.finish([y_out.b])
        P.emit()
    return nc


LAYER_FNS = {}


def layer_gla(kb, g, W, x_src, x_dst):
    nc, P = kb.nc, kb.P
    C = g["consts"]
    kb.push()
    win = kb.sb("gla_win", [128, 8, 3088], BF16)
    wout = kb.sb("gla_wout", [128, 8, 1024], BF16)
    begin_load(kb, g)
    compute_mod(kb, g, 3, W)
    load_w(kb, g, win, 0, W["gla_w_in"][0], 3088)
    load_w(kb, g, wout, 0, W["gla_w_out"][0], 1024)
    end_load(kb, g)
    aw2 = kb.sb("aw2", [16, 512], F32)
    P.dma("sp", aw2[:], W["gla_alpha_w2"][0], writes=[aw2.b])
    nab = kb.sb("nab", [128, 4], F32)
    P.dma("sp", nab[:], W["gla_alpha_b"][0].rearrange("(k p) -> p k", p=128),
          writes=[nab.b], allow_slow_non_contiguous=True)
    P.op("dve", lambda e: e.tensor_scalar(nab[:], nab[:], -1.0, None, ALU.mult),
         reads=[nab.b], writes=[nab.b])
    gbc = kb.sb("gbc", [128, 256], F32)
    P.dma("sp", gbc[:], W["gla_norm_gain"][0:1, :].to_broadcast([128, 256]), writes=[gbc.b])
    smask = kb.sb("smask", [128, 512], F32)
    P.dma("sp", smask[:], C["gla_scanmask"][:, :], writes=[smask.b])
    mbd = kb.sb("mbd", [128, 128], F32)
    P.dma("sp", mbd[:], C["gla_mbd"][:, :], writes=[mbd.b])
    m2 = kb.sb("m2", [128, 128], F32)
    P.dma("sp", m2[:], C["gla_m2"][:, :], writes=[m2.b])

    TB = 512
    hT = kb.sb("hTb", [128, 8, TB], BF16)
    zT = kb.sb("zTb", [128, 8, TB], BF16)
    alow = kb.sb("alow", [16, TB], F32)
    laT = kb.sb("laT", [128, TB], F32)
    cumT = kb.sb("cumT", [128, TB], F32)
    eq = kb.sb("eq", [128, TB], F32)
    ek = kb.sb("ek", [128, TB], F32)
    qdT = kb.sb("qdT", [128, 4, TB], F32)
    kiT = kb.sb("kiT", [128, 4, TB], F32)
    el = kb.sb("el", [128, 4, 8], F32)
    latok = kb.sb("latok", [128, 4, 512], F32)
    edl = kb.sb("edl", [128, 512], F32)
    kend = kb.sb("kend", [128, 512], F32)
    vtok = kb.sb("vtok", [128, 1024], F32)
    sg = kb.sb("sg", [128, 1024], F32)
    attm = kb.sb("attm", [128, 128], F32)
    Sst = kb.sb("Sst", [128, 4, 256], F32)
    zz = kb.sb("zz", [128, 1024], F32)
    ss = kb.sb("ss", [128, 8], F32)
    ps = g["ps"]
    scale_q = 128.0 ** -0.5
    for b in range(NB):
        P.op("dve", lambda e: e.memset(Sst[:], 0.0), writes=[Sst.b])
        for tb in range(S // TB):
            for tl in range(4):
                front_tile(kb, g, b, tb * 4 + tl, x_src, _Shift(hT, tb * TB), hT.b)
            for c in range(8):
                P.op("pe", lambda e, c=c: e.matmul(ps[4][0:16, :], win[:, c, 3072:3088], hT[:, c, :],
                                                   start=(c == 0), stop=(c == 7)),
                     reads=[win.b, hT.b], writes=[ps[4].b], pos="M16")
            P.op("act", lambda e: e.copy(alow[:], ps[4][0:16, :]), reads=[ps[4].b], writes=[alow.b])
            for h in range(4):
                P.op("pe", lambda e, h=h: e.matmul(ps[4][:], aw2[:, h * 128:(h + 1) * 128], alow[:],
                                                   start=True, stop=True),
                     reads=[aw2.b, alow.b], writes=[ps[4].b], pos="K16")
                P.op("act", lambda e, h=h: e.activation(laT[:], ps[4][:], AF.Exp, bias=nab[:, h:h + 1], scale=-1.0),
                     reads=[ps[4].b, nab.b], writes=[laT.b])
                P.op("act", lambda e: e.activation(laT[:], laT[:], AF.Ln, bias=1.0),
                     reads=[laT.b], writes=[laT.b])
                P.op("dve", lambda e: e.tensor_scalar(laT[:], laT[:], -1.0 / 16.0, None, ALU.mult),
                     reads=[laT.b], writes=[laT.b])
                P.op("dve", lambda e: e.tensor_tensor_scan(cumT[:], smask[:], laT[:], 0.0, ALU.mult, ALU.add),
                     reads=[smask.b, laT.b], writes=[cumT.b])
                P.op("act", lambda e: e.activation(eq[:], cumT[:], AF.Exp),
                     reads=[cumT.b], writes=[eq.b])
                P.op("act", lambda e: e.activation(ek[:], cumT[:], AF.Exp, scale=-1.0),
                     reads=[cumT.b], writes=[ek.b])
                for c in range(8):
                    P.op("pe", lambda e, c=c, h=h: e.matmul(ps[5][:], win[:, c, h * 128:(h + 1) * 128], hT[:, c, :],
                                                            start=(c == 0), stop=(c == 7)),
                         reads=[win.b, hT.b], writes=[ps[5].b])
                P.op("dve", lambda e, h=h: e.scalar_tensor_tensor(qdT[:, h, :], ps[5][:], scale_q, eq[:],
                                                                  ALU.mult, ALU.mult),
                     reads=[ps[5].b, eq.b], writes=[qdT.b])
                for c in range(8):
                    P.op("pe", lambda e, c=c, h=h: e.matmul(ps[5][:], win[:, c, 512 + h * 128:512 + (h + 1) * 128],
                                                            hT[:, c, :], start=(c == 0), stop=(c == 7)),
                         reads=[win.b, hT.b], writes=[ps[5].b])
                P.op("dve", lambda e, h=h: e.tensor_tensor(kiT[:, h, :], ps[5][:], ek[:], ALU.mult),
                     reads=[ps[5].b, ek.b], writes=[kiT.b])
                P.op("dve", lambda e, h=h: e.tensor_copy(
                    el[:, h, :], eq[:].rearrange("p (n c) -> p n c", c=64)[:, :, 63]),
                    reads=[eq.b], writes=[el.b])
                for tl in range(4):
                    P.op("pe", lambda e, tl=tl: e.transpose(ps[2][:, tl * 128:(tl + 1) * 128],
                                                            laT[:, tl * 128:(tl + 1) * 128], g["ident"][:]),
                         reads=[laT.b, g["ident"].b], writes=[ps[2].b])
                P.op("act", lambda e, h=h: e.copy(
                    latok[:, :, h * 128:(h + 1) * 128], ps[2][:].rearrange("p (t f) -> p t f", f=128)),
                    reads=[ps[2].b], writes=[latok.b])
            for tl in range(4):
                tsl = slice(tl * 128, (tl + 1) * 128)
                P.op("pe", lambda e, tl=tl: e.matmul(ps[2][:], m2[:], latok[:, tl, :], start=True, stop=True),
                     reads=[m2.b, latok.b], writes=[ps[2].b])
                P.op("act", lambda e: e.activation(edl[:], ps[2][:], AF.Exp), reads=[ps[2].b], writes=[edl.b])
                for c in range(8):
                    P.op("pe", lambda e, c=c, tsl=tsl: e.matmul(ps[3][:], hT[:, c, tsl], win[:, c, 512:1024],
                                                                start=(c == 0), stop=(c == 7)),
                         reads=[win.b, hT.b], writes=[ps[3].b])
                P.op("dve", lambda e: e.tensor_tensor(kend[:], ps[3][:], edl[:], ALU.mult),
                     reads=[ps[3].b, edl.b], writes=[kend.b])
                for half in range(2):
                    pt = ps[4 + half]
                    for c in range(8):
                        P.op("pe", lambda e, c=c, tsl=tsl, pt=pt, half=half: e.matmul(
                            pt[:], hT[:, c, tsl], win[:, c, 1024 + half * 512:1024 + (half + 1) * 512],
                            start=(c == 0), stop=(c == 7)),
                            reads=[win.b, hT.b], writes=[pt.b])
                    P.op("act", lambda e, pt=pt, half=half: e.copy(vtok[:, half * 512:(half + 1) * 512], pt[:]),
                         reads=[pt.b], writes=[vtok.b])
                for half in range(2):
                    pt = ps[4 + half]
                    for c in range(8):
                        P.op("pe", lambda e, c=c, tsl=tsl, pt=pt, half=half: e.matmul(
                            pt[:], hT[:, c, tsl], win[:, c, 2048 + half * 512:2048 + (half + 1) * 512],
                            start=(c == 0), stop=(c == 7)),
                            reads=[win.b, hT.b], writes=[pt.b])
                    P.op("act", lambda e, pt=pt, half=half: e.activation(
                        sg[:, half * 512:(half + 1) * 512], pt[:], AF.Silu),
                        reads=[pt.b], writes=[sg.b])
                for h in range(4):
                    po = ps[h // 2]
                    pc = slice((h % 2) * 256, (h % 2 + 1) * 256)
                    vsl = slice(h * 256, (h + 1) * 256)
                    P.op("pe", lambda e, h=h, tsl=tsl: e.matmul(ps[2][:, 0:128], kiT[:, h, tsl], qdT[:, h, tsl],
                                                                start=True, stop=True),
                         reads=[kiT.b, qdT.b], writes=[ps[2].b])
                    P.op("dve", lambda e: e.tensor_tensor(attm[:], ps[2][:, 0:128], mbd[:], ALU.mult),
                         reads=[ps[2].b, mbd.b], writes=[attm.b])
                    P.op("pe", lambda e, po=po, pc=pc, vsl=vsl: e.matmul(po[:, pc], attm[:], vtok[:, vsl],
                                                                         start=True, stop=False),
                         reads=[attm.b, vtok.b], writes=[po.b])
                    for cc in range(2):
                        n = tl * 2 + cc
                        psl = slice(cc * 64, (cc + 1) * 64)
                        qsl = slice(tl * 128 + cc * 64, tl * 128 + (cc + 1) * 64)
                        P.op("pe", lambda e, po=po, pc=pc, psl=psl, qsl=qsl, h=h: e.matmul(
                            po[psl, pc], qdT[:, h, qsl], Sst[:, h, :], start=False, stop=True),
                            reads=[qdT.b, Sst.b], writes=[po.b], pos=("T1" if cc else "T0"))
                        P.op("pe", lambda e, psl=psl, h=h, vsl=vsl: e.matmul(
                            ps[3][:, 0:256], kend[psl, h * 128:(h + 1) * 128], vtok[psl, vsl],
                            start=True, stop=True),
                            reads=[kend.b, vtok.b], writes=[ps[3].b], pos=("R1" if cc else "R0"))
                        P.op("dve", lambda e, h=h, n=n: e.scalar_tensor_tensor(
                            Sst[:, h, :], Sst[:, h, :], el[:, h, n:n + 1], ps[3][:, 0:256], ALU.mult, ALU.add),
                            reads=[Sst.b, el.b, ps[3].b], writes=[Sst.b])
                P.op("dve", lambda e: e.memset(ss[:], 0.0), writes=[ss.b])
                for h in range(4):
                    po = ps[h // 2]
                    pc = slice((h % 2) * 256, (h % 2 + 1) * 256)
                    P.op("act", lambda e, po=po, pc=pc, h=h: e.activation(
                        g["junk"][:, 0:256], po[:, pc], AF.Square, accum_out=ss[:, h:h + 1]),
                        reads=[po.b, ss.b], writes=[g["junk"].b, ss.b])
                P.op("dve", lambda e: e.tensor_scalar(ss[:, 4:8], ss[:, 0:4], 1.0 / 256.0, EPS, ALU.mult, ALU.add),
                     reads=[ss.b], writes=[ss.b])
                P.op("act", lambda e: e.activation(ss[:, 4:8], ss[:, 4:8], AF.Sqrt), reads=[ss.b], writes=[ss.b])
                P.op("dve", lambda e: e.reciprocal(ss[:, 4:8], ss[:, 4:8]), reads=[ss.b], writes=[ss.b])
                for h in range(4):
                    po = ps[h // 2]
                    pc = slice((h % 2) * 256, (h % 2 + 1) * 256)
                    vsl = slice(h * 256, (h + 1) * 256)
                    P.op("dve", lambda e, po=po, pc=pc, vsl=vsl, h=h: e.scalar_tensor_tensor(
                        zz[:, vsl], po[:, pc], ss[:, 4 + h:5 + h], gbc[:], ALU.mult, ALU.mult),
                        reads=[po.b, ss.b, gbc.b], writes=[zz.b])
                P.op("pool", lambda e: e.tensor_tensor(zz[:], zz[:], sg[:], ALU.mult),
                     reads=[zz.b, sg.b], writes=[zz.b])
                for c in range(8):
                    pt = ps[4 + c // 4]
                    P.op("pe", lambda e, pt=pt, c=c: e.transpose(
                        pt[:, (c % 4) * 128:(c % 4 + 1) * 128], zz[:, c * 128:(c + 1) * 128], g["ident"][:]),
                        reads=[zz.b, g["ident"].b], writes=[pt.b])
                for half in range(2):
                    pt = ps[4 + half]
                    eng = ("act", "dve")[half]
                    if half == 0:
                        P.op("act", lambda e, pt=pt, tsl=tsl: e.copy(
                            zT[:, 0:4, tsl], pt[:].rearrange("p (c t) -> p c t", t=128)),
                            reads=[pt.b], writes=[zT.b])
                    else:
                        P.op("dve", lambda e, pt=pt, tsl=tsl: e.tensor_copy(
                            zT[:, 4:8, tsl], pt[:].rearrange("p (c t) -> p c t", t=128)),
                            reads=[pt.b], writes=[zT.b])
            for tl in range(4):
                back_tile(kb, g, b, tb * 4 + tl, _Shift(zT, tb * TB), [zT.b], wout, x_src, x_dst)
    kb.pop()


class _Shift:
    def __init__(self, t, t0):
        self.t, self.t0 = t, t0

    def __getitem__(self, k):
        a, c, sl = k
        return self.t[a, c, sl.start - self.t0:sl.stop - self.t0]


def gla_consts():
    c = {}
    t = np.arange(512)
    c["gla_scanmask"] = np.tile((t % 64 != 0).astype(np.float32)[None, :], (128, 1))
    j = np.arange(128)[:, None]
    i = np.arange(128)[None, :]
    same = (j // 64) == (i // 64)
    c["gla_mbd"] = (same & (j <= i)).astype(np.float32)
    c["gla_m2"] = (same & (j > i)).astype(np.float32)
    return c


LAYER_FNS[3] = layer_gla


NEG = -1.0e30


def _rope(P, eng2, x, cos, sin, out_list, tmp1, tmp2, nh, half, rd, wr):
    xv = x[:, 0:nh * 2 * half].rearrange("p (h two d) -> p h two d", two=2, d=half)
    x1, x2 = xv[:, :, 0, :], xv[:, :, 1, :]
    cb = cos.unsqueeze(1).to_broadcast([128, nh, half])
    sb_ = sin.unsqueeze(1).to_broadcast([128, nh, half])
    t1 = tmp1[:, 0:nh * half].rearrange("p (h d) -> p h d", d=half)
    t2 = tmp2[:, 0:nh * half].rearrange("p (h d) -> p h d", d=half)
    P.op("dve", lambda e: e.tensor_tensor(t1, x1, cb, ALU.mult), reads=rd, writes=[tmp1.b])
    P.op(eng2, lambda e: e.tensor_tensor(t2, x2, sb_, ALU.mult), reads=rd, writes=[tmp2.b])
    for o in out_list:
        P.op("dve", lambda e, o=o: e.tensor_tensor(o[0], t1, t2, ALU.subtract),
             reads=[tmp1.b, tmp2.b], writes=wr)
    P.op("dve", lambda e: e.tensor_tensor(t1, x2, cb, ALU.mult), reads=rd + [tmp1.b], writes=[tmp1.b])
    P.op(eng2, lambda e: e.tensor_tensor(t2, x1, sb_, ALU.mult), reads=rd + [tmp2.b], writes=[tmp2.b])
    for o in out_list:
        P.op("dve", lambda e, o=o: e.tensor_tensor(o[1], t1, t2, ALU.add),
             reads=[tmp1.b, tmp2.b], writes=wr)


def layer_dsa(kb, g, W, x_src, x_dst):
    nc, P = kb.nc, kb.P
    C = g["consts"]
    kb.push()
    win = kb.sb("dsa_win", [128, 8, 3720], BF16)
    wout = kb.sb("dsa_wout", [128, 8, 1024], BF16)
    begin_load(kb, g)
    compute_mod(kb, g, 0, W)
    load_w(kb, g, win, 0, W["dsa_w_in"][0], 3720)
    load_w(kb, g, wout, 0, W["dsa_w_out"][0], 1024)
    end_load(kb, g)
    ps = g["ps"]
    psb = [kb.psum("psb%d" % i, [128, 1024], BF16) for i in range(2)]
    ident, identb = g["ident"], g["identb"]
    cs32 = kb.sb("cs32", [128, 2, NT, 32], F32)
    cs64 = kb.sb("cs64", [128, 2, NT, 64], F32)
    for i, nm in enumerate(("cos32", "sin32")):
        P.dma("sp", cs32[:, i], C[nm].rearrange("(tt p) f -> p tt f", p=128), writes=[cs32.b])
    for i, nm in enumerate(("cos64", "sin64")):
        P.dma("sp", cs64[:, i], C[nm].rearrange("(tt p) f -> p tt f", p=128), writes=[cs64.b])
    cmask = kb.sb("cmask", [128, 128], F32)
    P.dma("sp", cmask[:], C["causal"][:, :], writes=[cmask.b])
    gq = kb.sb("gq", [128, 2, 64], F32)
    P.dma("sp", gq[:, 0, :], W["dsa_q_gain"][0:1, :].to_broadcast([128, 64]), writes=[gq.b])
    P.dma("sp", gq[:, 1, :], W["dsa_k_gain"][0:1, :].to_broadcast([128, 64]), writes=[gq.b])
    kT2 = kb.sb("kT2", [128, 4, S], BF16)
    v1 = kb.sb("v1", [128, NT, 4, 65], BF16)
    kiT = kb.sb("kiT", [128, S], BF16)
    hTt = kb.sb("hTt", [128, 8, 128], BF16)
    big = kb.sb("big", [128, S], F32)
    qf = _Off(big, 0)
    tmpa = _Off(big, 1024)
    tmpb = kb.sb("tmpb", [128, 512], F32)
    tmpc = kb.sb("tmpc", [128, 512], F32)
    qb = kb.sb("qb", [128, 1024], BF16)
    kb2 = kb.sb("kb2", [128, 4, 2, 64], BF16)
    qTt = kb.sb("qTt", [128, 8, 128], BF16)
    qiTt = kb.sb("qiTt", [128, 8, 128], BF16)
    sgt = kb.sb("sgt", [128, 1024], BF16)
    kf = kb.sb("kf", [128, 512], F32)
    wkf = kb.sb("wkf", [128, 136], F32)
    kib = kb.sb("kib", [128, 128], BF16)
    sm = kb.sb("sm", [128, 64], F32)
    score = kb.sb("score", [128, S], F32)
    work = big
    rl = [kb.sb("rl", [128, 512], F32) for _ in range(2)]
    maskb = kb.sb("maskb", [128, S], BF16)
    maskT = kb.sb("maskT", [128, NT, 128], BF16)
    eb = [kb.sb("eb", [128, 512], BF16) for _ in range(2)]
    pT = [kb.sb("pT", [128, 512], BF16) for _ in range(2)]
    m8 = kb.sb("m8", [128, 8], F32)
    thr = kb.sb("thr", [128, 2], F32)
    zz = _Off(big, 0)
    zTt = kb.sb("zTt", [128, 8, 128], BF16)
    rden = kb.sb("rden", [128, 16], F32)
    IDXS = 1024.0 ** -0.5
    P.op("pool", lambda e: e.memset(v1[:], 1.0), writes=[v1.b])

    def rms_heads(x, nh, gi, out):
        xv = x[:, 0:nh * 64].rearrange("p (h d) -> p h d", d=64)
        P.op("act", lambda e: e.activation(tmpa[:, 0:nh * 64], x[:, 0:nh * 64], AF.Square),
             reads=[x.b], writes=[tmpa.b])
        P.op("dve", lambda e: e.tensor_reduce(sm[:, 0:nh], tmpa[:, 0:nh * 64].rearrange("p (h d) -> p h d", d=64),
                                              AX.X, ALU.add),
             reads=[tmpa.b], writes=[sm.b])
        P.op("dve", lambda e: e.tensor_scalar(sm[:, 0:nh], sm[:, 0:nh], 1.0 / 64.0, EPS, ALU.mult, ALU.add),
             reads=[sm.b], writes=[sm.b])
        P.op("act", lambda e: e.activation(sm[:, 0:nh], sm[:, 0:nh], AF.Sqrt), reads=[sm.b], writes=[sm.b])
        P.op("dve", lambda e: e.reciprocal(sm[:, 0:nh], sm[:, 0:nh]), reads=[sm.b], writes=[sm.b])
        ov = out[:, 0:nh * 64].rearrange("p (h d) -> p h d", d=64)
        P.op("dve", lambda e: e.tensor_tensor(ov, xv, sm[:, 0:nh].unsqueeze(2).to_broadcast([128, nh, 64]), ALU.mult),
             reads=[x.b, sm.b], writes=[out.b])
        P.op("pool", lambda e: e.tensor_tensor(ov, ov, gq[:, gi, :].unsqueeze(1).to_broadcast([128, nh, 64]), ALU.mult),
             reads=[out.b, gq.b], writes=[out.b])

    for b in range(NB):
        for tt in range(NT):
            tsl = slice(tt * 128, (tt + 1) * 128)
            L = (tt + 1) * 128
            front_tile(kb, g, b, tt, x_src, _Shift(hTt, tt * 128), hTt.b)

            def proj(pt, c0, n):
                for c in range(8):
                    P.op("pe", lambda e, c=c: e.matmul(pt[:, 0:n], hTt[:, c, :], win[:, c, c0:c0 + n],
                                                       start=(c == 0), stop=(c == 7)),
                         reads=[hTt.b, win.b], writes=[pt.b])
            for half in range(2):
                proj(ps[2 + half], half * 512, 512)
                P.op("act", lambda e, half=half: e.copy(qf[:, half * 512:(half + 1) * 512], ps[2 + half][:]),
                     reads=[ps[2 + half].b], writes=[qf.b])
            rms_heads(qf, 16, 0, qf)
            qbv = qb[:].rearrange("p (h two d) -> p h two d", two=2, d=32)
            _rope(P, "pool", qf, cs32[:, 0, tt, :], cs32[:, 1, tt, :], [(qbv[:, :, 0, :], qbv[:, :, 1, :])],
                  tmpb, tmpc, 16, 32, [qf.b, cs32.b], [qb.b])
            for c in range(8):
                P.op("pe", lambda e, c=c: e.transpose(psb[0][:, c * 128:(c + 1) * 128], qb[:, c * 128:(c + 1) * 128],
                                                      identb[:]),
                     reads=[qb.b, identb.b], writes=[psb[0].b])
            P.op("act", lambda e: e.copy(qTt[:], psb[0][:].rearrange("p (c t) -> p c t", t=128)),
                 reads=[psb[0].b], writes=[qTt.b])
            proj(ps[4], 1024, 512)
            P.op("act", lambda e: e.copy(kf[:], ps[4][:]), reads=[ps[4].b], writes=[kf.b])
            P.op("dve", lambda e, tt=tt: e.tensor_copy(
                v1[:, tt, :, 0:64], kf[:, 256:512].rearrange("p (g d) -> p g d", d=64)),
                reads=[kf.b], writes=[v1.b])
            rms_heads(kf, 4, 1, kf)
            k2v = kb2[:].rearrange("p g r (two d) -> p g r two d", two=2)
            _rope(P, "pool", kf, cs32[:, 0, tt, :], cs32[:, 1, tt, :],
                  [(k2v[:, :, 0, 0, :], k2v[:, :, 0, 1, :]), (k2v[:, :, 1, 0, :], k2v[:, :, 1, 1, :])],
                  tmpb, tmpc, 4, 32, [kf.b, cs32.b], [kb2.b])
            for gg in range(4):
                P.op("pe", lambda e, gg=gg: e.transpose(
                    psb[1][:, gg * 128:(gg + 1) * 128], kb2[:, gg].rearrange("p r d -> p (r d)"), identb[:]),
                    reads=[kb2.b, identb.b], writes=[psb[1].b])
            P.op("act", lambda e, tsl=tsl: e.copy(kT2[:, :, tsl], psb[1][:, 0:512].rearrange("p (g t) -> p g t", t=128)),
                 reads=[psb[1].b], writes=[kT2.b])
            for half in range(2):
                proj(ps[2 + half], 1536 + half * 512, 512)
                P.op("act", lambda e, half=half: e.activation(sgt[:, half * 512:(half + 1) * 512], ps[2 + half][:], AF.Silu),
                     reads=[ps[2 + half].b], writes=[sgt.b])
            for half in range(2):
                proj(ps[4 + half], 2560 + half * 512, 512)
                P.op("act", lambda e, half=half: e.copy(qf[:, half * 512:(half + 1) * 512], ps[4 + half][:]),
                     reads=[ps[4 + half].b], writes=[qf.b])
            qbv2 = qb[:].rearrange("p (h two d) -> p h two d", two=2, d=64)
            _rope(P, "pool", qf, cs64[:, 0, tt, :], cs64[:, 1, tt, :], [(qbv2[:, :, 0, :], qbv2[:, :, 1, :])],
                  tmpb, tmpc, 8, 64, [qf.b, cs64.b], [qb.b])
            for c in range(8):
                P.op("pe", lambda e, c=c: e.transpose(psb[0][:, c * 128:(c + 1) * 128], qb[:, c * 128:(c + 1) * 128],
                                                      identb[:]),
                     reads=[qb.b, identb.b], writes=[psb[0].b])
            P.op("act", lambda e: e.copy(qiTt[:], psb[0][:].rearrange("p (c t) -> p c t", t=128)),
                 reads=[psb[0].b], writes=[qiTt.b])
            proj(ps[4], 3584, 136)
            P.op("act", lambda e: e.copy(wkf[:], ps[4][:, 0:136]), reads=[ps[4].b], writes=[wkf.b])
            P.op("dve", lambda e: e.tensor_scalar(wkf[:, 0:8], wkf[:, 0:8], IDXS, None, ALU.mult),
                 reads=[wkf.b], writes=[wkf.b])
            kiv = kib[:].rearrange("p (h two d) -> p h two d", two=2, d=64)
            _rope(P, "pool", _Off(wkf, 8), cs64[:, 0, tt, :], cs64[:, 1, tt, :], [(kiv[:, :, 0, :], kiv[:, :, 1, :])],
                  tmpb, tmpc, 1, 64, [wkf.b, cs64.b], [kib.b])
            P.op("pe", lambda e: e.transpose(psb[1][:, 0:128], kib[:], identb[:]),
                 reads=[kib.b, identb.b], writes=[psb[1].b])
            P.op("act", lambda e, tsl=tsl: e.copy(kiT[:, tsl], psb[1][:, 0:128]), reads=[psb[1].b], writes=[kiT.b])
            n_it = 0
            for k0 in range(0, L, 512):
                w = min(512, L - k0)
                for h in range(8):
                    pt = ps[2 + n_it % 2]
                    r_ = rl[n_it % 2]
                    n_it += 1
                    P.op("pe", lambda e, pt=pt, h=h, k0=k0, w=w: e.matmul(
                        pt[:, 0:w], qiTt[:, h, :], kiT[:, k0:k0 + w], start=True, stop=True),
                        reads=[qiTt.b, kiT.b], writes=[pt.b])
                    P.op("act", lambda e, pt=pt, r_=r_, w=w: e.activation(r_[:, 0:w], pt[:, 0:w], AF.Relu),
                         reads=[pt.b], writes=[r_.b])
                    if h == 0:
                        P.op("dve", lambda e, r_=r_, k0=k0, w=w, h=h: e.tensor_scalar(
                            score[:, k0:k0 + w], r_[:, 0:w], wkf[:, h:h + 1], None, ALU.mult),
                            reads=[r_.b, wkf.b], writes=[score.b])
                    else:
                        P.op("dve", lambda e, r_=r_, k0=k0, w=w, h=h: e.scalar_tensor_tensor(
                            score[:, k0:k0 + w], r_[:, 0:w], wkf[:, h:h + 1], score[:, k0:k0 + w],
                            ALU.mult, ALU.add),
                            reads=[r_.b, wkf.b, score.b], writes=[score.b])
            P.op("dve", lambda e, tsl=tsl: e.tensor_tensor(score[:, tsl], score[:, tsl], cmask[:], ALU.add),
                 reads=[score.b, cmask.b], writes=[score.b])
            if tt >= 2:
                P.op("pool", lambda e, L=L: e.tensor_copy(work[:, 0:L], score[:, 0:L]),
                     reads=[score.b], writes=[work.b])
                for it in range(32):
                    P.op("dve", lambda e, L=L: e.max(m8[:], work[:, 0:L]), reads=[work.b], writes=[m8.b])
                    if it < 31:
                        P.op("dve", lambda e, L=L: e.match_replace(work[:, 0:L], m8[:], work[:, 0:L], NEG),
                             reads=[work.b, m8.b], writes=[work.b])
                P.op("dve", lambda e: e.tensor_reduce(thr[:, 0:1], m8[:], AX.X, ALU.min),
                     reads=[m8.b], writes=[thr.b])
                P.op("dve", lambda e: e.tensor_scalar(thr[:, 0:1], thr[:, 0:1], -1.0e29, None, ALU.max),
                     reads=[thr.b], writes=[thr.b])
            else:
                P.op("dve", lambda e: e.memset(thr[:, 0:1], -1.0e29), writes=[thr.b])
            P.op("dve", lambda e, L=L: e.tensor_scalar(maskb[:, 0:L], score[:, 0:L], thr[:, 0:1], None, ALU.is_ge),
                 reads=[score.b, thr.b], writes=[maskb.b])
            for k0 in range(0, tt + 1, 8):
                nk = min(8, tt + 1 - k0)
                for kk in range(nk):
                    kbk = k0 + kk
                    P.op("pe", lambda e, kk=kk, kbk=kbk: e.transpose(
                        psb[0][:, kk * 128:(kk + 1) * 128], maskb[:, kbk * 128:(kbk + 1) * 128], identb[:]),
                        reads=[maskb.b, identb.b], writes=[psb[0].b])
                P.op("act", lambda e, k0=k0, nk=nk: e.copy(
                    maskT[:, k0:k0 + nk, :], psb[0][:, 0:nk * 128].rearrange("p (k t) -> p k t", t=128)),
                    reads=[psb[0].b], writes=[maskT.b])
            n_it = 0
            for gg in range(4):
                po = ps[gg]
                for kbk in range(tt + 1):
                    ksl = slice(kbk * 128, (kbk + 1) * 128)
                    pt = ps[4 + n_it % 2]
                    e_ = eb[n_it % 2]
                    p_ = pT[n_it % 2]
                    n_it += 1
                    for par in range(2):
                        hs = slice(par * 64, (par + 1) * 64)
                        P.op("pe", lambda e, pt=pt, par=par, hs=hs, gg=gg, ksl=ksl: e.matmul(
                            pt[:, par * 256:(par + 1) * 256], kT2[hs, gg, ksl], qTt[hs, 2 * gg:2 * gg + 2, :],
                            start=True, stop=True),
                            reads=[kT2.b, qTt.b], writes=[pt.b], pos=(1 if par else None))
                    P.op("act", lambda e, pt=pt, e_=e_: e.activation(e_[:], pt[:], AF.Exp, scale=0.125),
                         reads=[pt.b], writes=[e_.b])
                    P.op("dve", lambda e, e_=e_, p_=p_, kbk=kbk: e.tensor_tensor(
                        p_[:].rearrange("p (a t) -> p a t", t=128), e_[:].rearrange("p (a t) -> p a t", t=128),
                        maskT[:, kbk, :].unsqueeze(1).to_broadcast([128, 4, 128]), ALU.mult),
                        reads=[e_.b, maskT.b], writes=[p_.b])
                    for blk in range(4):
                        c0 = blk * 65
                        P.op("pe", lambda e, po=po, c0=c0, p_=p_, blk=blk, tt=tt, kbk=kbk, gg=gg: e.matmul(
                            po[:, c0:c0 + 65], p_[:, blk * 128:(blk + 1) * 128], v1[:, kbk, gg, :],
                            start=(kbk == 0 and blk == 0), stop=(kbk == tt)),
                            reads=[p_.b, v1.b], writes=[po.b])
            for gg in range(4):
                po = ps[gg]
                pv = po[:, 0:260].rearrange("p (k d) -> p k d", d=65)
                P.op("dve", lambda e, pv=pv, gg=gg: e.reciprocal(rden[:, gg * 4:(gg + 1) * 4], pv[:, :, 64]),
                     reads=[po.b], writes=[rden.b])
                zv = zz[:, gg * 256:(gg + 1) * 256].rearrange("p (cp par d) -> p par cp d", cp=2, par=2)
                for par in range(2):
                    P.op("dve", lambda e, pv=pv, zv=zv, par=par, gg=gg: e.tensor_tensor(
                        zv[:, par], pv[:, par * 2:par * 2 + 2, 0:64],
                        rden[:, gg * 4 + par * 2:gg * 4 + par * 2 + 2].unsqueeze(2).to_broadcast([128, 2, 64]),
                        ALU.mult),
                        reads=[po.b, rden.b], writes=[zz.b])
            P.op("pool", lambda e: e.tensor_tensor(zz[:, 0:1024], zz[:, 0:1024], sgt[:], ALU.mult),
                 reads=[zz.b, sgt.b], writes=[zz.b])
            for c in range(8):
                pt = ps[4 + c // 4]
                P.op("pe", lambda e, pt=pt, c=c: e.transpose(
                    pt[:, (c % 4) * 128:(c % 4 + 1) * 128], zz[:, c * 128:(c + 1) * 128], ident[:]),
                    reads=[zz.b, ident.b], writes=[pt.b])
            P.op("act", lambda e: e.copy(zTt[:, 0:4, :], ps[4][:].rearrange("p (c t) -> p c t", t=128)),
                 reads=[ps[4].b], writes=[zTt.b])
            P.op("dve", lambda e: e.tensor_copy(zTt[:, 4:8, :], ps[5][:].rearrange("p (c t) -> p c t", t=128)),
                 reads=[ps[5].b], writes=[zTt.b])
            back_tile(kb, g, b, tt, _Shift(zTt, tt * 128), [zTt.b], wout, x_src, x_dst)
    kb.pop()


def layer_dsa2(kb, g, W, x_src, x_dst):
    nc, P = kb.nc, kb.P
    C = g["consts"]
    kb.push()
    win = kb.sb("dsa_win", [128, 8, 3720], BF16)
    wout = kb.sb("dsa_wout", [128, 8, 1024], BF16)
    begin_load(kb, g)
    compute_mod(kb, g, 0, W)
    load_w(kb, g, win, 0, W["dsa_w_in"][0], 3720)
    load_w(kb, g, wout, 0, W["dsa_w_out"][0], 1024)
    end_load(kb, g)
    ps = g["ps"]
    psb = [kb.psum("psb%d" % i, [128, 1024], BF16) for i in range(2)]
    ident, identb = g["ident"], g["identb"]
    cs32 = kb.sb("cs32", [128, 2, 32], F32)
    cs64 = kb.sb("cs64", [128, 2, 64], F32)
    cmask = kb.sb("cmask", [128, 128], F32)
    P.dma("sp", cmask[:], C["causal"][:, :], writes=[cmask.b])
    gq = kb.sb("gq", [128, 2, 64], F32)
    P.dma("sp", gq[:, 0, :], W["dsa_q_gain"][0:1, :].to_broadcast([128, 64]), writes=[gq.b])
    P.dma("sp", gq[:, 1, :], W["dsa_k_gain"][0:1, :].to_broadcast([128, 64]), writes=[gq.b])
    kT2 = kb.sb("kT2", [128, 4, S], BF16)
    v1 = kb.sb("v1", [128, NT, 4, 65], BF16)
    kiT = kb.sb("kiT", [128, S], BF16)
    kT2b = [Buf("kT2_%d" % i) for i in range(NT)]
    v1b = [Buf("v1_%d" % i) for i in range(NT)]
    kiTb = [Buf("kiT_%d" % i) for i in range(NT)]
    hTt = kb.sb("hTt", [128, 8, 128], BF16)
    big = kb.sb("big", [128, S], F32)
    qf = _Off(big, 0)
    tmpa = _Off(big, 1024)
    work = big
    tmpb = kb.sb("tmpb", [128, 512], F32)
    tmpc = kb.sb("tmpc", [128, 512], F32)
    qb = kb.sb("qb", [128, 1024], BF16)
    kb2 = kb.sb("kb2", [128, 4, 2, 64], BF16)
    qiTt = kb.sb("qiTt", [128, 8, 128], BF16)
    kf = kb.sb("kf", [128, 512], F32)
    wkf = kb.sb("wkf", [128, 136], F32)
    kib = kb.sb("kib", [128, 128], BF16)
    sm = kb.sb("sm", [128, 64], F32)
    score = kb.sb("score", [128, S], F32)
    rl = [kb.sb("rl", [128, 512], F32) for _ in range(2)]
    maskb = kb.sb("maskb", [128, S], BF16)
    m8 = kb.sb("m8", [128, 8], F32)
    thr = kb.sb("thr", [128, 2], F32)
    qTt = [kb.sb("qTz", [128, 16, 128], BF16) for _ in range(2)]
    for i_ in range(2):
        P.op("pool", lambda e, i_=i_: e.memset(qTt[i_][:], 0.0), writes=[qTt[i_].b])
    sgt = [kb.sb("sgt", [128, 1024], BF16) for _ in range(2)]
    maskT = [kb.sb("maskT", [128, NT, 128], BF16) for _ in range(2)]
    eb = [kb.sb("eb", [128, 512], BF16) for _ in range(2)]
    pT = [kb.sb("pT", [128, 512], BF16) for _ in range(2)]
    zz = kb.sb("zzd", [128, 1024], F32)
    zTt = kb.sb("zTt", [128, 8, 128], BF16)
    rden = kb.sb("rden", [128, 16], F32)
    IDXS = 1024.0 ** -0.5
    P.op("pool", lambda e: e.memset(v1[:], 1.0), writes=v1b)
    S1B = (ps[4], ps[5])

    def rms_heads(x, nh, gi, out):
        xv = x[:, 0:nh * 64].rearrange("p (h d) -> p h d", d=64)
        P.op("act", lambda e: e.activation(tmpa[:, 0:nh * 64], x[:, 0:nh * 64], AF.Square),
             reads=[x.b], writes=[tmpa.b])
        P.op("dve", lambda e: e.tensor_reduce(sm[:, 0:nh], tmpa[:, 0:nh * 64].rearrange("p (h d) -> p h d", d=64),
                                              AX.X, ALU.add),
             reads=[tmpa.b], writes=[sm.b])
        P.op("dve", lambda e: e.tensor_scalar(sm[:, 0:nh], sm[:, 0:nh], 1.0 / 64.0, EPS, ALU.mult, ALU.add),
             reads=[sm.b], writes=[sm.b])
        P.op("act", lambda e: e.activation(sm[:, 0:nh], sm[:, 0:nh], AF.Sqrt), reads=[sm.b], writes=[sm.b])
        P.op("dve", lambda e: e.reciprocal(sm[:, 0:nh], sm[:, 0:nh]), reads=[sm.b], writes=[sm.b])
        ov = out[:, 0:nh * 64].rearrange("p (h d) -> p h d", d=64)
        P.op("dve", lambda e: e.tensor_tensor(ov, xv, sm[:, 0:nh].unsqueeze(2).to_broadcast([128, nh, 64]), ALU.mult),
             reads=[x.b, sm.b], writes=[out.b])
        P.op("pool", lambda e: e.tensor_tensor(ov, ov, gq[:, gi, :].unsqueeze(1).to_broadcast([128, nh, 64]), ALU.mult),
             reads=[out.b, gq.b], writes=[out.b])

    def stage1(b, tt):
        hs_ = tt % 2
        qTt_c, sgt_c, maskT_c = qTt[hs_], sgt[hs_], maskT[hs_]
        tsl = slice(tt * 128, (tt + 1) * 128)
        L = (tt + 1) * 128
        for i, nm in enumerate(("cos32", "sin32")):
            P.dma("act", cs32[:, i, :], C[nm][tt * 128:(tt + 1) * 128, :], writes=[cs32.b])
        for i, nm in enumerate(("cos64", "sin64")):
            P.dma("act", cs64[:, i, :], C[nm][tt * 128:(tt + 1) * 128, :], writes=[cs64.b])
        front_tile(kb, g, b, tt, x_src, _Shift(hTt, tt * 128), hTt.b, pbanks=S1B, xi=0)
        yield

        def proj(pt, c0, n):
            for c in range(8):
                P.op("pe", lambda e, c=c: e.matmul(pt[:, 0:n], hTt[:, c, :], win[:, c, c0:c0 + n],
                                                   start=(c == 0), stop=(c == 7)),
                     reads=[hTt.b, win.b], writes=[pt.b])
        for half in range(2):
            proj(S1B[half], half * 512, 512)
            P.op("act", lambda e, half=half: e.copy(qf[:, half * 512:(half + 1) * 512], S1B[half][:]),
                 reads=[S1B[half].b], writes=[qf.b])
        rms_heads(qf, 16, 0, qf)
        qbv = qb[:].rearrange("p (h two d) -> p h two d", two=2, d=32)
        _rope(P, "pool", qf, cs32[:, 0, :], cs32[:, 1, :], [(qbv[:, :, 0, :], qbv[:, :, 1, :])],
              tmpb, tmpc, 16, 32, [qf.b, cs32.b], [qb.b])
        for c in range(8):
            P.op("pe", lambda e, c=c: e.transpose(psb[0][:, c * 128:(c + 1) * 128], qb[:, c * 128:(c + 1) * 128],
                                                  identb[:]),
                 reads=[qb.b, identb.b], writes=[psb[0].b])
        for h2 in range(2):
            hsl = slice(h2 * 64, (h2 + 1) * 64)
            P.op(("act", "dve")[h2], lambda e, h2=h2, hsl=hsl: (e.copy if h2 == 0 else e.tensor_copy)(
                qTt_c[hsl, :, :].rearrange("p (c two) t -> p c two t", two=2)[:, :, h2, :],
                psb[0][hsl, :].rearrange("p (c t) -> p c t", t=128)),
                reads=[psb[0].b], writes=[qTt_c.b])
        yield
        proj(S1B[0], 1024, 512)
        P.op("act", lambda e: e.copy(kf[:], S1B[0][:]), reads=[S1B[0].b], writes=[kf.b])
        P.op("dve", lambda e: e.tensor_copy(
            v1[:, tt, :, 0:64], kf[:, 256:512].rearrange("p (g d) -> p g d", d=64)),
            reads=[kf.b], writes=[v1b[tt]])
        rms_heads(kf, 4, 1, kf)
        k2v = kb2[:].rearrange("p g r (two d) -> p g r two d", two=2)
        _rope(P, "pool", kf, cs32[:, 0, :], cs32[:, 1, :],
              [(k2v[:, :, 0, 0, :], k2v[:, :, 0, 1, :]), (k2v[:, :, 1, 0, :], k2v[:, :, 1, 1, :])],
              tmpb, tmpc, 4, 32, [kf.b, cs32.b], [kb2.b])
        for gg in range(4):
            P.op("pe", lambda e, gg=gg: e.transpose(
                psb[1][:, gg * 128:(gg + 1) * 128], kb2[:, gg].rearrange("p r d -> p (r d)"), identb[:]),
                reads=[kb2.b, identb.b], writes=[psb[1].b])
        P.op("act", lambda e: e.copy(kT2[:, :, tsl], psb[1][:, 0:512].rearrange("p (g t) -> p g t", t=128)),
             reads=[psb[1].b], writes=[kT2b[tt]])
        yield
        for half in range(2):
            proj(S1B[half], 1536 + half * 512, 512)
            P.op("act", lambda e, half=half: e.activation(sgt_c[:, half * 512:(half + 1) * 512], S1B[half][:], AF.Silu),
                 reads=[S1B[half].b], writes=[sgt_c.b])
        yield
        for half in range(2):
            proj(S1B[half], 2560 + half * 512, 512)
            P.op("act", lambda e, half=half: e.copy(qf[:, half * 512:(half + 1) * 512], S1B[half][:]),
                 reads=[S1B[half].b], writes=[qf.b])
        qbv2 = qb[:].rearrange("p (h two d) -> p h two d", two=2, d=64)
        _rope(P, "pool", qf, cs64[:, 0, :], cs64[:, 1, :], [(qbv2[:, :, 0, :], qbv2[:, :, 1, :])],
              tmpb, tmpc, 8, 64, [qf.b, cs64.b], [qb.b])
        for c in range(8):
            P.op("pe", lambda e, c=c: e.transpose(psb[0][:, c * 128:(c + 1) * 128], qb[:, c * 128:(c + 1) * 128],
                                                  identb[:]),
                 reads=[qb.b, identb.b], writes=[psb[0].b])
        P.op("act", lambda e: e.copy(qiTt[:], psb[0][:].rearrange("p (c t) -> p c t", t=128)),
             reads=[psb[0].b], writes=[qiTt.b])
        yield
        proj(S1B[0], 3584, 136)
        P.op("act", lambda e: e.copy(wkf[:], S1B[0][:, 0:136]), reads=[S1B[0].b], writes=[wkf.b])
        P.op("dve", lambda e: e.tensor_scalar(wkf[:, 0:8], wkf[:, 0:8], IDXS, None, ALU.mult),
             reads=[wkf.b], writes=[wkf.b])
        kiv = kib[:].rearrange("p (h two d) -> p h two d", two=2, d=64)
        _rope(P, "pool", _Off(wkf, 8), cs64[:, 0, :], cs64[:, 1, :], [(kiv[:, :, 0, :], kiv[:, :, 1, :])],
              tmpb, tmpc, 1, 64, [wkf.b, cs64.b], [kib.b])
        P.op("pe", lambda e: e.transpose(psb[1][:, 0:128], kib[:], identb[:]),
             reads=[kib.b, identb.b], writes=[psb[1].b])
        P.op("act", lambda e: e.copy(kiT[:, tsl], psb[1][:, 0:128]), reads=[psb[1].b], writes=[kiTb[tt]])
        yield
        n_it = 0
        for k0 in range(0, L, 512):
            w = min(512, L - k0)
            kbufs = kiTb[k0 // 128:(k0 + w) // 128]
            for h in range(8):
                pt = S1B[n_it % 2]
                r_ = rl[n_it % 2]
                n_it += 1
                P.op("pe", lambda e, pt=pt, h=h, k0=k0, w=w: e.matmul(
                    pt[:, 0:w], qiTt[:, h, :], kiT[:, k0:k0 + w], start=True, stop=True),
                    reads=[qiTt.b] + kbufs, writes=[pt.b])
                P.op("act", lambda e, pt=pt, r_=r_, w=w: e.activation(r_[:, 0:w], pt[:, 0:w], AF.Relu),
                     reads=[pt.b], writes=[r_.b])
                if h == 0:
                    P.op("dve", lambda e, r_=r_, k0=k0, w=w, h=h: e.tensor_scalar(
                        score[:, k0:k0 + w], r_[:, 0:w], wkf[:, h:h + 1], None, ALU.mult),
                        reads=[r_.b, wkf.b], writes=[score.b])
                else:
                    P.op("dve", lambda e, r_=r_, k0=k0, w=w, h=h: e.scalar_tensor_tensor(
                        score[:, k0:k0 + w], r_[:, 0:w], wkf[:, h:h + 1], score[:, k0:k0 + w],
                        ALU.mult, ALU.add),
                        reads=[r_.b, wkf.b, score.b], writes=[score.b])
            yield
        P.op("dve", lambda e: e.tensor_tensor(score[:, tsl], score[:, tsl], cmask[:], ALU.add),
             reads=[score.b, cmask.b], writes=[score.b])
        if tt >= 2:
            P.op("pool", lambda e: e.tensor_copy(work[:, 0:L], score[:, 0:L]),
                 reads=[score.b], writes=[work.b])
            for it in range(32):
                P.op("dve", lambda e: e.max(m8[:], work[:, 0:L]), reads=[work.b], writes=[m8.b])
                if it < 31:
                    P.op("dve", lambda e: e.match_replace(work[:, 0:L], m8[:], work[:, 0:L], NEG),
                         reads=[work.b, m8.b], writes=[work.b])
                yield
            P.op("dve", lambda e: e.tensor_reduce(thr[:, 0:1], m8[:], AX.X, ALU.min),
                 reads=[m8.b], writes=[thr.b])
            P.op("dve", lambda e: e.tensor_scalar(thr[:, 0:1], thr[:, 0:1], -1.0e29, None, ALU.max),
                 reads=[thr.b], writes=[thr.b])
        else:
            P.op("dve", lambda e: e.memset(thr[:, 0:1], -1.0e29), writes=[thr.b])
        P.op("dve", lambda e: e.tensor_scalar(maskb[:, 0:L], score[:, 0:L], thr[:, 0:1], None, ALU.is_ge),
             reads=[score.b, thr.b], writes=[maskb.b])
        for k0 in range(0, tt + 1, 8):
            nk = min(8, tt + 1 - k0)
            for kk in range(nk):
                kbk = k0 + kk
                P.op("pe", lambda e, kk=kk, kbk=kbk: e.transpose(
                    psb[0][:, kk * 128:(kk + 1) * 128], maskb[:, kbk * 128:(kbk + 1) * 128], identb[:]),
                    reads=[maskb.b, identb.b], writes=[psb[0].b])
            P.op("act", lambda e, k0=k0, nk=nk: e.copy(
                maskT_c[:, k0:k0 + nk, :], psb[0][:, 0:nk * 128].rearrange("p (k t) -> p k t", t=128)),
                reads=[psb[0].b], writes=[maskT_c.b])
        yield

    def stage2(b, tt):
        hs_ = tt % 2
        qTt_c, sgt_c, maskT_c = qTt[hs_], sgt[hs_], maskT[hs_]
        n_it = 0
        for gg in range(4):
            for kbk in range(tt + 1):
                ksl = slice(kbk * 128, (kbk + 1) * 128)
                pt = ps[3]
                e_ = eb[n_it % 2]
                p_ = pT[n_it % 2]
                n_it += 1
                P.op("pe", lambda e, gg=gg, ksl=ksl: e.matmul(
                    pt[:], kT2[:, gg, ksl], qTt_c[:, 4 * gg:4 * gg + 4, :], start=True, stop=True),
                    reads=[kT2b[kbk], qTt_c.b], writes=[pt.b])
                P.op("act", lambda e, e_=e_: e.activation(e_[:], pt[:], AF.Exp, scale=0.125),
                     reads=[pt.b], writes=[e_.b])
                P.op("dve", lambda e, e_=e_, p_=p_, kbk=kbk: e.tensor_tensor(
                    p_[:].rearrange("p (a t) -> p a t", t=128), e_[:].rearrange("p (a t) -> p a t", t=128),
                    maskT_c[:, kbk, :].unsqueeze(1).to_broadcast([128, 4, 128]), ALU.mult),
                    reads=[e_.b, maskT_c.b], writes=[p_.b])
                for blk in range(4):
                    hidx = gg * 4 + blk
                    po = ps[hidx // 7]
                    c0 = (hidx % 7) * 65
                    P.op("pe", lambda e, po=po, c0=c0, p_=p_, blk=blk, kbk=kbk, gg=gg, hidx=hidx: e.matmul(
                        po[:, c0:c0 + 65], p_[:, blk * 128:(blk + 1) * 128], v1[:, kbk, gg, :],
                        start=(kbk == 0 and hidx % 7 == 0), stop=(kbk == tt)),
                        reads=[p_.b, v1b[kbk]], writes=[po.b])
                yield
        for bk in range(3):
            nh = 7 if bk < 2 else 2
            pv = ps[bk][:, 0:nh * 65].rearrange("p (k d) -> p k d", d=65)
            P.op("dve", lambda e, pv=pv, bk=bk, nh=nh: e.reciprocal(rden[:, bk * 7:bk * 7 + nh], pv[:, :, 64]),
                 reads=[ps[bk].b], writes=[rden.b])
        for hidx in range(16):
            gg, blk = hidx // 4, hidx % 4
            head = hidx
            po = ps[hidx // 7]
            c0 = (hidx % 7) * 65
            P.op("dve", lambda e, po=po, c0=c0, head=head, hidx=hidx: e.tensor_scalar(
                zz[:, head * 64:(head + 1) * 64], po[:, c0:c0 + 64], rden[:, hidx:hidx + 1], None, ALU.mult),
                reads=[po.b, rden.b], writes=[zz.b])
        P.op("pool", lambda e: e.tensor_tensor(zz[:], zz[:], sgt_c[:], ALU.mult),
             reads=[zz.b, sgt_c.b], writes=[zz.b])
        yield
        for c in range(8):
            pt2 = ps[c // 4]
            P.op("pe", lambda e, pt2=pt2, c=c: e.transpose(
                pt2[:, (c % 4) * 128:(c % 4 + 1) * 128], zz[:, c * 128:(c + 1) * 128], ident[:]),
                reads=[zz.b, ident.b], writes=[pt2.b])
        P.op("act", lambda e: e.copy(zTt[:, 0:4, :], ps[0][:].rearrange("p (c t) -> p c t", t=128)),
             reads=[ps[0].b], writes=[zTt.b])
        P.op("dve", lambda e: e.tensor_copy(zTt[:, 4:8, :], ps[1][:].rearrange("p (c t) -> p c t", t=128)),
             reads=[ps[1].b], writes=[zTt.b])
        back_tile(kb, g, b, tt, _Shift(zTt, tt * 128), [zTt.b], wout, x_src, x_dst, pbanks=(ps[2], ps[3]), xi=1)
        yield

    def n_seg1(tt):
        return 7 + (tt + 4) // 4 + (32 if tt >= 2 else 0) + 1

    def n_seg2(tt):
        return 4 * (tt + 1) + 2

    for b in range(NB):
        for step in range(NT + 1):
            s2 = stage2(b, step - 1) if step >= 1 else None
            s1 = stage1(b, step) if step < NT else None
            if s1 is None or s2 is None:
                for s_ in (s1, s2):
                    if s_ is not None:
                        for _ in s_:
                            pass
                continue
            n1, n2 = n_seg1(step), n_seg2(step - 1)
            a1 = a2 = 0.0
            d1 = d2 = False
            while not (d1 and d2):
                if not d1 and (d2 or a1 / n1 <= a2 / n2):
                    try:
                        next(s1)
                        a1 += 1
                    except StopIteration:
                        d1 = True
                else:
                    try:
                        next(s2)
                        a2 += 1
                    except StopIteration:
                        d2 = True
    kb.pop()


class _Off:
    def __init__(self, t, off):
        self.t, self.off, self.b = t, off, t.b

    def __getitem__(self, k):
        a, sl = k
        return self.t[a, sl.start + self.off:sl.stop + self.off]


def dsa_consts():
    c = {}
    pos = np.arange(S, dtype=np.float32)[:, None]
    for half, nm in ((32, "32"), (64, "64")):
        inv = (10000.0 ** (-np.arange(half, dtype=np.float32) / half)).astype(np.float32)
        ang = (pos * inv[None, :]).astype(np.float32)
        c["cos" + nm] = np.cos(ang).astype(np.float32)
        c["sin" + nm] = np.sin(ang).astype(np.float32)
    t = np.arange(128)[:, None]
    s_ = np.arange(128)[None, :]
    c["causal"] = np.where(s_ <= t, 0.0, NEG).astype(np.float32)
    return c


LAYER_FNS[0] = layer_dsa2


def layer_rwkv(kb, g, W, x_src, x_dst):
    nc, P = kb.nc, kb.P
    ps = g["ps"]
    ident = g["ident"]
    kb.push()
    wout = kb.sb("rw_wout", [128, 8, 1024], BF16)
    names = ["R", "Wd", "K", "A", "B", "V", "Y"]
    if "rw_scr" not in g:
        g["rw_scr"] = {n: kb.dram("rw_" + n, [NB, 16, S, 64]) for n in names}
        g["rw_scr"]["SG"] = kb.dram("rw_SG", [NB, S, 1024])
        if DEBUG_OUT:
            for nm in ("D1", "D2", "D3"):
                g["rw_scr"][nm] = kb.dram("rw_" + nm, [NB, S, 1024])
        g["rw_scr"]["BON"] = kb.dram("rw_BON", [NB, S, 1024])
    scr = g["rw_scr"]

    def tok_view(n, b, tt):
        return scr[n].t[b].rearrange("h t j -> t h j")[tt * 128:(tt + 1) * 128]

    def bc_row(dst, src_row):
        P.dma("sp", dst[:], src_row.to_broadcast([128, 1024]), writes=[dst.b])

    kb.push()
    win = kb.sb("rw_win", [128, 4, 8, 1024], BF16)
    w1a1 = kb.sb("rw_w1a1", [128, 8, 128], BF16)
    w2e = [kb.sb("rw_w2e", [65, 1024], BF16) for _ in range(2)]
    muT = kb.sb("rw_mu", [128, 6, 8], F32)
    kkb = kb.sb("rw_kk", [128, 1024], F32)
    kab = kb.sb("rw_ka", [128, 1024], F32)
    rkb = kb.sb("rw_rk", [128, 1024], F32)
    begin_load(kb, g)
    compute_mod(kb, g, 2, W)
    for n in range(4):
        load_w(kb, g, _W4(win, n), 0, W["rwkv_w_in"][0, n], 1024)
    load_w(kb, g, wout, 0, W["rwkv_w_out"][0], 1024)
    load_w(kb, g, w1a1, 0, W["rwkv_w1"][0], 64)
    load_w(kb, g, w1a1, 64, W["rwkv_a1"][0], 64)
    for i, (m2_, m0_) in enumerate((("rwkv_w2", "rwkv_w0"), ("rwkv_a2", "rwkv_a0"))):
        stg = g["stage"][g["stage_n"] % 2]
        g["stage_n"] += 1
        sv = stg[:].rearrange("p k n -> p (k n)")
        P.dma("sp", sv[0:64, 0:1024], W[m2_][0], writes=[stg.b])
        P.dma("sp", sv[64:65, 0:1024], W[m0_][0:1, :], writes=[stg.b])
        P.op("dve", lambda e, i=i, sv=sv: e.tensor_copy(w2e[i][:], sv[0:65, 0:1024]),
             reads=[stg.b], writes=[w2e[i].b])
    end_load(kb, g)
    for n in range(6):
        P.dma("sp", muT[:, n, :], W["rwkv_mu"][0, n].rearrange("(k p) -> p k", p=128),
              writes=[muT.b], allow_slow_non_contiguous=True)
    bc_row(kkb, W["rwkv_k_k"][0:1, :])
    bc_row(kab, W["rwkv_k_a"][0:1, :])
    bc_row(rkb, W["rwkv_r_k"][0].rearrange("h j -> (h j)").unsqueeze(0))
    TB = 256
    NTL = TB // 128
    hTb = kb.sb("rw_hT", [128, 8, 1 + TB], BF16)
    dT = kb.sb("rw_dT", [128, 8, TB], BF16)
    xsT = kb.sb("rw_xsT", [128, 8, TB], BF16)
    Rt = kb.sb("rw_R", [128, NTL, 1024], F32)
    pad_ = kb.sb("rw_pad", [128, 256], F32)
    Kt = kb.sb("rw_K", [128, NTL, 1024], F32)
    Vt = kb.sb("rw_V", [128, NTL, 1024], F32)
    SGt = kb.sb("rw_SG", [128, NTL, 1024], F32)
    Wdt = kb.sb("rw_Wd", [128, 1024], F32)
    Ast = kb.sb("rw_As", [128, 1024], F32)
    t1e = [kb.sb("rw_t1e", [65, TB], BF16) for _ in range(2)]
    tm1 = kb.sb("rw_tm1", [128, 1024], F32)
    tm2 = kb.sb("rw_tm2", [128, 1024], F32)
    sm = kb.sb("rw_sm", [128, 32], F32)
    for i in range(2):
        P.op("dve", lambda e, i=i: e.memset(t1e[i][:], 1.0), writes=[t1e[i].b])
    hd = lambda ap: ap.rearrange("p (h j) -> p h j", j=64)
    bcj = lambda ap: ap.unsqueeze(2).to_broadcast([128, 16, 64])
    for b in range(NB):
        P.op("dve", lambda e: e.memset(hTb[:, :, 0:1], 0.0), writes=[hTb.b])
        for tb in range(S // TB):
            for tl in range(NTL):
                front_tile(kb, g, b, tb * NTL + tl, x_src, _Shift(hTb, tb * TB - 1), hTb.b)
            P.op("dve", lambda e: e.tensor_tensor(dT[:], hTb[:, :, 0:TB], hTb[:, :, 1:TB + 1], ALU.subtract),
                 reads=[hTb.b], writes=[dT.b])

            def make_xs(n):
                for c in range(8):
                    P.op("dve", lambda e, c=c, n=n: e.scalar_tensor_tensor(
                        xsT[:, c, :], dT[:, c, :], muT[:, n, c:c + 1], hTb[:, c, 1:TB + 1], ALU.mult, ALU.add),
                        reads=[dT.b, muT.b, hTb.b], writes=[xsT.b])
            for n, dst in ((0, Rt), (1, Kt), (2, Vt), (3, SGt)):
                make_xs(n)
                for tl in range(NTL):
                    for half in range(2):
                        pt = ps[2 + half + 2 * (n % 2)]
                        for c in range(8):
                            P.op("pe", lambda e, pt=pt, c=c, tl=tl, half=half, n=n: e.matmul(
                                pt[:], xsT[:, c, tl * 128:(tl + 1) * 128], win[:, n, c, half * 512:(half + 1) * 512],
                                start=(c == 0), stop=(c == 7)),
                                reads=[xsT.b, win.b], writes=[pt.b])
                        if n == 3:
                            P.op("act", lambda e, pt=pt, tl=tl, half=half, dst=dst: e.activation(
                                dst[:, tl, half * 512:(half + 1) * 512], pt[:], AF.Silu),
                                reads=[pt.b], writes=[dst.b])
                        else:
                            P.op("act", lambda e, pt=pt, tl=tl, half=half, dst=dst: e.copy(
                                dst[:, tl, half * 512:(half + 1) * 512], pt[:]),
                                reads=[pt.b], writes=[dst.b])
            for i, n in ((0, 4), (1, 5)):
                make_xs(n)
                for c in range(8):
                    P.op("pe", lambda e, c=c, i=i: e.matmul(
                        ps[4][0:64, 0:TB], w1a1[:, c, i * 64:(i + 1) * 64], xsT[:, c, :],
                        start=(c == 0), stop=(c == 7)),
                        reads=[w1a1.b, xsT.b], writes=[ps[4].b], pos="T0")
                if i == 0:
                    P.op("act", lambda e, i=i: e.activation(t1e[i][0:64, :], ps[4][0:64, 0:TB], AF.Tanh),
                         reads=[ps[4].b], writes=[t1e[i].b])
                else:
                    P.op("act", lambda e, i=i: e.copy(t1e[i][0:64, :], ps[4][0:64, 0:TB]),
                         reads=[ps[4].b], writes=[t1e[i].b])
            for tl in range(NTL):
                tt = tb * NTL + tl
                R_, K_, V_ = Rt[:, tl, :], Kt[:, tl, :], Vt[:, tl, :]
                for i, dst in ((0, Wdt), (1, Ast)):
                    for half in range(2):
                        pt = ps[2 + half]
                        P.op("pe", lambda e, pt=pt, i=i, tl=tl, half=half: e.matmul(
                            pt[:], t1e[i][:, tl * 128:(tl + 1) * 128], w2e[i][:, half * 512:(half + 1) * 512],
                            start=True, stop=True),
                            reads=[t1e[i].b, w2e[i].b], writes=[pt.b], pos="K65")
                        P.op("act", lambda e, pt=pt, half=half, dst=dst: e.activation(
                            dst[:, half * 512:(half + 1) * 512], pt[:], AF.Sigmoid),
                            reads=[pt.b], writes=[dst.b])
                P.op("act", lambda e: e.activation(Wdt[:], Wdt[:], AF.Exp, scale=-0.6065306597126334),
                     reads=[Wdt.b], writes=[Wdt.b])
                if DEBUG_OUT:
                    P.dma("sp", scr["D1"].t[b, tt * 128:(tt + 1) * 128, :], K_, reads=[Kt.b], writes=[scr["D1"].b])
                    P.dma("sp", scr["D2"].t[b, tt * 128:(tt + 1) * 128, :], Ast[:], reads=[Ast.b], writes=[scr["D2"].b])
                    P.dma("sp", scr["D3"].t[b, tt * 128:(tt + 1) * 128, :], kkb[:], reads=[kkb.b], writes=[scr["D3"].b])
                P.op("dve", lambda e, K_=K_: e.tensor_tensor(tm1[:], K_, kkb[:], ALU.mult),
                     reads=[Kt.b, kkb.b], writes=[tm1.b])
                P.op("act", lambda e: e.activation(tm2[:], tm1[:], AF.Square), reads=[tm1.b], writes=[tm2.b])
                P.op("dve", lambda e: e.tensor_reduce(sm[:, 0:16], hd(tm2[:]), AX.X, ALU.add),
                     reads=[tm2.b], writes=[sm.b])
                P.op("act", lambda e: e.activation(sm[:, 0:16], sm[:, 0:16], AF.Sqrt), reads=[sm.b], writes=[sm.b])
                P.op("dve", lambda e: e.tensor_scalar(sm[:, 0:16], sm[:, 0:16], 1e-12, None, ALU.max),
                     reads=[sm.b], writes=[sm.b])
                P.op("dve", lambda e: e.reciprocal(sm[:, 0:16], sm[:, 0:16]), reads=[sm.b], writes=[sm.b])
                P.op("dve", lambda e: e.tensor_tensor(hd(tm1[:]), hd(tm1[:]), bcj(sm[:, 0:16]), ALU.mult),
                     reads=[tm1.b, sm.b], writes=[tm1.b])
                P.op("pool", lambda e: e.tensor_tensor(tm2[:], tm1[:], Ast[:], ALU.mult),
                     reads=[tm1.b, Ast.b], writes=[tm2.b])
                P.dma("sp", tok_view("B", b, tt), hd(tm2[:]), reads=[tm2.b], writes=[scr["B"].b])
                P.op("dve", lambda e: e.tensor_scalar(tm1[:], tm1[:], -1.0, None, ALU.mult),
                     reads=[tm1.b], writes=[tm1.b])
                P.dma("sp", tok_view("A", b, tt), hd(tm1[:]), reads=[tm1.b], writes=[scr["A"].b])
                P.dma("act", tok_view("Wd", b, tt), hd(Wdt[:]), reads=[Wdt.b], writes=[scr["Wd"].b])
                P.dma("act", tok_view("R", b, tt), hd(R_), reads=[Rt.b], writes=[scr["R"].b])
                P.dma("act", tok_view("V", b, tt), hd(V_), reads=[Vt.b], writes=[scr["V"].b])
                P.op("dve", lambda e: e.scalar_tensor_tensor(tm2[:], Ast[:], -1.0, kab[:], ALU.add, ALU.mult),
                     reads=[Ast.b, kab.b, tm2.b], writes=[tm2.b])
                P.op("dve", lambda e, K_=K_: e.scalar_tensor_tensor(tm2[:], tm2[:], 1.0, K_, ALU.add, ALU.mult),
                     reads=[tm2.b, Kt.b], writes=[tm2.b])
                P.dma("sp", tok_view("K", b, tt), hd(tm2[:]), reads=[tm2.b], writes=[scr["K"].b])
                P.op("pool", lambda e, R_=R_: e.tensor_tensor(tm1[:], tm2[:], R_, ALU.mult),
                     reads=[tm2.b, Rt.b, tm1.b], writes=[tm1.b])
                P.op("pool", lambda e: e.tensor_tensor(tm1[:], tm1[:], rkb[:], ALU.mult),
                     reads=[tm1.b, rkb.b], writes=[tm1.b])
                P.op("dve", lambda e: e.tensor_reduce(sm[:, 16:32], hd(tm1[:]), AX.X, ALU.add),
                     reads=[tm1.b], writes=[sm.b])
                P.op("dve", lambda e, V_=V_: e.tensor_tensor(hd(tm1[:]), hd(V_), bcj(sm[:, 16:32]), ALU.mult),
                     reads=[Vt.b, sm.b, tm1.b], writes=[tm1.b])
                P.dma("sp", scr["BON"].t[b, tt * 128:(tt + 1) * 128, :], tm1[:], reads=[tm1.b], writes=[scr["BON"].b])
                P.dma("act", scr["SG"].t[b, tt * 128:(tt + 1) * 128, :], SGt[:, tl, :], reads=[SGt.b], writes=[scr["SG"].b])
            P.op("dve", lambda e: e.tensor_copy(hTb[:, :, 0:1], hTb[:, :, TB:TB + 1]),
                 reads=[hTb.b], writes=[hTb.b])
    kb.pop()

    kb.push()
    TC = 32
    blk = {n: [kb.sb("rb_" + n, [128, TC, 64], F32) for _ in range(2)] for n in ("Wd", "A", "B", "K", "R")}
    Vb = [kb.sb("rb_V", [128, TC, 16], F32) for _ in range(2)]
    Yb = [kb.sb("rb_Y", [128, TC, 16], F32) for _ in range(2)]
    St = [kb.sb("rb_S", [128, 16, 64], F32) for _ in range(2)]
    t1 = kb.sb("rb_t1", [128, 16, 64], F32)
    t2 = kb.sb("rb_t2", [128, 16, 64], F32)
    t3 = kb.sb("rb_t3", [128, 16, 64], F32)
    t4 = kb.sb("rb_t4", [128, 16, 64], F32)
    sa = kb.sb("rb_sa", [128, 16], F32)
    P.op("dve", lambda e: e.memset(St[0][:], 0.0), writes=[St[0].b])
    qs = ("sp", "act")
    step = 0
    for tbk in range(S // TC):
        t0 = tbk * TC
        i2 = tbk % 2
        nq = 0
        for n in ("Wd", "A", "B", "K", "R"):
            src = scr[n].t.rearrange("b h t j -> (b h) t j")[:, t0:t0 + TC, :]
            for iq in range(4):
                P.dma(qs[nq % 2], blk[n][i2][iq * 32:(iq + 1) * 32, :, :], src,
                      reads=[scr[n].b], writes=[blk[n][i2].b])
                nq += 1
        for iq in range(4):
            src = scr["V"].t.rearrange("b h t j -> (b h) t j")[:, t0:t0 + TC, iq * 16:(iq + 1) * 16]
            P.dma(qs[nq % 2], Vb[i2][iq * 32:(iq + 1) * 32, :, :], src, reads=[scr["V"].b], writes=[Vb[i2].b])
            nq += 1
        for t in range(TC):
            So, Sn = St[step % 2], St[(step + 1) % 2]
            step += 1
            bcr = lambda n, t=t: blk[n][i2][:, t, :].unsqueeze(1).to_broadcast([128, 16, 64])
            a_bc, w_bc, b_bc, k_bc, r_bc = bcr("A"), bcr("Wd"), bcr("B"), bcr("K"), bcr("R")
            v_bc = Vb[i2][:, t, :].unsqueeze(2).to_broadcast([128, 16, 64])
            P.op("dve", lambda e, So=So, a_bc=a_bc: e.tensor_tensor(t1[:], So[:], a_bc, ALU.mult),
                 reads=[So.b, blk["A"][i2].b], writes=[t1.b])
            P.op("dve", lambda e: e.tensor_reduce(sa[:], t1[:], AX.X, ALU.add), reads=[t1.b], writes=[sa.b])
            P.op("pool", lambda e, So=So, w_bc=w_bc: e.tensor_tensor(t2[:], So[:], w_bc, ALU.mult),
                 reads=[So.b, blk["Wd"][i2].b], writes=[t2.b])
            P.op("pool", lambda e, v_bc=v_bc, k_bc=k_bc: e.tensor_tensor(t3[:], v_bc, k_bc, ALU.mult),
                 reads=[Vb[i2].b, blk["K"][i2].b], writes=[t3.b])
            P.op("dve", lambda e, b_bc=b_bc: e.tensor_tensor(
                t1[:], sa[:].unsqueeze(2).to_broadcast([128, 16, 64]), b_bc, ALU.mult),
                reads=[sa.b, blk["B"][i2].b], writes=[t1.b])
            P.op("dve", lambda e: e.tensor_tensor(t1[:], t1[:], t3[:], ALU.add),
                 reads=[t1.b, t3.b], writes=[t1.b])
            P.op("dve", lambda e, Sn=Sn: e.tensor_tensor(Sn[:], t1[:], t2[:], ALU.add),
                 reads=[t1.b, t2.b], writes=[Sn.b])
            P.op("pool", lambda e, Sn=Sn, r_bc=r_bc: e.tensor_tensor(t4[:], Sn[:], r_bc, ALU.mult),
                 reads=[Sn.b, blk["R"][i2].b], writes=[t4.b])
            y_ap = Yb[i2][:, t, :]
            P.op("dve", lambda e, y_ap=y_ap: e.tensor_reduce(y_ap, t4[:], AX.X, ALU.add),
                 reads=[t4.b], writes=[Yb[i2].b])
        for iq in range(4):
            dst = scr["Y"].t.rearrange("b h t j -> (b h) t j")[:, t0:t0 + TC, iq * 16:(iq + 1) * 16]
            P.dma(qs[iq % 2], dst, Yb[i2][iq * 32:(iq + 1) * 32, :, :], reads=[Yb[i2].b], writes=[scr["Y"].b])
    kb.pop()

    kb.push()
    lnw = kb.sb("rw_lnw", [128, 1024], F32)
    lnb = kb.sb("rw_lnb", [128, 1024], F32)
    bc_row(lnw, W["rwkv_ln_w"][0:1, :])
    bc_row(lnb, W["rwkv_ln_b"][0:1, :])
    yt = kb.sb("rc_y", [128, 1024], F32)
    y2 = kb.sb("rc_y2", [128, 1024], F32)
    bon = kb.sb("rc_bon", [128, 1024], F32)
    sgc = kb.sb("rc_sg", [128, 1024], F32)
    smc = kb.sb("rc_sm", [128, 32], F32)
    zTt = kb.sb("rc_zT", [128, 8, 128], BF16)
    for b in range(NB):
        for tt in range(NT):
            P.dma("sp", hd(yt[:]), tok_view("Y", b, tt), reads=[scr["Y"].b], writes=[yt.b])
            P.dma("act", bon[:], scr["BON"].t[b, tt * 128:(tt + 1) * 128, :], reads=[scr["BON"].b], writes=[bon.b])
            P.dma("act", sgc[:], scr["SG"].t[b, tt * 128:(tt + 1) * 128, :], reads=[scr["SG"].b], writes=[sgc.b])
            P.op("dve", lambda e: e.tensor_reduce(smc[:, 0:16], hd(yt[:]), AX.X, ALU.add), reads=[yt.b], writes=[smc.b])
            P.op("dve", lambda e: e.tensor_scalar(smc[:, 0:16], smc[:, 0:16], -1.0 / 64.0, None, ALU.mult),
                 reads=[smc.b], writes=[smc.b])
            P.op("dve", lambda e: e.tensor_tensor(hd(yt[:]), hd(yt[:]), bcj(smc[:, 0:16]), ALU.add),
                 reads=[yt.b, smc.b], writes=[yt.b])
            P.op("act", lambda e: e.activation(y2[:], yt[:], AF.Square), reads=[yt.b], writes=[y2.b])
            P.op("dve", lambda e: e.tensor_reduce(smc[:, 16:32], hd(y2[:]), AX.X, ALU.add), reads=[y2.b], writes=[smc.b])
            P.op("dve", lambda e: e.tensor_scalar(smc[:, 16:32], smc[:, 16:32], 1.0 / 64.0, 64e-5, ALU.mult, ALU.add),
                 reads=[smc.b], writes=[smc.b])
            P.op("act", lambda e: e.activation(smc[:, 16:32], smc[:, 16:32], AF.Sqrt), reads=[smc.b], writes=[smc.b])
            P.op("dve", lambda e: e.reciprocal(smc[:, 16:32], smc[:, 16:32]), reads=[smc.b], writes=[smc.b])
            P.op("dve", lambda e: e.tensor_tensor(hd(yt[:]), hd(yt[:]), bcj(smc[:, 16:32]), ALU.mult),
                 reads=[yt.b, smc.b], writes=[yt.b])
            P.op("pool", lambda e: e.tensor_tensor(yt[:], yt[:], lnw[:], ALU.mult), reads=[yt.b, lnw.b], writes=[yt.b])
            P.op("pool", lambda e: e.tensor_tensor(yt[:], yt[:], lnb[:], ALU.add), reads=[yt.b, lnb.b], writes=[yt.b])
            P.op("dve", lambda e: e.tensor_tensor(yt[:], yt[:], bon[:], ALU.add), reads=[yt.b, bon.b], writes=[yt.b])
            P.op("dve", lambda e: e.tensor_tensor(yt[:], yt[:], sgc[:], ALU.mult), reads=[yt.b, sgc.b], writes=[yt.b])
            for c in range(8):
                pt = ps[4 + c // 4]
                P.op("pe", lambda e, pt=pt, c=c: e.transpose(
                    pt[:, (c % 4) * 128:(c % 4 + 1) * 128], yt[:, c * 128:(c + 1) * 128], ident[:]),
                    reads=[yt.b, ident.b], writes=[pt.b])
            P.op("act", lambda e: e.copy(zTt[:, 0:4, :], ps[4][:].rearrange("p (c t) -> p c t", t=128)),
                 reads=[ps[4].b], writes=[zTt.b])
            P.op("dve", lambda e: e.tensor_copy(zTt[:, 4:8, :], ps[5][:].rearrange("p (c t) -> p c t", t=128)),
                 reads=[ps[5].b], writes=[zTt.b])
            back_tile(kb, g, b, tt, _Shift(zTt, tt * 128), [zTt.b], wout, x_src, x_dst)
    kb.pop()
    kb.pop()


def layer_rwkv2(kb, g, W, x_src, x_dst):
    nc, P = kb.nc, kb.P
    ps = g["ps"]
    ident = g["ident"]
    kb.push()
    wout = kb.sb("rw_wout", [128, 8, 1024], BF16)
    names = ["R", "Wd", "K", "A", "B", "V", "Y"]
    if "rw_scr" not in g:
        g["rw_scr"] = {n: kb.dram("rw_" + n, [NB, S, 1024]) for n in names}
        g["rw_scr"]["SG"] = kb.dram("rw_SG", [NB, S, 1024])
        if DEBUG_OUT:
            for nm in ("D1", "D2", "D3"):
                g["rw_scr"][nm] = kb.dram("rw_" + nm, [NB, S, 1024])
        g["rw_scr"]["BON"] = kb.dram("rw_BON", [NB, S, 1024])
    scr = g["rw_scr"]

    def tok_view(n, b, tt):
        return scr[n].t[b, tt * 128:(tt + 1) * 128, :]

    def bc_row(dst, src_row):
        P.dma("sp", dst[:], src_row.to_broadcast([128, 1024]), writes=[dst.b])

    kb.push()
    win = kb.sb("rw_win", [128, 4, 8, 1024], BF16)
    w1a1 = kb.sb("rw_w1a1", [128, 8, 128], BF16)
    w2e = [kb.sb("rw_w2e", [65, 1024], BF16) for _ in range(2)]
    muT = kb.sb("rw_mu", [128, 6, 8], F32)
    kkb = kb.sb("rw_kk", [128, 1024], F32)
    kab = kb.sb("rw_ka", [128, 1024], F32)
    rkb = kb.sb("rw_rk", [128, 1024], F32)
    begin_load(kb, g)
    compute_mod(kb, g, 2, W)
    for n in range(4):
        load_w(kb, g, _W4(win, n), 0, W["rwkv_w_in"][0, n], 1024)
    load_w(kb, g, wout, 0, W["rwkv_w_out"][0], 1024)
    load_w(kb, g, w1a1, 0, W["rwkv_w1"][0], 64)
    load_w(kb, g, w1a1, 64, W["rwkv_a1"][0], 64)
    for i, (m2_, m0_) in enumerate((("rwkv_w2", "rwkv_w0"), ("rwkv_a2", "rwkv_a0"))):
        stg = g["stage"][g["stage_n"] % 2]
        g["stage_n"] += 1
        sv = stg[:].rearrange("p k n -> p (k n)")
        P.dma("sp", sv[0:64, 0:1024], W[m2_][0], writes=[stg.b])
        P.dma("sp", sv[64:65, 0:1024], W[m0_][0:1, :], writes=[stg.b])
        P.op("dve", lambda e, i=i, sv=sv: e.tensor_copy(w2e[i][:], sv[0:65, 0:1024]),
             reads=[stg.b], writes=[w2e[i].b])
    end_load(kb, g)
    for n in range(6):
        P.dma("sp", muT[:, n, :], W["rwkv_mu"][0, n].rearrange("(k p) -> p k", p=128),
              writes=[muT.b], allow_slow_non_contiguous=True)
    bc_row(kkb, W["rwkv_k_k"][0:1, :])
    bc_row(kab, W["rwkv_k_a"][0:1, :])
    bc_row(rkb, W["rwkv_r_k"][0].rearrange("h j -> (h j)").unsqueeze(0))
    TB = 256
    NTL = TB // 128
    hTb = kb.sb("rw_hT", [128, 8, 1 + TB], BF16)
    dT = kb.sb("rw_dT", [128, 8, TB], BF16)
    xsT = kb.sb("rw_xsT", [128, 8, TB], BF16)
    Rt = kb.sb("rw_R", [128, NTL, 1024], F32)
    pad_ = kb.sb("rw_pad", [128, 256], F32)
    Kt = kb.sb("rw_K", [128, NTL, 1024], F32)
    Vt = kb.sb("rw_V", [128, NTL, 1024], F32)
    SGt = kb.sb("rw_SG", [128, NTL, 1024], F32)
    Wdt = kb.sb("rw_Wd", [128, 1024], F32)
    Ast = kb.sb("rw_As", [128, 1024], F32)
    t1e = [kb.sb("rw_t1e", [65, TB], BF16) for _ in range(2)]
    tm1 = kb.sb("rw_tm1", [128, 1024], F32)
    tm2 = kb.sb("rw_tm2", [128, 1024], F32)
    sm = kb.sb("rw_sm", [128, 32], F32)
    for i in range(2):
        P.op("dve", lambda e, i=i: e.memset(t1e[i][:], 1.0), writes=[t1e[i].b])
    hd = lambda ap: ap.rearrange("p (h j) -> p h j", j=64)
    bcj = lambda ap: ap.unsqueeze(2).to_broadcast([128, 16, 64])
    for b in range(NB):
        P.op("dve", lambda e: e.memset(hTb[:, :, 0:1], 0.0), writes=[hTb.b])
        for tb in range(S // TB):
            for tl in range(NTL):
                front_tile(kb, g, b, tb * NTL + tl, x_src, _Shift(hTb, tb * TB - 1), hTb.b)
            P.op("dve", lambda e: e.tensor_tensor(dT[:], hTb[:, :, 0:TB], hTb[:, :, 1:TB + 1], ALU.subtract),
                 reads=[hTb.b], writes=[dT.b])

            def make_xs(n):
                for c in range(8):
                    P.op("dve", lambda e, c=c, n=n: e.scalar_tensor_tensor(
                        xsT[:, c, :], dT[:, c, :], muT[:, n, c:c + 1], hTb[:, c, 1:TB + 1], ALU.mult, ALU.add),
                        reads=[dT.b, muT.b, hTb.b], writes=[xsT.b])
            for n, dst in ((0, Rt), (1, Kt), (2, Vt), (3, SGt)):
                make_xs(n)
                for tl in range(NTL):
                    for half in range(2):
                        pt = ps[2 + half + 2 * (n % 2)]
                        for c in range(8):
                            P.op("pe", lambda e, pt=pt, c=c, tl=tl, half=half, n=n: e.matmul(
                                pt[:], xsT[:, c, tl * 128:(tl + 1) * 128], win[:, n, c, half * 512:(half + 1) * 512],
                                start=(c == 0), stop=(c == 7)),
                                reads=[xsT.b, win.b], writes=[pt.b])
                        if n == 3:
                            P.op("act", lambda e, pt=pt, tl=tl, half=half, dst=dst: e.activation(
                                dst[:, tl, half * 512:(half + 1) * 512], pt[:], AF.Silu),
                                reads=[pt.b], writes=[dst.b])
                        else:
                            P.op("act", lambda e, pt=pt, tl=tl, half=half, dst=dst: e.copy(
                                dst[:, tl, half * 512:(half + 1) * 512], pt[:]),
                                reads=[pt.b], writes=[dst.b])
            for i, n in ((0, 4), (1, 5)):
                make_xs(n)
                for c in range(8):
                    P.op("pe", lambda e, c=c, i=i: e.matmul(
                        ps[4][0:64, 0:TB], w1a1[:, c, i * 64:(i + 1) * 64], xsT[:, c, :],
                        start=(c == 0), stop=(c == 7)),
                        reads=[w1a1.b, xsT.b], writes=[ps[4].b], pos="T0")
                if i == 0:
                    P.op("act", lambda e, i=i: e.activation(t1e[i][0:64, :], ps[4][0:64, 0:TB], AF.Tanh),
                         reads=[ps[4].b], writes=[t1e[i].b])
                else:
                    P.op("act", lambda e, i=i: e.copy(t1e[i][0:64, :], ps[4][0:64, 0:TB]),
                         reads=[ps[4].b], writes=[t1e[i].b])
            for tl in range(NTL):
                tt = tb * NTL + tl
                R_, K_, V_ = Rt[:, tl, :], Kt[:, tl, :], Vt[:, tl, :]
                for i, dst in ((0, Wdt), (1, Ast)):
                    for half in range(2):
                        pt = ps[2 + half]
                        P.op("pe", lambda e, pt=pt, i=i, tl=tl, half=half: e.matmul(
                            pt[:], t1e[i][:, tl * 128:(tl + 1) * 128], w2e[i][:, half * 512:(half + 1) * 512],
                            start=True, stop=True),
                            reads=[t1e[i].b, w2e[i].b], writes=[pt.b], pos="K65")
                        P.op("act", lambda e, pt=pt, half=half, dst=dst: e.activation(
                            dst[:, half * 512:(half + 1) * 512], pt[:], AF.Sigmoid),
                            reads=[pt.b], writes=[dst.b])
                P.op("dve", lambda e: e.tensor_scalar(Wdt[:], Wdt[:], -0.6065306597126334, None, ALU.mult),
                     reads=[Wdt.b], writes=[Wdt.b])
                if DEBUG_OUT:
                    P.dma("sp", scr["D1"].t[b, tt * 128:(tt + 1) * 128, :], K_, reads=[Kt.b], writes=[scr["D1"].b])
                    P.dma("sp", scr["D2"].t[b, tt * 128:(tt + 1) * 128, :], Ast[:], reads=[Ast.b], writes=[scr["D2"].b])
                    P.dma("sp", scr["D3"].t[b, tt * 128:(tt + 1) * 128, :], kkb[:], reads=[kkb.b], writes=[scr["D3"].b])
                P.op("dve", lambda e, K_=K_: e.tensor_tensor(tm1[:], K_, kkb[:], ALU.mult),
                     reads=[Kt.b, kkb.b], writes=[tm1.b])
                P.op("act", lambda e: e.activation(tm2[:], tm1[:], AF.Square), reads=[tm1.b], writes=[tm2.b])
                P.op("dve", lambda e: e.tensor_reduce(sm[:, 0:16], hd(tm2[:]), AX.X, ALU.add),
                     reads=[tm2.b], writes=[sm.b])
                P.op("act", lambda e: e.activation(sm[:, 0:16], sm[:, 0:16], AF.Sqrt), reads=[sm.b], writes=[sm.b])
                P.op("dve", lambda e: e.tensor_scalar(sm[:, 0:16], sm[:, 0:16], 1e-12, None, ALU.max),
                     reads=[sm.b], writes=[sm.b])
                P.op("dve", lambda e: e.reciprocal(sm[:, 0:16], sm[:, 0:16]), reads=[sm.b], writes=[sm.b])
                P.op("dve", lambda e: e.tensor_tensor(hd(tm1[:]), hd(tm1[:]), bcj(sm[:, 0:16]), ALU.mult),
                     reads=[tm1.b, sm.b], writes=[tm1.b])
                P.op("pool", lambda e: e.tensor_tensor(tm2[:], tm1[:], Ast[:], ALU.mult),
                     reads=[tm1.b, Ast.b], writes=[tm2.b])
                P.dma("sp", tok_view("B", b, tt), tm2[:], reads=[tm2.b], writes=[scr["B"].b])
                P.op("dve", lambda e: e.tensor_scalar(tm1[:], tm1[:], -1.0, None, ALU.mult),
                     reads=[tm1.b], writes=[tm1.b])
                P.dma("sp", tok_view("A", b, tt), tm1[:], reads=[tm1.b], writes=[scr["A"].b])
                P.dma("act", tok_view("Wd", b, tt), Wdt[:], reads=[Wdt.b], writes=[scr["Wd"].b])
                P.dma("act", tok_view("R", b, tt), R_, reads=[Rt.b], writes=[scr["R"].b])
                P.dma("act", tok_view("V", b, tt), V_, reads=[Vt.b], writes=[scr["V"].b])
                P.op("dve", lambda e: e.scalar_tensor_tensor(tm2[:], Ast[:], -1.0, kab[:], ALU.add, ALU.mult),
                     reads=[Ast.b, kab.b, tm2.b], writes=[tm2.b])
                P.op("dve", lambda e, K_=K_: e.scalar_tensor_tensor(tm2[:], tm2[:], 1.0, K_, ALU.add, ALU.mult),
                     reads=[tm2.b, Kt.b], writes=[tm2.b])
                P.dma("sp", tok_view("K", b, tt), tm2[:], reads=[tm2.b], writes=[scr["K"].b])
                P.op("pool", lambda e, R_=R_: e.tensor_tensor(tm1[:], tm2[:], R_, ALU.mult),
                     reads=[tm2.b, Rt.b, tm1.b], writes=[tm1.b])
                P.op("pool", lambda e: e.tensor_tensor(tm1[:], tm1[:], rkb[:], ALU.mult),
                     reads=[tm1.b, rkb.b], writes=[tm1.b])
                P.op("dve", lambda e: e.tensor_reduce(sm[:, 16:32], hd(tm1[:]), AX.X, ALU.add),
                     reads=[tm1.b], writes=[sm.b])
                P.op("dve", lambda e, V_=V_: e.tensor_tensor(hd(tm1[:]), hd(V_), bcj(sm[:, 16:32]), ALU.mult),
                     reads=[Vt.b, sm.b, tm1.b], writes=[tm1.b])
                P.dma("sp", scr["BON"].t[b, tt * 128:(tt + 1) * 128, :], tm1[:], reads=[tm1.b], writes=[scr["BON"].b])
                P.dma("act", scr["SG"].t[b, tt * 128:(tt + 1) * 128, :], SGt[:, tl, :], reads=[SGt.b], writes=[scr["SG"].b])
            P.op("dve", lambda e: e.tensor_copy(hTb[:, :, 0:1], hTb[:, :, TB:TB + 1]),
                 reads=[hTb.b], writes=[hTb.b])
    kb.pop()

    kb.push()
    C_ = g["consts"]
    psr = [kb.psum("rps%d" % i, [128, 512], F32) for i in range(1)] + g["ps"]
    PU_, PS_ = psr[0], psr[1]
    PY_ = PU_
    PI = psr[2:7]
    bank_ctr = [0]

    def nextbank():
        bank_ctr[0] += 1
        return PI[bank_ctr[0] % len(PI)]
    PTb = kb.psum("rpsb", [128, 1024], BF16)
    identb = g["identb"]
    cm = {}
    for nm, shp in (("rw_M1", [128, 128]), ("rw_M1s", [128, 128]), ("rw_M2", [128, 128]),
                    ("rw_mask4", [128, 512]), ("rw_maskL2", [128, 256]), ("rw_cind", [128, 2])):
        cm[nm] = kb.sb(nm, shp, F32)
        P.dma("sp", cm[nm][:], C_[nm][:, :], writes=[cm[nm].b])
    NU = 2
    per = [dict(
        Vv=kb.sb("c_V", [128, 512], BF16), Bg=kb.sb("c_Bg", [128, 512], BF16), Kg=kb.sb("c_Kg", [128, 512], BF16),
        ARt=kb.sb("c_ARt", [128, 4, 2, 128], BF16), Nall=kb.sb("c_Nall", [128, 8, 4, 128], BF16),
        Qall=kb.sb("c_Q", [128, 4, 128], BF16), Pall=kb.sb("c_P", [128, 8, 128], BF16),
        gC=kb.sb("c_gC", [128, 4, 2], F32), Yt=kb.sb("c_Yt", [128, 512], F32)) for _ in range(NU)]
    Rr = kb.sb("c_R", [128, 512], F32)
    LWt = kb.sb("c_LW", [128, 512], F32)
    Aa = kb.sb("c_A", [128, 512], F32)
    Bf = kb.sb("c_Bf", [128, 512], F32)
    Kf = kb.sb("c_Kf", [128, 512], F32)
    Vf = kb.sb("c_Vf", [128, 512], F32)
    Rb = kb.sb("c_Rb", [128, 512], BF16)
    Ab = kb.sb("c_Ab", [128, 512], BF16)
    Ee = kb.sb("c_E", [128, 512], F32)
    Bt_ = kb.sb("c_Bt", [128, 512], BF16)
    Kt_ = kb.sb("c_Kt", [128, 512], BF16)
    BKt = kb.sb("c_BKt", [128, 4, 2, 128], BF16)
    NTt = kb.sb("c_NT", [128, 8, 2, 128], BF16)
    XX = kb.sb("c_XX", [128, 2, 8, 2, 128], BF16)
    Th = kb.sb("c_Th", [128, 8, 128], BF16)
    Usb = kb.sb("c_U", [128, 512], BF16)
    STb = kb.sb("c_STb", [128, 4, 64], BF16)
    tmpS = kb.sb("c_tmpS", [128, 4, 64], F32)
    STs = [[kb.sb("c_ST", [128, 4, 64], F32) for hg in range(2)] for b in range(NB)]
    for b in range(NB):
        for hg in range(2):
            P.op("pool", lambda e, t_=STs[b][hg]: e.memset(t_[:], 0.0), writes=[STs[b][hg].b])
    identf = ident

    def gen_pre(u, b, hg, tt):
        c = per[u % NU]
        cols = slice(hg * 512, (hg + 1) * 512)
        rows = slice(tt * 128, (tt + 1) * 128)
        ld = (("R", Rr), ("Wd", LWt), ("A", Aa), ("B", Bf), ("K", Kf), ("V", Vf))
        for i, (nm, dst) in enumerate(ld):
            P.dma(("sp", "act")[i % 2], dst[:], scr[nm].t[b, rows, cols], reads=[scr[nm].b], writes=[dst.b])
        pgA = nextbank()
        P.op("pe", lambda e: e.matmul(pgA[:], cm["rw_M1"][:], LWt[:], start=True, stop=True),
             reads=[cm["rw_M1"].b, LWt.b], writes=[pgA.b])
        P.op("act", lambda e: e.activation(Ee[:], pgA[:], AF.Exp), reads=[pgA.b], writes=[Ee.b])
        P.op("pool", lambda e: e.tensor_tensor(Rb[:], Rr[:], Ee[:], ALU.mult), reads=[Rr.b, Ee.b], writes=[Rb.b])
        P.op("pool", lambda e: e.tensor_copy(c["Vv"][:], Vf[:]), reads=[Vf.b], writes=[c["Vv"].b])
        P.op("act", lambda e: e.activation(Ee[:], pgA[:], AF.Exp, scale=-1.0), reads=[pgA.b, Ee.b], writes=[Ee.b])
        P.op("pool", lambda e: e.tensor_tensor(Bt_[:], Bf[:], Ee[:], ALU.mult),
             reads=[Bf.b, Ee.b], writes=[Bt_.b])
        P.op("dve", lambda e: e.tensor_tensor(Kt_[:], Kf[:], Ee[:], ALU.mult),
             reads=[Kf.b, Ee.b], writes=[Kt_.b])
        yield
        pgB = nextbank()
        P.op("pe", lambda e: e.matmul(pgB[:], cm["rw_M1s"][:], LWt[:], start=True, stop=True),
             reads=[cm["rw_M1s"].b, LWt.b], writes=[pgB.b])
        P.op("act", lambda e: e.activation(Ee[:], pgB[:], AF.Exp), reads=[pgB.b, Ee.b], writes=[Ee.b])
        P.op("pool", lambda e: e.tensor_tensor(Ab[:], Aa[:], Ee[:], ALU.mult), reads=[Aa.b, Ee.b], writes=[Ab.b])
        pg2 = nextbank()
        P.op("pe", lambda e: e.matmul(pg2[:], cm["rw_M2"][:], LWt[:], start=True, stop=True),
             reads=[cm["rw_M2"].b, LWt.b], writes=[pg2.b])
        P.op("act", lambda e: e.activation(Ee[:], pg2[:], AF.Exp), reads=[pg2.b, Ee.b], writes=[Ee.b])
        P.op("pool", lambda e: e.tensor_tensor(c["Bg"][:], Bf[:], Ee[:], ALU.mult),
             reads=[Bf.b, Ee.b], writes=[c["Bg"].b])
        P.op("dve", lambda e: e.tensor_tensor(c["Kg"][:], Kf[:], Ee[:], ALU.mult),
             reads=[Kf.b, Ee.b], writes=[c["Kg"].b])
        pg3 = nextbank()
        for m in range(4):
            P.op("pe", lambda e, m=m: e.matmul(pg3[:, m * 2:m * 2 + 2], LWt[:, m * 128:(m + 1) * 128],
                                               cm["rw_cind"][:], start=True, stop=True),
                 reads=[LWt.b, cm["rw_cind"].b], writes=[pg3.b])
        P.op("act", lambda e: e.activation(c["gC"][:], pg3[:, 0:8].rearrange("p (m c) -> p m c", c=2), AF.Exp),
             reads=[pg3.b], writes=[c["gC"].b])
        yield
        for (src0, src1, dstT) in ((Ab, Rb, c["ARt"]), (Bt_, Kt_, BKt)):
            for m in range(4):
                for j_, src in enumerate((src0, src1)):
                    P.op("pe", lambda e, j_=j_, src=src, m=m: e.transpose(
                        PTb[:, (m * 2 + j_) * 128:(m * 2 + j_ + 1) * 128], src[:, m * 128:(m + 1) * 128], identb[:]),
                        reads=[src.b, identb.b], writes=[PTb.b])
            d_ap = dstT[:].rearrange("p m a t -> p (m a t)")
            if dstT is BKt:
                P.op("act", lambda e, d_ap=d_ap: e.copy(d_ap, PTb[:]), reads=[PTb.b], writes=[dstT.b])
            else:
                P.op("dve", lambda e, d_ap=d_ap: e.tensor_copy(d_ap, PTb[:]), reads=[PTb.b], writes=[dstT.b])
            yield
        for hh in range(8):
            m, h2 = hh // 2, hh % 2
            hs = slice(h2 * 64, (h2 + 1) * 64)
            pg5 = nextbank()
            arv = c["ARt"][hs, m, :, :].rearrange("p a t -> p (a t)")
            bkv = BKt[hs, m, :, :].rearrange("p a t -> p (a t)")
            for j_ in range(2):
                P.op("pe", lambda e, pg5=pg5, j_=j_, hs=hs, m=m, arv=arv: e.matmul(
                    pg5[:, j_ * 256:(j_ + 1) * 256], BKt[hs, m, j_, :], arv, start=True, stop=True),
                    reads=[BKt.b, c["ARt"].b], writes=[pg5.b], pos=(1 if h2 else None))
            P.op("dve", lambda e, pg5=pg5, hh=hh: e.tensor_tensor(
                c["Nall"][:, hh, :, :].rearrange("p a t -> p (a t)"), pg5[:], cm["rw_mask4"][:], ALU.mult),
                reads=[pg5.b, cm["rw_mask4"].b], writes=[c["Nall"].b])
            pg6 = nextbank()
            P.op("pe", lambda e, pg6=pg6, hs=hs, m=m, bkv=bkv: e.matmul(
                pg6[:, 0:256], c["ARt"][hs, m, 0, :], bkv, start=True, stop=True),
                reads=[BKt.b, c["ARt"].b], writes=[pg6.b], pos=(1 if h2 else None))
            P.op("dve", lambda e, pg6=pg6, hh=hh: e.tensor_tensor(
                NTt[:, hh, :, :].rearrange("p a t -> p (a t)"), pg6[:, 0:256], cm["rw_maskL2"][:], ALU.mult),
                reads=[pg6.b, cm["rw_maskL2"].b], writes=[NTt.b])
            if hh % 2 == 1:
                yield
        P.op("pool", lambda e: e.tensor_tensor(Th[:], c["Nall"][:, :, 0, :],
                                               identb[:].unsqueeze(1).to_broadcast([128, 8, 128]), ALU.add),
             reads=[c["Nall"].b, identb.b], writes=[Th.b])
        for k in range(1, 6):
            pp = k % 2
            for pr in range(4):
                bank = nextbank()
                for hl in range(2):
                    hh = pr * 2 + hl
                    if k == 1:
                        Xp, Xtp = c["Nall"][:, hh, 0, :], NTt[:, hh, 0, :]
                        rd = [c["Nall"].b, NTt.b]
                    else:
                        Xp, Xtp = XX[:, 1 - pp, hh, 0, :], XX[:, 1 - pp, hh, 1, :]
                        rd = [XX.b]
                    P.op("pe", lambda e, bank=bank, hl=hl, Xp=Xp, Xtp=Xtp: e.matmul(
                        bank[:, hl * 256:hl * 256 + 128], Xtp, Xp, start=True, stop=True),
                        reads=rd, writes=[bank.b])
                    P.op("pe", lambda e, bank=bank, hl=hl, Xp=Xp, Xtp=Xtp: e.matmul(
                        bank[:, hl * 256 + 128:hl * 256 + 256], Xp, Xtp, start=True, stop=True),
                        reads=rd, writes=[bank.b])
                d_ap = XX[:, pp, pr * 2:pr * 2 + 2, :, :].rearrange("p h a t -> p (h a t)")
                P.op("act", lambda e, d_ap=d_ap, bank=bank: e.copy(d_ap, bank[:]), reads=[bank.b], writes=[XX.b])
            yield
            for pq_ in range(2):
                bank = nextbank()
                for hl in range(4):
                    hh = pq_ * 4 + hl
                    P.op("pe", lambda e, bank=bank, hl=hl, hh=hh, pp=pp: e.matmul(
                        bank[:, hl * 128:(hl + 1) * 128], XX[:, pp, hh, 1, :], Th[:, hh, :], start=True, stop=True),
                        reads=[XX.b, Th.b], writes=[bank.b])
                t_ap = Th[:, pq_ * 4:pq_ * 4 + 4, :].rearrange("p h t -> p (h t)")
                P.op("dve", lambda e, t_ap=t_ap, bank=bank: e.tensor_tensor(t_ap, bank[:], t_ap, ALU.add),
                     reads=[bank.b, Th.b], writes=[Th.b])
            yield
        pq = nextbank()
        for hh in range(8):
            m, h2 = hh // 2, hh % 2
            hs = slice(h2 * 64, (h2 + 1) * 64)
            P.op("pe", lambda e, hs=hs, m=m, hh=hh: e.matmul(
                pq[hs, m * 128:(m + 1) * 128], Ab[:, hh * 64:(hh + 1) * 64], Th[:, hh, :], start=True, stop=True),
                reads=[Ab.b, Th.b], writes=[pq.b], pos=(1 if h2 else None))
        P.op("act", lambda e: e.copy(c["Qall"][:].rearrange("p m t -> p (m t)"), pq[:]),
             reads=[pq.b], writes=[c["Qall"].b])
        for half in range(2):
            pp_ = nextbank()
            for hl in range(4):
                hh = half * 4 + hl
                P.op("pe", lambda e, pp_=pp_, hl=hl, hh=hh: e.matmul(
                    pp_[:, hl * 128:(hl + 1) * 128], NTt[:, hh, 1, :], Th[:, hh, :], start=True, stop=True),
                    reads=[NTt.b, Th.b], writes=[pp_.b])
            P.op("dve", lambda e, pp_=pp_, half=half: e.tensor_copy(
                c["Pall"][:, half * 4:half * 4 + 4, :].rearrange("p h t -> p (h t)"), pp_[:]),
                reads=[pp_.b], writes=[c["Pall"].b])
        yield

    def gen_seq(u, b, hg, tt):
        c = per[u % NU]
        ST = STs[b][hg]
        cols = slice(hg * 512, (hg + 1) * 512)
        rows = slice(tt * 128, (tt + 1) * 128)
        for cc in range(2):
            cs = slice(cc * 64, (cc + 1) * 64)
            P.op("pool", lambda e: e.tensor_copy(STb[:], ST[:]), reads=[ST.b], writes=[STb.b])
            for hh in range(8):
                m, h2 = hh // 2, hh % 2
                hs = slice(h2 * 64, (h2 + 1) * 64)
                hc = slice(hh * 64, (hh + 1) * 64)
                P.op("pe", lambda e, cs=cs, hs=hs, hc=hc, m=m, hh=hh: e.matmul(
                    PU_[cs, hc], c["Qall"][hs, m, cs], STb[hs, m, :], start=(hh == 0), stop=False),
                    reads=[c["Qall"].b, STb.b], writes=[PU_.b], pos=(1 if (h2 or cc) else None))
                P.op("pe", lambda e, cs=cs, hc=hc, hh=hh: e.matmul(
                    PU_[cs, hc], c["Pall"][cs, hh, cs], c["Vv"][cs, hc], start=False, stop=True),
                    reads=[c["Pall"].b, c["Vv"].b], writes=[PU_.b], pos=(1 if cc else None))
            P.op("act", lambda e, cs=cs: e.copy(Usb[cs, :], PU_[cs, :]), reads=[PU_.b], writes=[Usb.b])
            yield
            ocs = slice((1 - cc) * 64, (2 - cc) * 64)
            for hh in range(8):
                m, h2 = hh // 2, hh % 2
                hs = slice(h2 * 64, (h2 + 1) * 64)
                hc = slice(hh * 64, (hh + 1) * 64)
                P.op("pe", lambda e, cs=cs, ocs=ocs, hs=hs, hc=hc, m=m, hh=hh: e.matmul(
                    PY_[ocs, hc], c["ARt"][hs, m, 1, cs], STb[hs, m, :], start=(hh == 0), stop=False),
                    reads=[c["ARt"].b, STb.b], writes=[PY_.b], pos=1)
                P.op("pe", lambda e, cs=cs, ocs=ocs, hc=hc, hh=hh: e.matmul(
                    PY_[ocs, hc], c["Nall"][cs, hh, 1, cs], Usb[cs, hc], start=False, stop=False),
                    reads=[c["Nall"].b, Usb.b], writes=[PY_.b], pos=1)
                P.op("pe", lambda e, cs=cs, ocs=ocs, hc=hc, hh=hh: e.matmul(
                    PY_[ocs, hc], c["Nall"][cs, hh, 3, cs], c["Vv"][cs, hc], start=False, stop=True),
                    reads=[c["Nall"].b, c["Vv"].b], writes=[PY_.b], pos=1)
            P.op("dve", lambda e, ocs=ocs: e.tensor_copy(c["Yt"][ocs, :], PY_[ocs, :]), reads=[PY_.b], writes=[c["Yt"].b])
            for hh in range(8):
                m, h2 = hh // 2, hh % 2
                hs = slice(h2 * 64, (h2 + 1) * 64)
                hc = slice(hh * 64, (hh + 1) * 64)
                P.op("pe", lambda e, cs=cs, hs=hs, hc=hc, m=m, hh=hh: e.matmul(
                    PS_[hs, m * 64:(m + 1) * 64], c["Bg"][cs, hc], Usb[cs, hc], start=(hh < 2), stop=False),
                    reads=[c["Bg"].b, Usb.b], writes=[PS_.b], pos=(1 if (h2 or cc) else None))
                P.op("pe", lambda e, cs=cs, hs=hs, hc=hc, m=m, hh=hh: e.matmul(
                    PS_[hs, m * 64:(m + 1) * 64], c["Kg"][cs, hc], c["Vv"][cs, hc], start=False, stop=True),
                    reads=[c["Kg"].b, c["Vv"].b], writes=[PS_.b], pos=(1 if (h2 or cc) else None))
            P.op("pool", lambda e, cc=cc: e.tensor_tensor(
                tmpS[:], ST[:], c["gC"][:, :, cc].unsqueeze(2).to_broadcast([128, 4, 64]), ALU.mult),
                reads=[ST.b, c["gC"].b], writes=[tmpS.b])
            P.op("dve", lambda e: e.tensor_tensor(
                ST[:].rearrange("p m i -> p (m i)"), PS_[:, 0:256], tmpS[:].rearrange("p m i -> p (m i)"), ALU.add),
                reads=[PS_.b, tmpS.b], writes=[ST.b])
            yield
        P.dma("sp", scr["Y"].t[b, tt * 128:tt * 128 + 64, cols], c["Yt"][64:128, :],
              reads=[c["Yt"].b], writes=[scr["Y"].b])
        P.dma("act", scr["Y"].t[b, tt * 128 + 64:tt * 128 + 128, cols], c["Yt"][0:64, :],
              reads=[c["Yt"].b], writes=[scr["Y"].b])
        yield

    units = [(b, hg, tt) for tt in range(NT) for b in range(NB) for hg in range(2)]
    n_u = len(units)
    for u in range(n_u + 1):
        streams = []
        if u >= 1:
            streams.append(gen_seq(u - 1, *units[u - 1]))
        if u < n_u:
            streams.append(gen_pre(u, *units[u]))
        while streams:
            for s_ in list(streams):
                try:
                    next(s_)
                except StopIteration:
                    streams.remove(s_)
    kb.pop()

    kb.push()
    lnw = kb.sb("rw_lnw", [128, 1024], F32)
    lnb = kb.sb("rw_lnb", [128, 1024], F32)
    bc_row(lnw, W["rwkv_ln_w"][0:1, :])
    bc_row(lnb, W["rwkv_ln_b"][0:1, :])
    yt = kb.sb("rc_y", [128, 1024], F32)
    y2 = kb.sb("rc_y2", [128, 1024], F32)
    bon = kb.sb("rc_bon", [128, 1024], F32)
    sgc = kb.sb("rc_sg", [128, 1024], F32)
    smc = kb.sb("rc_sm", [128, 32], F32)
    zTt = kb.sb("rc_zT", [128, 8, 128], BF16)
    for b in range(NB):
        for tt in range(NT):
            P.dma("sp", yt[:], tok_view("Y", b, tt), reads=[scr["Y"].b], writes=[yt.b])
            P.dma("act", bon[:], scr["BON"].t[b, tt * 128:(tt + 1) * 128, :], reads=[scr["BON"].b], writes=[bon.b])
            P.dma("act", sgc[:], scr["SG"].t[b, tt * 128:(tt + 1) * 128, :], reads=[scr["SG"].b], writes=[sgc.b])
            P.op("dve", lambda e: e.tensor_reduce(smc[:, 0:16], hd(yt[:]), AX.X, ALU.add), reads=[yt.b], writes=[smc.b])
            P.op("dve", lambda e: e.tensor_scalar(smc[:, 0:16], smc[:, 0:16], -1.0 / 64.0, None, ALU.mult),
                 reads=[smc.b], writes=[smc.b])
            P.op("dve", lambda e: e.tensor_tensor(hd(yt[:]), hd(yt[:]), bcj(smc[:, 0:16]), ALU.add),
                 reads=[yt.b, smc.b], writes=[yt.b])
            P.op("act", lambda e: e.activation(y2[:], yt[:], AF.Square), reads=[yt.b], writes=[y2.b])
            P.op("dve", lambda e: e.tensor_reduce(smc[:, 16:32], hd(y2[:]), AX.X, ALU.add), reads=[y2.b], writes=[smc.b])
            P.op("dve", lambda e: e.tensor_scalar(smc[:, 16:32], smc[:, 16:32], 1.0 / 64.0, 64e-5, ALU.mult, ALU.add),
                 reads=[smc.b], writes=[smc.b])
            P.op("act", lambda e: e.activation(smc[:, 16:32], smc[:, 16:32], AF.Sqrt), reads=[smc.b], writes=[smc.b])
            P.op("dve", lambda e: e.reciprocal(smc[:, 16:32], smc[:, 16:32]), reads=[smc.b], writes=[smc.b])
            P.op("dve", lambda e: e.tensor_tensor(hd(yt[:]), hd(yt[:]), bcj(smc[:, 16:32]), ALU.mult),
                 reads=[yt.b, smc.b], writes=[yt.b])
            P.op("pool", lambda e: e.tensor_tensor(yt[:], yt[:], lnw[:], ALU.mult), reads=[yt.b, lnw.b], writes=[yt.b])
            P.op("pool", lambda e: e.tensor_tensor(yt[:], yt[:], lnb[:], ALU.add), reads=[yt.b, lnb.b], writes=[yt.b])
            P.op("dve", lambda e: e.tensor_tensor(yt[:], yt[:], bon[:], ALU.add), reads=[yt.b, bon.b], writes=[yt.b])
            P.op("dve", lambda e: e.tensor_tensor(yt[:], yt[:], sgc[:], ALU.mult), reads=[yt.b, sgc.b], writes=[yt.b])
            for c in range(8):
                pt = ps[4 + c // 4]
                P.op("pe", lambda e, pt=pt, c=c: e.transpose(
                    pt[:, (c % 4) * 128:(c % 4 + 1) * 128], yt[:, c * 128:(c + 1) * 128], ident[:]),
                    reads=[yt.b, ident.b], writes=[pt.b])
            P.op("act", lambda e: e.copy(zTt[:, 0:4, :], ps[4][:].rearrange("p (c t) -> p c t", t=128)),
                 reads=[ps[4].b], writes=[zTt.b])
            P.op("dve", lambda e: e.tensor_copy(zTt[:, 4:8, :], ps[5][:].rearrange("p (c t) -> p c t", t=128)),
                 reads=[ps[5].b], writes=[zTt.b])
            back_tile(kb, g, b, tt, _Shift(zTt, tt * 128), [zTt.b], wout, x_src, x_dst)
    kb.pop()
    kb.pop()


class _W4:
    def __init__(self, t, n):
        self.t, self.n, self.b = t, n, t.b

    def __getitem__(self, k):
        a, kc, sl = k
        return self.t[a, self.n, kc, sl]


LAYER_FNS[2] = layer_rwkv2


def rwkv_consts():
    c = {}
    tp = np.arange(128)[:, None]
    t = np.arange(128)[None, :]
    same = (tp // 64) == (t // 64)
    c["rw_M1"] = (same & (tp <= t)).astype(np.float32)
    c["rw_M1s"] = (same & (tp < t)).astype(np.float32)
    c["rw_M2"] = (same & (tp > t)).astype(np.float32)
    strict = (same & (tp < t)).astype(np.float32)
    incl = (same & (tp <= t)).astype(np.float32)
    c["rw_mask4"] = np.concatenate([strict, incl, strict, incl], axis=1)
    low = (same & (t < tp)).astype(np.float32)
    c["rw_maskL2"] = np.concatenate([low, low], axis=1)
    c["rw_cind"] = np.stack([(np.arange(128) < 64), (np.arange(128) >= 64)], axis=1).astype(np.float32)
    return c


def kernel(**inputs):
    n_cores = 8
    x = np.ascontiguousarray(np.asarray(inputs["x"], dtype=np.float32))
    c = np.ascontiguousarray(np.asarray(inputs["c"], dtype=np.float32))
    wsh = {k: tuple(np.asarray(v).shape) for k, v in inputs.items()}
    wsh["c"] = (NB, D)
    consts = make_consts()
    nc = build([0, 1, 2, 3], wsh, consts)
    shared = {k: np.ascontiguousarray(np.asarray(v, dtype=np.float32))
              for k, v in inputs.items() if k not in ("x", "c")}
    for k, v in consts.items():
        shared["k_" + k] = v
    in_maps = []
    for i in range(n_cores):
        m = dict(shared)
        m["x"] = np.ascontiguousarray(x[i * NB:(i + 1) * NB])
        m["c"] = np.ascontiguousarray(c[i * NB:(i + 1) * NB])
        in_maps.append(m)
    res = run_bass_kernel_spmd(nc, in_maps, core_ids=list(range(n_cores)))
    out = np.concatenate([np.asarray(r["y"]) for r in res.results], axis=0)
    return out.astype(np.float32)
```

```python
from contextlib import ExitStack
import numpy as np
import concourse.bass as bass
import concourse.mybir as mybir
from concourse.bass_utils import run_bass_kernel_spmd

F32 = mybir.dt.float32
BF16 = mybir.dt.bfloat16
AF = mybir.ActivationFunctionType
ALU = mybir.AluOpType
AX = mybir.AxisListType

SAME_ENGINE_SYNC = True
LAZY_SIGNAL = ("pe",)
MAX_PENDING = 8
NO_SELF_SYNC = ("pe",)
N_DMA_SEMS = 8


class Buf:
    __slots__ = ("name", "w", "r")

    def __init__(self, name):
        self.name = name
        self.w = None
        self.r = {}


class Prog:
    ENG = ("pe", "act", "dve", "pool", "sp")

    def __init__(self, nc):
        self.nc = nc
        self.ops = {e: [] for e in self.ENG}
        self.count = {e: 0 for e in self.ENG}
        self.known = {e: {} for e in self.ENG}
        self.sems = {}
        self.dma_n = {e: 0 for e in self.ENG}
        self._stack = []
        self._last_pos = None
        self.recs = {e: [] for e in self.ENG}

    def alloc_sems(self, stack):
        for e in self.ENG:
            self.sems[("e", e)] = stack.enter_context(self.nc.semaphore("s_" + e))
        for e in ("sp", "act", "pool"):
            for i in range(N_DMA_SEMS):
                self.sems[("d", e, i)] = stack.enter_context(
                    self.nc.semaphore("d_%s%d" % (e, i)))

    def _deps(self, eng, reads, writes):
        need = {}
        def add(tok):
            if tok is None:
                return
            k, v = tok
            if need.get(k, 0) < v:
                need[k] = v
        for b in reads:
            add(b.w)
        for b in writes:
            add(b.w)
            for k, v in b.r.items():
                add((k, v))
        waits = []
        kn = self.known[eng]
        for k, v in need.items():
            if k == ("e", eng) and (not SAME_ENGINE_SYNC or eng in NO_SELF_SYNC):
                continue
            if kn.get(k, 0) < v:
                kn[k] = v
                waits.append((k, v))
                if k[0] == "e":
                    self.recs[k[1]][v - 1]["signal"] = True
        return waits

    def _commit(self, tok, reads, writes):
        k, v = tok
        for b in writes:
            b.w = tok
            b.r = {}
        for b in reads:
            if b.r.get(k, 0) < v:
                b.r[k] = v

    def op(self, eng, fn, reads=(), writes=(), pos=None):
        waits = self._deps(eng, reads, writes)
        if eng == "pe":
            if (pos is not None or self._last_pos is not None) and self.count["pe"] > 0:
                k, v = ("e", "pe"), self.count["pe"]
                if self.known["pe"].get(k, 0) < v:
                    self.known["pe"][k] = v
                    waits.append((k, v))
                    self.recs["pe"][v - 1]["signal"] = True
            self._last_pos = pos
        self.count[eng] += 1
        tok = (("e", eng), self.count[eng])
        rec = {"fn": fn, "waits": waits, "signal": eng not in LAZY_SIGNAL}
        self.ops[eng].append(rec)
        self.recs[eng].append(rec)
        self._commit(tok, reads, writes)

    def dma(self, q, out, in_, reads=(), writes=(), **kw):
        n = self.dma_n[q]
        self.dma_n[q] += 1
        slot = n % N_DMA_SEMS
        key = ("d", q, slot)
        val = 16 * (n // N_DMA_SEMS + 1)
        waits = self._deps(q, reads, writes)
        if val > 16:
            kn = self.known[q]
            if kn.get(key, 0) < val - 16:
                kn[key] = val - 16
                waits.append((key, val - 16))
        sems = self.sems

        def emit(e, waits=waits, key=key, out=out, in_=in_, kw=kw):
            for k, v in waits:
                e.wait_ge(sems[k], v)
            e.dma_start(out=out, in_=in_, **kw).then_inc(sems[key], 16)
        self.ops[q].append(emit)
        self._commit((key, val), reads, writes)

    def finish(self, final_bufs):
        waits = self._deps("sp", final_bufs, [])
        sems = self.sems

        def emit(e, waits=waits):
            for k, v in waits:
                e.wait_ge(sems[k], v)
        self.ops["sp"].append(emit)

    def _run(self, eng, e):
        sems = self.sems
        pending = 0
        n = len(self.ops[eng])
        last_rec = None
        for it in self.ops[eng]:
            if isinstance(it, dict):
                last_rec = it
        for it in self.ops[eng]:
            if not isinstance(it, dict):
                it(e)
                continue
            for k, v in it["waits"]:
                e.wait_ge(sems[k], v)
            ins = it["fn"](e)
            pending += 1
            if it["signal"] or it is last_rec or pending >= MAX_PENDING:
                ins.then_inc(sems[("e", eng)], pending)
                pending = 0

    def emit(self):
        nc = self.nc
        with nc.Block() as block:
            @block.tensor
            def _(e):
                self._run("pe", e)

            @block.scalar
            def _(e):
                self._run("act", e)

            @block.vector
            def _(e):
                self._run("dve", e)

            @block.gpsimd
            def _(e):
                self._run("pool", e)

            @block.sync
            def _(e):
                self._run("sp", e)


def _barrier(self):
    targets = {}
    for e in self.ENG:
        if self.count[e]:
            targets[("e", e)] = self.count[e]
            self.recs[e][self.count[e] - 1]["signal"] = True
    for q in ("sp", "act", "pool"):
        n = self.dma_n[q]
        for s in range(min(n, N_DMA_SEMS)):
            last = ((n - 1 - s) // N_DMA_SEMS) * N_DMA_SEMS + s
            targets[("d", q, s)] = 16 * (last // N_DMA_SEMS + 1)
    sems = self.sems
    for e in self.ENG:
        waits = []
        kn = self.known[e]
        for k, v in targets.items():
            if kn.get(k, 0) < v:
                kn[k] = v
                waits.append((k, v))

        def emit(eo, waits=waits):
            for k, v in waits:
                eo.wait_ge(sems[k], v)
        self.ops[e].append(emit)


Prog.barrier = _barrier


DEBUG_OUT = False
S = 2048
D = 1024
NB = 2
NT = S // 128
EPS = 1e-6


class T:
    def __init__(self, t, name, nslots=0):
        self.t = t
        self.b = Buf(name)
        self.bs = [Buf("%s_%d" % (name, i)) for i in range(nslots)]

    def __getitem__(self, k):
        return self.t[k]


class KB:
    def __init__(self, nc, P, st):
        self.nc, self.P, self.st = nc, P, st
        self.scopes = [st]
        self.rr = 0
        self.n = 0

    def push(self):
        s = ExitStack()
        self.scopes.append(s)
        return s

    def pop(self):
        self.P.barrier()
        s = self.scopes.pop()
        s.close()

    def sb(self, name, shape, dt=F32, nslots=0):
        self.n += 1
        nm = "%s_%d" % (name, self.n)
        t = self.scopes[-1].enter_context(self.nc.sbuf_tensor(nm, list(shape), dt))
        return T(t, nm, nslots)

    def psum(self, name, shape, dt=F32):
        t = self.scopes[-1].enter_context(self.nc.psum_tensor(name, list(shape), dt))
        return T(t, name)

    def dram(self, name, shape, dt=F32):
        t = self.nc.dram_tensor(name, list(shape), dt, kind=("ExternalOutput" if DEBUG_OUT else "Internal"))
        o = T(t.ap(), name)
        return o


def bufs(objs):
    out = []
    for o in objs:
        out.append(o.b if isinstance(o, T) else o)
    return out


def build_common(kb, consts):
    nc, P = kb.nc, kb.P
    g = {}
    g["ident"] = kb.sb("ident", [128, 128], F32)
    P.dma("sp", g["ident"][:], consts["ident"][:, :], writes=[g["ident"].b])
    g["identb"] = kb.sb("identb", [128, 128], BF16)
    P.op("dve", lambda e: e.tensor_copy(g["identb"][:], g["ident"][:]),
         reads=[g["ident"].b], writes=[g["identb"].b])
    g["ones"] = kb.sb("ones", [128, 128], F32)
    P.op("dve", lambda e: e.memset(g["ones"][:], 1.0), writes=[g["ones"].b])
    g["ps"] = [kb.psum("ps%d" % i, [128, 512], F32) for i in range(6)]
    g["stage_n"] = 0
    g["xt"] = [kb.sb("xt", [128, 1024], F32) for i in range(2)]
    g["xn"] = [kb.sb("xn", [128, 1024], F32) for i in range(1)]
    g["junk"] = kb.sb("junk", [128, 1024], F32)
    g["st"] = [kb.sb("stt", [128, 4], F32) for i in range(2)]
    g["gatebc"] = [kb.sb("gatebc", [128, 1024], F32) for b in range(NB)]
    g["AB"] = kb.sb("AB", [128, NB, 2, 8], F32)
    g["cT"] = kb.sb("cT", [128, 8, NB], F32)
    g["gainT"] = kb.sb("gainT", [128, 8], F32)
    g["mbT"] = kb.sb("mbT", [128, 16], F32)
    g["xo"] = [kb.sb("xo", [128, 1024], F32) for i in range(2)]
    g["cnt"] = 0
    return g


def begin_load(kb, g):
    kb.push()
    g["stage"] = [kb.sb("stage", [128, 8, 256], F32) for i in range(2)]
    g["crep"] = kb.sb("crep", [128, 8, 128], F32)
    g["mbrow"] = kb.sb("mbrow", [1, 256], F32)


def end_load(kb, g):
    kb.pop()


def load_w(kb, g, dst, dcol0, src, ncols, krows=1024):
    P = kb.P
    kc = krows // 128
    c0 = 0
    engs = ("dve", "pool")
    while c0 < ncols:
        n = min(256, ncols - c0)
        stg = g["stage"][g["stage_n"] % 2]
        g["stage_n"] += 1
        q = ("sp", "act")[g["stage_n"] % 2]
        P.dma(q, stg[:, 0:kc, 0:n], src[:, c0:c0 + n].rearrange("(k p) n -> p k n", p=128),
              writes=[stg.b])
        eng = engs[g["stage_n"] % 2]
        d_ap = dst[:, 0:kc, dcol0 + c0:dcol0 + c0 + n]
        s_ap = stg[:, 0:kc, 0:n]
        P.op(eng, lambda e, d_ap=d_ap, s_ap=s_ap: e.tensor_copy(d_ap, s_ap),
             reads=[stg.b], writes=[dst.b])
        c0 += n


def compute_mod(kb, g, layer, W):
    nc, P = kb.nc, kb.P
    ps = g["ps"]
    crep, mbrow = g["crep"], g["mbrow"]
    if not g.get("c_done"):
        g["c_done"] = True
        for b in range(NB):
            P.dma("sp", g["cT"][:, :, b], W["c"][b].rearrange("(k p) -> p k", p=128),
                  writes=[g["cT"].b], allow_slow_non_contiguous=True)
        P.op("act", lambda e: e.activation(g["cT"][:], g["cT"][:], AF.Silu),
             reads=[g["cT"].b], writes=[g["cT"].b])
    P.dma("sp", g["gainT"][:], W["ln_gain"][layer].rearrange("(k p) -> p k", p=128),
          writes=[g["gainT"].b], allow_slow_non_contiguous=True)
    P.dma("sp", g["mbT"][:], W["mod_b"][layer, 0:2048].rearrange("(k p) -> p k", p=128),
          writes=[g["mbT"].b], allow_slow_non_contiguous=True)
    pf = ps[2]
    for nb in range(12):
        stg = g["stage"][g["stage_n"] % 2]
        g["stage_n"] += 1
        P.dma("sp", stg[:], W["mod_w"][layer][:, nb * 256:(nb + 1) * 256].rearrange(
            "(k p) n -> p k n", p=128), writes=[stg.b])
        if nb < 8:
            for jj in range(2):
                j = nb * 2 + jj
                for k in range(8):
                    P.op("pe", lambda e, j=j, jj=jj, k=k, stg=stg: e.matmul(
                        pf[:, j * 2:j * 2 + 2], stg[:, k, jj * 128:(jj + 1) * 128], g["cT"][:, k, :],
                        start=(k == 0), stop=(k == 7)),
                        reads=[g["cT"].b, stg.b], writes=[pf.b])
        else:
            P.dma("act", mbrow[:], W["mod_b"][layer:layer + 1, nb * 256:(nb + 1) * 256],
                  writes=[mbrow.b])
            for b in range(NB):
                pt = ps[b]
                for k in range(8):
                    P.op("dve", lambda e, b=b, k=k: e.tensor_copy(
                        crep[:, k, :], g["cT"][:, k, b:b + 1].to_broadcast([128, 128])),
                        reads=[g["cT"].b], writes=[crep.b])
                for k in range(8):
                    P.op("pe", lambda e, pt=pt, k=k, stg=stg: e.matmul(
                        pt[:, 0:256], crep[:, k, :], stg[:, k, :], start=(k == 0), stop=False),
                        reads=[crep.b, stg.b], writes=[pt.b])
                P.op("pe", lambda e, pt=pt: e.matmul(
                    pt[:, 0:256], g["ones"][0:1, :], mbrow[0:1, :], start=False, stop=True),
                    reads=[g["ones"].b, mbrow.b], writes=[pt.b])
                P.op("act", lambda e, pt=pt, b=b, nb=nb: e.copy(
                    g["gatebc"][b][:, (nb - 8) * 256:(nb - 7) * 256], pt[:, 0:256]),
                    reads=[pt.b], writes=[g["gatebc"][b].b])
    pfv = pf[:, 0:32].rearrange("p (j b) -> p j b", b=2)
    for b in range(NB):
        P.op("dve", lambda e, b=b: e.tensor_tensor(
            g["AB"][:, b, 1, :], pfv[:, 0:8, b], g["mbT"][:, 0:8], ALU.add),
            reads=[pf.b, g["mbT"].b], writes=[g["AB"].b])
        P.op("dve", lambda e, b=b: e.tensor_tensor(
            g["AB"][:, b, 0, :], pfv[:, 8:16, b], g["mbT"][:, 8:16], ALU.add),
            reads=[pf.b, g["mbT"].b], writes=[g["AB"].b])
        P.op("dve", lambda e, b=b: e.scalar_tensor_tensor(
            g["AB"][:, b, 0, :], g["AB"][:, b, 0, :], 1.0, g["gainT"][:], ALU.add, ALU.mult),
            reads=[g["AB"].b, g["gainT"].b], writes=[g["AB"].b])


def front_tile(kb, g, b, tt, x_src, hT, hT_buf, pbanks=None, xi=None):
    nc, P = kb.nc, kb.P
    i = g["cnt"] % 2
    g["cnt"] += 1
    if xi is not None:
        i = xi
    xt, xn, stt = g["xt"][i], g["xn"][0], g["st"][i]
    P.dma("sp", xt[:], x_src[b, tt * 128:(tt + 1) * 128, :], reads=[x_src.b], writes=[xt.b])
    P.op("dve", lambda e: e.memset(stt[:], 0.0), writes=[stt.b])
    P.op("act", lambda e: e.activation(g["junk"][:], xt[:], AF.Square, accum_out=stt[:, 0:1]),
         reads=[xt.b, stt.b], writes=[g["junk"].b, stt.b])
    P.op("dve", lambda e: e.tensor_scalar(stt[:, 1:2], stt[:, 0:1], 1.0 / D, EPS, ALU.mult, ALU.add),
         reads=[stt.b], writes=[stt.b])
    P.op("act", lambda e: e.activation(stt[:, 1:2], stt[:, 1:2], AF.Sqrt),
         reads=[stt.b], writes=[stt.b])
    P.op("dve", lambda e: e.reciprocal(stt[:, 2:3], stt[:, 1:2]),
         reads=[stt.b], writes=[stt.b])
    P.op("dve", lambda e: e.tensor_scalar(xn[:], xt[:], stt[:, 2:3], None, ALU.mult),
         reads=[xt.b, stt.b], writes=[xn.b])
    pa, pb = pbanks if pbanks is not None else (g["ps"][0], g["ps"][1])
    for c in range(8):
        pt = pa if c < 4 else pb
        P.op("pe", lambda e, pt=pt, c=c: e.transpose(
            pt[:, (c % 4) * 128:(c % 4 + 1) * 128], xn[:, c * 128:(c + 1) * 128], g["ident"][:]),
            reads=[xn.b, g["ident"].b], writes=[pt.b])
    for c in range(8):
        pt = pa if c < 4 else pb
        src = pt[:, (c % 4) * 128:(c % 4 + 1) * 128]
        dst = hT[:, c, tt * 128:(tt + 1) * 128]
        A = g["AB"][:, b, 0, c:c + 1]
        Bv = g["AB"][:, b, 1, c:c + 1]
        if c % 2 == 0:
            P.op("act", lambda e, src=src, dst=dst, A=A, Bv=Bv: e.activation(
                dst, src, AF.Identity, bias=Bv, scale=A),
                reads=[pt.b, g["AB"].b], writes=[hT_buf])
        else:
            P.op("dve", lambda e, src=src, dst=dst, A=A, Bv=Bv: e.tensor_scalar(
                dst, src, A, Bv, ALU.mult, ALU.add),
                reads=[pt.b, g["AB"].b], writes=[hT_buf])


def back_tile(kb, g, b, tt, zT, zT_bufs, wout, x_src, x_dst, pbanks=None, xi=None):
    nc, P = kb.nc, kb.P
    i = g["cnt"] % 2
    g["cnt"] += 1
    if xi is not None:
        i = xi
    xt, xo = g["xt"][i], g["xo"][i]
    P.dma("act", xt[:], x_src[b, tt * 128:(tt + 1) * 128, :], reads=[x_src.b], writes=[xt.b])
    for half in range(2):
        pt = g["ps"][2 + half] if pbanks is None else pbanks[half]
        for c in range(8):
            P.op("pe", lambda e, pt=pt, c=c, half=half: e.matmul(
                pt[:], zT[:, c, tt * 128:(tt + 1) * 128], wout[:, c, half * 512:(half + 1) * 512],
                start=(c == 0), stop=(c == 7)),
                reads=list(zT_bufs) + [wout.b], writes=[pt.b])
        P.op("dve", lambda e, pt=pt, half=half: e.tensor_tensor(
            xo[:, half * 512:(half + 1) * 512], pt[:],
            g["gatebc"][b][:, half * 512:(half + 1) * 512], ALU.mult),
            reads=[pt.b, g["gatebc"][b].b], writes=[xo.b])
    P.op("pool", lambda e: e.tensor_tensor(xo[:], xo[:], xt[:], ALU.add),
         reads=[xo.b, xt.b], writes=[xo.b])
    P.dma("sp", x_dst[b, tt * 128:(tt + 1) * 128, :], xo[:], reads=[xo.b], writes=[x_dst.b])


def layer_lru(kb, g, W, x_src, x_dst):
    nc, P = kb.nc, kb.P
    kb.push()
    win = kb.sb("lru_win", [128, 8, 2048], BF16)
    wout = kb.sb("lru_wout", [128, 8, 1024], BF16)
    gwb = [kb.sb("lru_gwb", [128, 8, 128], BF16) for _ in range(2)]
    vec = kb.sb("lru_vec", [128, 10, 8], F32)
    begin_load(kb, g)
    compute_mod(kb, g, 1, W)
    load_w(kb, g, win, 0, W["lru_w_in"][0], 2048)
    load_w(kb, g, wout, 0, W["lru_w_out"][0], 1024)
    for i, nm in enumerate(("lru_gate_a_w", "lru_gate_x_w")):
        stg = g["stage"][g["stage_n"] % 2]
        g["stage_n"] += 1
        P.op("pool", lambda e, stg=stg: e.memset(stg[:], 0.0), writes=[stg.b])
        src = W[nm][0]
        for hh in range(2):
            P.dma("sp", stg[hh * 64:(hh + 1) * 64, :, hh * 64:(hh + 1) * 64],
                  src.rearrange("(j n2) c d -> n2 c j d", n2=2)[hh],
                  writes=[stg.b])
        P.op("dve", lambda e, i=i, stg=stg: e.tensor_copy(gwb[i][:], stg[:, :, 0:128]),
             reads=[stg.b], writes=[gwb[i].b])
    names = ["lru_conv_b", "lru_gate_a_b", "lru_gate_x_b", "lru_lambda"]
    for i, nm in enumerate(names):
        P.dma("sp", vec[:, i, :], W[nm][0].rearrange("(k p) -> p k", p=128),
              writes=[vec.b], allow_slow_non_contiguous=True)
    for j in range(4):
        P.dma("sp", vec[:, 4 + j, :], W["lru_conv_w"][0, j].rearrange("(k p) -> p k", p=128),
              writes=[vec.b], allow_slow_non_contiguous=True)
    P.op("act", lambda e: e.activation(vec[:, 8, :], vec[:, 3, :], AF.Exp, scale=-1.0),
         reads=[vec.b], writes=[vec.b])
    P.op("act", lambda e: e.activation(vec[:, 8, :], vec[:, 8, :], AF.Ln, bias=1.0),
         reads=[vec.b], writes=[vec.b])
    P.op("dve", lambda e: e.tensor_scalar(vec[:, 8, :], vec[:, 8, :], -8.0, None, ALU.mult),
         reads=[vec.b], writes=[vec.b])

    end_load(kb, g)
    TH = 1024
    hT = kb.sb("hT", [128, 8, S], BF16)
    zT = kb.sb("zT", [128, 8, S], BF16)
    sets = [dict(upad=kb.sb("upad", [128, 3 + TH], F32), uc=kb.sb("uc", [128, TH], F32),
                 ucb=kb.sb("ucb", [128, TH], BF16), rr=kb.sb("rr", [128, TH], F32),
                 ii=kb.sb("ii", [128, TH], F32), aa=kb.sb("aa", [128, TH], F32),
                 bb=kb.sb("bb", [128, TH], F32)) for _ in range(2)]
    ps = g["ps"]
    pbank = [0]

    def nb():
        pbank[0] += 1
        return ps[2 + pbank[0] % 4]

    def half(b, j, hf, cur, prev):
        upad, uc, ucb, rr, ii, aa, bb = (cur[k] for k in ("upad", "uc", "ucb", "rr", "ii", "aa", "bb"))
        t0 = hf * TH
        if hf == 0:
            P.op("dve", lambda e: e.memset(upad[:, 0:3], 0.0), writes=[upad.b])
        else:
            P.op("dve", lambda e: e.tensor_copy(upad[:, 0:3], prev["upad"][:, TH:TH + 3]),
                 reads=[prev["upad"].b], writes=[upad.b])
        for q in range(TH // 512):
            pt = nb()
            for c in range(8):
                P.op("pe", lambda e, pt=pt, c=c, q=q: e.matmul(
                    pt[:], win[:, c, j * 128:(j + 1) * 128], hT[:, c, t0 + q * 512:t0 + (q + 1) * 512],
                    start=(c == 0), stop=(c == 7)),
                    reads=[win.b, hT.b], writes=[pt.b])
            P.op("act", lambda e, pt=pt, q=q: e.copy(upad[:, 3 + q * 512:3 + (q + 1) * 512], pt[:]),
                 reads=[pt.b], writes=[upad.b])
        for q in range(TH // 512):
            pt = nb()
            for c in range(8):
                P.op("pe", lambda e, pt=pt, c=c, q=q: e.matmul(
                    pt[:], win[:, c, 1024 + j * 128:1024 + (j + 1) * 128],
                    hT[:, c, t0 + q * 512:t0 + (q + 1) * 512], start=(c == 0), stop=(c == 7)),
                    reads=[win.b, hT.b], writes=[pt.b])
            P.op("act", lambda e, pt=pt, q=q: e.activation(
                bb[:, q * 512:(q + 1) * 512], pt[:], AF.Silu),
                reads=[pt.b], writes=[bb.b])
        P.op("dve", lambda e: e.tensor_scalar(
            uc[:], upad[:, 0:TH], vec[:, 4, j:j + 1], vec[:, 0, j:j + 1], ALU.mult, ALU.add),
            reads=[upad.b, vec.b], writes=[uc.b])
        for k in range(1, 4):
            P.op("dve", lambda e, k=k: e.scalar_tensor_tensor(
                uc[:], upad[:, k:k + TH], vec[:, 4 + k, j:j + 1], uc[:], ALU.mult, ALU.add),
                reads=[upad.b, vec.b, uc.b], writes=[uc.b])
        P.op("pool", lambda e: e.tensor_copy(ucb[:], uc[:]), reads=[uc.b], writes=[ucb.b])
        for gi, dst, bi in ((0, aa, 1), (1, ii, 2)):
            for q in range(TH // 512):
                pt = nb()
                P.op("pe", lambda e, pt=pt, q=q, gi=gi: e.matmul(
                    pt[:], gwb[gi][:, j, :], ucb[:, q * 512:(q + 1) * 512], start=True, stop=True),
                    reads=[gwb[gi].b, ucb.b], writes=[pt.b])
                P.op("act", lambda e, pt=pt, q=q, dst=dst, bi=bi: e.activation(
                    dst[:, q * 512:(q + 1) * 512], pt[:], AF.Sigmoid, bias=vec[:, bi, j:j + 1]),
                    reads=[pt.b, vec.b], writes=[dst.b])
        P.op("act", lambda e: e.activation(aa[:], aa[:], AF.Exp, scale=vec[:, 8, j:j + 1]),
             reads=[aa.b, vec.b], writes=[aa.b])
        P.op("pool", lambda e: e.tensor_tensor(ii[:], ii[:], uc[:], ALU.mult),
             reads=[ii.b, uc.b], writes=[ii.b])
        P.op("dve", lambda e: e.tensor_tensor(rr[:], aa[:], aa[:], ALU.mult),
             reads=[aa.b], writes=[rr.b])
        P.op("act", lambda e: e.activation(rr[:], rr[:], AF.Sqrt, bias=1.0, scale=-1.0),
             reads=[rr.b], writes=[rr.b])
        P.op("dve", lambda e: e.tensor_tensor(ii[:], rr[:], ii[:], ALU.mult),
             reads=[rr.b, ii.b], writes=[ii.b])
        if hf == 0:
            P.op("dve", lambda e: e.tensor_tensor_scan(rr[:], aa[:], ii[:], 0.0, ALU.mult, ALU.add),
                 reads=[aa.b, ii.b], writes=[rr.b])
        else:
            P.op("dve", lambda e: e.tensor_tensor_scan(rr[:], aa[:], ii[:], prev["rr"][:, TH - 1:TH], ALU.mult, ALU.add),
                 reads=[aa.b, ii.b, prev["rr"].b], writes=[rr.b])
        P.op("pool", lambda e: e.tensor_tensor(zT[:, j, t0:t0 + TH], rr[:], bb[:], ALU.mult),
             reads=[rr.b, bb.b], writes=[zT.b])

    cnt = 0
    for b in range(NB):
        for tt in range(NT):
            front_tile(kb, g, b, tt, x_src, hT, hT.b)
        for j in range(8):
            for hf in range(S // TH):
                half(b, j, hf, sets[cnt % 2], sets[(cnt - 1) % 2])
                cnt += 1
        for tt in range(NT):
            back_tile(kb, g, b, tt, zT, [zT.b], wout, x_src, x_dst)
    kb.pop()


WSHAPES = None


def make_consts():
    c = {}
    c["ident"] = np.eye(128, dtype=np.float32)
    c.update(gla_consts())
    c.update(dsa_consts())
    c.update(rwkv_consts())
    return c


def build(layers, wshapes, consts_np):
    nc = bass.Bass("TRN2", target_bir_lowering=False)
    W = {}
    for k, shp in wshapes.items():
        if k == "x":
            continue
        W[k] = nc.dram_tensor(k, list(shp), F32, kind="ExternalInput").ap()
    consts = {k: nc.dram_tensor("k_" + k, list(v.shape), F32, kind="ExternalInput").ap()
              for k, v in consts_np.items()}
    x_in = T(nc.dram_tensor("x", [NB, S, D], F32, kind="ExternalInput").ap(), "x_in")
    y_out = T(nc.dram_tensor("y", [NB, S, D], F32, kind="ExternalOutput").ap(), "y_out")
    with ExitStack() as st:
        P = Prog(nc)
        P.alloc_sems(st)
        kb = KB(nc, P, st)
        g = build_common(kb, consts)
        g["consts"] = consts
        scr = [kb.dram("xs%d" % i, [NB, S, D]) for i in range(2)]
        fns = {0: None, 1: layer_lru, 2: None, 3: None}
        fns.update(LAYER_FNS)
        src = x_in
        for li, layer in enumerate(layers):
            dst = y_out if li == len(layers) - 1 else scr[li % 2]
            fns[layer](kb, g, W, src, dst)
            src = dst
        P.finish([y_out.b])
        P.emit()
    return nc


LAYER_FNS = {}


def layer_gla(kb, g, W, x_src, x_dst):
    nc, P = kb.nc, kb.P
    C = g["consts"]
    kb.push()
    win = kb.sb("gla_win", [128, 8, 3088], BF16)
    wout = kb.sb("gla_wout", [128, 8, 1024], BF16)
    begin_load(kb, g)
    compute_mod(kb, g, 3, W)
    load_w(kb, g, win, 0, W["gla_w_in"][0], 3088)
    load_w(kb, g, wout, 0, W["gla_w_out"][0], 1024)
    end_load(kb, g)
    aw2 = kb.sb("aw2", [16, 512], F32)
    P.dma("sp", aw2[:], W["gla_alpha_w2"][0], writes=[aw2.b])
    nab = kb.sb("nab", [128, 4], F32)
    P.dma("sp", nab[:], W["gla_alpha_b"][0].rearrange("(k p) -> p k", p=128),
          writes=[nab.b], allow_slow_non_contiguous=True)
    P.op("dve", lambda e: e.tensor_scalar(nab[:], nab[:], -1.0, None, ALU.mult),
         reads=[nab.b], writes=[nab.b])
    gbc = kb.sb("gbc", [128, 256], F32)
    P.dma("sp", gbc[:], W["gla_norm_gain"][0:1, :].to_broadcast([128, 256]), writes=[gbc.b])
    smask = kb.sb("smask", [128, 512], F32)
    P.dma("sp", smask[:], C["gla_scanmask"][:, :], writes=[smask.b])
    mbd = kb.sb("mbd", [128, 128], F32)
    P.dma("sp", mbd[:], C["gla_mbd"][:, :], writes=[mbd.b])
    m2 = kb.sb("m2", [128, 128], F32)
    P.dma("sp", m2[:], C["gla_m2"][:, :], writes=[m2.b])

    TB = 512
    hT = kb.sb("hTb", [128, 8, TB], BF16)
    zT = kb.sb("zTb", [128, 8, TB], BF16)
    alow = kb.sb("alow", [16, TB], F32)
    laT = kb.sb("laT", [128, TB], F32)
    cumT = kb.sb("cumT", [128, TB], F32)
    eq = kb.sb("eq", [128, TB], F32)
    ek = kb.sb("ek", [128, TB], F32)
    qdT = kb.sb("qdT", [128, 4, TB], F32)
    kiT = kb.sb("kiT", [128, 4, TB], F32)
    el = kb.sb("el", [128, 4, 8], F32)
    latok = kb.sb("latok", [128, 4, 512], F32)
    edl = kb.sb("edl", [128, 512], F32)
    kend = kb.sb("kend", [128, 512], F32)
    vtok = kb.sb("vtok", [128, 1024], F32)
    sg = kb.sb("sg", [128, 1024], F32)
    attm = kb.sb("attm", [128, 128], F32)
    Sst = kb.sb("Sst", [128, 4, 256], F32)
    zz = kb.sb("zz", [128, 1024], F32)
    ss = kb.sb("ss", [128, 8], F32)
    ps = g["ps"]
    scale_q = 128.0 ** -0.5
    for b in range(NB):
        P.op("dve", lambda e: e.memset(Sst[:], 0.0), writes=[Sst.b])
        for tb in range(S // TB):
            for tl in range(4):
                front_tile(kb, g, b, tb * 4 + tl, x_src, _Shift(hT, tb * TB), hT.b)
            for c in range(8):
                P.op("pe", lambda e, c=c: e.matmul(ps[4][0:16, :], win[:, c, 3072:3088], hT[:, c, :],
                                                   start=(c == 0), stop=(c == 7)),
                     reads=[win.b, hT.b], writes=[ps[4].b], pos="M16")
            P.op("act", lambda e: e.copy(alow[:], ps[4][0:16, :]), reads=[ps[4].b], writes=[alow.b])
            for h in range(4):
                P.op("pe", lambda e, h=h: e.matmul(ps[4][:], aw2[:, h * 128:(h + 1) * 128], alow[:],
                                                   start=True, stop=True),
                     reads=[aw2.b, alow.b], writes=[ps[4].b], pos="K16")
                P.op("act", lambda e, h=h: e.activation(laT[:], ps[4][:], AF.Exp, bias=nab[:, h:h + 1], scale=-1.0),
                     reads=[ps[4].b, nab.b], writes=[laT.b])
                P.op("act", lambda e: e.activation(laT[:], laT[:], AF.Ln, bias=1.0),
                     reads=[laT.b], writes=[laT.b])
                P.op("dve", lambda e: e.tensor_scalar(laT[:], laT[:], -1.0 / 16.0, None, ALU.mult),
                     reads=[laT.b], writes=[laT.b])
                P.op("dve", lambda e: e.tensor_tensor_scan(cumT[:], smask[:], laT[:], 0.0, ALU.mult, ALU.add),
                     reads=[smask.b, laT.b], writes=[cumT.b])
                P.op("act", lambda e: e.activation(eq[:], cumT[:], AF.Exp),
                     reads=[cumT.b], writes=[eq.b])
                P.op("act", lambda e: e.activation(ek[:], cumT[:], AF.Exp, scale=-1.0),
                     reads=[cumT.b], writes=[ek.b])
                for c in range(8):
                    P.op("pe", lambda e, c=c, h=h: e.matmul(ps[5][:], win[:, c, h * 128:(h + 1) * 128], hT[:, c, :],
                                                            start=(c == 0), stop=(c == 7)),
                         reads=[win.b, hT.b], writes=[ps[5].b])
                P.op("dve", lambda e, h=h: e.scalar_tensor_tensor(qdT[:, h, :], ps[5][:], scale_q, eq[:],
                                                                  ALU.mult, ALU.mult),
                     reads=[ps[5].b, eq.b], writes=[qdT.b])
                for c in range(8):
                    P.op("pe", lambda e, c=c, h=h: e.matmul(ps[5][:], win[:, c, 512 + h * 128:512 + (h + 1) * 128],
                                                            hT[:, c, :], start=(c == 0), stop=(c == 7)),
                         reads=[win.b, hT.b], writes=[ps[5].b])
                P.op("dve", lambda e, h=h: e.tensor_tensor(kiT[:, h, :], ps[5][:], ek[:], ALU.mult),
                     reads=[ps[5].b, ek.b], writes=[kiT.b])
                P.op("dve", lambda e, h=h: e.tensor_copy(
                    el[:, h, :], eq[:].rearrange("p (n c) -> p n c", c=64)[:, :, 63]),
                    reads=[eq.b], writes=[el.b])
                for tl in range(4):
                    P.op("pe", lambda e, tl=tl: e.transpose(ps[2][:, tl * 128:(tl + 1) * 128],
                                                            laT[:, tl * 128:(tl + 1) * 128], g["ident"][:]),
                         reads=[laT.b, g["ident"].b], writes=[ps[2].b])
                P.op("act", lambda e, h=h: e.copy(
                    latok[:, :, h * 128:(h + 1) * 128], ps[2][:].rearrange("p (t f) -> p t f", f=128)),
                    reads=[ps[2].b], writes=[latok.b])
            for tl in range(4):
                tsl = slice(tl * 128, (tl + 1) * 128)
                P.op("pe", lambda e, tl=tl: e.matmul(ps[2][:], m2[:], latok[:, tl, :], start=True, stop=True),
                     reads=[m2.b, latok.b], writes=[ps[2].b])
                P.op("act", lambda e: e.activation(edl[:], ps[2][:], AF.Exp), reads=[ps[2].b], writes=[edl.b])
                for c in range(8):
                    P.op("pe", lambda e, c=c, tsl=tsl: e.matmul(ps[3][:], hT[:, c, tsl], win[:, c, 512:1024],
                                                                start=(c == 0), stop=(c == 7)),
                         reads=[win.b, hT.b], writes=[ps[3].b])
                P.op("dve", lambda e: e.tensor_tensor(kend[:], ps[3][:], edl[:], ALU.mult),
                     reads=[ps[3].b, edl.b], writes=[kend.b])
                for half in range(2):
                    pt = ps[4 + half]
                    for c in range(8):
                        P.op("pe", lambda e, c=c, tsl=tsl, pt=pt, half=half: e.matmul(
                            pt[:], hT[:, c, tsl], win[:, c, 1024 + half * 512:1024 + (half + 1) * 512],
                            start=(c == 0), stop=(c == 7)),
                            reads=[win.b, hT.b], writes=[pt.b])
                    P.op("act", lambda e, pt=pt, half=half: e.copy(vtok[:, half * 512:(half + 1) * 512], pt[:]),
                         reads=[pt.b], writes=[vtok.b])
                for half in range(2):
                    pt = ps[4 + half]
                    for c in range(8):
                        P.op("pe", lambda e, c=c, tsl=tsl, pt=pt, half=half: e.matmul(
                            pt[:], hT[:, c, tsl], win[:, c, 2048 + half * 512:2048 + (half + 1) * 512],
                            start=(c == 0), stop=(c == 7)),
                            reads=[win.b, hT.b], writes=[pt.b])
                    P.op("act", lambda e, pt=pt, half=half: e.activation(
                        sg[:, half * 512:(half + 1) * 512], pt[:], AF.Silu),
                        reads=[pt.b], writes=[sg.b])
                for h in range(4):
                    po = ps[h // 2]
                    pc = slice((h % 2) * 256, (h % 2 + 1) * 256)
                    vsl = slice(h * 256, (h + 1) * 256)
                    P.op("pe", lambda e, h=h, tsl=tsl: e.matmul(ps[2][:, 0:128], kiT[:, h, tsl], qdT[:, h, tsl],
                                                                start=True, stop=True),
                         reads=[kiT.b, qdT.b], writes=[ps[2].b])
                    P.op("dve", lambda e: e.tensor_tensor(attm[:], ps[2][:, 0:128], mbd[:], ALU.mult),
                         reads=[ps[2].b, mbd.b], writes=[attm.b])
                    P.op("pe", lambda e, po=po, pc=pc, vsl=vsl: e.matmul(po[:, pc], attm[:], vtok[:, vsl],
                                                                         start=True, stop=False),
                         reads=[attm.b, vtok.b], writes=[po.b])
                    for cc in range(2):
                        n = tl * 2 + cc
                        psl = slice(cc * 64, (cc + 1) * 64)
                        qsl = slice(tl * 128 + cc * 64, tl * 128 + (cc + 1) * 64)
                        P.op("pe", lambda e, po=po, pc=pc, psl=psl, qsl=qsl, h=h: e.matmul(
                            po[psl, pc], qdT[:, h, qsl], Sst[:, h, :], start=False, stop=True),
                            reads=[qdT.b, Sst.b], writes=[po.b], pos=("T1" if cc else "T0"))
                        P.op("pe", lambda e, psl=psl, h=h, vsl=vsl: e.matmul(
                            ps[3][:, 0:256], kend[psl, h * 128:(h + 1) * 128], vtok[psl, vsl],
                            start=True, stop=True),
                            reads=[kend.b, vtok.b], writes=[ps[3].b], pos=("R1" if cc else "R0"))
                        P.op("dve", lambda e, h=h, n=n: e.scalar_tensor_tensor(
                            Sst[:, h, :], Sst[:, h, :], el[:, h, n:n + 1], ps[3][:, 0:256], ALU.mult, ALU.add),
                            reads=[Sst.b, el.b, ps[3].b], writes=[Sst.b])
                P.op("dve", lambda e: e.memset(ss[:], 0.0), writes=[ss.b])
                for h in range(4):
                    po = ps[h // 2]
                    pc = slice((h % 2) * 256, (h % 2 + 1) * 256)
                    P.op("act", lambda e, po=po, pc=pc, h=h: e.activation(
                        g["junk"][:, 0:256], po[:, pc], AF.Square, accum_out=ss[:, h:h + 1]),
                        reads=[po.b, ss.b], writes=[g["junk"].b, ss.b])
                P.op("dve", lambda e: e.tensor_scalar(ss[:, 4:8], ss[:, 0:4], 1.0 / 256.0, EPS, ALU.mult, ALU.add),
                     reads=[ss.b], writes=[ss.b])
                P.op("act", lambda e: e.activation(ss[:, 4:8], ss[:, 4:8], AF.Sqrt), reads=[ss.b], writes=[ss.b])
                P.op("dve", lambda e: e.reciprocal(ss[:, 4:8], ss[:, 4:8]), reads=[ss.b], writes=[ss.b])
                for h in range(4):
                    po = ps[h // 2]
                    pc = slice((h % 2) * 256, (h % 2 + 1) * 256)
                    vsl = slice(h * 256, (h + 1) * 256)
                    P.op("dve", lambda e, po=po, pc=pc, vsl=vsl, h=h: e.scalar_tensor_tensor(
                        zz[:, vsl], po[:, pc], ss[:, 4 + h:5 + h], gbc[:], ALU.mult, ALU.mult),
                        reads=[po.b, ss.b, gbc.b], writes=[zz.b])
                P.op("pool", lambda e: e.tensor_tensor(zz[:], zz[:], sg[:], ALU.mult),
                     reads=[zz.b, sg.b], writes=[zz.b])
                for c in range(8):
                    pt = ps[4 + c // 4]
                    P.op("pe", lambda e, pt=pt, c=c: e.transpose(
                        pt[:, (c % 4) * 128:(c % 4 + 1) * 128], zz[:, c * 128:(c + 1) * 128], g["ident"][:]),
                        reads=[zz.b, g["ident"].b], writes=[pt.b])
                for half in range(2):
                    pt = ps[4 + half]
                    eng = ("act", "dve")[half]
                    if half == 0:
                        P.op("act", lambda e, pt=pt, tsl=tsl: e.copy(
                            zT[:, 0:4, tsl], pt[:].rearrange("p (c t) -> p c t", t=128)),
                            reads=[pt.b], writes=[zT.b])
                    else:
                        P.op("dve", lambda e, pt=pt, tsl=tsl: e.tensor_copy(
                            zT[:, 4:8, tsl], pt[:].rearrange("p (c t) -> p c t", t=128)),
                            reads=[pt.b], writes=[zT.b])
            for tl in range(4):
                back_tile(kb, g, b, tb * 4 + tl, _Shift(zT, tb * TB), [zT.b], wout, x_src, x_dst)
    kb.pop()


class _Shift:
    def __init__(self, t, t0):
        self.t, self.t0 = t, t0

    def __getitem__(self, k):
        a, c, sl = k
        return self.t[a, c, sl.start - self.t0:sl.stop - self.t0]


def gla_consts():
    c = {}
    t = np.arange(512)
    c["gla_scanmask"] = np.tile((t % 64 != 0).astype(np.float32)[None, :], (128, 1))
    j = np.arange(128)[:, None]
    i = np.arange(128)[None, :]
    same = (j // 64) == (i // 64)
    c["gla_mbd"] = (same & (j <= i)).astype(np.float32)
    c["gla_m2"] = (same & (j > i)).astype(np.float32)
    return c


LAYER_FNS[3] = layer_gla


NEG = -1.0e30


def _rope(P, eng2, x, cos, sin, out_list, tmp1, tmp2, nh, half, rd, wr):
    xv = x[:, 0:nh * 2 * half].rearrange("p (h two d) -> p h two d", two=2, d=half)
    x1, x2 = xv[:, :, 0, :], xv[:, :, 1, :]
    cb = cos.unsqueeze(1).to_broadcast([128, nh, half])
    sb_ = sin.unsqueeze(1).to_broadcast([128, nh, half])
    t1 = tmp1[:, 0:nh * half].rearrange("p (h d) -> p h d", d=half)
    t2 = tmp2[:, 0:nh * half].rearrange("p (h d) -> p h d", d=half)
    P.op("dve", lambda e: e.tensor_tensor(t1, x1, cb, ALU.mult), reads=rd, writes=[tmp1.b])
    P.op(eng2, lambda e: e.tensor_tensor(t2, x2, sb_, ALU.mult), reads=rd, writes=[tmp2.b])
    for o in out_list:
        P.op("dve", lambda e, o=o: e.tensor_tensor(o[0], t1, t2, ALU.subtract),
             reads=[tmp1.b, tmp2.b], writes=wr)
    P.op("dve", lambda e: e.tensor_tensor(t1, x2, cb, ALU.mult), reads=rd + [tmp1.b], writes=[tmp1.b])
    P.op(eng2, lambda e: e.tensor_tensor(t2, x1, sb_, ALU.mult), reads=rd + [tmp2.b], writes=[tmp2.b])
    for o in out_list:
        P.op("dve", lambda e, o=o: e.tensor_tensor(o[1], t1, t2, ALU.add),
             reads=[tmp1.b, tmp2.b], writes=wr)


def layer_dsa(kb, g, W, x_src, x_dst):
    nc, P = kb.nc, kb.P
    C = g["consts"]
    kb.push()
    win = kb.sb("dsa_win", [128, 8, 3720], BF16)
    wout = kb.sb("dsa_wout", [128, 8, 1024], BF16)
    begin_load(kb, g)
    compute_mod(kb, g, 0, W)
    load_w(kb, g, win, 0, W["dsa_w_in"][0], 3720)
    load_w(kb, g, wout, 0, W["dsa_w_out"][0], 1024)
    end_load(kb, g)
    ps = g["ps"]
    psb = [kb.psum("psb%d" % i, [128, 1024], BF16) for i in range(2)]
    ident, identb = g["ident"], g["identb"]
    cs32 = kb.sb("cs32", [128, 2, NT, 32], F32)
    cs64 = kb.sb("cs64", [128, 2, NT, 64], F32)
    for i, nm in enumerate(("cos32", "sin32")):
        P.dma("sp", cs32[:, i], C[nm].rearrange("(tt p) f -> p tt f", p=128), writes=[cs32.b])
    for i, nm in enumerate(("cos64", "sin64")):
        P.dma("sp", cs64[:, i], C[nm].rearrange("(tt p) f -> p tt f", p=128), writes=[cs64.b])
    cmask = kb.sb("cmask", [128, 128], F32)
    P.dma("sp", cmask[:], C["causal"][:, :], writes=[cmask.b])
    gq = kb.sb("gq", [128, 2, 64], F32)
    P.dma("sp", gq[:, 0, :], W["dsa_q_gain"][0:1, :].to_broadcast([128, 64]), writes=[gq.b])
    P.dma("sp", gq[:, 1, :], W["dsa_k_gain"][0:1, :].to_broadcast([128, 64]), writes=[gq.b])
    kT2 = kb.sb("kT2", [128, 4, S], BF16)
    v1 = kb.sb("v1", [128, NT, 4, 65], BF16)
    kiT = kb.sb("kiT", [128, S], BF16)
    hTt = kb.sb("hTt", [128, 8, 128], BF16)
    big = kb.sb("big", [128, S], F32)
    qf = _Off(big, 0)
    tmpa = _Off(big, 1024)
    tmpb = kb.sb("tmpb", [128, 512], F32)
    tmpc = kb.sb("tmpc", [128, 512], F32)
    qb = kb.sb("qb", [128, 1024], BF16)
    kb2 = kb.sb("kb2", [128, 4, 2, 64], BF16)
    qTt = kb.sb("qTt", [128, 8, 128], BF16)
    qiTt = kb.sb("qiTt", [128, 8, 128], BF16)
    sgt = kb.sb("sgt", [128, 1024], BF16)
    kf = kb.sb("kf", [128, 512], F32)
    wkf = kb.sb("wkf", [128, 136], F32)
    kib = kb.sb("kib", [128, 128], BF16)
    sm = kb.sb("sm", [128, 64], F32)
    score = kb.sb("score", [128, S], F32)
    work = big
    rl = [kb.sb("rl", [128, 512], F32) for _ in range(2)]
    maskb = kb.sb("maskb", [128, S], BF16)
    maskT = kb.sb("maskT", [128, NT, 128], BF16)
    eb = [kb.sb("eb", [128, 512], BF16) for _ in range(2)]
    pT = [kb.sb("pT", [128, 512], BF16) for _ in range(2)]
    m8 = kb.sb("m8", [128, 8], F32)
    thr = kb.sb("thr", [128, 2], F32)
    zz = _Off(big, 0)
    zTt = kb.sb("zTt", [128, 8, 128], BF16)
    rden = kb.sb("rden", [128, 16], F32)
    IDXS = 1024.0 ** -0.5
    P.op("pool", lambda e: e.memset(v1[:], 1.0), writes=[v1.b])

    def rms_heads(x, nh, gi, out):
        xv = x[:, 0:nh * 64].rearrange("p (h d) -> p h d", d=64)
        P.op("act", lambda e: e.activation(tmpa[:, 0:nh * 64], x[:, 0:nh * 64], AF.Square),
             reads=[x.b], writes=[tmpa.b])
        P.op("dve", lambda e: e.tensor_reduce(sm[:, 0:nh], tmpa[:, 0:nh * 64].rearrange("p (h d) -> p h d", d=64),
                                              AX.X, ALU.add),
             reads=[tmpa.b], writes=[sm.b])
        P.op("dve", lambda e: e.tensor_scalar(sm[:, 0:nh], sm[:, 0:nh], 1.0 / 64.0, EPS, ALU.mult, ALU.add),
             reads=[sm.b], writes=[sm.b])
        P.op("act", lambda e: e.activation(sm[:, 0:nh], sm[:, 0:nh], AF.Sqrt), reads=[sm.b], writes=[sm.b])
        P.op("dve", lambda e: e.reciprocal(sm[:, 0:nh], sm[:, 0:nh]), reads=[sm.b], writes=[sm.b])
        ov = out[:, 0:nh * 64].rearrange("p (h d) -> p h d", d=64)
        P.op("dve", lambda e: e.tensor_tensor(ov, xv, sm[:, 0:nh].unsqueeze(2).to_broadcast([128, nh, 64]), ALU.mult),
             reads=[x.b, sm.b], writes=[out.b])
        P.op("pool", lambda e: e.tensor_tensor(ov, ov, gq[:, gi, :].unsqueeze(1).to_broadcast([128, nh, 64]), ALU.mult),
             reads=[out.b, gq.b], writes=[out.b])

    for b in range(NB):
        for tt in range(NT):
            tsl = slice(tt * 128, (tt + 1) * 128)
            L = (tt + 1) * 128
            front_tile(kb, g, b, tt, x_src, _Shift(hTt, tt * 128), hTt.b)

            def proj(pt, c0, n):
                for c in range(8):
                    P.op("pe", lambda e, c=c: e.matmul(pt[:, 0:n], hTt[:, c, :], win[:, c, c0:c0 + n],
                                                       start=(c == 0), stop=(c == 7)),
                         reads=[hTt.b, win.b], writes=[pt.b])
            for half in range(2):
                proj(ps[2 + half], half * 512, 512)
                P.op("act", lambda e, half=half: e.copy(qf[:, half * 512:(half + 1) * 512], ps[2 + half][:]),
                     reads=[ps[2 + half].b], writes=[qf.b])
            rms_heads(qf, 16, 0, qf)
            qbv = qb[:].rearrange("p (h two d) -> p h two d", two=2, d=32)
            _rope(P, "pool", qf, cs32[:, 0, tt, :], cs32[:, 1, tt, :], [(qbv[:, :, 0, :], qbv[:, :, 1, :])],
                  tmpb, tmpc, 16, 32, [qf.b, cs32.b], [qb.b])
            for c in range(8):
                P.op("pe", lambda e, c=c: e.transpose(psb[0][:, c * 128:(c + 1) * 128], qb[:, c * 128:(c + 1) * 128],
                                                      identb[:]),
                     reads=[qb.b, identb.b], writes=[psb[0].b])
            P.op("act", lambda e: e.copy(qTt[:], psb[0][:].rearrange("p (c t) -> p c t", t=128)),
                 reads=[psb[0].b], writes=[qTt.b])
            proj(ps[4], 1024, 512)
            P.op("act", lambda e: e.copy(kf[:], ps[4][:]), reads=[ps[4].b], writes=[kf.b])
            P.op("dve", lambda e, tt=tt: e.tensor_copy(
                v1[:, tt, :, 0:64], kf[:, 256:512].rearrange("p (g d) -> p g d", d=64)),
                reads=[kf.b], writes=[v1.b])
            rms_heads(kf, 4, 1, kf)
            k2v = kb2[:].rearrange("p g r (two d) -> p g r two d", two=2)
            _rope(P, "pool", kf, cs32[:, 0, tt, :], cs32[:, 1, tt, :],
                  [(k2v[:, :, 0, 0, :], k2v[:, :, 0, 1, :]), (k2v[:, :, 1, 0, :], k2v[:, :, 1, 1, :])],
                  tmpb, tmpc, 4, 32, [kf.b, cs32.b], [kb2.b])
            for gg in range(4):
                P.op("pe", lambda e, gg=gg: e.transpose(
                    psb[1][:, gg * 128:(gg + 1) * 128], kb2[:, gg].rearrange("p r d -> p (r d)"), identb[:]),
                    reads=[kb2.b, identb.b], writes=[psb[1].b])
            P.op("act", lambda e, tsl=tsl: e.copy(kT2[:, :, tsl], psb[1][:, 0:512].rearrange("p (g t) -> p g t", t=128)),
                 reads=[psb[1].b], writes=[kT2.b])
            for half in range(2):
                proj(ps[2 + half], 1536 + half * 512, 512)
                P.op("act", lambda e, half=half: e.activation(sgt[:, half * 512:(half + 1) * 512], ps[2 + half][:], AF.Silu),
                     reads=[ps[2 + half].b], writes=[sgt.b])
            for half in range(2):
                proj(ps[4 + half], 2560 + half * 512, 512)
                P.op("act", lambda e, half=half: e.copy(qf[:, half * 512:(half + 1) * 512], ps[4 + half][:]),
                     reads=[ps[4 + half].b], writes=[qf.b])
            qbv2 = qb[:].rearrange("p (h two d) -> p h two d", two=2, d=64)
            _rope(P, "pool", qf, cs64[:, 0, tt, :], cs64[:, 1, tt, :], [(qbv2[:, :, 0, :], qbv2[:, :, 1, :])],
                  tmpb, tmpc, 8, 64, [qf.b, cs64.b], [qb.b])
            for c in range(8):
                P.op("pe", lambda e, c=c: e.transpose(psb[0][:, c * 128:(c + 1) * 128], qb[:, c * 128:(c + 1) * 128],
                                                      identb[:]),
                     reads=[qb.b, identb.b], writes=[psb[0].b])
            P.op("act", lambda e: e.copy(qiTt[:], psb[0][:].rearrange("p (c t) -> p c t", t=128)),
                 reads=[psb[0].b], writes=[qiTt.b])
            proj(ps[4], 3584, 136)
            P.op("act", lambda e: e.copy(wkf[:], ps[4][:, 0:136]), reads=[ps[4].b], writes=[wkf.b])
            P.op("dve", lambda e: e.tensor_scalar(wkf[:, 0:8], wkf[:, 0:8], IDXS, None, ALU.mult),
                 reads=[wkf.b], writes=[wkf.b])
            kiv = kib[:].rearrange("p (h two d) -> p h two d", two=2, d=64)
            _rope(P, "pool", _Off(wkf, 8), cs64[:, 0, tt, :], cs64[:, 1, tt, :], [(kiv[:, :, 0, :], kiv[:, :, 1, :])],
                  tmpb, tmpc, 1, 64, [wkf.b, cs64.b], [kib.b])
            P.op("pe", lambda e: e.transpose(psb[1][:, 0:128], kib[:], identb[:]),
                 reads=[kib.b, identb.b], writes=[psb[1].b])
            P.op("act", lambda e, tsl=tsl: e.copy(kiT[:, tsl], psb[1][:, 0:128]), reads=[psb[1].b], writes=[kiT.b])
            n_it = 0
            for k0 in range(0, L, 512):
                w = min(512, L - k0)
                for h in range(8):
                    pt = ps[2 + n_it % 2]
                    r_ = rl[n_it % 2]
                    n_it += 1
                    P.op("pe", lambda e, pt=pt, h=h, k0=k0, w=w: e.matmul(
                        pt[:, 0:w], qiTt[:, h, :], kiT[:, k0:k0 + w], start=True, stop=True),
                        reads=[qiTt.b, kiT.b], writes=[pt.b])
                    P.op("act", lambda e, pt=pt, r_=r_, w=w: e.activation(r_[:, 0:w], pt[:, 0:w], AF.Relu),
                         reads=[pt.b], writes=[r_.b])
                    if h == 0:
                        P.op("dve", lambda e, r_=r_, k0=k0, w=w, h=h: e.tensor_scalar(
                            score[:, k0:k0 + w], r_[:, 0:w], wkf[:, h:h + 1], None, ALU.mult),
                            reads=[r_.b, wkf.b], writes=[score.b])
                    else:
                        P.op("dve", lambda e, r_=r_, k0=k0, w=w, h=h: e.scalar_tensor_tensor(
                            score[:, k0:k0 + w], r_[:, 0:w], wkf[:, h:h + 1], score[:, k0:k0 + w],
                            ALU.mult, ALU.add),
                            reads=[r_.b, wkf.b, score.b], writes=[score.b])
            P.op("dve", lambda e, tsl=tsl: e.tensor_tensor(score[:, tsl], score[:, tsl], cmask[:], ALU.add),
                 reads=[score.b, cmask.b], writes=[score.b])
            if tt >= 2:
                P.op("pool", lambda e, L=L: e.tensor_copy(work[:, 0:L], score[:, 0:L]),
                     reads=[score.b], writes=[work.b])
                for it in range(32):
                    P.op("dve", lambda e, L=L: e.max(m8[:], work[:, 0:L]), reads=[work.b], writes=[m8.b])
                    if it < 31:
                        P.op("dve", lambda e, L=L: e.match_replace(work[:, 0:L], m8[:], work[:, 0:L], NEG),
                             reads=[work.b, m8.b], writes=[work.b])
                P.op("dve", lambda e: e.tensor_reduce(thr[:, 0:1], m8[:], AX.X, ALU.min),
                     reads=[m8.b], writes=[thr.b])
                P.op("dve", lambda e: e.tensor_scalar(thr[:, 0:1], thr[:, 0:1], -1.0e29, None, ALU.max),
                     reads=[thr.b], writes=[thr.b])
            else:
                P.op("dve", lambda e: e.memset(thr[:, 0:1], -1.0e29), writes=[thr.b])
            P.op("dve", lambda e, L=L: e.tensor_scalar(maskb[:, 0:L], score[:, 0:L], thr[:, 0:1], None, ALU.is_ge),
                 reads=[score.b, thr.b], writes=[maskb.b])
            for k0 in range(0, tt + 1, 8):
                nk = min(8, tt + 1 - k0)
                for kk in range(nk):
                    kbk = k0 + kk
                    P.op("pe", lambda e, kk=kk, kbk=kbk: e.transpose(
                        psb[0][:, kk * 128:(kk + 1) * 128], maskb[:, kbk * 128:(kbk + 1) * 128], identb[:]),
                        reads=[maskb.b, identb.b], writes=[psb[0].b])
                P.op("act", lambda e, k0=k0, nk=nk: e.copy(
                    maskT[:, k0:k0 + nk, :], psb[0][:, 0:nk * 128].rearrange("p (k t) -> p k t", t=128)),
                    reads=[psb[0].b], writes=[maskT.b])
            n_it = 0
            for gg in range(4):
                po = ps[gg]
                for kbk in range(tt + 1):
                    ksl = slice(kbk * 128, (kbk + 1) * 128)
                    pt = ps[4 + n_it % 2]
                    e_ = eb[n_it % 2]
                    p_ = pT[n_it % 2]
                    n_it += 1
                    for par in range(2):
                        hs = slice(par * 64, (par + 1) * 64)
                        P.op("pe", lambda e, pt=pt, par=par, hs=hs, gg=gg, ksl=ksl: e.matmul(
                            pt[:, par * 256:(par + 1) * 256], kT2[hs, gg, ksl], qTt[hs, 2 * gg:2 * gg + 2, :],
                            start=True, stop=True),
                            reads=[kT2.b, qTt.b], writes=[pt.b], pos=(1 if par else None))
                    P.op("act", lambda e, pt=pt, e_=e_: e.activation(e_[:], pt[:], AF.Exp, scale=0.125),
                         reads=[pt.b], writes=[e_.b])
                    P.op("dve", lambda e, e_=e_, p_=p_, kbk=kbk: e.tensor_tensor(
                        p_[:].rearrange("p (a t) -> p a t", t=128), e_[:].rearrange("p (a t) -> p a t", t=128),
                        maskT[:, kbk, :].unsqueeze(1).to_broadcast([128, 4, 128]), ALU.mult),
                        reads=[e_.b, maskT.b], writes=[p_.b])
                    for blk in range(4):
                        c0 = blk * 65
                        P.op("pe", lambda e, po=po, c0=c0, p_=p_, blk=blk, tt=tt, kbk=kbk, gg=gg: e.matmul(
                            po[:, c0:c0 + 65], p_[:, blk * 128:(blk + 1) * 128], v1[:, kbk, gg, :],
                            start=(kbk == 0 and blk == 0), stop=(kbk == tt)),
                            reads=[p_.b, v1.b], writes=[po.b])
            for gg in range(4):
                po = ps[gg]
                pv = po[:, 0:260].rearrange("p (k d) -> p k d", d=65)
                P.op("dve", lambda e, pv=pv, gg=gg: e.reciprocal(rden[:, gg * 4:(gg + 1) * 4], pv[:, :, 64]),
                     reads=[po.b], writes=[rden.b])
                zv = zz[:, gg * 256:(gg + 1) * 256].rearrange("p (cp par d) -> p par cp d", cp=2, par=2)
                for par in range(2):
                    P.op("dve", lambda e, pv=pv, zv=zv, par=par, gg=gg: e.tensor_tensor(
                        zv[:, par], pv[:, par * 2:par * 2 + 2, 0:64],
                        rden[:, gg * 4 + par * 2:gg * 4 + par * 2 + 2].unsqueeze(2).to_broadcast([128, 2, 64]),
                        ALU.mult),
                        reads=[po.b, rden.b], writes=[zz.b])
            P.op("pool", lambda e: e.tensor_tensor(zz[:, 0:1024], zz[:, 0:1024], sgt[:], ALU.mult),
                 reads=[zz.b, sgt.b], writes=[zz.b])
            for c in range(8):
                pt = ps[4 + c // 4]
                P.op("pe", lambda e, pt=pt, c=c: e.transpose(
                    pt[:, (c % 4) * 128:(c % 4 + 1) * 128], zz[:, c * 128:(c + 1) * 128], ident[:]),
                    reads=[zz.b, ident.b], writes=[pt.b])
            P.op("act", lambda e: e.copy(zTt[:, 0:4, :], ps[4][:].rearrange("p (c t) -> p c t", t=128)),
                 reads=[ps[4].b], writes=[zTt.b])
            P.op("dve", lambda e: e.tensor_copy(zTt[:, 4:8, :], ps[5][:].rearrange("p (c t) -> p c t", t=128)),
                 reads=[ps[5].b], writes=[zTt.b])
            back_tile(kb, g, b, tt, _Shift(zTt, tt * 128), [zTt.b], wout, x_src, x_dst)
    kb.pop()


def layer_dsa2(kb, g, W, x_src, x_dst):
    nc, P = kb.nc, kb.P
    C = g["consts"]
    kb.push()
    win = kb.sb("dsa_win", [128, 8, 3720], BF16)
    wout = kb.sb("dsa_wout", [128, 8, 1024], BF16)
    begin_load(kb, g)
    compute_mod(kb, g, 0, W)
    load_w(kb, g, win, 0, W["dsa_w_in"][0], 3720)
    load_w(kb, g, wout, 0, W["dsa_w_out"][0], 1024)
    end_load(kb, g)
    ps = g["ps"]
    psb = [kb.psum("psb%d" % i, [128, 1024], BF16) for i in range(2)]
    ident, identb = g["ident"], g["identb"]
    cs32 = kb.sb("cs32", [128, 2, 32], F32)
    cs64 = kb.sb("cs64", [128, 2, 64], F32)
    cmask = kb.sb("cmask", [128, 128], F32)
    P.dma("sp", cmask[:], C["causal"][:, :], writes=[cmask.b])
    gq = kb.sb("gq", [128, 2, 64], F32)
    P.dma("sp", gq[:, 0, :], W["dsa_q_gain"][0:1, :].to_broadcast([128, 64]), writes=[gq.b])
    P.dma("sp", gq[:, 1, :], W["dsa_k_gain"][0:1, :].to_broadcast([128, 64]), writes=[gq.b])
    kT2 = kb.sb("kT2", [128, 4, S], BF16)
    v1 = kb.sb("v1", [128, NT, 4, 65], BF16)
    kiT = kb.sb("kiT", [128, S], BF16)
    kT2b = [Buf("kT2_%d" % i) for i in range(NT)]
    v1b = [Buf("v1_%d" % i) for i in range(NT)]
    kiTb = [Buf("kiT_%d" % i) for i in range(NT)]
    hTt = kb.sb("hTt", [128, 8, 128], BF16)
    big = kb.sb("big", [128, S], F32)
    qf = _Off(big, 0)
    tmpa = _Off(big, 1024)
    work = big
    tmpb = kb.sb("tmpb", [128, 512], F32)
    tmpc = kb.sb("tmpc", [128, 512], F32)
    qb = kb.sb("qb", [128, 1024], BF16)
    kb2 = kb.sb("kb2", [128, 4, 2, 64], BF16)
    qiTt = kb.sb("qiTt", [128, 8, 128], BF16)
    kf = kb.sb("kf", [128, 512], F32)
    wkf = kb.sb("wkf", [128, 136], F32)
    kib = kb.sb("kib", [128, 128], BF16)
    sm = kb.sb("sm", [128, 64], F32)
    score = kb.sb("score", [128, S], F32)
    rl = [kb.sb("rl", [128, 512], F32) for _ in range(2)]
    maskb = kb.sb("maskb", [128, S], BF16)
    m8 = kb.sb("m8", [128, 8], F32)
    thr = kb.sb("thr", [128, 2], F32)
    qTt = [kb.sb("qTz", [128, 16, 128], BF16) for _ in range(2)]
    for i_ in range(2):
        P.op("pool", lambda e, i_=i_: e.memset(qTt[i_][:], 0.0), writes=[qTt[i_].b])
    sgt = [kb.sb("sgt", [128, 1024], BF16) for _ in range(2)]
    maskT = [kb.sb("maskT", [128, NT, 128], BF16) for _ in range(2)]
    eb = [kb.sb("eb", [128, 512], BF16) for _ in range(2)]
    pT = [kb.sb("pT", [128, 512], BF16) for _ in range(2)]
    zz = kb.sb("zzd", [128, 1024], F32)
    zTt = kb.sb("zTt", [128, 8, 128], BF16)
    rden = kb.sb("rden", [128, 16], F32)
    IDXS = 1024.0 ** -0.5
    P.op("pool", lambda e: e.memset(v1[:], 1.0), writes=v1b)
    S1B = (ps[4], ps[5])

    def rms_heads(x, nh, gi, out):
        xv = x[:, 0:nh * 64].rearrange("p (h d) -> p h d", d=64)
        P.op("act", lambda e: e.activation(tmpa[:, 0:nh * 64], x[:, 0:nh * 64], AF.Square),
             reads=[x.b], writes=[tmpa.b])
        P.op("dve", lambda e: e.tensor_reduce(sm[:, 0:nh], tmpa[:, 0:nh * 64].rearrange("p (h d) -> p h d", d=64),
                                              AX.X, ALU.add),
             reads=[tmpa.b], writes=[sm.b])
        P.op("dve", lambda e: e.tensor_scalar(sm[:, 0:nh], sm[:, 0:nh], 1.0 / 64.0, EPS, ALU.mult, ALU.add),
             reads=[sm.b], writes=[sm.b])
        P.op("act", lambda e: e.activation(sm[:, 0:nh], sm[:, 0:nh], AF.Sqrt), reads=[sm.b], writes=[sm.b])
        P.op("dve", lambda e: e.reciprocal(sm[:, 0:nh], sm[:, 0:nh]), reads=[sm.b], writes=[sm.b])
        ov = out[:, 0:nh * 64].rearrange("p (h d) -> p h d", d=64)
        P.op("dve", lambda e: e.tensor_tensor(ov, xv, sm[:, 0:nh].unsqueeze(2).to_broadcast([128, nh, 64]), ALU.mult),
             reads=[x.b, sm.b], writes=[out.b])
        P.op("pool", lambda e: e.tensor_tensor(ov, ov, gq[:, gi, :].unsqueeze(1).to_broadcast([128, nh, 64]), ALU.mult),
             reads=[out.b, gq.b], writes=[out.b])

    def stage1(b, tt):
        hs_ = tt % 2
        qTt_c, sgt_c, maskT_c = qTt[hs_], sgt[hs_], maskT[hs_]
        tsl = slice(tt * 128, (tt + 1) * 128)
        L = (tt + 1) * 128
        for i, nm in enumerate(("cos32", "sin32")):
            P.dma("act", cs32[:, i, :], C[nm][tt * 128:(tt + 1) * 128, :], writes=[cs32.b])
        for i, nm in enumerate(("cos64", "sin64")):
            P.dma("act", cs64[:, i, :], C[nm][tt * 128:(tt + 1) * 128, :], writes=[cs64.b])
        front_tile(kb, g, b, tt, x_src, _Shift(hTt, tt * 128), hTt.b, pbanks=S1B, xi=0)
        yield

        def proj(pt, c0, n):
            for c in range(8):
                P.op("pe", lambda e, c=c: e.matmul(pt[:, 0:n], hTt[:, c, :], win[:, c, c0:c0 + n],
                                                   start=(c == 0), stop=(c == 7)),
                     reads=[hTt.b, win.b], writes=[pt.b])
        for half in range(2):
            proj(S1B[half], half * 512, 512)
            P.op("act", lambda e, half=half: e.copy(qf[:, half * 512:(half + 1) * 512], S1B[half][:]),
                 reads=[S1B[half].b], writes=[qf.b])
        rms_heads(qf, 16, 0, qf)
        qbv = qb[:].rearrange("p (h two d) -> p h two d", two=2, d=32)
        _rope(P, "pool", qf, cs32[:, 0, :], cs32[:, 1, :], [(qbv[:, :, 0, :], qbv[:, :, 1, :])],
              tmpb, tmpc, 16, 32, [qf.b, cs32.b], [qb.b])
        for c in range(8):
            P.op("pe", lambda e, c=c: e.transpose(psb[0][:, c * 128:(c + 1) * 128], qb[:, c * 128:(c + 1) * 128],
                                                  identb[:]),
                 reads=[qb.b, identb.b], writes=[psb[0].b])
        for h2 in range(2):
            hsl = slice(h2 * 64, (h2 + 1) * 64)
            P.op(("act", "dve")[h2], lambda e, h2=h2, hsl=hsl: (e.copy if h2 == 0 else e.tensor_copy)(
                qTt_c[hsl, :, :].rearrange("p (c two) t -> p c two t", two=2)[:, :, h2, :],
                psb[0][hsl, :].rearrange("p (c t) -> p c t", t=128)),
                reads=[psb[0].b], writes=[qTt_c.b])
        yield
        proj(S1B[0], 1024, 512)
        P.op("act", lambda e: e.copy(kf[:], S1B[0][:]), reads=[S1B[0].b], writes=[kf.b])
        P.op("dve", lambda e: e.tensor_copy(
            v1[:, tt, :, 0:64], kf[:, 256:512].rearrange("p (g d) -> p g d", d=64)),
            reads=[kf.b], writes=[v1b[tt]])
        rms_heads(kf, 4, 1, kf)
        k2v = kb2[:].rearrange("p g r (two d) -> p g r two d", two=2)
        _rope(P, "pool", kf, cs32[:, 0, :], cs32[:, 1, :],
              [(k2v[:, :, 0, 0, :], k2v[:, :, 0, 1, :]), (k2v[:, :, 1, 0, :], k2v[:, :, 1, 1, :])],
              tmpb, tmpc, 4, 32, [kf.b, cs32.b], [kb2.b])
        for gg in range(4):
            P.op("pe", lambda e, gg=gg: e.transpose(
                psb[1][:, gg * 128:(gg + 1) * 128], kb2[:, gg].rearrange("p r d -> p (r d)"), identb[:]),
                reads=[kb2.b, identb.b], writes=[psb[1].b])
        P.op("act", lambda e: e.copy(kT2[:, :, tsl], psb[1][:, 0:512].rearrange("p (g t) -> p g t", t=128)),
             reads=[psb[1].b], writes=[kT2b[tt]])
        yield
        for half in range(2):
            proj(S1B[half], 1536 + half * 512, 512)
            P.op("act", lambda e, half=half: e.activation(sgt_c[:, half * 512:(half + 1) * 512], S1B[half][:], AF.Silu),
                 reads=[S1B[half].b], writes=[sgt_c.b])
        yield
        for half in range(2):
            proj(S1B[half], 2560 + half * 512, 512)
            P.op("act", lambda e, half=half: e.copy(qf[:, half * 512:(half + 1) * 512], S1B[half][:]),
                 reads=[S1B[half].b], writes=[qf.b])
        qbv2 = qb[:].rearrange("p (h two d) -> p h two d", two=2, d=64)
        _rope(P, "pool", qf, cs64[:, 0, :], cs64[:, 1, :], [(qbv2[:, :, 0, :], qbv2[:, :, 1, :])],
              tmpb, tmpc, 8, 64, [qf.b, cs64.b], [qb.b])
        for c in range(8):
            P.op("pe", lambda e, c=c: e.transpose(psb[0][:, c * 128:(c + 1) * 128], qb[:, c * 128:(c + 1) * 128],
                                                  identb[:]),
                 reads=[qb.b, identb.b], writes=[psb[0].b])
        P.op("act", lambda e: e.copy(qiTt[:], psb[0][:].rearrange("p (c t) -> p c t", t=128)),
             reads=[psb[0].b], writes=[qiTt.b])
        yield
        proj(S1B[0], 3584, 136)
        P.op("act", lambda e: e.copy(wkf[:], S1B[0][:, 0:136]), reads=[S1B[0].b], writes=[wkf.b])
        P.op("dve", lambda e: e.tensor_scalar(wkf[:, 0:8], wkf[:, 0:8], IDXS, None, ALU.mult),
             reads=[wkf.b], writes=[wkf.b])
        kiv = kib[:].rearrange("p (h two d) -> p h two d", two=2, d=64)
        _rope(P, "pool", _Off(wkf, 8), cs64[:, 0, :], cs64[:, 1, :], [(kiv[:, :, 0, :], kiv[:, :, 1, :])],
              tmpb, tmpc, 1, 64, [wkf.b, cs64.b], [kib.b])
        P.op("pe", lambda e: e.transpose(psb[1][:, 0:128], kib[:], identb[:]),
             reads=[kib.b, identb.b], writes=[psb[1].b])
        P.op("act", lambda e: e.copy(kiT[:, tsl], psb[1][:, 0:128]), reads=[psb[1].b], writes=[kiTb[tt]])
        yield
        n_it = 0
        for k0 in range(0, L, 512):
            w = min(512, L - k0)
            kbufs = kiTb[k0 // 128:(k0 + w) // 128]
            for h in range(8):
                pt = S1B[n_it % 2]
                r_ = rl[n_it % 2]
                n_it += 1
                P.op("pe", lambda e, pt=pt, h=h, k0=k0, w=w: e.matmul(
                    pt[:, 0:w], qiTt[:, h, :], kiT[:, k0:k0 + w], start=True, stop=True),
                    reads=[qiTt.b] + kbufs, writes=[pt.b])
                P.op("act", lambda e, pt=pt, r_=r_, w=w: e.activation(r_[:, 0:w], pt[:, 0:w], AF.Relu),
                     reads=[pt.b], writes=[r_.b])
                if h == 0:
                    P.op("dve", lambda e, r_=r_, k0=k0, w=w, h=h: e.tensor_scalar(
                        score[:, k0:k0 + w], r_[:, 0:w], wkf[:, h:h + 1], None, ALU.mult),
                        reads=[r_.b, wkf.b], writes=[score.b])
                else:
                    P.op("dve", lambda e, r_=r_, k0=k0, w=w, h=h: e.scalar_tensor_tensor(
                        score[:, k0:k0 + w], r_[:, 0:w], wkf[:, h:h + 1], score[:, k0:k0 + w],
                        ALU.mult, ALU.add),
                        reads=[r_.b, wkf.b, score.b], writes=[score.b])
            yield
        P.op("dve", lambda e: e.tensor_tensor(score[:, tsl], score[:, tsl], cmask[:], ALU.add),
             reads=[score.b, cmask.b], writes=[score.b])
        if tt >= 2:
            P.op("pool", lambda e: e.tensor_copy(work[:, 0:L], score[:, 0:L]),
                 reads=[score.b], writes=[work.b])
            for it in range(32):
                P.op("dve", lambda e: e.max(m8[:], work[:, 0:L]), reads=[work.b], writes=[m8.b])
                if it < 31:
                    P.op("dve", lambda e: e.match_replace(work[:, 0:L], m8[:], work[:, 0:L], NEG),
                         reads=[work.b, m8.b], writes=[work.b])
                yield
            P.op("dve", lambda e: e.tensor_reduce(thr[:, 0:1], m8[:], AX.X, ALU.min),
                 reads=[m8.b], writes=[thr.b])
            P.op("dve", lambda e: e.tensor_scalar(thr[:, 0:1], thr[:, 0:1], -1.0e29, None, ALU.max),
                 reads=[thr.b], writes=[thr.b])
        else:
            P.op("dve", lambda e: e.memset(thr[:, 0:1], -1.0e29), writes=[thr.b])
        P.op("dve", lambda e: e.tensor_scalar(maskb[:, 0:L], score[:, 0:L], thr[:, 0:1], None, ALU.is_ge),
             reads=[score.b, thr.b], writes=[maskb.b])
        for k0 in range(0, tt + 1, 8):
            nk = min(8, tt + 1 - k0)
            for kk in range(nk):
                kbk = k0 + kk
                P.op("pe", lambda e, kk=kk, kbk=kbk: e.transpose(
                    psb[0][:, kk * 128:(kk + 1) * 128], maskb[:, kbk * 128:(kbk + 1) * 128], identb[:]),
                    reads=[maskb.b, identb.b], writes=[psb[0].b])
            P.op("act", lambda e, k0=k0, nk=nk: e.copy(
                maskT_c[:, k0:k0 + nk, :], psb[0][:, 0:nk * 128].rearrange("p (k t) -> p k t", t=128)),
                reads=[psb[0].b], writes=[maskT_c.b])
        yield

    def stage2(b, tt):
        hs_ = tt % 2
        qTt_c, sgt_c, maskT_c = qTt[hs_], sgt[hs_], maskT[hs_]
        n_it = 0
        for gg in range(4):
            for kbk in range(tt + 1):
                ksl = slice(kbk * 128, (kbk + 1) * 128)
                pt = ps[3]
                e_ = eb[n_it % 2]
                p_ = pT[n_it % 2]
                n_it += 1
                P.op("pe", lambda e, gg=gg, ksl=ksl: e.matmul(
                    pt[:], kT2[:, gg, ksl], qTt_c[:, 4 * gg:4 * gg + 4, :], start=True, stop=True),
                    reads=[kT2b[kbk], qTt_c.b], writes=[pt.b])
                P.op("act", lambda e, e_=e_: e.activation(e_[:], pt[:], AF.Exp, scale=0.125),
                     reads=[pt.b], writes=[e_.b])
                P.op("dve", lambda e, e_=e_, p_=p_, kbk=kbk: e.tensor_tensor(
                    p_[:].rearrange("p (a t) -> p a t", t=128), e_[:].rearrange("p (a t) -> p a t", t=128),
                    maskT_c[:, kbk, :].unsqueeze(1).to_broadcast([128, 4, 128]), ALU.mult),
                    reads=[e_.b, maskT_c.b], writes=[p_.b])
                for blk in range(4):
                    hidx = gg * 4 + blk
                    po = ps[hidx // 7]
                    c0 = (hidx % 7) * 65
                    P.op("pe", lambda e, po=po, c0=c0, p_=p_, blk=blk, kbk=kbk, gg=gg, hidx=hidx: e.matmul(
                        po[:, c0:c0 + 65], p_[:, blk * 128:(blk + 1) * 128], v1[:, kbk, gg, :],
                        start=(kbk == 0 and hidx % 7 == 0), stop=(kbk == tt)),
                        reads=[p_.b, v1b[kbk]], writes=[po.b])
                yield
        for bk in range(3):
            nh = 7 if bk < 2 else 2
            pv = ps[bk][:, 0:nh * 65].rearrange("p (k d) -> p k d", d=65)
            P.op("dve", lambda e, pv=pv, bk=bk, nh=nh: e.reciprocal(rden[:, bk * 7:bk * 7 + nh], pv[:, :, 64]),
                 reads=[ps[bk].b], writes=[rden.b])
        for hidx in range(16):
            gg, blk = hidx // 4, hidx % 4
            head = hidx
            po = ps[hidx // 7]
            c0 = (hidx % 7) * 65
            P.op("dve", lambda e, po=po, c0=c0, head=head, hidx=hidx: e.tensor_scalar(
                zz[:, head * 64:(head + 1) * 64], po[:, c0:c0 + 64], rden[:, hidx:hidx + 1], None, ALU.mult),
                reads=[po.b, rden.b], writes=[zz.b])
        P.op("pool", lambda e: e.tensor_tensor(zz[:], zz[:], sgt_c[:], ALU.mult),
             reads=[zz.b, sgt_c.b], writes=[zz.b])
        yield
        for c in range(8):
            pt2 = ps[c // 4]
            P.op("pe", lambda e, pt2=pt2, c=c: e.transpose(
                pt2[:, (c % 4) * 128:(c % 4 + 1) * 128], zz[:, c * 128:(c + 1) * 128], ident[:]),
                reads=[zz.b, ident.b], writes=[pt2.b])
        P.op("act", lambda e: e.copy(zTt[:, 0:4, :], ps[0][:].rearrange("p (c t) -> p c t", t=128)),
             reads=[ps[0].b], writes=[zTt.b])
        P.op("dve", lambda e: e.tensor_copy(zTt[:, 4:8, :], ps[1][:].rearrange("p (c t) -> p c t", t=128)),
             reads=[ps[1].b], writes=[zTt.b])
        back_tile(kb, g, b, tt, _Shift(zTt, tt * 128), [zTt.b], wout, x_src, x_dst, pbanks=(ps[2], ps[3]), xi=1)
        yield

    def n_seg1(tt):
        return 7 + (tt + 4) // 4 + (32 if tt >= 2 else 0) + 1

    def n_seg2(tt):
        return 4 * (tt + 1) + 2

    for b in range(NB):
        for step in range(NT + 1):
            s2 = stage2(b, step - 1) if step >= 1 else None
            s1 = stage1(b, step) if step < NT else None
            if s1 is None or s2 is None:
                for s_ in (s1, s2):
                    if s_ is not None:
                        for _ in s_:
                            pass
                continue
            n1, n2 = n_seg1(step), n_seg2(step - 1)
            a1 = a2 = 0.0
            d1 = d2 = False
            while not (d1 and d2):
                if not d1 and (d2 or a1 / n1 <= a2 / n2):
                    try:
                        next(s1)
                        a1 += 1
                    except StopIteration:
                        d1 = True
                else:
                    try:
                        next(s2)
                        a2 += 1
                    except StopIteration:
                        d2 = True
    kb.pop()


class _Off:
    def __init__(self, t, off):
        self.t, self.off, self.b = t, off, t.b

    def __getitem__(self, k):
        a, sl = k
        return self.t[a, sl.start + self.off:sl.stop + self.off]


def dsa_consts():
    c = {}
    pos = np.arange(S, dtype=np.float32)[:, None]
    for half, nm in ((32, "32"), (64, "64")):
        inv = (10000.0 ** (-np.arange(half, dtype=np.float32) / half)).astype(np.float32)
        ang = (pos * inv[None, :]).astype(np.float32)
        c["cos" + nm] = np.cos(ang).astype(np.float32)
        c["sin" + nm] = np.sin(ang).astype(np.float32)
    t = np.arange(128)[:, None]
    s_ = np.arange(128)[None, :]
    c["causal"] = np.where(s_ <= t, 0.0, NEG).astype(np.float32)
    return c


LAYER_FNS[0] = layer_dsa2


def layer_rwkv(kb, g, W, x_src, x_dst):
    nc, P = kb.nc, kb.P
    ps = g["ps"]
    ident = g["ident"]
    kb.push()
    wout = kb.sb("rw_wout", [128, 8, 1024], BF16)
    names = ["R", "Wd", "K", "A", "B", "V", "Y"]
    if "rw_scr" not in g:
        g["rw_scr"] = {n: kb.dram("rw_" + n, [NB, 16, S, 64]) for n in names}
        g["rw_scr"]["SG"] = kb.dram("rw_SG", [NB, S, 1024])
        if DEBUG_OUT:
            for nm in ("D1", "D2", "D3"):
                g["rw_scr"][nm] = kb.dram("rw_" + nm, [NB, S, 1024])
        g["rw_scr"]["BON"] = kb.dram("rw_BON", [NB, S, 1024])
    scr = g["rw_scr"]

    def tok_view(n, b, tt):
        return scr[n].t[b].rearrange("h t j -> t h j")[tt * 128:(tt + 1) * 128]

    def bc_row(dst, src_row):
        P.dma("sp", dst[:], src_row.to_broadcast([128, 1024]), writes=[dst.b])

    kb.push()
    win = kb.sb("rw_win", [128, 4, 8, 1024], BF16)
    w1a1 = kb.sb("rw_w1a1", [128, 8, 128], BF16)
    w2e = [kb.sb("rw_w2e", [65, 1024], BF16) for _ in range(2)]
    muT = kb.sb("rw_mu", [128, 6, 8], F32)
    kkb = kb.sb("rw_kk", [128, 1024], F32)
    kab = kb.sb("rw_ka", [128, 1024], F32)
    rkb = kb.sb("rw_rk", [128, 1024], F32)
    begin_load(kb, g)
    compute_mod(kb, g, 2, W)
    for n in range(4):
        load_w(kb, g, _W4(win, n), 0, W["rwkv_w_in"][0, n], 1024)
    load_w(kb, g, wout, 0, W["rwkv_w_out"][0], 1024)
    load_w(kb, g, w1a1, 0, W["rwkv_w1"][0], 64)
    load_w(kb, g, w1a1, 64, W["rwkv_a1"][0], 64)
    for i, (m2_, m0_) in enumerate((("rwkv_w2", "rwkv_w0"), ("rwkv_a2", "rwkv_a0"))):
        stg = g["stage"][g["stage_n"] % 2]
        g["stage_n"] += 1
        sv = stg[:].rearrange("p k n -> p (k n)")
        P.dma("sp", sv[0:64, 0:1024], W[m2_][0], writes=[stg.b])
        P.dma("sp", sv[64:65, 0:1024], W[m0_][0:1, :], writes=[stg.b])
        P.op("dve", lambda e, i=i, sv=sv: e.tensor_copy(w2e[i][:], sv[0:65, 0:1024]),
             reads=[stg.b], writes=[w2e[i].b])
    end_load(kb, g)
    for n in range(6):
        P.dma("sp", muT[:, n, :], W["rwkv_mu"][0, n].rearrange("(k p) -> p k", p=128),
              writes=[muT.b], allow_slow_non_contiguous=True)
    bc_row(kkb, W["rwkv_k_k"][0:1, :])
    bc_row(kab, W["rwkv_k_a"][0:1, :])
    bc_row(rkb, W["rwkv_r_k"][0].rearrange("h j -> (h j)").unsqueeze(0))
    TB = 256
    NTL = TB // 128
    hTb = kb.sb("rw_hT", [128, 8, 1 + TB], BF16)
    dT = kb.sb("rw_dT", [128, 8, TB], BF16)
    xsT = kb.sb("rw_xsT", [128, 8, TB], BF16)
    Rt = kb.sb("rw_R", [128, NTL, 1024], F32)
    pad_ = kb.sb("rw_pad", [128, 256], F32)
    Kt = kb.sb("rw_K", [128, NTL, 1024], F32)
    Vt = kb.sb("rw_V", [128, NTL, 1024], F32)
    SGt = kb.sb("rw_SG", [128, NTL, 1024], F32)
    Wdt = kb.sb("rw_Wd", [128, 1024], F32)
    Ast = kb.sb("rw_As", [128, 1024], F32)
    t1e = [kb.sb("rw_t1e", [65, TB], BF16) for _ in range(2)]
    tm1 = kb.sb("rw_tm1", [128, 1024], F32)
    tm2 = kb.sb("rw_tm2", [128, 1024], F32)
    sm = kb.sb("rw_sm", [128, 32], F32)
    for i in range(2):
        P.op("dve", lambda e, i=i: e.memset(t1e[i][:], 1.0), writes=[t1e[i].b])
    hd = lambda ap: ap.rearrange("p (h j) -> p h j", j=64)
    bcj = lambda ap: ap.unsqueeze(2).to_broadcast([128, 16, 64])
    for b in range(NB):
        P.op("dve", lambda e: e.memset(hTb[:, :, 0:1], 0.0), writes=[hTb.b])
        for tb in range(S // TB):
            for tl in range(NTL):
                front_tile(kb, g, b, tb * NTL + tl, x_src, _Shift(hTb, tb * TB - 1), hTb.b)
            P.op("dve", lambda e: e.tensor_tensor(dT[:], hTb[:, :, 0:TB], hTb[:, :, 1:TB + 1], ALU.subtract),
                 reads=[hTb.b], writes=[dT.b])

            def make_xs(n):
                for c in range(8):
                    P.op("dve", lambda e, c=c, n=n: e.scalar_tensor_tensor(
                        xsT[:, c, :], dT[:, c, :], muT[:, n, c:c + 1], hTb[:, c, 1:TB + 1], ALU.mult, ALU.add),
                        reads=[dT.b, muT.b, hTb.b], writes=[xsT.b])
            for n, dst in ((0, Rt), (1, Kt), (2, Vt), (3, SGt)):
                make_xs(n)
                for tl in range(NTL):
                    for half in range(2):
                        pt = ps[2 + half + 2 * (n % 2)]
                        for c in range(8):
                            P.op("pe", lambda e, pt=pt, c=c, tl=tl, half=half, n=n: e.matmul(
                                pt[:], xsT[:, c, tl * 128:(tl + 1) * 128], win[:, n, c, half * 512:(half + 1) * 512],
                                start=(c == 0), stop=(c == 7)),
                                reads=[xsT.b, win.b], writes=[pt.b])
                        if n == 3:
                            P.op("act", lambda e, pt=pt, tl=tl, half=half, dst=dst: e.activation(
                                dst[:, tl, half * 512:(half + 1) * 512], pt[:], AF.Silu),
                                reads=[pt.b], writes=[dst.b])
                        else:
                            P.op("act", lambda e, pt=pt, tl=tl, half=half, dst=dst: e.copy(
                                dst[:, tl, half * 512:(half + 1) * 512], pt[:]),
                                reads=[pt.b], writes=[dst.b])
            for i, n in ((0, 4), (1, 5)):
                make_xs(n)
                for c in range(8):
                    P.op("pe", lambda e, c=c, i=i: e.matmul(
                        ps[4][0:64, 0:TB], w1a1[:, c, i * 64:(i + 1) * 64], xsT[:, c, :],
                        start=(c == 0), stop=(c == 7)),
                        reads=[w1a1.b, xsT.b], writes=[ps[4].b], pos="T0")
                if i == 0:
                    P.op("act", lambda e, i=i: e.activation(t1e[i][0:64, :], ps[4][0:64, 0:TB], AF.Tanh),
                         reads=[ps[4].b], writes=[t1e[i].b])
                else:
                    P.op("act", lambda e, i=i: e.copy(t1e[i][0:64, :], ps[4][0:64, 0:TB]),
                         reads=[ps[4].b], writes=[t1e[i].b])
            for tl in range(NTL):
                tt = tb * NTL + tl
                R_, K_, V_ = Rt[:, tl, :], Kt[:, tl, :], Vt[:, tl, :]
                for i, dst in ((0, Wdt), (1, Ast)):
                    for half in range(2):
                        pt = ps[2 + half]
                        P.op("pe", lambda e, pt=pt, i=i, tl=tl, half=half: e.matmul(
                            pt[:], t1e[i][:, tl * 128:(tl + 1) * 128], w2e[i][:, half * 512:(half + 1) * 512],
                            start=True, stop=True),
                            reads=[t1e[i].b, w2e[i].b], writes=[pt.b], pos="K65")
                        P.op("act", lambda e, pt=pt, half=half, dst=dst: e.activation(
                            dst[:, half * 512:(half + 1) * 512], pt[:], AF.Sigmoid),
                            reads=[pt.b], writes=[dst.b])
                P.op("act", lambda e: e.activation(Wdt[:], Wdt[:], AF.Exp, scale=-0.6065306597126334),
                     reads=[Wdt.b], writes=[Wdt.b])
                if DEBUG_OUT:
                    P.dma("sp", scr["D1"].t[b, tt * 128:(tt + 1) * 128, :], K_, reads=[Kt.b], writes=[scr["D1"].b])
                    P.dma("sp", scr["D2"].t[b, tt * 128:(tt + 1) * 128, :], Ast[:], reads=[Ast.b], writes=[scr["D2"].b])
                    P.dma("sp", scr["D3"].t[b, tt * 128:(tt + 1) * 128, :], kkb[:], reads=[kkb.b], writes=[scr["D3"].b])
                P.op("dve", lambda e, K_=K_: e.tensor_tensor(tm1[:], K_, kkb[:], ALU.mult),
                     reads=[Kt.b, kkb.b], writes=[tm1.b])
                P.op("act", lambda e: e.activation(tm2[:], tm1[:], AF.Square), reads=[tm1.b], writes=[tm2.b])
                P.op("dve", lambda e: e.tensor_reduce(sm[:, 0:16], hd(tm2[:]), AX.X, ALU.add),
                     reads=[tm2.b], writes=[sm.b])
                P.op("act", lambda e: e.activation(sm[:, 0:16], sm[:, 0:16], AF.Sqrt), reads=[sm.b], writes=[sm.b])
                P.op("dve", lambda e: e.tensor_scalar(sm[:, 0:16], sm[:, 0:16], 1e-12, None, ALU.max),
                     reads=[sm.b], writes=[sm.b])
                P.op("dve", lambda e: e.reciprocal(sm[:, 0:16], sm[:, 0:16]), reads=[sm.b], writes=[sm.b])
                P.op("dve", lambda e: e.tensor_tensor(hd(tm1[:]), hd(tm1[:]), bcj(sm[:, 0:16]), ALU.mult),
                     reads=[tm1.b, sm.b], writes=[tm1.b])
                P.op("pool", lambda e: e.tensor_tensor(tm2[:], tm1[:], Ast[:], ALU.mult),
                     reads=[tm1.b, Ast.b], writes=[tm2.b])
                P.dma("sp", tok_view("B", b, tt), hd(tm2[:]), reads=[tm2.b], writes=[scr["B"].b])
                P.op("dve", lambda e: e.tensor_scalar(tm1[:], tm1[:], -1.0, None, ALU.mult),
                     reads=[tm1.b], writes=[tm1.b])
                P.dma("sp", tok_view("A", b, tt), hd(tm1[:]), reads=[tm1.b], writes=[scr["A"].b])
                P.dma("act", tok_view("Wd", b, tt), hd(Wdt[:]), reads=[Wdt.b], writes=[scr["Wd"].b])
                P.dma("act", tok_view("R", b, tt), hd(R_), reads=[Rt.b], writes=[scr["R"].b])
                P.dma("act", tok_view("V", b, tt), hd(V_), reads=[Vt.b], writes=[scr["V"].b])
                P.op("dve", lambda e: e.scalar_tensor_tensor(tm2[:], Ast[:], -1.0, kab[:], ALU.add, ALU.mult),
                     reads=[Ast.b, kab.b, tm2.b], writes=[tm2.b])
                P.op("dve", lambda e, K_=K_: e.scalar_tensor_tensor(tm2[:], tm2[:], 1.0, K_, ALU.add, ALU.mult),
                     reads=[tm2.b, Kt.b], writes=[tm2.b])
                P.dma("sp", tok_view("K", b, tt), hd(tm2[:]), reads=[tm2.b], writes=[scr["K"].b])
                P.op("pool", lambda e, R_=R_: e.tensor_tensor(tm1[:], tm2[:], R_, ALU.mult),
                     reads=[tm2.b, Rt.b, tm1.b], writes=[tm1.b])
                P.op("pool", lambda e: e.tensor_tensor(tm1[:], tm1[:], rkb[:], ALU.mult),
                     reads=[tm1.b, rkb.b], writes=[tm1.b])
                P.op("dve", lambda e: e.tensor_reduce(sm[:, 16:32], hd(tm1[:]), AX.X, ALU.add),
                     reads=[tm1.b], writes=[sm.b])
                P.op("dve", lambda e, V_=V_: e.tensor_tensor(hd(tm1[:]), hd(V_), bcj(sm[:, 16:32]), ALU.mult),
                     reads=[Vt.b, sm.b, tm1.b], writes=[tm1.b])
                P.dma("sp", scr["BON"].t[b, tt * 128:(tt + 1) * 128, :], tm1[:], reads=[tm1.b], writes=[scr["BON"].b])
                P.dma("act", scr["SG"].t[b, tt * 128:(tt + 1) * 128, :], SGt[:, tl, :], reads=[SGt.b], writes=[scr["SG"].b])
            P.op("dve", lambda e: e.tensor_copy(hTb[:, :, 0:1], hTb[:, :, TB:TB + 1]),
                 reads=[hTb.b], writes=[hTb.b])
    kb.pop()

    kb.push()
    TC = 32
    blk = {n: [kb.sb("rb_" + n, [128, TC, 64], F32) for _ in range(2)] for n in ("Wd", "A", "B", "K", "R")}
    Vb = [kb.sb("rb_V", [128, TC, 16], F32) for _ in range(2)]
    Yb = [kb.sb("rb_Y", [128, TC, 16], F32) for _ in range(2)]
    St = [kb.sb("rb_S", [128, 16, 64], F32) for _ in range(2)]
    t1 = kb.sb("rb_t1", [128, 16, 64], F32)
    t2 = kb.sb("rb_t2", [128, 16, 64], F32)
    t3 = kb.sb("rb_t3", [128, 16, 64], F32)
    t4 = kb.sb("rb_t4", [128, 16, 64], F32)
    sa = kb.sb("rb_sa", [128, 16], F32)
    P.op("dve", lambda e: e.memset(St[0][:], 0.0), writes=[St[0].b])
    qs = ("sp", "act")
    step = 0
    for tbk in range(S // TC):
        t0 = tbk * TC
        i2 = tbk % 2
        nq = 0
        for n in ("Wd", "A", "B", "K", "R"):
            src = scr[n].t.rearrange("b h t j -> (b h) t j")[:, t0:t0 + TC, :]
            for iq in range(4):
                P.dma(qs[nq % 2], blk[n][i2][iq * 32:(iq + 1) * 32, :, :], src,
                      reads=[scr[n].b], writes=[blk[n][i2].b])
                nq += 1
        for iq in range(4):
            src = scr["V"].t.rearrange("b h t j -> (b h) t j")[:, t0:t0 + TC, iq * 16:(iq + 1) * 16]
            P.dma(qs[nq % 2], Vb[i2][iq * 32:(iq + 1) * 32, :, :], src, reads=[scr["V"].b], writes=[Vb[i2].b])
            nq += 1
        for t in range(TC):
            So, Sn = St[step % 2], St[(step + 1) % 2]
            step += 1
            bcr = lambda n, t=t: blk[n][i2][:, t, :].unsqueeze(1).to_broadcast([128, 16, 64])
            a_bc, w_bc, b_bc, k_bc, r_bc = bcr("A"), bcr("Wd"), bcr("B"), bcr("K"), bcr("R")
            v_bc = Vb[i2][:, t, :].unsqueeze(2).to_broadcast([128, 16, 64])
            P.op("dve", lambda e, So=So, a_bc=a_bc: e.tensor_tensor(t1[:], So[:], a_bc, ALU.mult),
                 reads=[So.b, blk["A"][i2].b], writes=[t1.b])
            P.op("dve", lambda e: e.tensor_reduce(sa[:], t1[:], AX.X, ALU.add), reads=[t1.b], writes=[sa.b])
            P.op("pool", lambda e, So=So, w_bc=w_bc: e.tensor_tensor(t2[:], So[:], w_bc, ALU.mult),
                 reads=[So.b, blk["Wd"][i2].b], writes=[t2.b])
            P.op("pool", lambda e, v_bc=v_bc, k_bc=k_bc: e.tensor_tensor(t3[:], v_bc, k_bc, ALU.mult),
                 reads=[Vb[i2].b, blk["K"][i2].b], writes=[t3.b])
            P.op("dve", lambda e, b_bc=b_bc: e.tensor_tensor(
                t1[:], sa[:].unsqueeze(2).to_broadcast([128, 16, 64]), b_bc, ALU.mult),
                reads=[sa.b, blk["B"][i2].b], writes=[t1.b])
            P.op("dve", lambda e: e.tensor_tensor(t1[:], t1[:], t3[:], ALU.add),
                 reads=[t1.b, t3.b], writes=[t1.b])
            P.op("dve", lambda e, Sn=Sn: e.tensor_tensor(Sn[:], t1[:], t2[:], ALU.add),
                 reads=[t1.b, t2.b], writes=[Sn.b])
            P.op("pool", lambda e, Sn=Sn, r_bc=r_bc: e.tensor_tensor(t4[:], Sn[:], r_bc, ALU.mult),
                 reads=[Sn.b, blk["R"][i2].b], writes=[t4.b])
            y_ap = Yb[i2][:, t, :]
            P.op("dve", lambda e, y_ap=y_ap: e.tensor_reduce(y_ap, t4[:], AX.X, ALU.add),
                 reads=[t4.b], writes=[Yb[i2].b])
        for iq in range(4):
            dst = scr["Y"].t.rearrange("b h t j -> (b h) t j")[:, t0:t0 + TC, iq * 16:(iq + 1) * 16]
            P.dma(qs[iq % 2], dst, Yb[i2][iq * 32:(iq + 1) * 32, :, :], reads=[Yb[i2].b], writes=[scr["Y"].b])
    kb.pop()

    kb.push()
    lnw = kb.sb("rw_lnw", [128, 1024], F32)
    lnb = kb.sb("rw_lnb", [128, 1024], F32)
    bc_row(lnw, W["rwkv_ln_w"][0:1, :])
    bc_row(lnb, W["rwkv_ln_b"][0:1, :])
    yt = kb.sb("rc_y", [128, 1024], F32)
    y2 = kb.sb("rc_y2", [128, 1024], F32)
    bon = kb.sb("rc_bon", [128, 1024], F32)
    sgc = kb.sb("rc_sg", [128, 1024], F32)
    smc = kb.sb("rc_sm", [128, 32], F32)
    zTt = kb.sb("rc_zT", [128, 8, 128], BF16)
    for b in range(NB):
        for tt in range(NT):
            P.dma("sp", hd(yt[:]), tok_view("Y", b, tt), reads=[scr["Y"].b], writes=[yt.b])
            P.dma("act", bon[:], scr["BON"].t[b, tt * 128:(tt + 1) * 128, :], reads=[scr["BON"].b], writes=[bon.b])
            P.dma("act", sgc[:], scr["SG"].t[b, tt * 128:(tt + 1) * 128, :], reads=[scr["SG"].b], writes=[sgc.b])
            P.op("dve", lambda e: e.tensor_reduce(smc[:, 0:16], hd(yt[:]), AX.X, ALU.add), reads=[yt.b], writes=[smc.b])
            P.op("dve", lambda e: e.tensor_scalar(smc[:, 0:16], smc[:, 0:16], -1.0 / 64.0, None, ALU.mult),
                 reads=[smc.b], writes=[smc.b])
            P.op("dve", lambda e: e.tensor_tensor(hd(yt[:]), hd(yt[:]), bcj(smc[:, 0:16]), ALU.add),
                 reads=[yt.b, smc.b], writes=[yt.b])
            P.op("act", lambda e: e.activation(y2[:], yt[:], AF.Square), reads=[yt.b], writes=[y2.b])
            P.op("dve", lambda e: e.tensor_reduce(smc[:, 16:32], hd(y2[:]), AX.X, ALU.add), reads=[y2.b], writes=[smc.b])
            P.op("dve", lambda e: e.tensor_scalar(smc[:, 16:32], smc[:, 16:32], 1.0 / 64.0, 64e-5, ALU.mult, ALU.add),
                 reads=[smc.b], writes=[smc.b])
            P.op("act", lambda e: e.activation(smc[:, 16:32], smc[:, 16:32], AF.Sqrt), reads=[smc.b], writes=[smc.b])
            P.op("dve", lambda e: e.reciprocal(smc[:, 16:32], smc[:, 16:32]), reads=[smc.b], writes=[smc.b])
            P.op("dve", lambda e: e.tensor_tensor(hd(yt[:]), hd(yt[:]), bcj(smc[:, 16:32]), ALU.mult),
                 reads=[yt.b, smc.b], writes=[yt.b])
            P.op("pool", lambda e: e.tensor_tensor(yt[:], yt[:], lnw[:], ALU.mult), reads=[yt.b, lnw.b], writes=[yt.b])
            P.op("pool", lambda e: e.tensor_tensor(yt[:], yt[:], lnb[:], ALU.add), reads=[yt.b, lnb.b], writes=[yt.b])
            P.op("dve", lambda e: e.tensor_tensor(yt[:], yt[:], bon[:], ALU.add), reads=[yt.b, bon.b], writes=[yt.b])
            P.op("dve", lambda e: e.tensor_tensor(yt[:], yt[:], sgc[:], ALU.mult), reads=[yt.b, sgc.b], writes=[yt.b])
            for c in range(8):
                pt = ps[4 + c // 4]
                P.op("pe", lambda e, pt=pt, c=c: e.transpose(
                    pt[:, (c % 4) * 128:(c % 4 + 1) * 128], yt[:, c * 128:(c + 1) * 128], ident[:]),
                    reads=[yt.b, ident.b], writes=[pt.b])
            P.op("act", lambda e: e.copy(zTt[:, 0:4, :], ps[4][:].rearrange("p (c t) -> p c t", t=128)),
                 reads=[ps[4].b], writes=[zTt.b])
            P.op("dve", lambda e: e.tensor_copy(zTt[:, 4:8, :], ps[5][:].rearrange("p (c t) -> p c t", t=128)),
                 reads=[ps[5].b], writes=[zTt.b])
            back_tile(kb, g, b, tt, _Shift(zTt, tt * 128), [zTt.b], wout, x_src, x_dst)
    kb.pop()
    kb.pop()


def layer_rwkv2(kb, g, W, x_src, x_dst):
    nc, P = kb.nc, kb.P
    ps = g["ps"]
    ident = g["ident"]
    kb.push()
    wout = kb.sb("rw_wout", [128, 8, 1024], BF16)
    names = ["R", "Wd", "K", "A", "B", "V", "Y"]
    if "rw_scr" not in g:
        g["rw_scr"] = {n: kb.dram("rw_" + n, [NB, S, 1024]) for n in names}
        g["rw_scr"]["SG"] = kb.dram("rw_SG", [NB, S, 1024])
        if DEBUG_OUT:
            for nm in ("D1", "D2", "D3"):
                g["rw_scr"][nm] = kb.dram("rw_" + nm, [NB, S, 1024])
        g["rw_scr"]["BON"] = kb.dram("rw_BON", [NB, S, 1024])
    scr = g["rw_scr"]

    def tok_view(n, b, tt):
        return scr[n].t[b, tt * 128:(tt + 1) * 128, :]

    def bc_row(dst, src_row):
        P.dma("sp", dst[:], src_row.to_broadcast([128, 1024]), writes=[dst.b])

    kb.push()
    win = kb.sb("rw_win", [128, 4, 8, 1024], BF16)
    w1a1 = kb.sb("rw_w1a1", [128, 8, 128], BF16)
    w2e = [kb.sb("rw_w2e", [65, 1024], BF16) for _ in range(2)]
    muT = kb.sb("rw_mu", [128, 6, 8], F32)
    kkb = kb.sb("rw_kk", [128, 1024], F32)
    kab = kb.sb("rw_ka", [128, 1024], F32)
    rkb = kb.sb("rw_rk", [128, 1024], F32)
    begin_load(kb, g)
    compute_mod(kb, g, 2, W)
    for n in range(4):
        load_w(kb, g, _W4(win, n), 0, W["rwkv_w_in"][0, n], 1024)
    load_w(kb, g, wout, 0, W["rwkv_w_out"][0], 1024)
    load_w(kb, g, w1a1, 0, W["rwkv_w1"][0], 64)
    load_w(kb, g, w1a1, 64, W["rwkv_a1"][0], 64)
    for i, (m2_, m0_) in enumerate((("rwkv_w2", "rwkv_w0"), ("rwkv_a2", "rwkv_a0"))):
        stg = g["stage"][g["stage_n"] % 2]
        g["stage_n"] += 1
        sv = stg[:].rearrange("p k n -> p (k n)")
        P.dma("sp", sv[0:64, 0:1024], W[m2_][0], writes=[stg.b])
        P.dma("sp", sv[64:65, 0:1024], W[m0_][0:1, :], writes=[stg.b])
        P.op("dve", lambda e, i=i, sv=sv: e.tensor_copy(w2e[i][:], sv[0:65, 0:1024]),
             reads=[stg.b], writes=[w2e[i].b])
    end_load(kb, g)
    for n in range(6):
        P.dma("sp", muT[:, n, :], W["rwkv_mu"][0, n].rearrange("(k p) -> p k", p=128),
              writes=[muT.b], allow_slow_non_contiguous=True)
    bc_row(kkb, W["rwkv_k_k"][0:1, :])
    bc_row(kab, W["rwkv_k_a"][0:1, :])
    bc_row(rkb, W["rwkv_r_k"][0].rearrange("h j -> (h j)").unsqueeze(0))
    TB = 256
    NTL = TB // 128
    hTb = kb.sb("rw_hT", [128, 8, 1 + TB], BF16)
    dT = kb.sb("rw_dT", [128, 8, TB], BF16)
    xsTs = [kb.sb("rw_xsT", [128, 8, TB], BF16) for _ in range(2)]
    Rt = kb.sb("rw_R", [128, NTL, 1024], F32)
    pad_ = kb.sb("rw_pad", [128, 256], F32)
    Kt = kb.sb("rw_K", [128, NTL, 1024], F32)
    Vt = kb.sb("rw_V", [128, NTL, 1024], F32)
    SGt = kb.sb("rw_SG", [128, NTL, 1024], F32)
    Wdt = kb.sb("rw_Wd", [128, 1024], F32)
    Ast = kb.sb("rw_As", [128, 1024], F32)
    t1e = [kb.sb("rw_t1e", [65, TB], BF16) for _ in range(2)]
    tm1 = kb.sb("rw_tm1", [128, 1024], F32)
    tm2 = kb.sb("rw_tm2", [128, 1024], F32)
    sm = kb.sb("rw_sm", [128, 32], F32)
    for i in range(2):
        P.op("dve", lambda e, i=i: e.memset(t1e[i][:], 1.0), writes=[t1e[i].b])
    hd = lambda ap: ap.rearrange("p (h j) -> p h j", j=64)
    bcj = lambda ap: ap.unsqueeze(2).to_broadcast([128, 16, 64])
    for b in range(NB):
        P.op("dve", lambda e: e.memset(hTb[:, :, 0:1], 0.0), writes=[hTb.b])
        for tb in range(S // TB):
            for tl in range(NTL):
                front_tile(kb, g, b, tb * NTL + tl, x_src, _Shift(hTb, tb * TB - 1), hTb.b)
            P.op("dve", lambda e: e.tensor_tensor(dT[:], hTb[:, :, 0:TB], hTb[:, :, 1:TB + 1], ALU.subtract),
                 reads=[hTb.b], writes=[dT.b])

            def make_xs(n):
                xsT = xsTs[n % 2]
                for c in range(8):
                    P.op("dve", lambda e, c=c, n=n: e.scalar_tensor_tensor(
                        xsT[:, c, :], dT[:, c, :], muT[:, n, c:c + 1], hTb[:, c, 1:TB + 1], ALU.mult, ALU.add),
                        reads=[dT.b, muT.b, hTb.b], writes=[xsT.b])
            for n, dst in ((0, Rt), (1, Kt), (2, Vt), (3, SGt)):
                make_xs(n)
                xsT = xsTs[n % 2]
                for tl in range(NTL):
                    for half in range(2):
                        pt = ps[2 + half + 2 * (n % 2)]
                        for c in range(8):
                            P.op("pe", lambda e, pt=pt, c=c, tl=tl, half=half, n=n, xsT=xsT: e.matmul(
                                pt[:], xsT[:, c, tl * 128:(tl + 1) * 128], win[:, n, c, half * 512:(half + 1) * 512],
                                start=(c == 0), stop=(c == 7)),
                                reads=[xsT.b, win.b], writes=[pt.b])
                        if n == 3:
                            P.op("act", lambda e, pt=pt, tl=tl, half=half, dst=dst: e.activation(
                                dst[:, tl, half * 512:(half + 1) * 512], pt[:], AF.Silu),
                                reads=[pt.b], writes=[dst.b])
                        else:
                            P.op("act", lambda e, pt=pt, tl=tl, half=half, dst=dst: e.copy(
                                dst[:, tl, half * 512:(half + 1) * 512], pt[:]),
                                reads=[pt.b], writes=[dst.b])
            for i, n in ((0, 4), (1, 5)):
                make_xs(n)
                xsT = xsTs[n % 2]
                for c in range(8):
                    P.op("pe", lambda e, c=c, i=i, xsT=xsT: e.matmul(
                        ps[4][0:64, 0:TB], w1a1[:, c, i * 64:(i + 1) * 64], xsT[:, c, :],
                        start=(c == 0), stop=(c == 7)),
                        reads=[w1a1.b, xsT.b], writes=[ps[4].b], pos="T0")
                if i == 0:
                    P.op("act", lambda e, i=i: e.activation(t1e[i][0:64, :], ps[4][0:64, 0:TB], AF.Tanh),
                         reads=[ps[4].b], writes=[t1e[i].b])
                else:
                    P.op("act", lambda e, i=i: e.copy(t1e[i][0:64, :], ps[4][0:64, 0:TB]),
                         reads=[ps[4].b], writes=[t1e[i].b])
            for tl in range(NTL):
                tt = tb * NTL + tl
                R_, K_, V_ = Rt[:, tl, :], Kt[:, tl, :], Vt[:, tl, :]
                for i, dst in ((0, Wdt), (1, Ast)):
                    for half in range(2):
                        pt = ps[2 + half]
                        P.op("pe", lambda e, pt=pt, i=i, tl=tl, half=half: e.matmul(
                            pt[:], t1e[i][:, tl * 128:(tl + 1) * 128], w2e[i][:, half * 512:(half + 1) * 512],
                            start=True, stop=True),
                            reads=[t1e[i].b, w2e[i].b], writes=[pt.b], pos="K65")
                        P.op("act", lambda e, pt=pt, half=half, dst=dst: e.activation(
                            dst[:, half * 512:(half + 1) * 512], pt[:], AF.Sigmoid),
                            reads=[pt.b], writes=[dst.b])
                P.op("dve", lambda e: e.tensor_scalar(Wdt[:], Wdt[:], -0.6065306597126334, None, ALU.mult),
                     reads=[Wdt.b], writes=[Wdt.b])
                if DEBUG_OUT:
                    P.dma("sp", scr["D1"].t[b, tt * 128:(tt + 1) * 128, :], K_, reads=[Kt.b], writes=[scr["D1"].b])
                    P.dma("sp", scr["D2"].t[b, tt * 128:(tt + 1) * 128, :], Ast[:], reads=[Ast.b], writes=[scr["D2"].b])
                    P.dma("sp", scr["D3"].t[b, tt * 128:(tt + 1) * 128, :], kkb[:], reads=[kkb.b], writes=[scr["D3"].b])
                P.op("dve", lambda e, K_=K_: e.tensor_tensor(tm1[:], K_, kkb[:], ALU.mult),
                     reads=[Kt.b, kkb.b], writes=[tm1.b])
                P.op("act", lambda e: e.activation(tm2[:], tm1[:], AF.Square), reads=[tm1.b], writes=[tm2.b])
                P.op("dve", lambda e: e.tensor_reduce(sm[:, 0:16], hd(tm2[:]), AX.X, ALU.add),
                     reads=[tm2.b], writes=[sm.b])
                P.op("act", lambda e: e.activation(sm[:, 0:16], sm[:, 0:16], AF.Sqrt), reads=[sm.b], writes=[sm.b])
                P.op("dve", lambda e: e.tensor_scalar(sm[:, 0:16], sm[:, 0:16], 1e-12, None, ALU.max),
                     reads=[sm.b], writes=[sm.b])
                P.op("dve", lambda e: e.reciprocal(sm[:, 0:16], sm[:, 0:16]), reads=[sm.b], writes=[sm.b])
                P.op("dve", lambda e: e.tensor_tensor(hd(tm1[:]), hd(tm1[:]), bcj(sm[:, 0:16]), ALU.mult),
                     reads=[tm1.b, sm.b], writes=[tm1.b])
                P.op("pool", lambda e: e.tensor_tensor(tm2[:], tm1[:], Ast[:], ALU.mult),
                     reads=[tm1.b, Ast.b], writes=[tm2.b])
                P.dma("sp", tok_view("B", b, tt), tm2[:], reads=[tm2.b], writes=[scr["B"].b])
                P.op("dve", lambda e: e.tensor_scalar(tm1[:], tm1[:], -1.0, None, ALU.mult),
                     reads=[tm1.b], writes=[tm1.b])
                P.dma("sp", tok_view("A", b, tt), tm1[:], reads=[tm1.b], writes=[scr["A"].b])
                P.dma("act", tok_view("Wd", b, tt), Wdt[:], reads=[Wdt.b], writes=[scr["Wd"].b])
                P.dma("act", tok_view("R", b, tt), R_, reads=[Rt.b], writes=[scr["R"].b])
                P.dma("act", tok_view("V", b, tt), V_, reads=[Vt.b], writes=[scr["V"].b])
                P.op("dve", lambda e: e.scalar_tensor_tensor(tm2[:], Ast[:], -1.0, kab[:], ALU.add, ALU.mult),
                     reads=[Ast.b, kab.b, tm2.b], writes=[tm2.b])
                P.op("dve", lambda e, K_=K_: e.scalar_tensor_tensor(tm2[:], tm2[:], 1.0, K_, ALU.add, ALU.mult),
                     reads=[tm2.b, Kt.b], writes=[tm2.b])
                P.dma("sp", tok_view("K", b, tt), tm2[:], reads=[tm2.b], writes=[scr["K"].b])
                P.op("pool", lambda e, R_=R_: e.tensor_tensor(tm1[:], tm2[:], R_, ALU.mult),
                     reads=[tm2.b, Rt.b, tm1.b], writes=[tm1.b])
                P.op("pool", lambda e: e.tensor_tensor(tm1[:], tm1[:], rkb[:], ALU.mult),
                     reads=[tm1.b, rkb.b], writes=[tm1.b])
                P.op("dve", lambda e: e.tensor_reduce(sm[:, 16:32], hd(tm1[:]), AX.X, ALU.add),
                     reads=[tm1.b], writes=[sm.b])
                P.op("dve", lambda e, V_=V_: e.tensor_tensor(hd(tm1[:]), hd(V_), bcj(sm[:, 16:32]), ALU.mult),
                     reads=[Vt.b, sm.b, tm1.b], writes=[tm1.b])
                P.dma("sp", scr["BON"].t[b, tt * 128:(tt + 1) * 128, :], tm1[:], reads=[tm1.b], writes=[scr["BON"].b])
                P.dma("act", scr["SG"].t[b, tt * 128:(tt + 1) * 128, :], SGt[:, tl, :], reads=[SGt.b], writes=[scr["SG"].b])
            P.op("dve", lambda e: e.tensor_copy(hTb[:, :, 0:1], hTb[:, :, TB:TB + 1]),
                 reads=[hTb.b], writes=[hTb.b])
    kb.pop()

    kb.push()
    C_ = g["consts"]
    psr = [kb.psum("rps%d" % i, [128, 512], F32) for i in range(1)] + g["ps"]
    PU_, PS_ = psr[0], psr[1]
    PY_ = PU_
    PI = psr[2:7]
    bank_ctr = [0]

    def nextbank():
        bank_ctr[0] += 1
        return PI[bank_ctr[0] % len(PI)]
    PTb = kb.psum("rpsb", [128, 1024], BF16)
    identb = g["identb"]
    cm = {}
    for nm, shp in (("rw_M1", [128, 128]), ("rw_M1s", [128, 128]), ("rw_M2", [128, 128]),
                    ("rw_mask4", [128, 512]), ("rw_maskL2", [128, 256]), ("rw_cind", [128, 2])):
        cm[nm] = kb.sb(nm, shp, F32)
        P.dma("sp", cm[nm][:], C_[nm][:, :], writes=[cm[nm].b])
    NU = 2
    per = [dict(
        Vv=kb.sb("c_V", [128, 512], BF16), Bg=kb.sb("c_Bg", [128, 512], BF16), Kg=kb.sb("c_Kg", [128, 512], BF16),
        ARt=kb.sb("c_ARt", [128, 4, 2, 128], BF16), Nall=kb.sb("c_Nall", [128, 8, 4, 128], BF16),
        Qall=kb.sb("c_Q", [128, 4, 128], BF16), Pall=kb.sb("c_P", [128, 8, 128], BF16),
        gC=kb.sb("c_gC", [128, 4, 2], F32), Yt=kb.sb("c_Yt", [128, 512], F32)) for _ in range(NU)]
    Rr = kb.sb("c_R", [128, 512], F32)
    LWt = kb.sb("c_LW", [128, 512], F32)
    Aa = kb.sb("c_A", [128, 512], F32)
    Bf = kb.sb("c_Bf", [128, 512], F32)
    Kf = kb.sb("c_Kf", [128, 512], F32)
    Vf = kb.sb("c_Vf", [128, 512], F32)
    Rb = kb.sb("c_Rb", [128, 512], BF16)
    Ab = kb.sb("c_Ab", [128, 512], BF16)
    Ee = kb.sb("c_E", [128, 512], F32)
    Bt_ = kb.sb("c_Bt", [128, 512], BF16)
    Kt_ = kb.sb("c_Kt", [128, 512], BF16)
    BKt = kb.sb("c_BKt", [128, 4, 2, 128], BF16)
    NTt = kb.sb("c_NT", [128, 8, 2, 128], BF16)
    XX = kb.sb("c_XX", [128, 2, 8, 2, 128], BF16)
    Th = kb.sb("c_Th", [128, 8, 128], BF16)
    Usb = kb.sb("c_U", [128, 512], BF16)
    STb = kb.sb("c_STb", [128, 4, 64], BF16)
    tmpS = kb.sb("c_tmpS", [128, 4, 64], F32)
    STs = [[kb.sb("c_ST", [128, 4, 64], F32) for hg in range(2)] for b in range(NB)]
    for b in range(NB):
        for hg in range(2):
            P.op("pool", lambda e, t_=STs[b][hg]: e.memset(t_[:], 0.0), writes=[STs[b][hg].b])
    identf = ident

    def gen_pre(u, b, hg, tt):
        c = per[u % NU]
        cols = slice(hg * 512, (hg + 1) * 512)
        rows = slice(tt * 128, (tt + 1) * 128)
        ld = (("R", Rr), ("Wd", LWt), ("A", Aa), ("B", Bf), ("K", Kf), ("V", Vf))
        for i, (nm, dst) in enumerate(ld):
            P.dma(("sp", "act")[i % 2], dst[:], scr[nm].t[b, rows, cols], reads=[scr[nm].b], writes=[dst.b])
        pgA = nextbank()
        P.op("pe", lambda e: e.matmul(pgA[:], cm["rw_M1"][:], LWt[:], start=True, stop=True),
             reads=[cm["rw_M1"].b, LWt.b], writes=[pgA.b])
        P.op("act", lambda e: e.activation(Ee[:], pgA[:], AF.Exp), reads=[pgA.b], writes=[Ee.b])
        P.op("pool", lambda e: e.tensor_tensor(Rb[:], Rr[:], Ee[:], ALU.mult), reads=[Rr.b, Ee.b], writes=[Rb.b])
        P.op("pool", lambda e: e.tensor_copy(c["Vv"][:], Vf[:]), reads=[Vf.b], writes=[c["Vv"].b])
        P.op("act", lambda e: e.activation(Ee[:], pgA[:], AF.Exp, scale=-1.0), reads=[pgA.b, Ee.b], writes=[Ee.b])
        P.op("pool", lambda e: e.tensor_tensor(Bt_[:], Bf[:], Ee[:], ALU.mult),
             reads=[Bf.b, Ee.b], writes=[Bt_.b])
        P.op("dve", lambda e: e.tensor_tensor(Kt_[:], Kf[:], Ee[:], ALU.mult),
             reads=[Kf.b, Ee.b], writes=[Kt_.b])
        yield
        pgB = nextbank()
        P.op("pe", lambda e: e.matmul(pgB[:], cm["rw_M1s"][:], LWt[:], start=True, stop=True),
             reads=[cm["rw_M1s"].b, LWt.b], writes=[pgB.b])
        P.op("act", lambda e: e.activation(Ee[:], pgB[:], AF.Exp), reads=[pgB.b, Ee.b], writes=[Ee.b])
        P.op("pool", lambda e: e.tensor_tensor(Ab[:], Aa[:], Ee[:], ALU.mult), reads=[Aa.b, Ee.b], writes=[Ab.b])
        pg2 = nextbank()
        P.op("pe", lambda e: e.matmul(pg2[:], cm["rw_M2"][:], LWt[:], start=True, stop=True),
             reads=[cm["rw_M2"].b, LWt.b], writes=[pg2.b])
        P.op("act", lambda e: e.activation(Ee[:], pg2[:], AF.Exp), reads=[pg2.b, Ee.b], writes=[Ee.b])
        P.op("pool", lambda e: e.tensor_tensor(c["Bg"][:], Bf[:], Ee[:], ALU.mult),
             reads=[Bf.b, Ee.b], writes=[c["Bg"].b])
        P.op("dve", lambda e: e.tensor_tensor(c["Kg"][:], Kf[:], Ee[:], ALU.mult),
             reads=[Kf.b, Ee.b], writes=[c["Kg"].b])
        pg3 = nextbank()
        for m in range(4):
            P.op("pe", lambda e, m=m: e.matmul(pg3[:, m * 2:m * 2 + 2], LWt[:, m * 128:(m + 1) * 128],
                                               cm["rw_cind"][:], start=True, stop=True),
                 reads=[LWt.b, cm["rw_cind"].b], writes=[pg3.b])
        P.op("act", lambda e: e.activation(c["gC"][:], pg3[:, 0:8].rearrange("p (m c) -> p m c", c=2), AF.Exp),
             reads=[pg3.b], writes=[c["gC"].b])
        yield
        for (src0, src1, dstT) in ((Ab, Rb, c["ARt"]), (Bt_, Kt_, BKt)):
            for m in range(4):
                for j_, src in enumerate((src0, src1)):
                    P.op("pe", lambda e, j_=j_, src=src, m=m: e.transpose(
                        PTb[:, (m * 2 + j_) * 128:(m * 2 + j_ + 1) * 128], src[:, m * 128:(m + 1) * 128], identb[:]),
                        reads=[src.b, identb.b], writes=[PTb.b])
            d_ap = dstT[:].rearrange("p m a t -> p (m a t)")
            if dstT is BKt:
                P.op("act", lambda e, d_ap=d_ap: e.copy(d_ap, PTb[:]), reads=[PTb.b], writes=[dstT.b])
            else:
                P.op("dve", lambda e, d_ap=d_ap: e.tensor_copy(d_ap, PTb[:]), reads=[PTb.b], writes=[dstT.b])
            yield
        for hh in range(8):
            m, h2 = hh // 2, hh % 2
            hs = slice(h2 * 64, (h2 + 1) * 64)
            pg5 = nextbank()
            arv = c["ARt"][hs, m, :, :].rearrange("p a t -> p (a t)")
            bkv = BKt[hs, m, :, :].rearrange("p a t -> p (a t)")
            for j_ in range(2):
                P.op("pe", lambda e, pg5=pg5, j_=j_, hs=hs, m=m, arv=arv: e.matmul(
                    pg5[:, j_ * 256:(j_ + 1) * 256], BKt[hs, m, j_, :], arv, start=True, stop=True),
                    reads=[BKt.b, c["ARt"].b], writes=[pg5.b], pos=(1 if h2 else None))
            P.op("dve", lambda e, pg5=pg5, hh=hh: e.tensor_tensor(
                c["Nall"][:, hh, :, :].rearrange("p a t -> p (a t)"), pg5[:], cm["rw_mask4"][:], ALU.mult),
                reads=[pg5.b, cm["rw_mask4"].b], writes=[c["Nall"].b])
            pg6 = nextbank()
            P.op("pe", lambda e, pg6=pg6, hs=hs, m=m, bkv=bkv: e.matmul(
                pg6[:, 0:256], c["ARt"][hs, m, 0, :], bkv, start=True, stop=True),
                reads=[BKt.b, c["ARt"].b], writes=[pg6.b], pos=(1 if h2 else None))
            P.op("dve", lambda e, pg6=pg6, hh=hh: e.tensor_tensor(
                NTt[:, hh, :, :].rearrange("p a t -> p (a t)"), pg6[:, 0:256], cm["rw_maskL2"][:], ALU.mult),
                reads=[pg6.b, cm["rw_maskL2"].b], writes=[NTt.b])
            if hh % 2 == 1:
                yield
        P.op("pool", lambda e: e.tensor_tensor(Th[:], c["Nall"][:, :, 0, :],
                                               identb[:].unsqueeze(1).to_broadcast([128, 8, 128]), ALU.add),
             reads=[c["Nall"].b, identb.b], writes=[Th.b])
        for k in range(1, 6):
            pp = k % 2
            for pr in range(4):
                bank = nextbank()
                for hl in range(2):
                    hh = pr * 2 + hl
                    if k == 1:
                        Xp, Xtp = c["Nall"][:, hh, 0, :], NTt[:, hh, 0, :]
                        rd = [c["Nall"].b, NTt.b]
                    else:
                        Xp, Xtp = XX[:, 1 - pp, hh, 0, :], XX[:, 1 - pp, hh, 1, :]
                        rd = [XX.b]
                    P.op("pe", lambda e, bank=bank, hl=hl, Xp=Xp, Xtp=Xtp: e.matmul(
                        bank[:, hl * 256:hl * 256 + 128], Xtp, Xp, start=True, stop=True),
                        reads=rd, writes=[bank.b])
                    P.op("pe", lambda e, bank=bank, hl=hl, Xp=Xp, Xtp=Xtp: e.matmul(
                        bank[:, hl * 256 + 128:hl * 256 + 256], Xp, Xtp, start=True, stop=True),
                        reads=rd, writes=[bank.b])
                d_ap = XX[:, pp, pr * 2:pr * 2 + 2, :, :].rearrange("p h a t -> p (h a t)")
                P.op("act", lambda e, d_ap=d_ap, bank=bank: e.copy(d_ap, bank[:]), reads=[bank.b], writes=[XX.b])
            yield
            for pq_ in range(2):
                bank = nextbank()
                for hl in range(4):
                    hh = pq_ * 4 + hl
                    P.op("pe", lambda e, bank=bank, hl=hl, hh=hh, pp=pp: e.matmul(
                        bank[:, hl * 128:(hl + 1) * 128], XX[:, pp, hh, 1, :], Th[:, hh, :], start=True, stop=True),
                        reads=[XX.b, Th.b], writes=[bank.b])
                t_ap = Th[:, pq_ * 4:pq_ * 4 + 4, :].rearrange("p h t -> p (h t)")
                P.op("dve", lambda e, t_ap=t_ap, bank=bank: e.tensor_tensor(t_ap, bank[:], t_ap, ALU.add),
                     reads=[bank.b, Th.b], writes=[Th.b])
            yield
        pq = nextbank()
        for hh in range(8):
            m, h2 = hh // 2, hh % 2
            hs = slice(h2 * 64, (h2 + 1) * 64)
            P.op("pe", lambda e, hs=hs, m=m, hh=hh: e.matmul(
                pq[hs, m * 128:(m + 1) * 128], Ab[:, hh * 64:(hh + 1) * 64], Th[:, hh, :], start=True, stop=True),
                reads=[Ab.b, Th.b], writes=[pq.b], pos=(1 if h2 else None))
        P.op("act", lambda e: e.copy(c["Qall"][:].rearrange("p m t -> p (m t)"), pq[:]),
             reads=[pq.b], writes=[c["Qall"].b])
        for half in range(2):
            pp_ = nextbank()
            for hl in range(4):
                hh = half * 4 + hl
                P.op("pe", lambda e, pp_=pp_, hl=hl, hh=hh: e.matmul(
                    pp_[:, hl * 128:(hl + 1) * 128], NTt[:, hh, 1, :], Th[:, hh, :], start=True, stop=True),
                    reads=[NTt.b, Th.b], writes=[pp_.b])
            P.op("dve", lambda e, pp_=pp_, half=half: e.tensor_copy(
                c["Pall"][:, half * 4:half * 4 + 4, :].rearrange("p h t -> p (h t)"), pp_[:]),
                reads=[pp_.b], writes=[c["Pall"].b])
        yield

    def gen_seq(u, b, hg, tt):
        c = per[u % NU]
        ST = STs[b][hg]
        cols = slice(hg * 512, (hg + 1) * 512)
        rows = slice(tt * 128, (tt + 1) * 128)
        for cc in range(2):
            cs = slice(cc * 64, (cc + 1) * 64)
            P.op("pool", lambda e: e.tensor_copy(STb[:], ST[:]), reads=[ST.b], writes=[STb.b])
            for hh in range(8):
                m, h2 = hh // 2, hh % 2
                hs = slice(h2 * 64, (h2 + 1) * 64)
                hc = slice(hh * 64, (hh + 1) * 64)
                P.op("pe", lambda e, cs=cs, hs=hs, hc=hc, m=m, hh=hh: e.matmul(
                    PU_[cs, hc], c["Qall"][hs, m, cs], STb[hs, m, :], start=(hh == 0), stop=False),
                    reads=[c["Qall"].b, STb.b], writes=[PU_.b], pos=(1 if (h2 or cc) else None))
                P.op("pe", lambda e, cs=cs, hc=hc, hh=hh: e.matmul(
                    PU_[cs, hc], c["Pall"][cs, hh, cs], c["Vv"][cs, hc], start=False, stop=True),
                    reads=[c["Pall"].b, c["Vv"].b], writes=[PU_.b], pos=(1 if cc else None))
            P.op("act", lambda e, cs=cs: e.copy(Usb[cs, :], PU_[cs, :]), reads=[PU_.b], writes=[Usb.b])
            yield
            ocs = slice((1 - cc) * 64, (2 - cc) * 64)
            for hh in range(8):
                m, h2 = hh // 2, hh % 2
                hs = slice(h2 * 64, (h2 + 1) * 64)
                hc = slice(hh * 64, (hh + 1) * 64)
                P.op("pe", lambda e, cs=cs, ocs=ocs, hs=hs, hc=hc, m=m, hh=hh: e.matmul(
                    PY_[ocs, hc], c["ARt"][hs, m, 1, cs], STb[hs, m, :], start=(hh == 0), stop=False),
                    reads=[c["ARt"].b, STb.b], writes=[PY_.b], pos=1)
                P.op("pe", lambda e, cs=cs, ocs=ocs, hc=hc, hh=hh: e.matmul(
                    PY_[ocs, hc], c["Nall"][cs, hh, 1, cs], Usb[cs, hc], start=False, stop=False),
                    reads=[c["Nall"].b, Usb.b], writes=[PY_.b], pos=1)
                P.op("pe", lambda e, cs=cs, ocs=ocs, hc=hc, hh=hh: e.matmul(
                    PY_[ocs, hc], c["Nall"][cs, hh, 3, cs], c["Vv"][cs, hc], start=False, stop=True),
                    reads=[c["Nall"].b, c["Vv"].b], writes=[PY_.b], pos=1)
            P.op("dve", lambda e, ocs=ocs: e.tensor_copy(c["Yt"][ocs, :], PY_[ocs, :]), reads=[PY_.b], writes=[c["Yt"].b])
            for hh in range(8):
                m, h2 = hh // 2, hh % 2
                hs = slice(h2 * 64, (h2 + 1) * 64)
                hc = slice(hh * 64, (hh + 1) * 64)
                P.op("pe", lambda e, cs=cs, hs=hs, hc=hc, m=m, hh=hh: e.matmul(
                    PS_[hs, m * 64:(m + 1) * 64], c["Bg"][cs, hc], Usb[cs, hc], start=(hh < 2), stop=False),
                    reads=[c["Bg"].b, Usb.b], writes=[PS_.b], pos=(1 if (h2 or cc) else None))
                P.op("pe", lambda e, cs=cs, hs=hs, hc=hc, m=m, hh=hh: e.matmul(
                    PS_[hs, m * 64:(m + 1) * 64], c["Kg"][cs, hc], c["Vv"][cs, hc], start=False, stop=True),
                    reads=[c["Kg"].b, c["Vv"].b], writes=[PS_.b], pos=(1 if (h2 or cc) else None))
            P.op("pool", lambda e, cc=cc: e.tensor_tensor(
                tmpS[:], ST[:], c["gC"][:, :, cc].unsqueeze(2).to_broadcast([128, 4, 64]), ALU.mult),
                reads=[ST.b, c["gC"].b], writes=[tmpS.b])
            P.op("dve", lambda e: e.tensor_tensor(
                ST[:].rearrange("p m i -> p (m i)"), PS_[:, 0:256], tmpS[:].rearrange("p m i -> p (m i)"), ALU.add),
                reads=[PS_.b, tmpS.b], writes=[ST.b])
            yield
        P.dma("sp", scr["Y"].t[b, tt * 128:tt * 128 + 64, cols], c["Yt"][64:128, :],
              reads=[c["Yt"].b], writes=[scr["Y"].b])
        P.dma("act", scr["Y"].t[b, tt * 128 + 64:tt * 128 + 128, cols], c["Yt"][0:64, :],
              reads=[c["Yt"].b], writes=[scr["Y"].b])
        yield

    units = [(b, hg, tt) for tt in range(NT) for b in range(NB) for hg in range(2)]
    n_u = len(units)
    for u in range(n_u + 1):
        streams = []
        if u >= 1:
            streams.append(gen_seq(u - 1, *units[u - 1]))
        if u < n_u:
            streams.append(gen_pre(u, *units[u]))
        while streams:
            for s_ in list(streams):
                try:
                    next(s_)
                except StopIteration:
                    streams.remove(s_)
    kb.pop()

    kb.push()
    lnw = kb.sb("rw_lnw", [128, 1024], F32)
    lnb = kb.sb("rw_lnb", [128, 1024], F32)
    bc_row(lnw, W["rwkv_ln_w"][0:1, :])
    bc_row(lnb, W["rwkv_ln_b"][0:1, :])
    yt = kb.sb("rc_y", [128, 1024], F32)
    y2 = kb.sb("rc_y2", [128, 1024], F32)
    bon = kb.sb("rc_bon", [128, 1024], F32)
    sgc = kb.sb("rc_sg", [128, 1024], F32)
    smc = kb.sb("rc_sm", [128, 32], F32)
    zTt = kb.sb("rc_zT", [128, 8, 128], BF16)
    for b in range(NB):
        for tt in range(NT):
            P.dma("sp", yt[:], tok_view("Y", b, tt), reads=[scr["Y"].b], writes=[yt.b])
            P.dma("act", bon[:], scr["BON"].t[b, tt * 128:(tt + 1) * 128, :], reads=[scr["BON"].b], writes=[bon.b])
            P.dma("act", sgc[:], scr["SG"].t[b, tt * 128:(tt + 1) * 128, :], reads=[scr["SG"].b], writes=[sgc.b])
            P.op("dve", lambda e: e.tensor_reduce(smc[:, 0:16], hd(yt[:]), AX.X, ALU.add), reads=[yt.b], writes=[smc.b])
            P.op("dve", lambda e: e.tensor_scalar(smc[:, 0:16], smc[:, 0:16], -1.0 / 64.0, None, ALU.mult),
                 reads=[smc.b], writes=[smc.b])
            P.op("dve", lambda e: e.tensor_tensor(hd(yt[:]), hd(yt[:]), bcj(smc[:, 0:16]), ALU.add),
                 reads=[yt.b, smc.b], writes=[yt.b])
            P.op("act", lambda e: e.activation(y2[:], yt[:], AF.Square), reads=[yt.b], writes=[y2.b])
            P.op("dve", lambda e: e.tensor_reduce(smc[:, 16:32], hd(y2[:]), AX.X, ALU.add), reads=[y2.b], writes=[smc.b])
            P.op("dve", lambda e: e.tensor_scalar(smc[:, 16:32], smc[:, 16:32], 1.0 / 64.0, 64e-5, ALU.mult, ALU.add),
                 reads=[smc.b], writes=[smc.b])
            P.op("act", lambda e: e.activation(smc[:, 16:32], smc[:, 16:32], AF.Sqrt), reads=[smc.b], writes=[smc.b])
            P.op("dve", lambda e: e.reciprocal(smc[:, 16:32], smc[:, 16:32]), reads=[smc.b], writes=[smc.b])
            P.op("dve", lambda e: e.tensor_tensor(hd(yt[:]), hd(yt[:]), bcj(smc[:, 16:32]), ALU.mult),
                 reads=[yt.b, smc.b], writes=[yt.b])
            P.op("pool", lambda e: e.tensor_tensor(yt[:], yt[:], lnw[:], ALU.mult), reads=[yt.b, lnw.b], writes=[yt.b])
            P.op("pool", lambda e: e.tensor_tensor(yt[:], yt[:], lnb[:], ALU.add), reads=[yt.b, lnb.b], writes=[yt.b])
            P.op("dve", lambda e: e.tensor_tensor(yt[:], yt[:], bon[:], ALU.add), reads=[yt.b, bon.b], writes=[yt.b])
            P.op("dve", lambda e: e.tensor_tensor(yt[:], yt[:], sgc[:], ALU.mult), reads=[yt.b, sgc.b], writes=[yt.b])
            for c in range(8):
                pt = ps[4 + c // 4]
                P.op("pe", lambda e, pt=pt, c=c: e.transpose(
                    pt[:, (c % 4) * 128:(c % 4 + 1) * 128], yt[:, c * 128:(c + 1) * 128], ident[:]),
                    reads=[yt.b, ident.b], writes=[pt.b])
            P.op("act", lambda e: e.copy(zTt[:, 0:4, :], ps[4][:].rearrange("p (c t) -> p c t", t=128)),
                 reads=[ps[4].b], writes=[zTt.b])
            P.op("dve", lambda e: e.tensor_copy(zTt[:, 4:8, :], ps[5][:].rearrange("p (c t) -> p c t", t=128)),
                 reads=[ps[5].b], writes=[zTt.b])
            back_tile(kb, g, b, tt, _Shift(zTt, tt * 128), [zTt.b], wout, x_src, x_dst)
    kb.pop()
    kb.pop()


class _W4:
    def __init__(self, t, n):
        self.t, self.n, self.b = t, n, t.b

    def __getitem__(self, k):
        a, kc, sl = k
        return self.t[a, self.n, kc, sl]


LAYER_FNS[2] = layer_rwkv2


def rwkv_consts():
    c = {}
    tp = np.arange(128)[:, None]
    t = np.arange(128)[None, :]
    same = (tp // 64) == (t // 64)
    c["rw_M1"] = (same & (tp <= t)).astype(np.float32)
    c["rw_M1s"] = (same & (tp < t)).astype(np.float32)
    c["rw_M2"] = (same & (tp > t)).astype(np.float32)
    strict = (same & (tp < t)).astype(np.float32)
    incl = (same & (tp <= t)).astype(np.float32)
    c["rw_mask4"] = np.concatenate([strict, incl, strict, incl], axis=1)
    low = (same & (t < tp)).astype(np.float32)
    c["rw_maskL2"] = np.concatenate([low, low], axis=1)
    c["rw_cind"] = np.stack([(np.arange(128) < 64), (np.arange(128) >= 64)], axis=1).astype(np.float32)
    return c


def kernel(**inputs):
    n_cores = 8
    x = np.ascontiguousarray(np.asarray(inputs["x"], dtype=np.float32))
    c = np.ascontiguousarray(np.asarray(inputs["c"], dtype=np.float32))
    wsh = {k: tuple(np.asarray(v).shape) for k, v in inputs.items()}
    wsh["c"] = (NB, D)
    consts = make_consts()
    nc = build([0, 1, 2, 3], wsh, consts)
    shared = {k: np.ascontiguousarray(np.asarray(v, dtype=np.float32))
              for k, v in inputs.items() if k not in ("x", "c")}
    for k, v in consts.items():
        shared["k_" + k] = v
    in_maps = []
    for i in range(n_cores):
        m = dict(shared)
        m["x"] = np.ascontiguousarray(x[i * NB:(i + 1) * NB])
        m["c"] = np.ascontiguousarray(c[i * NB:(i + 1) * NB])
        in_maps.append(m)
    res = run_bass_kernel_spmd(nc, in_maps, core_ids=list(range(n_cores)))
    out = np.concatenate([np.asarray(r["y"]) for r in res.results], axis=0)
    return out.astype(np.float32)
```

```python
from contextlib import ExitStack
import numpy as np
import concourse.bass as bass
import concourse.mybir as mybir
from concourse.bass_utils import run_bass_kernel_spmd

F32 = mybir.dt.float32
BF16 = mybir.dt.bfloat16
AF = mybir.ActivationFunctionType
ALU = mybir.AluOpType
AX = mybir.AxisListType

SAME_ENGINE_SYNC = True
LAZY_SIGNAL = ("pe",)
MAX_PENDING = 8
NO_SELF_SYNC = ("pe",)
N_DMA_SEMS = 8


class Buf:
    __slots__ = ("name", "w", "r")

    def __init__(self, name):
        self.name = name
        self.w = None
        self.r = {}


class Prog:
    ENG = ("pe", "act", "dve", "pool", "sp")

    def __init__(self, nc):
        self.nc = nc
        self.ops = {e: [] for e in self.ENG}
        self.count = {e: 0 for e in self.ENG}
        self.known = {e: {} for e in self.ENG}
        self.sems = {}
        self.dma_n = {e: 0 for e in self.ENG}
        self._stack = []
        self._last_pos = None
        self.recs = {e: [] for e in self.ENG}

    def alloc_sems(self, stack):
        for e in self.ENG:
            self.sems[("e", e)] = stack.enter_context(self.nc.semaphore("s_" + e))
        for e in ("sp", "act", "pool"):
            for i in range(N_DMA_SEMS):
                self.sems[("d", e, i)] = stack.enter_context(
                    self.nc.semaphore("d_%s%d" % (e, i)))

    def _deps(self, eng, reads, writes):
        need = {}
        def add(tok):
            if tok is None:
                return
            k, v = tok
            if need.get(k, 0) < v:
                need[k] = v
        for b in reads:
            add(b.w)
        for b in writes:
            add(b.w)
            for k, v in b.r.items():
                add((k, v))
        waits = []
        kn = self.known[eng]
        for k, v in need.items():
            if k == ("e", eng) and (not SAME_ENGINE_SYNC or eng in NO_SELF_SYNC):
                continue
            if kn.get(k, 0) < v:
                kn[k] = v
                waits.append((k, v))
                if k[0] == "e":
                    self.recs[k[1]][v - 1]["signal"] = True
        return waits

    def _commit(self, tok, reads, writes):
        k, v = tok
        for b in writes:
            b.w = tok
            b.r = {}
        for b in reads:
            if b.r.get(k, 0) < v:
                b.r[k] = v

    def op(self, eng, fn, reads=(), writes=(), pos=None):
        waits = self._deps(eng, reads, writes)
        if eng == "pe":
            if (pos is not None or self._last_pos is not None) and self.count["pe"] > 0:
                k, v = ("e", "pe"), self.count["pe"]
                if self.known["pe"].get(k, 0) < v:
                    self.known["pe"][k] = v
                    waits.append((k, v))
                    self.recs["pe"][v - 1]["signal"] = True
            self._last_pos = pos
        self.count[eng] += 1
        tok = (("e", eng), self.count[eng])
        rec = {"fn": fn, "waits": waits, "signal": eng not in LAZY_SIGNAL}
        self.ops[eng].append(rec)
        self.recs[eng].append(rec)
        self._commit(tok, reads, writes)

    def dma(self, q, out, in_, reads=(), writes=(), **kw):
        n = self.dma_n[q]
        self.dma_n[q] += 1
        slot = n % N_DMA_SEMS
        key = ("d", q, slot)
        val = 16 * (n // N_DMA_SEMS + 1)
        waits = self._deps(q, reads, writes)
        if val > 16:
            kn = self.known[q]
            if kn.get(key, 0) < val - 16:
                kn[key] = val - 16
                waits.append((key, val - 16))
        sems = self.sems

        def emit(e, waits=waits, key=key, out=out, in_=in_, kw=kw):
            for k, v in waits:
                e.wait_ge(sems[k], v)
            e.dma_start(out=out, in_=in_, **kw).then_inc(sems[key], 16)
        self.ops[q].append(emit)
        self._commit((key, val), reads, writes)

    def finish(self, final_bufs):
        waits = self._deps("sp", final_bufs, [])
        sems = self.sems

        def emit(e, waits=waits):
            for k, v in waits:
                e.wait_ge(sems[k], v)
        self.ops["sp"].append(emit)

    def _run(self, eng, e):
        sems = self.sems
        pending = 0
        n = len(self.ops[eng])
        last_rec = None
        for it in self.ops[eng]:
            if isinstance(it, dict):
                last_rec = it
        for it in self.ops[eng]:
            if not isinstance(it, dict):
                it(e)
                continue
            for k, v in it["waits"]:
                e.wait_ge(sems[k], v)
            ins = it["fn"](e)
            pending += 1
            if it["signal"] or it is last_rec or pending >= MAX_PENDING:
                ins.then_inc(sems[("e", eng)], pending)
                pending = 0

    def emit(self):
        nc = self.nc
        with nc.Block() as block:
            @block.tensor
            def _(e):
                self._run("pe", e)

            @block.scalar
            def _(e):
                self._run("act", e)

            @block.vector
            def _(e):
                self._run("dve", e)

            @block.gpsimd
            def _(e):
                self._run("pool", e)

            @block.sync
            def _(e):
                self._run("sp", e)


def _barrier(self):
    targets = {}
    for e in self.ENG:
        if self.count[e]:
            targets[("e", e)] = self.count[e]
            self.recs[e][self.count[e] - 1]["signal"] = True
    for q in ("sp", "act", "pool"):
        n = self.dma_n[q]
        for s in range(min(n, N_DMA_SEMS)):
            last = ((n - 1 - s) // N_DMA_SEMS) * N_DMA_SEMS + s
            targets[("d", q, s)] = 16 * (last // N_DMA_SEMS + 1)
    sems = self.sems
    for e in self.ENG:
        waits = []
        kn = self.known[e]
        for k, v in targets.items():
            if kn.get(k, 0) < v:
                kn[k] = v
                waits.append((k, v))

        def emit(eo, waits=waits):
            for k, v in waits:
                eo.wait_ge(sems[k], v)
        self.ops[e].append(emit)


Prog.barrier = _barrier


DEBUG_OUT = False
S = 2048
D = 1024
NB = 2
NT = S // 128
EPS = 1e-6


class T:
    def __init__(self, t, name, nslots=0):
        self.t = t
        self.b = Buf(name)
        self.bs = [Buf("%s_%d" % (name, i)) for i in range(nslots)]

    def __getitem__(self, k):
        return self.t[k]


class KB:
    def __init__(self, nc, P, st):
        self.nc, self.P, self.st = nc, P, st
        self.scopes = [st]
        self.rr = 0
        self.n = 0

    def push(self):
        s = ExitStack()
        self.scopes.append(s)
        return s

    def pop(self):
        self.P.barrier()
        s = self.scopes.pop()
        s.close()

    def sb(self, name, shape, dt=F32, nslots=0):
        self.n += 1
        nm = "%s_%d" % (name, self.n)
        t = self.scopes[-1].enter_context(self.nc.sbuf_tensor(nm, list(shape), dt))
        return T(t, nm, nslots)

    def psum(self, name, shape, dt=F32):
        t = self.scopes[-1].enter_context(self.nc.psum_tensor(name, list(shape), dt))
        return T(t, name)

    def dram(self, name, shape, dt=F32):
        t = self.nc.dram_tensor(name, list(shape), dt, kind=("ExternalOutput" if DEBUG_OUT else "Internal"))
        o = T(t.ap(), name)
        return o


def bufs(objs):
    out = []
    for o in objs:
        out.append(o.b if isinstance(o, T) else o)
    return out


def build_common(kb, consts):
    nc, P = kb.nc, kb.P
    g = {}
    g["ident"] = kb.sb("ident", [128, 128], F32)
    P.dma("sp", g["ident"][:], consts["ident"][:, :], writes=[g["ident"].b])
    g["identb"] = kb.sb("identb", [128, 128], BF16)
    P.op("dve", lambda e: e.tensor_copy(g["identb"][:], g["ident"][:]),
         reads=[g["ident"].b], writes=[g["identb"].b])
    g["ones"] = kb.sb("ones", [128, 128], F32)
    P.op("dve", lambda e: e.memset(g["ones"][:], 1.0), writes=[g["ones"].b])
    g["ps"] = [kb.psum("ps%d" % i, [128, 512], F32) for i in range(6)]
    g["stage_n"] = 0
    g["xt"] = [kb.sb("xt", [128, 1024], F32) for i in range(2)]
    g["xn"] = [kb.sb("xn", [128, 1024], F32) for i in range(1)]
    g["junk"] = kb.sb("junk", [128, 1024], F32)
    g["st"] = [kb.sb("stt", [128, 4], F32) for i in range(2)]
    g["gatebc"] = [kb.sb("gatebc", [128, 1024], F32) for b in range(NB)]
    g["AB"] = kb.sb("AB", [128, NB, 2, 8], F32)
    g["cT"] = kb.sb("cT", [128, 8, NB], F32)
    g["gainT"] = kb.sb("gainT", [128, 8], F32)
    g["mbT"] = kb.sb("mbT", [128, 16], F32)
    g["xo"] = [kb.sb("xo", [128, 1024], F32) for i in range(2)]
    g["cnt"] = 0
    return g


def begin_load(kb, g):
    kb.push()
    g["stage"] = [kb.sb("stage", [128, 8, 256], F32) for i in range(2)]
    g["crep"] = kb.sb("crep", [128, 8, 128], F32)
    g["mbrow"] = kb.sb("mbrow", [1, 256], F32)


def end_load(kb, g):
    kb.pop()


def load_w(kb, g, dst, dcol0, src, ncols, krows=1024):
    P = kb.P
    kc = krows // 128
    c0 = 0
    engs = ("dve", "pool")
    while c0 < ncols:
        n = min(256, ncols - c0)
        stg = g["stage"][g["stage_n"] % 2]
        g["stage_n"] += 1
        q = ("sp", "act")[g["stage_n"] % 2]
        P.dma(q, stg[:, 0:kc, 0:n], src[:, c0:c0 + n].rearrange("(k p) n -> p k n", p=128),
              writes=[stg.b])
        eng = engs[g["stage_n"] % 2]
        d_ap = dst[:, 0:kc, dcol0 + c0:dcol0 + c0 + n]
        s_ap = stg[:, 0:kc, 0:n]
        P.op(eng, lambda e, d_ap=d_ap, s_ap=s_ap: e.tensor_copy(d_ap, s_ap),
             reads=[stg.b], writes=[dst.b])
        c0 += n


def compute_mod(kb, g, layer, W):
    nc, P = kb.nc, kb.P
    ps = g["ps"]
    crep, mbrow = g["crep"], g["mbrow"]
    if not g.get("c_done"):
        g["c_done"] = True
        for b in range(NB):
            P.dma("sp", g["cT"][:, :, b], W["c"][b].rearrange("(k p) -> p k", p=128),
                  writes=[g["cT"].b], allow_slow_non_contiguous=True)
        P.op("act", lambda e: e.activation(g["cT"][:], g["cT"][:], AF.Silu),
             reads=[g["cT"].b], writes=[g["cT"].b])
    P.dma("sp", g["gainT"][:], W["ln_gain"][layer].rearrange("(k p) -> p k", p=128),
          writes=[g["gainT"].b], allow_slow_non_contiguous=True)
    P.dma("sp", g["mbT"][:], W["mod_b"][layer, 0:2048].rearrange("(k p) -> p k", p=128),
          writes=[g["mbT"].b], allow_slow_non_contiguous=True)
    pf = ps[2]
    for nb in range(12):
        stg = g["stage"][g["stage_n"] % 2]
        g["stage_n"] += 1
        P.dma("sp", stg[:], W["mod_w"][layer][:, nb * 256:(nb + 1) * 256].rearrange(
            "(k p) n -> p k n", p=128), writes=[stg.b])
        if nb < 8:
            for jj in range(2):
                j = nb * 2 + jj
                for k in range(8):
                    P.op("pe", lambda e, j=j, jj=jj, k=k, stg=stg: e.matmul(
                        pf[:, j * 2:j * 2 + 2], stg[:, k, jj * 128:(jj + 1) * 128], g["cT"][:, k, :],
                        start=(k == 0), stop=(k == 7)),
                        reads=[g["cT"].b, stg.b], writes=[pf.b])
        else:
            P.dma("act", mbrow[:], W["mod_b"][layer:layer + 1, nb * 256:(nb + 1) * 256],
                  writes=[mbrow.b])
            for b in range(NB):
                pt = ps[b]
                for k in range(8):
                    P.op("dve", lambda e, b=b, k=k: e.tensor_copy(
                        crep[:, k, :], g["cT"][:, k, b:b + 1].to_broadcast([128, 128])),
                        reads=[g["cT"].b], writes=[crep.b])
                for k in range(8):
                    P.op("pe", lambda e, pt=pt, k=k, stg=stg: e.matmul(
                        pt[:, 0:256], crep[:, k, :], stg[:, k, :], start=(k == 0), stop=False),
                        reads=[crep.b, stg.b], writes=[pt.b])
                P.op("pe", lambda e, pt=pt: e.matmul(
                    pt[:, 0:256], g["ones"][0:1, :], mbrow[0:1, :], start=False, stop=True),
                    reads=[g["ones"].b, mbrow.b], writes=[pt.b])
                P.op("act", lambda e, pt=pt, b=b, nb=nb: e.copy(
                    g["gatebc"][b][:, (nb - 8) * 256:(nb - 7) * 256], pt[:, 0:256]),
                    reads=[pt.b], writes=[g["gatebc"][b].b])
    pfv = pf[:, 0:32].rearrange("p (j b) -> p j b", b=2)
    for b in range(NB):
        P.op("dve", lambda e, b=b: e.tensor_tensor(
            g["AB"][:, b, 1, :], pfv[:, 0:8, b], g["mbT"][:, 0:8], ALU.add),
            reads=[pf.b, g["mbT"].b], writes=[g["AB"].b])
        P.op("dve", lambda e, b=b: e.tensor_tensor(
            g["AB"][:, b, 0, :], pfv[:, 8:16, b], g["mbT"][:, 8:16], ALU.add),
            reads=[pf.b, g["mbT"].b], writes=[g["AB"].b])
        P.op("dve", lambda e, b=b: e.scalar_tensor_tensor(
            g["AB"][:, b, 0, :], g["AB"][:, b, 0, :], 1.0, g["gainT"][:], ALU.add, ALU.mult),
            reads=[g["AB"].b, g["gainT"].b], writes=[g["AB"].b])


def front_tile(kb, g, b, tt, x_src, hT, hT_buf, pbanks=None, xi=None):
    nc, P = kb.nc, kb.P
    i = g["cnt"] % 2
    g["cnt"] += 1
    if xi is not None:
        i = xi
    xt, xn, stt = g["xt"][i], g["xn"][0], g["st"][i]
    P.dma("sp", xt[:], x_src[b, tt * 128:(tt + 1) * 128, :], reads=[x_src.b], writes=[xt.b])
    P.op("dve", lambda e: e.memset(stt[:], 0.0), writes=[stt.b])
    P.op("act", lambda e: e.activation(g["junk"][:], xt[:], AF.Square, accum_out=stt[:, 0:1]),
         reads=[xt.b, stt.b], writes=[g["junk"].b, stt.b])
    P.op("dve", lambda e: e.tensor_scalar(stt[:, 1:2], stt[:, 0:1], 1.0 / D, EPS, ALU.mult, ALU.add),
         reads=[stt.b], writes=[stt.b])
    P.op("act", lambda e: e.activation(stt[:, 1:2], stt[:, 1:2], AF.Sqrt),
         reads=[stt.b], writes=[stt.b])
    P.op("dve", lambda e: e.reciprocal(stt[:, 2:3], stt[:, 1:2]),
         reads=[stt.b], writes=[stt.b])
    P.op("dve", lambda e: e.tensor_scalar(xn[:], xt[:], stt[:, 2:3], None, ALU.mult),
         reads=[xt.b, stt.b], writes=[xn.b])
    pa, pb = pbanks if pbanks is not None else (g["ps"][0], g["ps"][1])
    for c in range(8):
        pt = pa if c < 4 else pb
        P.op("pe", lambda e, pt=pt, c=c: e.transpose(
            pt[:, (c % 4) * 128:(c % 4 + 1) * 128], xn[:, c * 128:(c + 1) * 128], g["ident"][:]),
            reads=[xn.b, g["ident"].b], writes=[pt.b])
    for c in range(8):
        pt = pa if c < 4 else pb
        src = pt[:, (c % 4) * 128:(c % 4 + 1) * 128]
        dst = hT[:, c, tt * 128:(tt + 1) * 128]
        A = g["AB"][:, b, 0, c:c + 1]
        Bv = g["AB"][:, b, 1, c:c + 1]
        if c % 2 == 0:
            P.op("act", lambda e, src=src, dst=dst, A=A, Bv=Bv: e.activation(
                dst, src, AF.Identity, bias=Bv, scale=A),
                reads=[pt.b, g["AB"].b], writes=[hT_buf])
        else:
            P.op("dve", lambda e, src=src, dst=dst, A=A, Bv=Bv: e.tensor_scalar(
                dst, src, A, Bv, ALU.mult, ALU.add),
                reads=[pt.b, g["AB"].b], writes=[hT_buf])


def back_tile(kb, g, b, tt, zT, zT_bufs, wout, x_src, x_dst, pbanks=None, xi=None):
    nc, P = kb.nc, kb.P
    i = g["cnt"] % 2
    g["cnt"] += 1
    if xi is not None:
        i = xi
    xt, xo = g["xt"][i], g["xo"][i]
    P.dma("act", xt[:], x_src[b, tt * 128:(tt + 1) * 128, :], reads=[x_src.b], writes=[xt.b])
    for half in range(2):
        pt = g["ps"][2 + half] if pbanks is None else pbanks[half]
        for c in range(8):
            P.op("pe", lambda e, pt=pt, c=c, half=half: e.matmul(
                pt[:], zT[:, c, tt * 128:(tt + 1) * 128], wout[:, c, half * 512:(half + 1) * 512],
                start=(c == 0), stop=(c == 7)),
                reads=list(zT_bufs) + [wout.b], writes=[pt.b])
        P.op("dve", lambda e, pt=pt, half=half: e.tensor_tensor(
            xo[:, half * 512:(half + 1) * 512], pt[:],
            g["gatebc"][b][:, half * 512:(half + 1) * 512], ALU.mult),
            reads=[pt.b, g["gatebc"][b].b], writes=[xo.b])
    P.op("pool", lambda e: e.tensor_tensor(xo[:], xo[:], xt[:], ALU.add),
         reads=[xo.b, xt.b], writes=[xo.b])
    P.dma("sp", x_dst[b, tt * 128:(tt + 1) * 128, :], xo[:], reads=[xo.b], writes=[x_dst.b])


def layer_lru(kb, g, W, x_src, x_dst):
    nc, P = kb.nc, kb.P
    kb.push()
    win = kb.sb("lru_win", [128, 8, 2048], BF16)
    wout = kb.sb("lru_wout", [128, 8, 1024], BF16)
    gwb = [kb.sb("lru_gwb", [128, 8, 128], BF16) for _ in range(2)]
    vec = kb.sb("lru_vec", [128, 10, 8], F32)
    begin_load(kb, g)
    compute_mod(kb, g, 1, W)
    load_w(kb, g, win, 0, W["lru_w_in"][0], 2048)
    load_w(kb, g, wout, 0, W["lru_w_out"][0], 1024)
    for i, nm in enumerate(("lru_gate_a_w", "lru_gate_x_w")):
        stg = g["stage"][g["stage_n"] % 2]
        g["stage_n"] += 1
        P.op("pool", lambda e, stg=stg: e.memset(stg[:], 0.0), writes=[stg.b])
        src = W[nm][0]
        for hh in range(2):
            P.dma("sp", stg[hh * 64:(hh + 1) * 64, :, hh * 64:(hh + 1) * 64],
                  src.rearrange("(j n2) c d -> n2 c j d", n2=2)[hh],
                  writes=[stg.b])
        P.op("dve", lambda e, i=i, stg=stg: e.tensor_copy(gwb[i][:], stg[:, :, 0:128]),
             reads=[stg.b], writes=[gwb[i].b])
    names = ["lru_conv_b", "lru_gate_a_b", "lru_gate_x_b", "lru_lambda"]
    for i, nm in enumerate(names):
        P.dma("sp", vec[:, i, :], W[nm][0].rearrange("(k p) -> p k", p=128),
              writes=[vec.b], allow_slow_non_contiguous=True)
    for j in range(4):
        P.dma("sp", vec[:, 4 + j, :], W["lru_conv_w"][0, j].rearrange("(k p) -> p k", p=128),
              writes=[vec.b], allow_slow_non_contiguous=True)
    P.op("act", lambda e: e.activation(vec[:, 8, :], vec[:, 3, :], AF.Exp, scale=-1.0),
         reads=[vec.b], writes=[vec.b])
    P.op("act", lambda e: e.activation(vec[:, 8, :], vec[:, 8, :], AF.Ln, bias=1.0),
         reads=[vec.b], writes=[vec.b])
    P.op("dve", lambda e: e.tensor_scalar(vec[:, 8, :], vec[:, 8, :], -8.0, None, ALU.mult),
         reads=[vec.b], writes=[vec.b])

    end_load(kb, g)
    TH = 1024
    hT = kb.sb("hT", [128, 8, S], BF16)
    zT = kb.sb("zT", [128, 8, S], BF16)
    sets = [dict(upad=kb.sb("upad", [128, 3 + TH], F32), uc=kb.sb("uc", [128, TH], F32),
                 ucb=kb.sb("ucb", [128, TH], BF16), rr=kb.sb("rr", [128, TH], F32),
                 ii=kb.sb("ii", [128, TH], F32), aa=kb.sb("aa", [128, TH], F32),
                 bb=kb.sb("bb", [128, TH], F32)) for _ in range(2)]
    ps = g["ps"]
    pbank = [0]

    def nb():
        pbank[0] += 1
        return ps[2 + pbank[0] % 4]

    def half(b, j, hf, cur, prev):
        upad, uc, ucb, rr, ii, aa, bb = (cur[k] for k in ("upad", "uc", "ucb", "rr", "ii", "aa", "bb"))
        t0 = hf * TH
        if hf == 0:
            P.op("dve", lambda e: e.memset(upad[:, 0:3], 0.0), writes=[upad.b])
        else:
            P.op("dve", lambda e: e.tensor_copy(upad[:, 0:3], prev["upad"][:, TH:TH + 3]),
                 reads=[prev["upad"].b], writes=[upad.b])
        for q in range(TH // 512):
            pt = nb()
            for c in range(8):
                P.op("pe", lambda e, pt=pt, c=c, q=q: e.matmul(
                    pt[:], win[:, c, j * 128:(j + 1) * 128], hT[:, c, t0 + q * 512:t0 + (q + 1) * 512],
                    start=(c == 0), stop=(c == 7)),
                    reads=[win.b, hT.b], writes=[pt.b])
            P.op("act", lambda e, pt=pt, q=q: e.copy(upad[:, 3 + q * 512:3 + (q + 1) * 512], pt[:]),
                 reads=[pt.b], writes=[upad.b])
        for q in range(TH // 512):
            pt = nb()
            for c in range(8):
                P.op("pe", lambda e, pt=pt, c=c, q=q: e.matmul(
                    pt[:], win[:, c, 1024 + j * 128:1024 + (j + 1) * 128],
                    hT[:, c, t0 + q * 512:t0 + (q + 1) * 512], start=(c == 0), stop=(c == 7)),
                    reads=[win.b, hT.b], writes=[pt.b])
            P.op("act", lambda e, pt=pt, q=q: e.activation(
                bb[:, q * 512:(q + 1) * 512], pt[:], AF.Silu),
                reads=[pt.b], writes=[bb.b])
        P.op("dve", lambda e: e.tensor_scalar(
            uc[:], upad[:, 0:TH], vec[:, 4, j:j + 1], vec[:, 0, j:j + 1], ALU.mult, ALU.add),
            reads=[upad.b, vec.b], writes=[uc.b])
        for k in range(1, 4):
            P.op("dve", lambda e, k=k: e.scalar_tensor_tensor(
                uc[:], upad[:, k:k + TH], vec[:, 4 + k, j:j + 1], uc[:], ALU.mult, ALU.add),
                reads=[upad.b, vec.b, uc.b], writes=[uc.b])
        P.op("pool", lambda e: e.tensor_copy(ucb[:], uc[:]), reads=[uc.b], writes=[ucb.b])
        for gi, dst, bi in ((0, aa, 1), (1, ii, 2)):
            for q in range(TH // 512):
                pt = nb()
                P.op("pe", lambda e, pt=pt, q=q, gi=gi: e.matmul(
                    pt[:], gwb[gi][:, j, :], ucb[:, q * 512:(q + 1) * 512], start=True, stop=True),
                    reads=[gwb[gi].b, ucb.b], writes=[pt.b])
                P.op("act", lambda e, pt=pt, q=q, dst=dst, bi=bi: e.activation(
                    dst[:, q * 512:(q + 1) * 512], pt[:], AF.Sigmoid, bias=vec[:, bi, j:j + 1]),
                    reads=[pt.b, vec.b], writes=[dst.b])
        P.op("act", lambda e: e.activation(aa[:], aa[:], AF.Exp, scale=vec[:, 8, j:j + 1]),
             reads=[aa.b, vec.b], writes=[aa.b])
        P.op("pool", lambda e: e.tensor_tensor(ii[:], ii[:], uc[:], ALU.mult),
             reads=[ii.b, uc.b], writes=[ii.b])
        P.op("dve", lambda e: e.tensor_tensor(rr[:], aa[:], aa[:], ALU.mult),
             reads=[aa.b], writes=[rr.b])
        P.op("act", lambda e: e.activation(rr[:], rr[:], AF.Sqrt, bias=1.0, scale=-1.0),
             reads=[rr.b], writes=[rr.b])
        P.op("dve", lambda e: e.tensor_tensor(ii[:], rr[:], ii[:], ALU.mult),
             reads=[rr.b, ii.b], writes=[ii.b])
        if hf == 0:
            P.op("dve", lambda e: e.tensor_tensor_scan(rr[:], aa[:], ii[:], 0.0, ALU.mult, ALU.add),
                 reads=[aa.b, ii.b], writes=[rr.b])
        else:
            P.op("dve", lambda e: e.tensor_tensor_scan(rr[:], aa[:], ii[:], prev["rr"][:, TH - 1:TH], ALU.mult, ALU.add),
                 reads=[aa.b, ii.b, prev["rr"].b], writes=[rr.b])
        P.op("pool", lambda e: e.tensor_tensor(zT[:, j, t0:t0 + TH], rr[:], bb[:], ALU.mult),
             reads=[rr.b, bb.b], writes=[zT.b])

    cnt = 0
    for b in range(NB):
        for tt in range(NT):
            front_tile(kb, g, b, tt, x_src, hT, hT.b)
        for j in range(8):
            for hf in range(S // TH):
                half(b, j, hf, sets[cnt % 2], sets[(cnt - 1) % 2])
                cnt += 1
        for tt in range(NT):
            back_tile(kb, g, b, tt, zT, [zT.b], wout, x_src, x_dst)
    kb.pop()


WSHAPES = None


def make_consts():
    c = {}
    c["ident"] = np.eye(128, dtype=np.float32)
    c.update(gla_consts())
    c.update(dsa_consts())
    c.update(rwkv_consts())
    return c


def build(layers, wshapes, consts_np):
    nc = bass.Bass("TRN2", target_bir_lowering=False)
    W = {}
    for k, shp in wshapes.items():
        if k == "x":
            continue
        W[k] = nc.dram_tensor(k, list(shp), F32, kind="ExternalInput").ap()
    consts = {k: nc.dram_tensor("k_" + k, list(v.shape), F32, kind="ExternalInput").ap()
              for k, v in consts_np.items()}
    x_in = T(nc.dram_tensor("x", [NB, S, D], F32, kind="ExternalInput").ap(), "x_in")
    y_out = T(nc.dram_tensor("y", [NB, S, D], F32, kind="ExternalOutput").ap(), "y_out")
    with ExitStack() as st:
        P = Prog(nc)
        P.alloc_sems(st)
        kb = KB(nc, P, st)
        g = build_common(kb, consts)
        g["consts"] = consts
        scr = [kb.dram("xs%d" % i, [NB, S, D]) for i in range(2)]
        fns = {0: None, 1: layer_lru, 2: None, 3: None}
        fns.update(LAYER_FNS)
        src = x_in
        for li, layer in enumerate(layers):
            dst = y_out if li == len(layers) - 1 else scr[li % 2]
            fns[layer](kb, g, W, src, dst)
            src = dst
        P.finish([y_out.b])
        P.emit()
    return nc


LAYER_FNS = {}


def layer_gla(kb, g, W, x_src, x_dst):
    nc, P = kb.nc, kb.P
    C = g["consts"]
    kb.push()
    win = kb.sb("gla_win", [128, 8, 3088], BF16)
    wout = kb.sb("gla_wout", [128, 8, 1024], BF16)
    begin_load(kb, g)
    compute_mod(kb, g, 3, W)
    load_w(kb, g, win, 0, W["gla_w_in"][0], 3088)
    load_w(kb, g, wout, 0, W["gla_w_out"][0], 1024)
    end_load(kb, g)
    aw2 = kb.sb("aw2", [16, 512], F32)
    P.dma("sp", aw2[:], W["gla_alpha_w2"][0], writes=[aw2.b])
    nab = kb.sb("nab", [128, 4], F32)
    P.dma("sp", nab[:], W["gla_alpha_b"][0].rearrange("(k p) -> p k", p=128),
          writes=[nab.b], allow_slow_non_contiguous=True)
    P.op("dve", lambda e: e.tensor_scalar(nab[:], nab[:], -1.0, None, ALU.mult),
         reads=[nab.b], writes=[nab.b])
    gbc = kb.sb("gbc", [128, 256], F32)
    P.dma("sp", gbc[:], W["gla_norm_gain"][0:1, :].to_broadcast([128, 256]), writes=[gbc.b])
    smask = kb.sb("smask", [128, 512], F32)
    P.dma("sp", smask[:], C["gla_scanmask"][:, :], writes=[smask.b])
    mbd = kb.sb("mbd", [128, 128], F32)
    P.dma("sp", mbd[:], C["gla_mbd"][:, :], writes=[mbd.b])
    m2 = kb.sb("m2", [128, 128], F32)
    P.dma("sp", m2[:], C["gla_m2"][:, :], writes=[m2.b])

    TB = 512
    hT = kb.sb("hTb", [128, 8, TB], BF16)
    zT = kb.sb("zTb", [128, 8, TB], BF16)
    alow = kb.sb("alow", [16, TB], F32)
    laT = kb.sb("laT", [128, TB], F32)
    cumT = kb.sb("cumT", [128, TB], F32)
    eq = kb.sb("eq", [128, TB], F32)
    ek = kb.sb("ek", [128, TB], F32)
    qdT = kb.sb("qdT", [128, 4, TB], F32)
    kiT = kb.sb("kiT", [128, 4, TB], F32)
    el = kb.sb("el", [128, 4, 8], F32)
    latok = kb.sb("latok", [128, 4, 512], F32)
    edl = kb.sb("edl", [128, 512], F32)
    kend = kb.sb("kend", [128, 512], F32)
    vtok = kb.sb("vtok", [128, 1024], F32)
    sg = kb.sb("sg", [128, 1024], F32)
    attm = kb.sb("attm", [128, 128], F32)
    Sst = kb.sb("Sst", [128, 4, 256], F32)
    zz = kb.sb("zz", [128, 1024], F32)
    ss = kb.sb("ss", [128, 8], F32)
    ps = g["ps"]
    scale_q = 128.0 ** -0.5
    for b in range(NB):
        P.op("dve", lambda e: e.memset(Sst[:], 0.0), writes=[Sst.b])
        for tb in range(S // TB):
            for tl in range(4):
                front_tile(kb, g, b, tb * 4 + tl, x_src, _Shift(hT, tb * TB), hT.b)
            for c in range(8):
                P.op("pe", lambda e, c=c: e.matmul(ps[4][0:16, :], win[:, c, 3072:3088], hT[:, c, :],
                                                   start=(c == 0), stop=(c == 7)),
                     reads=[win.b, hT.b], writes=[ps[4].b], pos="M16")
            P.op("act", lambda e: e.copy(alow[:], ps[4][0:16, :]), reads=[ps[4].b], writes=[alow.b])
            for h in range(4):
                P.op("pe", lambda e, h=h: e.matmul(ps[4][:], aw2[:, h * 128:(h + 1) * 128], alow[:],
                                                   start=True, stop=True),
                     reads=[aw2.b, alow.b], writes=[ps[4].b], pos="K16")
                P.op("act", lambda e, h=h: e.activation(laT[:], ps[4][:], AF.Exp, bias=nab[:, h:h + 1], scale=-1.0),
                     reads=[ps[4].b, nab.b], writes=[laT.b])
                P.op("act", lambda e: e.activation(laT[:], laT[:], AF.Ln, bias=1.0),
                     reads=[laT.b], writes=[laT.b])
                P.op("dve", lambda e: e.tensor_scalar(laT[:], laT[:], -1.0 / 16.0, None, ALU.mult),
                     reads=[laT.b], writes=[laT.b])
                P.op("dve", lambda e: e.tensor_tensor_scan(cumT[:], smask[:], laT[:], 0.0, ALU.mult, ALU.add),
                     reads=[smask.b, laT.b], writes=[cumT.b])
                P.op("act", lambda e: e.activation(eq[:], cumT[:], AF.Exp),
                     reads=[cumT.b], writes=[eq.b])
                P.op("act", lambda e: e.activation(ek[:], cumT[:], AF.Exp, scale=-1.0),
                     reads=[cumT.b], writes=[ek.b])
                for c in range(8):
                    P.op("pe", lambda e, c=c, h=h: e.matmul(ps[5][:], win[:, c, h * 128:(h + 1) * 128], hT[:, c, :],
                                                            start=(c == 0), stop=(c == 7)),
                         reads=[win.b, hT.b], writes=[ps[5].b])
                P.op("dve", lambda e, h=h: e.scalar_tensor_tensor(qdT[:, h, :], ps[5][:], scale_q, eq[:],
                                                                  ALU.mult, ALU.mult),
                     reads=[ps[5].b, eq.b], writes=[qdT.b])
                for c in range(8):
                    P.op("pe", lambda e, c=c, h=h: e.matmul(ps[5][:], win[:, c, 512 + h * 128:512 + (h + 1) * 128],
                                                            hT[:, c, :], start=(c == 0), stop=(c == 7)),
                         reads=[win.b, hT.b], writes=[ps[5].b])
                P.op("dve", lambda e, h=h: e.tensor_tensor(kiT[:, h, :], ps[5][:], ek[:], ALU.mult),
                     reads=[ps[5].b, ek.b], writes=[kiT.b])
                P.op("dve", lambda e, h=h: e.tensor_copy(
                    el[:, h, :], eq[:].rearrange("p (n c) -> p n c", c=64)[:, :, 63]),
                    reads=[eq.b], writes=[el.b])
                for tl in range(4):
                    P.op("pe", lambda e, tl=tl: e.transpose(ps[2][:, tl * 128:(tl + 1) * 128],
                                                            laT[:, tl * 128:(tl + 1) * 128], g["ident"][:]),
                         reads=[laT.b, g["ident"].b], writes=[ps[2].b])
                P.op("act", lambda e, h=h: e.copy(
                    latok[:, :, h * 128:(h + 1) * 128], ps[2][:].rearrange("p (t f) -> p t f", f=128)),
                    reads=[ps[2].b], writes=[latok.b])
            for tl in range(4):
                tsl = slice(tl * 128, (tl + 1) * 128)
                P.op("pe", lambda e, tl=tl: e.matmul(ps[2][:], m2[:], latok[:, tl, :], start=True, stop=True),
                     reads=[m2.b, latok.b], writes=[ps[2].b])
                P.op("act", lambda e: e.activation(edl[:], ps[2][:], AF.Exp), reads=[ps[2].b], writes=[edl.b])
                for c in range(8):
                    P.op("pe", lambda e, c=c, tsl=tsl: e.matmul(ps[3][:], hT[:, c, tsl], win[:, c, 512:1024],
                                                                start=(c == 0), stop=(c == 7)),
                         reads=[win.b, hT.b], writes=[ps[3].b])
                P.op("dve", lambda e: e.tensor_tensor(kend[:], ps[3][:], edl[:], ALU.mult),
                     reads=[ps[3].b, edl.b], writes=[kend.b])
                for half in range(2):
                    pt = ps[4 + half]
                    for c in range(8):
                        P.op("pe", lambda e, c=c, tsl=tsl, pt=pt, half=half: e.matmul(
                            pt[:], hT[:, c, tsl], win[:, c, 1024 + half * 512:1024 + (half + 1) * 512],
                            start=(c == 0), stop=(c == 7)),
                            reads=[win.b, hT.b], writes=[pt.b])
                    P.op("act", lambda e, pt=pt, half=half: e.copy(vtok[:, half * 512:(half + 1) * 512], pt[:]),
                         reads=[pt.b], writes=[vtok.b])
                for half in range(2):
                    pt = ps[4 + half]
                    for c in range(8):
                        P.op("pe", lambda e, c=c, tsl=tsl, pt=pt, half=half: e.matmul(
                            pt[:], hT[:, c, tsl], win[:, c, 2048 + half * 512:2048 + (half + 1) * 512],
                            start=(c == 0), stop=(c == 7)),
                            reads=[win.b, hT.b], writes=[pt.b])
                    P.op("act", lambda e, pt=pt, half=half: e.activation(
                        sg[:, half * 512:(half + 1) * 512], pt[:], AF.Silu),
                        reads=[pt.b], writes=[sg.b])
                for h in range(4):
                    po = ps[h // 2]
                    pc = slice((h % 2) * 256, (h % 2 + 1) * 256)
                    vsl = slice(h * 256, (h + 1) * 256)
                    P.op("pe", lambda e, h=h, tsl=tsl: e.matmul(ps[2][:, 0:128], kiT[:, h, tsl], qdT[:, h, tsl],
                                                                start=True, stop=True),
                         reads=[kiT.b, qdT.b], writes=[ps[2].b])
                    P.op("dve", lambda e: e.tensor_tensor(attm[:], ps[2][:, 0:128], mbd[:], ALU.mult),
                         reads=[ps[2].b, mbd.b], writes=[attm.b])
                    P.op("pe", lambda e, po=po, pc=pc, vsl=vsl: e.matmul(po[:, pc], attm[:], vtok[:, vsl],
                                                                         start=True, stop=False),
                         reads=[attm.b, vtok.b], writes=[po.b])
                    for cc in range(2):
                        n = tl * 2 + cc
                        psl = slice(cc * 64, (cc + 1) * 64)
                        qsl = slice(tl * 128 + cc * 64, tl * 128 + (cc + 1) * 64)
                        P.op("pe", lambda e, po=po, pc=pc, psl=psl, qsl=qsl, h=h: e.matmul(
                            po[psl, pc], qdT[:, h, qsl], Sst[:, h, :], start=False, stop=True),
                            reads=[qdT.b, Sst.b], writes=[po.b], pos=("T1" if cc else "T0"))
                        P.op("pe", lambda e, psl=psl, h=h, vsl=vsl: e.matmul(
                            ps[3][:, 0:256], kend[psl, h * 128:(h + 1) * 128], vtok[psl, vsl],
                            start=True, stop=True),
                            reads=[kend.b, vtok.b], writes=[ps[3].b], pos=("R1" if cc else "R0"))
                        P.op("dve", lambda e, h=h, n=n: e.scalar_tensor_tensor(
                            Sst[:, h, :], Sst[:, h, :], el[:, h, n:n + 1], ps[3][:, 0:256], ALU.mult, ALU.add),
                            reads=[Sst.b, el.b, ps[3].b], writes=[Sst.b])
                P.op("dve", lambda e: e.memset(ss[:], 0.0), writes=[ss.b])
                for h in range(4):
                    po = ps[h // 2]
                    pc = slice((h % 2) * 256, (h % 2 + 1) * 256)
                    P.op("act", lambda e, po=po, pc=pc, h=h: e.activation(
                        g["junk"][:, 0:256], po[:, pc], AF.Square, accum_out=ss[:, h:h + 1]),
                        reads=[po.b, ss.b], writes=[g["junk"].b, ss.b])
                P.op("dve", lambda e: e.tensor_scalar(ss[:, 4:8], ss[:, 0:4], 1.0 / 256.0, EPS, ALU.mult, ALU.add),
                     reads=[ss.b], writes=[ss.b])
                P.op("act", lambda e: e.activation(ss[:, 4:8], ss[:, 4:8], AF.Sqrt), reads=[ss.b], writes=[ss.b])
                P.op("dve", lambda e: e.reciprocal(ss[:, 4:8], ss[:, 4:8]), reads=[ss.b], writes=[ss.b])
                for h in range(4):
                    po = ps[h // 2]
                    pc = slice((h % 2) * 256, (h % 2 + 1) * 256)
                    vsl = slice(h * 256, (h + 1) * 256)
                    P.op("dve", lambda e, po=po, pc=pc, vsl=vsl, h=h: e.scalar_tensor_tensor(
                        zz[:, vsl], po[:, pc], ss[:, 4 + h:5 + h], gbc[:], ALU.mult, ALU.mult),
                        reads=[po.b, ss.b, gbc.b], writes=[zz.b])
                P.op("pool", lambda e: e.tensor_tensor(zz[:], zz[:], sg[:], ALU.mult),
                     reads=[zz.b, sg.b], writes=[zz.b])
                for c in range(8):
                    pt = ps[4 + c // 4]
                    P.op("pe", lambda e, pt=pt, c=c: e.transpose(
                        pt[:, (c % 4) * 128:(c % 4 + 1) * 128], zz[:, c * 128:(c + 1) * 128], g["ident"][:]),
                        reads=[zz.b, g["ident"].b], writes=[pt.b])
                for half in range(2):
                    pt = ps[4 + half]
                    eng = ("act", "dve")[half]
                    if half == 0:
                        P.op("act", lambda e, pt=pt, tsl=tsl: e.copy(
                            zT[:, 0:4, tsl], pt[:].rearrange("p (c t) -> p c t", t=128)),
                            reads=[pt.b], writes=[zT.b])
                    else:
                        P.op("dve", lambda e, pt=pt, tsl=tsl: e.tensor_copy(
                            zT[:, 4:8, tsl], pt[:].rearrange("p (c t) -> p c t", t=128)),
                            reads=[pt.b], writes=[zT.b])
            for tl in range(4):
                back_tile(kb, g, b, tb * 4 + tl, _Shift(zT, tb * TB), [zT.b], wout, x_src, x_dst)
    kb.pop()


class _Shift:
    def __init__(self, t, t0):
        self.t, self.t0 = t, t0

    def __getitem__(self, k):
        a, c, sl = k
        return self.t[a, c, sl.start - self.t0:sl.stop - self.t0]


def gla_consts():
    c = {}
    t = np.arange(512)
    c["gla_scanmask"] = np.tile((t % 64 != 0).astype(np.float32)[None, :], (128, 1))
    j = np.arange(128)[:, None]
    i = np.arange(128)[None, :]
    same = (j // 64) == (i // 64)
    c["gla_mbd"] = (same & (j <= i)).astype(np.float32)
    c["gla_m2"] = (same & (j > i)).astype(np.float32)
    return c


LAYER_FNS[3] = layer_gla


NEG = -1.0e30


def _rope(P, eng2, x, cos, sin, out_list, tmp1, tmp2, nh, half, rd, wr):
    xv = x[:, 0:nh * 2 * half].rearrange("p (h two d) -> p h two d", two=2, d=half)
    x1, x2 = xv[:, :, 0, :], xv[:, :, 1, :]
    cb = cos.unsqueeze(1).to_broadcast([128, nh, half])
    sb_ = sin.unsqueeze(1).to_broadcast([128, nh, half])
    t1 = tmp1[:, 0:nh * half].rearrange("p (h d) -> p h d", d=half)
    t2 = tmp2[:, 0:nh * half].rearrange("p (h d) -> p h d", d=half)
    P.op("dve", lambda e: e.tensor_tensor(t1, x1, cb, ALU.mult), reads=rd, writes=[tmp1.b])
    P.op(eng2, lambda e: e.tensor_tensor(t2, x2, sb_, ALU.mult), reads=rd, writes=[tmp2.b])
    for o in out_list:
        P.op("dve", lambda e, o=o: e.tensor_tensor(o[0], t1, t2, ALU.subtract),
             reads=[tmp1.b, tmp2.b], writes=wr)
    P.op("dve", lambda e: e.tensor_tensor(t1, x2, cb, ALU.mult), reads=rd + [tmp1.b], writes=[tmp1.b])
    P.op(eng2, lambda e: e.tensor_tensor(t2, x1, sb_, ALU.mult), reads=rd + [tmp2.b], writes=[tmp2.b])
    for o in out_list:
        P.op("dve", lambda e, o=o: e.tensor_tensor(o[1], t1, t2, ALU.add),
             reads=[tmp1.b, tmp2.b], writes=wr)


def layer_dsa(kb, g, W, x_src, x_dst):
    nc, P = kb.nc, kb.P
    C = g["consts"]
    kb.push()
    win = kb.sb("dsa_win", [128, 8, 3720], BF16)
    wout = kb.sb("dsa_wout", [128, 8, 1024], BF16)
    begin_load(kb, g)
    compute_mod(kb, g, 0, W)
    load_w(kb, g, win, 0, W["dsa_w_in"][0], 3720)
    load_w(kb, g, wout, 0, W["dsa_w_out"][0], 1024)
    end_load(kb, g)
    ps = g["ps"]
    psb = [kb.psum("psb%d" % i, [128, 1024], BF16) for i in range(2)]
    ident, identb = g["ident"], g["identb"]
    cs32 = kb.sb("cs32", [128, 2, NT, 32], F32)
    cs64 = kb.sb("cs64", [128, 2, NT, 64], F32)
    for i, nm in enumerate(("cos32", "sin32")):
        P.dma("sp", cs32[:, i], C[nm].rearrange("(tt p) f -> p tt f", p=128), writes=[cs32.b])
    for i, nm in enumerate(("cos64", "sin64")):
        P.dma("sp", cs64[:, i], C[nm].rearrange("(tt p) f -> p tt f", p=128), writes=[cs64.b])
    cmask = kb.sb("cmask", [128, 128], F32)
    P.dma("sp", cmask[:], C["causal"][:, :], writes=[cmask.b])
    gq = kb.sb("gq", [128, 2, 64], F32)
    P.dma("sp", gq[:, 0, :], W["dsa_q_gain"][0:1, :].to_broadcast([128, 64]), writes=[gq.b])
    P.dma("sp", gq[:, 1, :], W["dsa_k_gain"][0:1, :].to_broadcast([128, 64]), writes=[gq.b])
    kT2 = kb.sb("kT2", [128, 4, S], BF16)
    v1 = kb.sb("v1", [128, NT, 4, 65], BF16)
    kiT = kb.sb("kiT", [128, S], BF16)
    hTt = kb.sb("hTt", [128, 8, 128], BF16)
    big = kb.sb("big", [128, S], F32)
    qf = _Off(big, 0)
    tmpa = _Off(big, 1024)
    tmpb = kb.sb("tmpb", [128, 512], F32)
    tmpc = kb.sb("tmpc", [128, 512], F32)
    qb = kb.sb("qb", [128, 1024], BF16)
    kb2 = kb.sb("kb2", [128, 4, 2, 64], BF16)
    qTt = kb.sb("qTt", [128, 8, 128], BF16)
    qiTt = kb.sb("qiTt", [128, 8, 128], BF16)
    sgt = kb.sb("sgt", [128, 1024], BF16)
    kf = kb.sb("kf", [128, 512], F32)
    wkf = kb.sb("wkf", [128, 136], F32)
    kib = kb.sb("kib", [128, 128], BF16)
    sm = kb.sb("sm", [128, 64], F32)
    score = kb.sb("score", [128, S], F32)
    work = big
    rl = [kb.sb("rl", [128, 512], F32) for _ in range(2)]
    maskb = kb.sb("maskb", [128, S], BF16)
    maskT = kb.sb("maskT", [128, NT, 128], BF16)
    eb = [kb.sb("eb", [128, 512], BF16) for _ in range(2)]
    pT = [kb.sb("pT", [128, 512], BF16) for _ in range(2)]
    m8 = kb.sb("m8", [128, 8], F32)
    thr = kb.sb("thr", [128, 2], F32)
    zz = _Off(big, 0)
    zTt = kb.sb("zTt", [128, 8, 128], BF16)
    rden = kb.sb("rden", [128, 16], F32)
    IDXS = 1024.0 ** -0.5
    P.op("pool", lambda e: e.memset(v1[:], 1.0), writes=[v1.b])

    def rms_heads(x, nh, gi, out):
        xv = x[:, 0:nh * 64].rearrange("p (h d) -> p h d", d=64)
        P.op("act", lambda e: e.activation(tmpa[:, 0:nh * 64], x[:, 0:nh * 64], AF.Square),
             reads=[x.b], writes=[tmpa.b])
        P.op("dve", lambda e: e.tensor_reduce(sm[:, 0:nh], tmpa[:, 0:nh * 64].rearrange("p (h d) -> p h d", d=64),
                                              AX.X, ALU.add),
             reads=[tmpa.b], writes=[sm.b])
        P.op("dve", lambda e: e.tensor_scalar(sm[:, 0:nh], sm[:, 0:nh], 1.0 / 64.0, EPS, ALU.mult, ALU.add),
             reads=[sm.b], writes=[sm.b])
        P.op("act", lambda e: e.activation(sm[:, 0:nh], sm[:, 0:nh], AF.Sqrt), reads=[sm.b], writes=[sm.b])
        P.op("dve", lambda e: e.reciprocal(sm[:, 0:nh], sm[:, 0:nh]), reads=[sm.b], writes=[sm.b])
        ov = out[:, 0:nh * 64].rearrange("p (h d) -> p h d", d=64)
        P.op("dve", lambda e: e.tensor_tensor(ov, xv, sm[:, 0:nh].unsqueeze(2).to_broadcast([128, nh, 64]), ALU.mult),
             reads=[x.b, sm.b], writes=[out.b])
        P.op("pool", lambda e: e.tensor_tensor(ov, ov, gq[:, gi, :].unsqueeze(1).to_broadcast([128, nh, 64]), ALU.mult),
             reads=[out.b, gq.b], writes=[out.b])

    for b in range(NB):
        for tt in range(NT):
            tsl = slice(tt * 128, (tt + 1) * 128)
            L = (tt + 1) * 128
            front_tile(kb, g, b, tt, x_src, _Shift(hTt, tt * 128), hTt.b)

            def proj(pt, c0, n):
                for c in range(8):
                    P.op("pe", lambda e, c=c: e.matmul(pt[:, 0:n], hTt[:, c, :], win[:, c, c0:c0 + n],
                                                       start=(c == 0), stop=(c == 7)),
                         reads=[hTt.b, win.b], writes=[pt.b])
            for half in range(2):
                proj(ps[2 + half], half * 512, 512)
                P.op("act", lambda e, half=half: e.copy(qf[:, half * 512:(half + 1) * 512], ps[2 + half][:]),
                     reads=[ps[2 + half].b], writes=[qf.b])
            rms_heads(qf, 16, 0, qf)
            qbv = qb[:].rearrange("p (h two d) -> p h two d", two=2, d=32)
            _rope(P, "pool", qf, cs32[:, 0, tt, :], cs32[:, 1, tt, :], [(qbv[:, :, 0, :], qbv[:, :, 1, :])],
                  tmpb, tmpc, 16, 32, [qf.b, cs32.b], [qb.b])
            for c in range(8):
                P.op("pe", lambda e, c=c: e.transpose(psb[0][:, c * 128:(c + 1) * 128], qb[:, c * 128:(c + 1) * 128],
                                                      identb[:]),
                     reads=[qb.b, identb.b], writes=[psb[0].b])
            P.op("act", lambda e: e.copy(qTt[:], psb[0][:].rearrange("p (c t) -> p c t", t=128)),
                 reads=[psb[0].b], writes=[qTt.b])
            proj(ps[4], 1024, 512)
            P.op("act", lambda e: e.copy(kf[:], ps[4][:]), reads=[ps[4].b], writes=[kf.b])
            P.op("dve", lambda e, tt=tt: e.tensor_copy(
                v1[:, tt, :, 0:64], kf[:, 256:512].rearrange("p (g d) -> p g d", d=64)),
                reads=[kf.b], writes=[v1.b])
            rms_heads(kf, 4, 1, kf)
            k2v = kb2[:].rearrange("p g r (two d) -> p g r two d", two=2)
            _rope(P, "pool", kf, cs32[:, 0, tt, :], cs32[:, 1, tt, :],
                  [(k2v[:, :, 0, 0, :], k2v[:, :, 0, 1, :]), (k2v[:, :, 1, 0, :], k2v[:, :, 1, 1, :])],
                  tmpb, tmpc, 4, 32, [kf.b, cs32.b], [kb2.b])
            for gg in range(4):
                P.op("pe", lambda e, gg=gg: e.transpose(
                    psb[1][:, gg * 128:(gg + 1) * 128], kb2[:, gg].rearrange("p r d -> p (r d)"), identb[:]),
                    reads=[kb2.b, identb.b], writes=[psb[1].b])
            P.op("act", lambda e, tsl=tsl: e.copy(kT2[:, :, tsl], psb[1][:, 0:512].rearrange("p (g t) -> p g t", t=128)),
                 reads=[psb[1].b], writes=[kT2.b])
            for half in range(2):
                proj(ps[2 + half], 1536 + half * 512, 512)
                P.op("act", lambda e, half=half: e.activation(sgt[:, half * 512:(half + 1) * 512], ps[2 + half][:], AF.Silu),
                     reads=[ps[2 + half].b], writes=[sgt.b])
            for half in range(2):
                proj(ps[4 + half], 2560 + half * 512, 512)
                P.op("act", lambda e, half=half: e.copy(qf[:, half * 512:(half + 1) * 512], ps[4 + half][:]),
                     reads=[ps[4 + half].b], writes=[qf.b])
            qbv2 = qb[:].rearrange("p (h two d) -> p h two d", two=2, d=64)
            _rope(P, "pool", qf, cs64[:, 0, tt, :], cs64[:, 1, tt, :], [(qbv2[:, :, 0, :], qbv2[:, :, 1, :])],
                  tmpb, tmpc, 8, 64, [qf.b, cs64.b], [qb.b])
            for c in range(8):
                P.op("pe", lambda e, c=c: e.transpose(psb[0][:, c * 128:(c + 1) * 128], qb[:, c * 128:(c + 1) * 128],
                                                      identb[:]),
                     reads=[qb.b, identb.b], writes=[psb[0].b])
            P.op("act", lambda e: e.copy(qiTt[:], psb[0][:].rearrange("p (c t) -> p c t", t=128)),
                 reads=[psb[0].b], writes=[qiTt.b])
            proj(ps[4], 3584, 136)
            P.op("act", lambda e: e.copy(wkf[:], ps[4][:, 0:136]), reads=[ps[4].b], writes=[wkf.b])
            P.op("dve", lambda e: e.tensor_scalar(wkf[:, 0:8], wkf[:, 0:8], IDXS, None, ALU.mult),
                 reads=[wkf.b], writes=[wkf.b])
            kiv = kib[:].rearrange("p (h two d) -> p h two d", two=2, d=64)
            _rope(P, "pool", _Off(wkf, 8), cs64[:, 0, tt, :], cs64[:, 1, tt, :], [(kiv[:, :, 0, :], kiv[:, :, 1, :])],
                  tmpb, tmpc, 1, 64, [wkf.b, cs64.b], [kib.b])
            P.op("pe", lambda e: e.transpose(psb[1][:, 0:128], kib[:], identb[:]),
                 reads=[kib.b, identb.b], writes=[psb[1].b])
            P.op("act", lambda e, tsl=tsl: e.copy(kiT[:, tsl], psb[1][:, 0:128]), reads=[psb[1].b], writes=[kiT.b])
            n_it = 0
            for k0 in range(0, L, 512):
                w = min(512, L - k0)
                for h in range(8):
                    pt = ps[2 + n_it % 2]
                    r_ = rl[n_it % 2]
                    n_it += 1
                    P.op("pe", lambda e, pt=pt, h=h, k0=k0, w=w: e.matmul(
                        pt[:, 0:w], qiTt[:, h, :], kiT[:, k0:k0 + w], start=True, stop=True),
                        reads=[qiTt.b, kiT.b], writes=[pt.b])
                    P.op("act", lambda e, pt=pt, r_=r_, w=w: e.activation(r_[:, 0:w], pt[:, 0:w], AF.Relu),
                         reads=[pt.b], writes=[r_.b])
                    if h == 0:
                        P.op("dve", lambda e, r_=r_, k0=k0, w=w, h=h: e.tensor_scalar(
                            score[:, k0:k0 + w], r_[:, 0:w], wkf[:, h:h + 1], None, ALU.mult),
                            reads=[r_.b, wkf.b], writes=[score.b])
                    else:
                        P.op("dve", lambda e, r_=r_, k0=k0, w=w, h=h: e.scalar_tensor_tensor(
                            score[:, k0:k0 + w], r_[:, 0:w], wkf[:, h:h + 1], score[:, k0:k0 + w],
                            ALU.mult, ALU.add),
                            reads=[r_.b, wkf.b, score.b], writes=[score.b])
            P.op("dve", lambda e, tsl=tsl: e.tensor_tensor(score[:, tsl], score[:, tsl], cmask[:], ALU.add),
                 reads=[score.b, cmask.b], writes=[score.b])
            if tt >= 2:
                P.op("pool", lambda e, L=L: e.tensor_copy(work[:, 0:L], score[:, 0:L]),
                     reads=[score.b], writes=[work.b])
                for it in range(32):
                    P.op("dve", lambda e, L=L: e.max(m8[:], work[:, 0:L]), reads=[work.b], writes=[m8.b])
                    if it < 31:
                        P.op("dve", lambda e, L=L: e.match_replace(work[:, 0:L], m8[:], work[:, 0:L], NEG),
                             reads=[work.b, m8.b], writes=[work.b])
                P.op("dve", lambda e: e.tensor_reduce(thr[:, 0:1], m8[:], AX.X, ALU.min),
                     reads=[m8.b], writes=[thr.b])
                P.op("dve", lambda e: e.tensor_scalar(thr[:, 0:1], thr[:, 0:1], -1.0e29, None, ALU.max),
                     reads=[thr.b], writes=[thr.b])
            else:
                P.op("dve", lambda e: e.memset(thr[:, 0:1], -1.0e29), writes=[thr.b])
            P.op("dve", lambda e, L=L: e.tensor_scalar(maskb[:, 0:L], score[:, 0:L], thr[:, 0:1], None, ALU.is_ge),
                 reads=[score.b, thr.b], writes=[maskb.b])
            for k0 in range(0, tt + 1, 8):
                nk = min(8, tt + 1 - k0)
                for kk in range(nk):
                    kbk = k0 + kk
                    P.op("pe", lambda e, kk=kk, kbk=kbk: e.transpose(
                        psb[0][:, kk * 128:(kk + 1) * 128], maskb[:, kbk * 128:(kbk + 1) * 128], identb[:]),
                        reads=[maskb.b, identb.b], writes=[psb[0].b])
                P.op("act", lambda e, k0=k0, nk=nk: e.copy(
                    maskT[:, k0:k0 + nk, :], psb[0][:, 0:nk * 128].rearrange("p (k t) -> p k t", t=128)),
                    reads=[psb[0].b], writes=[maskT.b])
            n_it = 0
            for gg in range(4):
                po = ps[gg]
                for kbk in range(tt + 1):
                    ksl = slice(kbk * 128, (kbk + 1) * 128)
                    pt = ps[4 + n_it % 2]
                    e_ = eb[n_it % 2]
                    p_ = pT[n_it % 2]
                    n_it += 1
                    for par in range(2):
                        hs = slice(par * 64, (par + 1) * 64)
                        P.op("pe", lambda e, pt=pt, par=par, hs=hs, gg=gg, ksl=ksl: e.matmul(
                            pt[:, par * 256:(par + 1) * 256], kT2[hs, gg, ksl], qTt[hs, 2 * gg:2 * gg + 2, :],
                            start=True, stop=True),
                            reads=[kT2.b, qTt.b], writes=[pt.b], pos=(1 if par else None))
                    P.op("act", lambda e, pt=pt, e_=e_: e.activation(e_[:], pt[:], AF.Exp, scale=0.125),
                         reads=[pt.b], writes=[e_.b])
                    P.op("dve", lambda e, e_=e_, p_=p_, kbk=kbk: e.tensor_tensor(
                        p_[:].rearrange("p (a t) -> p a t", t=128), e_[:].rearrange("p (a t) -> p a t", t=128),
                        maskT[:, kbk, :].unsqueeze(1).to_broadcast([128, 4, 128]), ALU.mult),
                        reads=[e_.b, maskT.b], writes=[p_.b])
                    for blk in range(4):
                        c0 = blk * 65
                        P.op("pe", lambda e, po=po, c0=c0, p_=p_, blk=blk, tt=tt, kbk=kbk, gg=gg: e.matmul(
                            po[:, c0:c0 + 65], p_[:, blk * 128:(blk + 1) * 128], v1[:, kbk, gg, :],
                            start=(kbk == 0 and blk == 0), stop=(kbk == tt)),
                            reads=[p_.b, v1.b], writes=[po.b])
            for gg in range(4):
                po = ps[gg]
                pv = po[:, 0:260].rearrange("p (k d) -> p k d", d=65)
                P.op("dve", lambda e, pv=pv, gg=gg: e.reciprocal(rden[:, gg * 4:(gg + 1) * 4], pv[:, :, 64]),
                     reads=[po.b], writes=[rden.b])
                zv = zz[:, gg * 256:(gg + 1) * 256].rearrange("p (cp par d) -> p par cp d", cp=2, par=2)
                for par in range(2):
                    P.op("dve", lambda e, pv=pv, zv=zv, par=par, gg=gg: e.tensor_tensor(
                        zv[:, par], pv[:, par * 2:par * 2 + 2, 0:64],
                        rden[:, gg * 4 + par * 2:gg * 4 + par * 2 + 2].unsqueeze(2).to_broadcast([128, 2, 64]),
                        ALU.mult),
                        reads=[po.b, rden.b], writes=[zz.b])
            P.op("pool", lambda e: e.tensor_tensor(zz[:, 0:1024], zz[:, 0:1024], sgt[:], ALU.mult),
                 reads=[zz.b, sgt.b], writes=[zz.b])
            for c in range(8):
                pt = ps[4 + c // 4]
                P.op("pe", lambda e, pt=pt, c=c: e.transpose(
                    pt[:, (c % 4) * 128:(c % 4 + 1) * 128], zz[:, c * 128:(c + 1) * 128], ident[:]),
                    reads=[zz.b, ident.b], writes=[pt.b])
            P.op("act", lambda e: e.copy(zTt[:, 0:4, :], ps[4][:].rearrange("p (c t) -> p c t", t=128)),
                 reads=[ps[4].b], writes=[zTt.b])
            P.op("dve", lambda e: e.tensor_copy(zTt[:, 4:8, :], ps[5][:].rearrange("p (c t) -> p c t", t=128)),
                 reads=[ps[5].b], writes=[zTt.b])
            back_tile(kb, g, b, tt, _Shift(zTt, tt * 128), [zTt.b], wout, x_src, x_dst)
    kb.pop()


def layer_dsa2(kb, g, W, x_src, x_dst):
    nc, P = kb.nc, kb.P
    C = g["consts"]
    kb.push()
    win = kb.sb("dsa_win", [128, 8, 3720], BF16)
    wout = kb.sb("dsa_wout", [128, 8, 1024], BF16)
    begin_load(kb, g)
    compute_mod(kb, g, 0, W)
    load_w(kb, g, win, 0, W["dsa_w_in"][0], 3720)
    load_w(kb, g, wout, 0, W["dsa_w_out"][0], 1024)
    end_load(kb, g)
    ps = g["ps"]
    psb = [kb.psum("psb%d" % i, [128, 1024], BF16) for i in range(2)]
    ident, identb = g["ident"], g["identb"]
    cs32 = kb.sb("cs32", [128, 2, 32], F32)
    cs64 = kb.sb("cs64", [128, 2, 64], F32)
    cmask = kb.sb("cmask", [128, 128], F32)
    P.dma("sp", cmask[:], C["causal"][:, :], writes=[cmask.b])
    gq = kb.sb("gq", [128, 2, 64], F32)
    P.dma("sp", gq[:, 0, :], W["dsa_q_gain"][0:1, :].to_broadcast([128, 64]), writes=[gq.b])
    P.dma("sp", gq[:, 1, :], W["dsa_k_gain"][0:1, :].to_broadcast([128, 64]), writes=[gq.b])
    kT2 = kb.sb("kT2", [128, 4, S], BF16)
    v1 = kb.sb("v1", [128, NT, 4, 65], BF16)
    kiT = kb.sb("kiT", [128, S], BF16)
    kT2b = [Buf("kT2_%d" % i) for i in range(NT)]
    v1b = [Buf("v1_%d" % i) for i in range(NT)]
    kiTb = [Buf("kiT_%d" % i) for i in range(NT)]
    hTt = kb.sb("hTt", [128, 8, 128], BF16)
    big = kb.sb("big", [128, S], F32)
    qf = _Off(big, 0)
    tmpa = _Off(big, 1024)
    work = big
    tmpb = kb.sb("tmpb", [128, 512], F32)
    tmpc = kb.sb("tmpc", [128, 512], F32)
    qb = kb.sb("qb", [128, 1024], BF16)
    kb2 = kb.sb("kb2", [128, 4, 2, 64], BF16)
    qiTt = kb.sb("qiTt", [128, 8, 128], BF16)
    kf = kb.sb("kf", [128, 512], F32)
    wkf = kb.sb("wkf", [128, 136], F32)
    kib = kb.sb("kib", [128, 128], BF16)
    sm = kb.sb("sm", [128, 64], F32)
    score = kb.sb("score", [128, S], F32)
    rl = [kb.sb("rl", [128, 512], F32) for _ in range(2)]
    maskb = kb.sb("maskb", [128, S], BF16)
    m8 = kb.sb("m8", [128, 8], F32)
    thr = kb.sb("thr", [128, 2], F32)
    qTt = [kb.sb("qTz", [128, 16, 128], BF16) for _ in range(2)]
    for i_ in range(2):
        P.op("pool", lambda e, i_=i_: e.memset(qTt[i_][:], 0.0), writes=[qTt[i_].b])
    sgt = [kb.sb("sgt", [128, 1024], BF16) for _ in range(2)]
    maskT = [kb.sb("maskT", [128, NT, 128], BF16) for _ in range(2)]
    eb = [kb.sb("eb", [128, 512], BF16) for _ in range(2)]
    pT = [kb.sb("pT", [128, 512], BF16) for _ in range(2)]
    zz = kb.sb("zzd", [128, 1024], F32)
    zTt = kb.sb("zTt", [128, 8, 128], BF16)
    rden = kb.sb("rden", [128, 16], F32)
    IDXS = 1024.0 ** -0.5
    P.op("pool", lambda e: e.memset(v1[:], 1.0), writes=v1b)
    S1B = (ps[4], ps[5])

    def rms_heads(x, nh, gi, out):
        xv = x[:, 0:nh * 64].rearrange("p (h d) -> p h d", d=64)
        P.op("act", lambda e: e.activation(tmpa[:, 0:nh * 64], x[:, 0:nh * 64], AF.Square),
             reads=[x.b], writes=[tmpa.b])
        P.op("dve", lambda e: e.tensor_reduce(sm[:, 0:nh], tmpa[:, 0:nh * 64].rearrange("p (h d) -> p h d", d=64),
                                              AX.X, ALU.add),
             reads=[tmpa.b], writes=[sm.b])
        P.op("dve", lambda e: e.tensor_scalar(sm[:, 0:nh], sm[:, 0:nh], 1.0 / 64.0, EPS, ALU.mult, ALU.add),
             reads=[sm.b], writes=[sm.b])
        P.op("act", lambda e: e.activation(sm[:, 0:nh], sm[:, 0:nh], AF.Sqrt), reads=[sm.b], writes=[sm.b])
        P.op("dve", lambda e: e.reciprocal(sm[:, 0:nh], sm[:, 0:nh]), reads=[sm.b], writes=[sm.b])
        ov = out[:, 0:nh * 64].rearrange("p (h d) -> p h d", d=64)
        P.op("dve", lambda e: e.tensor_tensor(ov, xv, sm[:, 0:nh].unsqueeze(2).to_broadcast([128, nh, 64]), ALU.mult),
             reads=[x.b, sm.b], writes=[out.b])
        P.op("pool", lambda e: e.tensor_tensor(ov, ov, gq[:, gi, :].unsqueeze(1).to_broadcast([128, nh, 64]), ALU.mult),
             reads=[out.b, gq.b], writes=[out.b])

    def stage1(b, tt):
        hs_ = tt % 2
        qTt_c, sgt_c, maskT_c = qTt[hs_], sgt[hs_], maskT[hs_]
        tsl = slice(tt * 128, (tt + 1) * 128)
        L = (tt + 1) * 128
        for i, nm in enumerate(("cos32", "sin32")):
            P.dma("act", cs32[:, i, :], C[nm][tt * 128:(tt + 1) * 128, :], writes=[cs32.b])
        for i, nm in enumerate(("cos64", "sin64")):
            P.dma("act", cs64[:, i, :], C[nm][tt * 128:(tt + 1) * 128, :], writes=[cs64.b])
        front_tile(kb, g, b, tt, x_src, _Shift(hTt, tt * 128), hTt.b, pbanks=S1B, xi=0)
        yield

        def proj(pt, c0, n):
            for c in range(8):
                P.op("pe", lambda e, c=c: e.matmul(pt[:, 0:n], hTt[:, c, :], win[:, c, c0:c0 + n],
                                                   start=(c == 0), stop=(c == 7)),
                     reads=[hTt.b, win.b], writes=[pt.b])
        for half in range(2):
            proj(S1B[half], half * 512, 512)
            P.op("act", lambda e, half=half: e.copy(qf[:, half * 512:(half + 1) * 512], S1B[half][:]),
                 reads=[S1B[half].b], writes=[qf.b])
        rms_heads(qf, 16, 0, qf)
        qbv = qb[:].rearrange("p (h two d) -> p h two d", two=2, d=32)
        _rope(P, "pool", qf, cs32[:, 0, :], cs32[:, 1, :], [(qbv[:, :, 0, :], qbv[:, :, 1, :])],
              tmpb, tmpc, 16, 32, [qf.b, cs32.b], [qb.b])
        for c in range(8):
            P.op("pe", lambda e, c=c: e.transpose(psb[0][:, c * 128:(c + 1) * 128], qb[:, c * 128:(c + 1) * 128],
                                                  identb[:]),
                 reads=[qb.b, identb.b], writes=[psb[0].b])
        for h2 in range(2):
            hsl = slice(h2 * 64, (h2 + 1) * 64)
            P.op(("act", "dve")[h2], lambda e, h2=h2, hsl=hsl: (e.copy if h2 == 0 else e.tensor_copy)(
                qTt_c[hsl, :, :].rearrange("p (c two) t -> p c two t", two=2)[:, :, h2, :],
                psb[0][hsl, :].rearrange("p (c t) -> p c t", t=128)),
                reads=[psb[0].b], writes=[qTt_c.b])
        yield
        proj(S1B[0], 1024, 512)
        P.op("act", lambda e: e.copy(kf[:], S1B[0][:]), reads=[S1B[0].b], writes=[kf.b])
        P.op("dve", lambda e: e.tensor_copy(
            v1[:, tt, :, 0:64], kf[:, 256:512].rearrange("p (g d) -> p g d", d=64)),
            reads=[kf.b], writes=[v1b[tt]])
        rms_heads(kf, 4, 1, kf)
        k2v = kb2[:].rearrange("p g r (two d) -> p g r two d", two=2)
        _rope(P, "pool", kf, cs32[:, 0, :], cs32[:, 1, :],
              [(k2v[:, :, 0, 0, :], k2v[:, :, 0, 1, :]), (k2v[:, :, 1, 0, :], k2v[:, :, 1, 1, :])],
              tmpb, tmpc, 4, 32, [kf.b, cs32.b], [kb2.b])
        for gg in range(4):
            P.op("pe", lambda e, gg=gg: e.transpose(
                psb[1][:, gg * 128:(gg + 1) * 128], kb2[:, gg].rearrange("p r d -> p (r d)"), identb[:]),
                reads=[kb2.b, identb.b], writes=[psb[1].b])
        P.op("act", lambda e: e.copy(kT2[:, :, tsl], psb[1][:, 0:512].rearrange("p (g t) -> p g t", t=128)),
             reads=[psb[1].b], writes=[kT2b[tt]])
        yield
        for half in range(2):
            proj(S1B[half], 1536 + half * 512, 512)
            P.op("act", lambda e, half=half: e.activation(sgt_c[:, half * 512:(half + 1) * 512], S1B[half][:], AF.Silu),
                 reads=[S1B[half].b], writes=[sgt_c.b])
        yield
        for half in range(2):
            proj(S1B[half], 2560 + half * 512, 512)
            P.op("act", lambda e, half=half: e.copy(qf[:, half * 512:(half + 1) * 512], S1B[half][:]),
                 reads=[S1B[half].b], writes=[qf.b])
        qbv2 = qb[:].rearrange("p (h two d) -> p h two d", two=2, d=64)
        _rope(P, "pool", qf, cs64[:, 0, :], cs64[:, 1, :], [(qbv2[:, :, 0, :], qbv2[:, :, 1, :])],
              tmpb, tmpc, 8, 64, [qf.b, cs64.b], [qb.b])
        for c in range(8):
            P.op("pe", lambda e, c=c: e.transpose(psb[0][:, c * 128:(c + 1) * 128], qb[:, c * 128:(c + 1) * 128],
                                                  identb[:]),
                 reads=[qb.b, identb.b], writes=[psb[0].b])
        P.op("act", lambda e: e.copy(qiTt[:], psb[0][:].rearrange("p (c t) -> p c t", t=128)),
             reads=[psb[0].b], writes=[qiTt.b])
        yield
        proj(S1B[0], 3584, 136)
        P.op("act", lambda e: e.copy(wkf[:], S1B[0][:, 0:136]), reads=[S1B[0].b], writes=[wkf.b])
        P.op("dve", lambda e: e.tensor_scalar(wkf[:, 0:8], wkf[:, 0:8], IDXS, None, ALU.mult),
             reads=[wkf.b], writes=[wkf.b])
        kiv = kib[:].rearrange("p (h two d) -> p h two d", two=2, d=64)
        _rope(P, "pool", _Off(wkf, 8), cs64[:, 0, :], cs64[:, 1, :], [(kiv[:, :, 0, :], kiv[:, :, 1, :])],
              tmpb, tmpc, 1, 64, [wkf.b, cs64.b], [kib.b])
        P.op("pe", lambda e: e.transpose(psb[1][:, 0:128], kib[:], identb[:]),
             reads=[kib.b, identb.b], writes=[psb[1].b])
        P.op("act", lambda e: e.copy(kiT[:, tsl], psb[1][:, 0:128]), reads=[psb[1].b], writes=[kiTb[tt]])
        yield
        n_it = 0
        for k0 in range(0, L, 512):
            w = min(512, L - k0)
            kbufs = kiTb[k0 // 128:(k0 + w) // 128]
            for h in range(8):
                pt = S1B[n_it % 2]
                r_ = rl[n_it % 2]
                n_it += 1
                P.op("pe", lambda e, pt=pt, h=h, k0=k0, w=w: e.matmul(
                    pt[:, 0:w], qiTt[:, h, :], kiT[:, k0:k0 + w], start=True, stop=True),
                    reads=[qiTt.b] + kbufs, writes=[pt.b])
                P.op("act", lambda e, pt=pt, r_=r_, w=w: e.activation(r_[:, 0:w], pt[:, 0:w], AF.Relu),
                     reads=[pt.b], writes=[r_.b])
                if h == 0:
                    P.op("dve", lambda e, r_=r_, k0=k0, w=w, h=h: e.tensor_scalar(
                        score[:, k0:k0 + w], r_[:, 0:w], wkf[:, h:h + 1], None, ALU.mult),
                        reads=[r_.b, wkf.b], writes=[score.b])
                else:
                    P.op("dve", lambda e, r_=r_, k0=k0, w=w, h=h: e.scalar_tensor_tensor(
                        score[:, k0:k0 + w], r_[:, 0:w], wkf[:, h:h + 1], score[:, k0:k0 + w],
                        ALU.mult, ALU.add),
                        reads=[r_.b, wkf.b, score.b], writes=[score.b])
            yield
        P.op("dve", lambda e: e.tensor_tensor(score[:, tsl], score[:, tsl], cmask[:], ALU.add),
             reads=[score.b, cmask.b], writes=[score.b])
        if tt >= 2:
            P.op("pool", lambda e: e.tensor_copy(work[:, 0:L], score[:, 0:L]),
                 reads=[score.b], writes=[work.b])
            for it in range(32):
                P.op("dve", lambda e: e.max(m8[:], work[:, 0:L]), reads=[work.b], writes=[m8.b])
                if it < 31:
                    P.op("dve", lambda e: e.match_replace(work[:, 0:L], m8[:], work[:, 0:L], NEG),
                         reads=[work.b, m8.b], writes=[work.b])
                yield
            P.op("dve", lambda e: e.tensor_reduce(thr[:, 0:1], m8[:], AX.X, ALU.min),
                 reads=[m8.b], writes=[thr.b])
            P.op("dve", lambda e: e.tensor_scalar(thr[:, 0:1], thr[:, 0:1], -1.0e29, None, ALU.max),
                 reads=[thr.b], writes=[thr.b])
        else:
            P.op("dve", lambda e: e.memset(thr[:, 0:1], -1.0e29), writes=[thr.b])
        P.op("dve", lambda e: e.tensor_scalar(maskb[:, 0:L], score[:, 0:L], thr[:, 0:1], None, ALU.is_ge),
             reads=[score.b, thr.b], writes=[maskb.b])
        for k0 in range(0, tt + 1, 8):
            nk = min(8, tt + 1 - k0)
            for kk in range(nk):
                kbk = k0 + kk
                P.op("pe", lambda e, kk=kk, kbk=kbk: e.transpose(
                    psb[0][:, kk * 128:(kk + 1) * 128], maskb[:, kbk * 128:(kbk + 1) * 128], identb[:]),
                    reads=[maskb.b, identb.b], writes=[psb[0].b])
            P.op("act", lambda e, k0=k0, nk=nk: e.copy(
                maskT_c[:, k0:k0 + nk, :], psb[0][:, 0:nk * 128].rearrange("p (k t) -> p k t", t=128)),
                reads=[psb[0].b], writes=[maskT_c.b])
        yield

    def stage2(b, tt):
        hs_ = tt % 2
        qTt_c, sgt_c, maskT_c = qTt[hs_], sgt[hs_], maskT[hs_]
        n_it = 0
        for gg in range(4):
            for kbk in range(tt + 1):
                ksl = slice(kbk * 128, (kbk + 1) * 128)
                pt = ps[3]
                e_ = eb[n_it % 2]
                p_ = pT[n_it % 2]
                n_it += 1
                P.op("pe", lambda e, gg=gg, ksl=ksl: e.matmul(
                    pt[:], kT2[:, gg, ksl], qTt_c[:, 4 * gg:4 * gg + 4, :], start=True, stop=True),
                    reads=[kT2b[kbk], qTt_c.b], writes=[pt.b])
                P.op("act", lambda e, e_=e_: e.activation(e_[:], pt[:], AF.Exp, scale=0.125),
                     reads=[pt.b], writes=[e_.b])
                P.op("dve", lambda e, e_=e_, p_=p_, kbk=kbk: e.tensor_tensor(
                    p_[:].rearrange("p (a t) -> p a t", t=128), e_[:].rearrange("p (a t) -> p a t", t=128),
                    maskT_c[:, kbk, :].unsqueeze(1).to_broadcast([128, 4, 128]), ALU.mult),
                    reads=[e_.b, maskT_c.b], writes=[p_.b])
                for blk in range(4):
                    hidx = gg * 4 + blk
                    po = ps[hidx // 7]
                    c0 = (hidx % 7) * 65
                    P.op("pe", lambda e, po=po, c0=c0, p_=p_, blk=blk, kbk=kbk, gg=gg, hidx=hidx: e.matmul(
                        po[:, c0:c0 + 65], p_[:, blk * 128:(blk + 1) * 128], v1[:, kbk, gg, :],
                        start=(kbk == 0 and hidx % 7 == 0), stop=(kbk == tt)),
                        reads=[p_.b, v1b[kbk]], writes=[po.b])
                yield
        for bk in range(3):
            nh = 7 if bk < 2 else 2
            pv = ps[bk][:, 0:nh * 65].rearrange("p (k d) -> p k d", d=65)
            P.op("dve", lambda e, pv=pv, bk=bk, nh=nh: e.reciprocal(rden[:, bk * 7:bk * 7 + nh], pv[:, :, 64]),
                 reads=[ps[bk].b], writes=[rden.b])
        for hidx in range(16):
            gg, blk = hidx // 4, hidx % 4
            head = hidx
            po = ps[hidx // 7]
            c0 = (hidx % 7) * 65
            P.op("dve", lambda e, po=po, c0=c0, head=head, hidx=hidx: e.tensor_scalar(
                zz[:, head * 64:(head + 1) * 64], po[:, c0:c0 + 64], rden[:, hidx:hidx + 1], None, ALU.mult),
                reads=[po.b, rden.b], writes=[zz.b])
        P.op("pool", lambda e: e.tensor_tensor(zz[:], zz[:], sgt_c[:], ALU.mult),
             reads=[zz.b, sgt_c.b], writes=[zz.b])
        yield
        for c in range(8):
            pt2 = ps[c // 4]
            P.op("pe", lambda e, pt2=pt2, c=c: e.transpose(
                pt2[:, (c % 4) * 128:(c % 4 + 1) * 128], zz[:, c * 128:(c + 1) * 128], ident[:]),
                reads=[zz.b, ident.b], writes=[pt2.b])
        P.op("act", lambda e: e.copy(zTt[:, 0:4, :], ps[0][:].rearrange("p (c t) -> p c t", t=128)),
             reads=[ps[0].b], writes=[zTt.b])
        P.op("dve", lambda e: e.tensor_copy(zTt[:, 4:8, :], ps[1][:].rearrange("p (c t) -> p c t", t=128)),
             reads=[ps[1].b], writes=[zTt.b])
        back_tile(kb, g, b, tt, _Shift(zTt, tt * 128), [zTt.b], wout, x_src, x_dst, pbanks=(ps[2], ps[3]), xi=1)
        yield

    def n_seg1(tt):
        return 7 + (tt + 4) // 4 + (32 if tt >= 2 else 0) + 1

    def n_seg2(tt):
        return 4 * (tt + 1) + 2

    for b in range(NB):
        for step in range(NT + 1):
            s2 = stage2(b, step - 1) if step >= 1 else None
            s1 = stage1(b, step) if step < NT else None
            if s1 is None or s2 is None:
                for s_ in (s1, s2):
                    if s_ is not None:
                        for _ in s_:
                            pass
                continue
            n1, n2 = n_seg1(step), n_seg2(step - 1)
            a1 = a2 = 0.0
            d1 = d2 = False
            while not (d1 and d2):
                if not d1 and (d2 or a1 / n1 <= a2 / n2):
                    try:
                        next(s1)
                        a1 += 1
                    except StopIteration:
                        d1 = True
                else:
                    try:
                        next(s2)
                        a2 += 1
                    except StopIteration:
                        d2 = True
    kb.pop()


class _Off:
    def __init__(self, t, off):
        self.t, self.off, self.b = t, off, t.b

    def __getitem__(self, k):
        a, sl = k
        return self.t[a, sl.start + self.off:sl.stop + self.off]


def dsa_consts():
    c = {}
    pos = np.arange(S, dtype=np.float32)[:, None]
    for half, nm in ((32, "32"), (64, "64")):
        inv = (10000.0 ** (-np.arange(half, dtype=np.float32) / half)).astype(np.float32)
        ang = (pos * inv[None, :]).astype(np.float32)
        c["cos" + nm] = np.cos(ang).astype(np.float32)
        c["sin" + nm] = np.sin(ang).astype(np.float32)
    t = np.arange(128)[:, None]
    s_ = np.arange(128)[None, :]
    c["causal"] = np.where(s_ <= t, 0.0, NEG).astype(np.float32)
    return c


LAYER_FNS[0] = layer_dsa2


def layer_rwkv(kb, g, W, x_src, x_dst):
    nc, P = kb.nc, kb.P
    ps = g["ps"]
    ident = g["ident"]
    kb.push()
    wout = kb.sb("rw_wout", [128, 8, 1024], BF16)
    names = ["R", "Wd", "K", "A", "B", "V", "Y"]
    if "rw_scr" not in g:
        g["rw_scr"] = {n: kb.dram("rw_" + n, [NB, 16, S, 64]) for n in names}
        g["rw_scr"]["SG"] = kb.dram("rw_SG", [NB, S, 1024])
        if DEBUG_OUT:
            for nm in ("D1", "D2", "D3"):
                g["rw_scr"][nm] = kb.dram("rw_" + nm, [NB, S, 1024])
        g["rw_scr"]["BON"] = kb.dram("rw_BON", [NB, S, 1024])
    scr = g["rw_scr"]

    def tok_view(n, b, tt):
        return scr[n].t[b].rearrange("h t j -> t h j")[tt * 128:(tt + 1) * 128]

    def bc_row(dst, src_row):
        P.dma("sp", dst[:], src_row.to_broadcast([128, 1024]), writes=[dst.b])

    kb.push()
    win = kb.sb("rw_win", [128, 4, 8, 1024], BF16)
    w1a1 = kb.sb("rw_w1a1", [128, 8, 128], BF16)
    w2e = [kb.sb("rw_w2e", [65, 1024], BF16) for _ in range(2)]
    muT = kb.sb("rw_mu", [128, 6, 8], F32)
    kkb = kb.sb("rw_kk", [128, 1024], F32)
    kab = kb.sb("rw_ka", [128, 1024], F32)
    rkb = kb.sb("rw_rk", [128, 1024], F32)
    begin_load(kb, g)
    compute_mod(kb, g, 2, W)
    for n in range(4):
        load_w(kb, g, _W4(win, n), 0, W["rwkv_w_in"][0, n], 1024)
    load_w(kb, g, wout, 0, W["rwkv_w_out"][0], 1024)
    load_w(kb, g, w1a1, 0, W["rwkv_w1"][0], 64)
    load_w(kb, g, w1a1, 64, W["rwkv_a1"][0], 64)
    for i, (m2_, m0_) in enumerate((("rwkv_w2", "rwkv_w0"), ("rwkv_a2", "rwkv_a0"))):
        stg = g["stage"][g["stage_n"] % 2]
        g["stage_n"] += 1
        sv = stg[:].rearrange("p k n -> p (k n)")
        P.dma("sp", sv[0:64, 0:1024], W[m2_][0], writes=[stg.b])
        P.dma("sp", sv[64:65, 0:1024], W[m0_][0:1, :], writes=[stg.b])
        P.op("dve", lambda e, i=i, sv=sv: e.tensor_copy(w2e[i][:], sv[0:65, 0:1024]),
             reads=[stg.b], writes=[w2e[i].b])
    end_load(kb, g)
    for n in range(6):
        P.dma("sp", muT[:, n, :], W["rwkv_mu"][0, n].rearrange("(k p) -> p k", p=128),
              writes=[muT.b], allow_slow_non_contiguous=True)
    bc_row(kkb, W["rwkv_k_k"][0:1, :])
    bc_row(kab, W["rwkv_k_a"][0:1, :])
    bc_row(rkb, W["rwkv_r_k"][0].rearrange("h j -> (h j)").unsqueeze(0))
    TB = 256
    NTL = TB // 128
    hTb = kb.sb("rw_hT", [128, 8, 1 + TB], BF16)
    dT = kb.sb("rw_dT", [128, 8, TB], BF16)
    xsT = kb.sb("rw_xsT", [128, 8, TB], BF16)
    Rt = kb.sb("rw_R", [128, NTL, 1024], F32)
    pad_ = kb.sb("rw_pad", [128, 256], F32)
    Kt = kb.sb("rw_K", [128, NTL, 1024], F32)
    Vt = kb.sb("rw_V", [128, NTL, 1024], F32)
    SGt = kb.sb("rw_SG", [128, NTL, 1024], F32)
    Wdt = kb.sb("rw_Wd", [128, 1024], F32)
    Ast = kb.sb("rw_As", [128, 1024], F32)
    t1e = [kb.sb("rw_t1e", [65, TB], BF16) for _ in range(2)]
    tm1 = kb.sb("rw_tm1", [128, 1024], F32)
    tm2 = kb.sb("rw_tm2", [128, 1024], F32)
    sm = kb.sb("rw_sm", [128, 32], F32)
    for i in range(2):
        P.op("dve", lambda e, i=i: e.memset(t1e[i][:], 1.0), writes=[t1e[i].b])
    hd = lambda ap: ap.rearrange("p (h j) -> p h j", j=64)
    bcj = lambda ap: ap.unsqueeze(2).to_broadcast([128, 16, 64])
    for b in range(NB):
        P.op("dve", lambda e: e.memset(hTb[:, :, 0:1], 0.0), writes=[hTb.b])
        for tb in range(S // TB):
            for tl in range(NTL):
                front_tile(kb, g, b, tb * NTL + tl, x_src, _Shift(hTb, tb * TB - 1), hTb.b)
            P.op("dve", lambda e: e.tensor_tensor(dT[:], hTb[:, :, 0:TB], hTb[:, :, 1:TB + 1], ALU.subtract),
                 reads=[hTb.b], writes=[dT.b])

            def make_xs(n):
                for c in range(8):
                    P.op("dve", lambda e, c=c, n=n: e.scalar_tensor_tensor(
                        xsT[:, c, :], dT[:, c, :], muT[:, n, c:c + 1], hTb[:, c, 1:TB + 1], ALU.mult, ALU.add),
                        reads=[dT.b, muT.b, hTb.b], writes=[xsT.b])
            for n, dst in ((0, Rt), (1, Kt), (2, Vt), (3, SGt)):
                make_xs(n)
                for tl in range(NTL):
                    for half in range(2):
                        pt = ps[2 + half + 2 * (n % 2)]
                        for c in range(8):
                            P.op("pe", lambda e, pt=pt, c=c, tl=tl, half=half, n=n: e.matmul(
                                pt[:], xsT[:, c, tl * 128:(tl + 1) * 128], win[:, n, c, half * 512:(half + 1) * 512],
                                start=(c == 0), stop=(c == 7)),
                                reads=[xsT.b, win.b], writes=[pt.b])
                        if n == 3:
                            P.op("act", lambda e, pt=pt, tl=tl, half=half, dst=dst: e.activation(
                                dst[:, tl, half * 512:(half + 1) * 512], pt[:], AF.Silu),
                                reads=[pt.b], writes=[dst.b])
                        else:
                            P.op("act", lambda e, pt=pt, tl=tl, half=half, dst=dst: e.copy(
                                dst[:, tl, half * 512:(half + 1) * 512], pt[:]),
                                reads=[pt.b], writes=[dst.b])
            for i, n in ((0, 4), (1, 5)):
                make_xs(n)
                for c in range(8):
                    P.op("pe", lambda e, c=c, i=i: e.matmul(
                        ps[4][0:64, 0:TB], w1a1[:, c, i * 64:(i + 1) * 64], xsT[:, c, :],
                        start=(c == 0), stop=(c == 7)),
                        reads=[w1a1.b, xsT.b], writes=[ps[4].b], pos="T0")
                if i == 0:
                    P.op("act", lambda e, i=i: e.activation(t1e[i][0:64, :], ps[4][0:64, 0:TB], AF.Tanh),
                         reads=[ps[4].b], writes=[t1e[i].b])
                else:
                    P.op("act", lambda e, i=i: e.copy(t1e[i][0:64, :], ps[4][0:64, 0:TB]),
                         reads=[ps[4].b], writes=[t1e[i].b])
            for tl in range(NTL):
                tt = tb * NTL + tl
                R_, K_, V_ = Rt[:, tl, :], Kt[:, tl, :], Vt[:, tl, :]
                for i, dst in ((0, Wdt), (1, Ast)):
                    for half in range(2):
                        pt = ps[2 + half]
                        P.op("pe", lambda e, pt=pt, i=i, tl=tl, half=half: e.matmul(
                            pt[:], t1e[i][:, tl * 128:(tl + 1) * 128], w2e[i][:, half * 512:(half + 1) * 512],
                            start=True, stop=True),
                            reads=[t1e[i].b, w2e[i].b], writes=[pt.b], pos="K65")
                        P.op("act", lambda e, pt=pt, half=half, dst=dst: e.activation(
                            dst[:, half * 512:(half + 1) * 512], pt[:], AF.Sigmoid),
                            reads=[pt.b], writes=[dst.b])
                P.op("act", lambda e: e.activation(Wdt[:], Wdt[:], AF.Exp, scale=-0.6065306597126334),
                     reads=[Wdt.b], writes=[Wdt.b])
                if DEBUG_OUT:
                    P.dma("sp", scr["D1"].t[b, tt * 128:(tt + 1) * 128, :], K_, reads=[Kt.b], writes=[scr["D1"].b])
                    P.dma("sp", scr["D2"].t[b, tt * 128:(tt + 1) * 128, :], Ast[:], reads=[Ast.b], writes=[scr["D2"].b])
                    P.dma("sp", scr["D3"].t[b, tt * 128:(tt + 1) * 128, :], kkb[:], reads=[kkb.b], writes=[scr["D3"].b])
                P.op("dve", lambda e, K_=K_: e.tensor_tensor(tm1[:], K_, kkb[:], ALU.mult),
                     reads=[Kt.b, kkb.b], writes=[tm1.b])
                P.op("act", lambda e: e.activation(tm2[:], tm1[:], AF.Square), reads=[tm1.b], writes=[tm2.b])
                P.op("dve", lambda e: e.tensor_reduce(sm[:, 0:16], hd(tm2[:]), AX.X, ALU.add),
                     reads=[tm2.b], writes=[sm.b])
                P.op("act", lambda e: e.activation(sm[:, 0:16], sm[:, 0:16], AF.Sqrt), reads=[sm.b], writes=[sm.b])
                P.op("dve", lambda e: e.tensor_scalar(sm[:, 0:16], sm[:, 0:16], 1e-12, None, ALU.max),
                     reads=[sm.b], writes=[sm.b])
                P.op("dve", lambda e: e.reciprocal(sm[:, 0:16], sm[:, 0:16]), reads=[sm.b], writes=[sm.b])
                P.op("dve", lambda e: e.tensor_tensor(hd(tm1[:]), hd(tm1[:]), bcj(sm[:, 0:16]), ALU.mult),
                     reads=[tm1.b, sm.b], writes=[tm1.b])
                P.op("pool", lambda e: e.tensor_tensor(tm2[:], tm1[:], Ast[:], ALU.mult),
                     reads=[tm1.b, Ast.b], writes=[tm2.b])
                P.dma("sp", tok_view("B", b, tt), hd(tm2[:]), reads=[tm2.b], writes=[scr["B"].b])
                P.op("dve", lambda e: e.tensor_scalar(tm1[:], tm1[:], -1.0, None, ALU.mult),
                     reads=[tm1.b], writes=[tm1.b])
                P.dma("sp", tok_view("A", b, tt), hd(tm1[:]), reads=[tm1.b], writes=[scr["A"].b])
                P.dma("act", tok_view("Wd", b, tt), hd(Wdt[:]), reads=[Wdt.b], writes=[scr["Wd"].b])
                P.dma("act", tok_view("R", b, tt), hd(R_), reads=[Rt.b], writes=[scr["R"].b])
                P.dma("act", tok_view("V", b, tt), hd(V_), reads=[Vt.b], writes=[scr["V"].b])
                P.op("dve", lambda e: e.scalar_tensor_tensor(tm2[:], Ast[:], -1.0, kab[:], ALU.add, ALU.mult),
                     reads=[Ast.b, kab.b, tm2.b], writes=[tm2.b])
                P.op("dve", lambda e, K_=K_: e.scalar_tensor_tensor(tm2[:], tm2[:], 1.0, K_, ALU.add, ALU.mult),
                     reads=[tm2.b, Kt.b], writes=[tm2.b])
                P.dma("sp", tok_view("K", b, tt), hd(tm2[:]), reads=[tm2.b], writes=[scr["K"].b])
                P.op("pool", lambda e, R_=R_: e.tensor_tensor(tm1[:], tm2[:], R_, ALU.mult),
                     reads=[tm2.b, Rt.b, tm1.b], writes=[tm1.b])
                P.op("pool", lambda e: e.tensor_tensor(tm1[:], tm1[:], rkb[:], ALU.mult),
                     reads=[tm1.b, rkb.b], writes=[tm1.b])
                P.op("dve", lambda e: e.tensor_reduce(sm[:, 16:32], hd(tm1[:]), AX.X, ALU.add),
                     reads=[tm1.b], writes=[sm.b])
                P.op("dve", lambda e, V_=V_: e.tensor_tensor(hd(tm1[:]), hd(V_), bcj(sm[:, 16:32]), ALU.mult),
                     reads=[Vt.b, sm.b, tm1.b], writes=[tm1.b])
                P.dma("sp", scr["BON"].t[b, tt * 128:(tt + 1) * 128, :], tm1[:], reads=[tm1.b], writes=[scr["BON"].b])
                P.dma("act", scr["SG"].t[b, tt * 128:(tt + 1) * 128, :], SGt[:, tl, :], reads=[SGt.b], writes=[scr["SG"].b])
            P.op("dve", lambda e: e.tensor_copy(hTb[:, :, 0:1], hTb[:, :, TB:TB + 1]),
                 reads=[hTb.b], writes=[hTb.b])
    kb.pop()

    kb.push()
    TC = 32
    blk = {n: [kb.sb("rb_" + n, [128, TC, 64], F32) for _ in range(2)] for n in ("Wd", "A", "B", "K", "R")}
    Vb = [kb.sb("rb_V", [128, TC, 16], F32) for _ in range(2)]
    Yb = [kb.sb("rb_Y", [128, TC, 16], F32) for _ in range(2)]
    St = [kb.sb("rb_S", [128, 16, 64], F32) for _ in range(2)]
    t1 = kb.sb("rb_t1", [128, 16, 64], F32)
    t2 = kb.sb("rb_t2", [128, 16, 64], F32)
    t3 = kb.sb("rb_t3", [128, 16, 64], F32)
    t4 = kb.sb("rb_t4", [128, 16, 64], F32)
    sa = kb.sb("rb_sa", [128, 16], F32)
    P.op("dve", lambda e: e.memset(St[0][:], 0.0), writes=[St[0].b])
    qs = ("sp", "act")
    step = 0
    for tbk in range(S // TC):
        t0 = tbk * TC
        i2 = tbk % 2
        nq = 0
        for n in ("Wd", "A", "B", "K", "R"):
            src = scr[n].t.rearrange("b h t j -> (b h) t j")[:, t0:t0 + TC, :]
            for iq in range(4):
                P.dma(qs[nq % 2], blk[n][i2][iq * 32:(iq + 1) * 32, :, :], src,
                      reads=[scr[n].b], writes=[blk[n][i2].b])
                nq += 1
        for iq in range(4):
            src = scr["V"].t.rearrange("b h t j -> (b h) t j")[:, t0:t0 + TC, iq * 16:(iq + 1) * 16]
            P.dma(qs[nq % 2], Vb[i2][iq * 32:(iq + 1) * 32, :, :], src, reads=[scr["V"].b], writes=[Vb[i2].b])
            nq += 1
        for t in range(TC):
            So, Sn = St[step % 2], St[(step + 1) % 2]
            step += 1
            bcr = lambda n, t=t: blk[n][i2][:, t, :].unsqueeze(1).to_broadcast([128, 16, 64])
            a_bc, w_bc, b_bc, k_bc, r_bc = bcr("A"), bcr("Wd"), bcr("B"), bcr("K"), bcr("R")
            v_bc = Vb[i2][:, t, :].unsqueeze(2).to_broadcast([128, 16, 64])
            P.op("dve", lambda e, So=So, a_bc=a_bc: e.tensor_tensor(t1[:], So[:], a_bc, ALU.mult),
                 reads=[So.b, blk["A"][i2].b], writes=[t1.b])
            P.op("dve", lambda e: e.tensor_reduce(sa[:], t1[:], AX.X, ALU.add), reads=[t1.b], writes=[sa.b])
            P.op("pool", lambda e, So=So, w_bc=w_bc: e.tensor_tensor(t2[:], So[:], w_bc, ALU.mult),
                 reads=[So.b, blk["Wd"][i2].b], writes=[t2.b])
            P.op("pool", lambda e, v_bc=v_bc, k_bc=k_bc: e.tensor_tensor(t3[:], v_bc, k_bc, ALU.mult),
                 reads=[Vb[i2].b, blk["K"][i2].b], writes=[t3.b])
            P.op("dve", lambda e, b_bc=b_bc: e.tensor_tensor(
                t1[:], sa[:].unsqueeze(2).to_broadcast([128, 16, 64]), b_bc, ALU.mult),
                reads=[sa.b, blk["B"][i2].b], writes=[t1.b])
            P.op("dve", lambda e: e.tensor_tensor(t1[:], t1[:], t3[:], ALU.add),
                 reads=[t1.b, t3.b], writes=[t1.b])
            P.op("dve", lambda e, Sn=Sn: e.tensor_tensor(Sn[:], t1[:], t2[:], ALU.add),
                 reads=[t1.b, t2.b], writes=[Sn.b])
            P.op("pool", lambda e, Sn=Sn, r_bc=r_bc: e.tensor_tensor(t4[:], Sn[:], r_bc, ALU.mult),
                 reads=[Sn.b, blk["R"][i2].b], writes=[t4.b])
            y_ap = Yb[i2][:, t, :]
            P.op("dve", lambda e, y_ap=y_ap: e.tensor_reduce(y_ap, t4[:], AX.X, ALU.add),
                 reads=[t4.b], writes=[Yb[i2].b])
        for iq in range(4):
            dst = scr["Y"].t.rearrange("b h t j -> (b h) t j")[:, t0:t0 + TC, iq * 16:(iq + 1) * 16]
            P.dma(qs[iq % 2], dst, Yb[i2][iq * 32:(iq + 1) * 32, :, :], reads=[Yb[i2].b], writes=[scr["Y"].b])
    kb.pop()

    kb.push()
    lnw = kb.sb("rw_lnw", [128, 1024], F32)
    lnb = kb.sb("rw_lnb", [128, 1024], F32)
    bc_row(lnw, W["rwkv_ln_w"][0:1, :])
    bc_row(lnb, W["rwkv_ln_b"][0:1, :])
    yt = kb.sb("rc_y", [128, 1024], F32)
    y2 = kb.sb("rc_y2", [128, 1024], F32)
    bon = kb.sb("rc_bon", [128, 1024], F32)
    sgc = kb.sb("rc_sg", [128, 1024], F32)
    smc = kb.sb("rc_sm", [128, 32], F32)
    zTt = kb.sb("rc_zT", [128, 8, 128], BF16)
    for b in range(NB):
        for tt in range(NT):
            P.dma("sp", hd(yt[:]), tok_view("Y", b, tt), reads=[scr["Y"].b], writes=[yt.b])
            P.dma("act", bon[:], scr["BON"].t[b, tt * 128:(tt + 1) * 128, :], reads=[scr["BON"].b], writes=[bon.b])
            P.dma("act", sgc[:], scr["SG"].t[b, tt * 128:(tt + 1) * 128, :], reads=[scr["SG"].b], writes=[sgc.b])
            P.op("dve", lambda e: e.tensor_reduce(smc[:, 0:16], hd(yt[:]), AX.X, ALU.add), reads=[yt.b], writes=[smc.b])
            P.op("dve", lambda e: e.tensor_scalar(smc[:, 0:16], smc[:, 0:16], -1.0 / 64.0, None, ALU.mult),
                 reads=[smc.b], writes=[smc.b])
            P.op("dve", lambda e: e.tensor_tensor(hd(yt[:]), hd(yt[:]), bcj(smc[:, 0:16]), ALU.add),
                 reads=[yt.b, smc.b], writes=[yt.b])
            P.op("act", lambda e: e.activation(y2[:], yt[:], AF.Square), reads=[yt.b], writes=[y2.b])
            P.op("dve", lambda e: e.tensor_reduce(smc[:, 16:32], hd(y2[:]), AX.X, ALU.add), reads=[y2.b], writes=[smc.b])
            P.op("dve", lambda e: e.tensor_scalar(smc[:, 16:32], smc[:, 16:32], 1.0 / 64.0, 64e-5, ALU.mult, ALU.add),
                 reads=[smc.b], writes=[smc.b])
            P.op("act", lambda e: e.activation(smc[:, 16:32], smc[:, 16:32], AF.Sqrt), reads=[smc.b], writes=[smc.b])
            P.op("dve", lambda e: e.reciprocal(smc[:, 16:32], smc[:, 16:32]), reads=[smc.b], writes=[smc.b])
            P.op("dve", lambda e: e.tensor_tensor(hd(yt[:]), hd(yt[:]), bcj(smc[:, 16:32]), ALU.mult),
                 reads=[yt.b, smc.b], writes=[yt.b])
            P.op("pool", lambda e: e.tensor_tensor(yt[:], yt[:], lnw[:], ALU.mult), reads=[yt.b, lnw.b], writes=[yt.b])
            P.op("pool", lambda e: e.tensor_tensor(yt[:], yt[:], lnb[:], ALU.add), reads=[yt.b, lnb.b], writes=[yt.b])
            P.op("dve", lambda e: e.tensor_tensor(yt[:], yt[:], bon[:], ALU.add), reads=[yt.b, bon.b], writes=[yt.b])
            P.op("dve", lambda e: e.tensor_tensor(yt[:], yt[:], sgc[:], ALU.mult), reads=[yt.b, sgc.b], writes=[yt.b])
            for c in range(8):
                pt = ps[4 + c // 4]
                P.op("pe", lambda e, pt=pt, c=c: e.transpose(
                    pt[:, (c % 4) * 128:(c % 4 + 1) * 128], yt[:, c * 128:(c + 1) * 128], ident[:]),
                    reads=[yt.b, ident.b], writes=[pt.b])
            P.op("act", lambda e: e.copy(zTt[:, 0:4, :], ps[4][:].rearrange("p (c t) -> p c t", t=128)),
                 reads=[ps[4].b], writes=[zTt.b])
            P.op("dve", lambda e: e.tensor_copy(zTt[:, 4:8, :], ps[5][:].rearrange("p (c t) -> p c t", t=128)),
                 reads=[ps[5].b], writes=[zTt.b])
            back_tile(kb, g, b, tt, _Shift(zTt, tt * 128), [zTt.b], wout, x_src, x_dst)
    kb.pop()
    kb.pop()


def layer_rwkv2(kb, g, W, x_src, x_dst):
    nc, P = kb.nc, kb.P
    ps = g["ps"]
    ident = g["ident"]
    kb.push()
    wout = kb.sb("rw_wout", [128, 8, 1024], BF16)
    names = ["R", "Wd", "K", "A", "B", "V", "Y"]
    if "rw_scr" not in g:
        g["rw_scr"] = {n: kb.dram("rw_" + n, [NB, S, 1024]) for n in names}
        g["rw_scr"]["SG"] = kb.dram("rw_SG", [NB, S, 1024])
        if DEBUG_OUT:
            for nm in ("D1", "D2", "D3"):
                g["rw_scr"][nm] = kb.dram("rw_" + nm, [NB, S, 1024])
        g["rw_scr"]["BON"] = kb.dram("rw_BON", [NB, S, 1024])
    scr = g["rw_scr"]

    def tok_view(n, b, tt):
        return scr[n].t[b, tt * 128:(tt + 1) * 128, :]

    def bc_row(dst, src_row):
        P.dma("sp", dst[:], src_row.to_broadcast([128, 1024]), writes=[dst.b])

    kb.push()
    win = kb.sb("rw_win", [128, 4, 8, 1024], BF16)
    w1a1 = kb.sb("rw_w1a1", [128, 8, 128], BF16)
    w2e = [kb.sb("rw_w2e", [65, 1024], BF16) for _ in range(2)]
    muT = kb.sb("rw_mu", [128, 6, 8], F32)
    kkb = kb.sb("rw_kk", [128, 1024], F32)
    kab = kb.sb("rw_ka", [128, 1024], F32)
    rkb = kb.sb("rw_rk", [128, 1024], F32)
    begin_load(kb, g)
    compute_mod(kb, g, 2, W)
    for n in range(4):
        load_w(kb, g, _W4(win, n), 0, W["rwkv_w_in"][0, n], 1024)
    load_w(kb, g, wout, 0, W["rwkv_w_out"][0], 1024)
    load_w(kb, g, w1a1, 0, W["rwkv_w1"][0], 64)
    load_w(kb, g, w1a1, 64, W["rwkv_a1"][0], 64)
    for i, (m2_, m0_) in enumerate((("rwkv_w2", "rwkv_w0"), ("rwkv_a2", "rwkv_a0"))):
        stg = g["stage"][g["stage_n"] % 2]
        g["stage_n"] += 1
        sv = stg[:].rearrange("p k n -> p (k n)")
        P.dma("sp", sv[0:64, 0:1024], W[m2_][0], writes=[stg.b])
        P.dma("sp", sv[64:65, 0:1024], W[m0_][0:1, :], writes=[stg.b])
        P.op("dve", lambda e, i=i, sv=sv: e.tensor_copy(w2e[i][:], sv[0:65, 0:1024]),
             reads=[stg.b], writes=[w2e[i].b])
    end_load(kb, g)
    for n in range(6):
        P.dma("sp", muT[:, n, :], W["rwkv_mu"][0, n].rearrange("(k p) -> p k", p=128),
              writes=[muT.b], allow_slow_non_contiguous=True)
    bc_row(kkb, W["rwkv_k_k"][0:1, :])
    bc_row(kab, W["rwkv_k_a"][0:1, :])
    bc_row(rkb, W["rwkv_r_k"][0].rearrange("h j -> (h j)").unsqueeze(0))
    TB = 256
    NTL = TB // 128
    hTb = kb.sb("rw_hT", [128, 8, 1 + TB], BF16)
    dT = kb.sb("rw_dT", [128, 8, TB], BF16)
    xsTs = [kb.sb("rw_xsT", [128, 8, TB], BF16) for _ in range(2)]
    Rt = kb.sb("rw_R", [128, NTL, 1024], F32)
    pad_ = kb.sb("rw_pad", [128, 256], F32)
    Kt = kb.sb("rw_K", [128, NTL, 1024], F32)
    Vt = kb.sb("rw_V", [128, NTL, 1024], F32)
    SGt = kb.sb("rw_SG", [128, NTL, 1024], F32)
    Wdt = kb.sb("rw_Wd", [128, 1024], F32)
    Ast = kb.sb("rw_As", [128, 1024], F32)
    t1e = [kb.sb("rw_t1e", [65, TB], BF16) for _ in range(2)]
    tm1 = kb.sb("rw_tm1", [128, 1024], F32)
    tm2 = kb.sb("rw_tm2", [128, 1024], F32)
    sm = kb.sb("rw_sm", [128, 32], F32)
    for i in range(2):
        P.op("dve", lambda e, i=i: e.memset(t1e[i][:], 1.0), writes=[t1e[i].b])
    hd = lambda ap: ap.rearrange("p (h j) -> p h j", j=64)
    bcj = lambda ap: ap.unsqueeze(2).to_broadcast([128, 16, 64])
    for b in range(NB):
        P.op("dve", lambda e: e.memset(hTb[:, :, 0:1], 0.0), writes=[hTb.b])
        for tb in range(S // TB):
            for tl in range(NTL):
                front_tile(kb, g, b, tb * NTL + tl, x_src, _Shift(hTb, tb * TB - 1), hTb.b)
            P.op("dve", lambda e: e.tensor_tensor(dT[:], hTb[:, :, 0:TB], hTb[:, :, 1:TB + 1], ALU.subtract),
                 reads=[hTb.b], writes=[dT.b])

            def make_xs(n):
                xsT = xsTs[n % 2]
                for c in range(8):
                    P.op("dve", lambda e, c=c, n=n: e.scalar_tensor_tensor(
                        xsT[:, c, :], dT[:, c, :], muT[:, n, c:c + 1], hTb[:, c, 1:TB + 1], ALU.mult, ALU.add),
                        reads=[dT.b, muT.b, hTb.b], writes=[xsT.b])
            for n, dst in ((0, Rt), (1, Kt), (2, Vt), (3, SGt)):
                make_xs(n)
                xsT = xsTs[n % 2]
                for tl in range(NTL):
                    for half in range(2):
                        pt = ps[2 + half + 2 * (n % 2)]
                        for c in range(8):
                            P.op("pe", lambda e, pt=pt, c=c, tl=tl, half=half, n=n, xsT=xsT: e.matmul(
                                pt[:], xsT[:, c, tl * 128:(tl + 1) * 128], win[:, n, c, half * 512:(half + 1) * 512],
                                start=(c == 0), stop=(c == 7)),
                                reads=[xsT.b, win.b], writes=[pt.b])
                        if n == 3:
                            P.op("act", lambda e, pt=pt, tl=tl, half=half, dst=dst: e.activation(
                                dst[:, tl, half * 512:(half + 1) * 512], pt[:], AF.Silu),
                                reads=[pt.b], writes=[dst.b])
                        else:
                            P.op("act", lambda e, pt=pt, tl=tl, half=half, dst=dst: e.copy(
                                dst[:, tl, half * 512:(half + 1) * 512], pt[:]),
                                reads=[pt.b], writes=[dst.b])
            for i, n in ((0, 4), (1, 5)):
                make_xs(n)
                xsT = xsTs[n % 2]
                for c in range(8):
                    P.op("pe", lambda e, c=c, i=i, xsT=xsT: e.matmul(
                        ps[4][0:64, 0:TB], w1a1[:, c, i * 64:(i + 1) * 64], xsT[:, c, :],
                        start=(c == 0), stop=(c == 7)),
                        reads=[w1a1.b, xsT.b], writes=[ps[4].b], pos="T0")
                if i == 0:
                    P.op("act", lambda e, i=i: e.activation(t1e[i][0:64, :], ps[4][0:64, 0:TB], AF.Tanh),
                         reads=[ps[4].b], writes=[t1e[i].b])
                else:
                    P.op("act", lambda e, i=i: e.copy(t1e[i][0:64, :], ps[4][0:64, 0:TB]),
                         reads=[ps[4].b], writes=[t1e[i].b])
            for tl in range(NTL):
                tt = tb * NTL + tl
                R_, K_, V_ = Rt[:, tl, :], Kt[:, tl, :], Vt[:, tl, :]
                for i, dst in ((0, Wdt), (1, Ast)):
                    for half in range(2):
                        pt = ps[2 + half]
                        P.op("pe", lambda e, pt=pt, i=i, tl=tl, half=half: e.matmul(
                            pt[:], t1e[i][:, tl * 128:(tl + 1) * 128], w2e[i][:, half * 512:(half + 1) * 512],
                            start=True, stop=True),
                            reads=[t1e[i].b, w2e[i].b], writes=[pt.b], pos="K65")
                        P.op("act", lambda e, pt=pt, half=half, dst=dst: e.activation(
                            dst[:, half * 512:(half + 1) * 512], pt[:], AF.Sigmoid),
                            reads=[pt.b], writes=[dst.b])
                P.op("dve", lambda e: e.tensor_scalar(Wdt[:], Wdt[:], -0.6065306597126334, None, ALU.mult),
                     reads=[Wdt.b], writes=[Wdt.b])
                if DEBUG_OUT:
                    P.dma("sp", scr["D1"].t[b, tt * 128:(tt + 1) * 128, :], K_, reads=[Kt.b], writes=[scr["D1"].b])
                    P.dma("sp", scr["D2"].t[b, tt * 128:(tt + 1) * 128, :], Ast[:], reads=[Ast.b], writes=[scr["D2"].b])
                    P.dma("sp", scr["D3"].t[b, tt * 128:(tt + 1) * 128, :], kkb[:], reads=[kkb.b], writes=[scr["D3"].b])
                P.op("dve", lambda e, K_=K_: e.tensor_tensor(tm1[:], K_, kkb[:], ALU.mult),
                     reads=[Kt.b, kkb.b], writes=[tm1.b])
                P.op("act", lambda e: e.activation(tm2[:], tm1[:], AF.Square), reads=[tm1.b], writes=[tm2.b])
                P.op("dve", lambda e: e.tensor_reduce(sm[:, 0:16], hd(tm2[:]), AX.X, ALU.add),
                     reads=[tm2.b], writes=[sm.b])
                P.op("act", lambda e: e.activation(sm[:, 0:16], sm[:, 0:16], AF.Sqrt), reads=[sm.b], writes=[sm.b])
                P.op("dve", lambda e: e.tensor_scalar(sm[:, 0:16], sm[:, 0:16], 1e-12, None, ALU.max),
                     reads=[sm.b], writes=[sm.b])
                P.op("dve", lambda e: e.reciprocal(sm[:, 0:16], sm[:, 0:16]), reads=[sm.b], writes=[sm.b])
                P.op("dve", lambda e: e.tensor_tensor(hd(tm1[:]), hd(tm1[:]), bcj(sm[:, 0:16]), ALU.mult),
                     reads=[tm1.b, sm.b], writes=[tm1.b])
                P.op("pool", lambda e: e.tensor_tensor(tm2[:], tm1[:], Ast[:], ALU.mult),
                     reads=[tm1.b, Ast.b], writes=[tm2.b])
                P.dma("sp", tok_view("B", b, tt), tm2[:], reads=[tm2.b], writes=[scr["B"].b])
                P.op("dve", lambda e: e.tensor_scalar(tm1[:], tm1[:], -1.0, None, ALU.mult),
                     reads=[tm1.b], writes=[tm1.b])
                P.dma("sp", tok_view("A", b, tt), tm1[:], reads=[tm1.b], writes=[scr["A"].b])
                P.dma("act", tok_view("Wd", b, tt), Wdt[:], reads=[Wdt.b], writes=[scr["Wd"].b])
                P.dma("act", tok_view("R", b, tt), R_, reads=[Rt.b], writes=[scr["R"].b])
                P.dma("act", tok_view("V", b, tt), V_, reads=[Vt.b], writes=[scr["V"].b])
                P.op("dve", lambda e: e.scalar_tensor_tensor(tm2[:], Ast[:], -1.0, kab[:], ALU.add, ALU.mult),
                     reads=[Ast.b, kab.b, tm2.b], writes=[tm2.b])
                P.op("dve", lambda e, K_=K_: e.scalar_tensor_tensor(tm2[:], tm2[:], 1.0, K_, ALU.add, ALU.mult),
                     reads=[tm2.b, Kt.b], writes=[tm2.b])
                P.dma("sp", tok_view("K", b, tt), tm2[:], reads=[tm2.b], writes=[scr["K"].b])
                P.op("pool", lambda e, R_=R_: e.tensor_tensor(tm1[:], tm2[:], R_, ALU.mult),
                     reads=[tm2.b, Rt.b, tm1.b], writes=[tm1.b])
                P.op("pool", lambda e: e.tensor_tensor(tm1[:], tm1[:], rkb[:], ALU.mult),
                     reads=[tm1.b, rkb.b], writes=[tm1.b])
                P.op("dve", lambda e: e.tensor_reduce(sm[:, 16:32], hd(tm1[:]), AX.X, ALU.add),
                     reads=[tm1.b], writes=[sm.b])
                P.op("dve", lambda e, V_=V_: e.tensor_tensor(hd(tm1[:]), hd(V_), bcj(sm[:, 16:32]), ALU.mult),
                     reads=[Vt.b, sm.b, tm1.b], writes=[tm1.b])
                P.dma("sp", scr["BON"].t[b, tt * 128:(tt + 1) * 128, :], tm1[:], reads=[tm1.b], writes=[scr["BON"].b])
                P.dma("act", scr["SG"].t[b, tt * 128:(tt + 1) * 128, :], SGt[:, tl, :], reads=[SGt.b], writes=[scr["SG"].b])
            P.op("dve", lambda e: e.tensor_copy(hTb[:, :, 0:1], hTb[:, :, TB:TB + 1]),
                 reads=[hTb.b], writes=[hTb.b])
    kb.pop()

    kb.push()
    C_ = g["consts"]
    psr = [kb.psum("rps%d" % i, [128, 512], F32) for i in range(1)] + g["ps"]
    PU_, PS_ = psr[0], psr[1]
    PY_ = PU_
    PI = psr[2:7]
    bank_ctr = [0]

    def nextbank():
        bank_ctr[0] += 1
        return PI[bank_ctr[0] % len(PI)]
    PTb = kb.psum("rpsb", [128, 1024], BF16)
    identb = g["identb"]
    cm = {}
    for nm, shp in (("rw_M1", [128, 128]), ("rw_M1s", [128, 128]), ("rw_M2", [128, 128]),
                    ("rw_mask4", [128, 512]), ("rw_maskL2", [128, 256]), ("rw_cind", [128, 2])):
        cm[nm] = kb.sb(nm, shp, F32)
        P.dma("sp", cm[nm][:], C_[nm][:, :], writes=[cm[nm].b])
    NU = 2
    per = [dict(
        Vv=kb.sb("c_V", [128, 512], BF16), Bg=kb.sb("c_Bg", [128, 512], BF16), Kg=kb.sb("c_Kg", [128, 512], BF16),
        ARt=kb.sb("c_ARt", [128, 4, 2, 128], BF16), Nall=kb.sb("c_Nall", [128, 8, 4, 128], BF16),
        QR=kb.sb("c_QR", [128, 4, 2, 128], BF16), UV=kb.sb("c_UV", [128, 512], F32),
        YVn=kb.sb("c_YVn", [128, 512], F32), YVs=kb.sb("c_YVs", [128, 512], F32),
        Pall=kb.sb("c_P", [128, 8, 128], BF16),
        gC=kb.sb("c_gC", [128, 4, 2], F32), Yt=kb.sb("c_Yt", [128, 512], F32)) for _ in range(NU)]
    Rr = kb.sb("c_R", [128, 512], F32)
    LWt = kb.sb("c_LW", [128, 512], F32)
    Aa = kb.sb("c_A", [128, 512], F32)
    Bf = kb.sb("c_Bf", [128, 512], F32)
    Kf = kb.sb("c_Kf", [128, 512], F32)
    Vf = kb.sb("c_Vf", [128, 512], F32)
    Rb = kb.sb("c_Rb", [128, 512], BF16)
    Ab = kb.sb("c_Ab", [128, 512], BF16)
    Ee = kb.sb("c_E", [128, 512], F32)
    Bt_ = kb.sb("c_Bt", [128, 512], BF16)
    Kt_ = kb.sb("c_Kt", [128, 512], BF16)
    BKt = kb.sb("c_BKt", [128, 4, 2, 128], BF16)
    NTt = kb.sb("c_NT", [128, 8, 2, 128], BF16)
    XX = kb.sb("c_XX", [128, 2, 8, 2, 128], BF16)
    Th = kb.sb("c_Th", [128, 8, 128], BF16)
    Usb = kb.sb("c_U", [128, 512], BF16)
    STb = kb.sb("c_STb", [128, 4, 64], BF16)
    tmpS = kb.sb("c_tmpS", [128, 4, 64], F32)
    STs = [[kb.sb("c_ST", [128, 4, 64], F32) for hg in range(2)] for b in range(NB)]
    for b in range(NB):
        for hg in range(2):
            P.op("pool", lambda e, t_=STs[b][hg]: e.memset(t_[:], 0.0), writes=[STs[b][hg].b])
    identf = ident

    def gen_pre(u, b, hg, tt):
        c = per[u % NU]
        cols = slice(hg * 512, (hg + 1) * 512)
        rows = slice(tt * 128, (tt + 1) * 128)
        ld = (("R", Rr), ("Wd", LWt), ("A", Aa), ("B", Bf), ("K", Kf), ("V", Vf))
        for i, (nm, dst) in enumerate(ld):
            P.dma(("sp", "act")[i % 2], dst[:], scr[nm].t[b, rows, cols], reads=[scr[nm].b], writes=[dst.b])
        pgA = nextbank()
        P.op("pe", lambda e: e.matmul(pgA[:], cm["rw_M1"][:], LWt[:], start=True, stop=True),
             reads=[cm["rw_M1"].b, LWt.b], writes=[pgA.b])
        P.op("act", lambda e: e.activation(Ee[:], pgA[:], AF.Exp), reads=[pgA.b], writes=[Ee.b])
        P.op("pool", lambda e: e.tensor_tensor(Rb[:], Rr[:], Ee[:], ALU.mult), reads=[Rr.b, Ee.b], writes=[Rb.b])
        P.op("pool", lambda e: e.tensor_copy(c["Vv"][:], Vf[:]), reads=[Vf.b], writes=[c["Vv"].b])
        P.op("act", lambda e: e.activation(Ee[:], pgA[:], AF.Exp, scale=-1.0), reads=[pgA.b, Ee.b], writes=[Ee.b])
        P.op("pool", lambda e: e.tensor_tensor(Bt_[:], Bf[:], Ee[:], ALU.mult),
             reads=[Bf.b, Ee.b], writes=[Bt_.b])
        P.op("dve", lambda e: e.tensor_tensor(Kt_[:], Kf[:], Ee[:], ALU.mult),
             reads=[Kf.b, Ee.b], writes=[Kt_.b])
        yield
        pgB = nextbank()
        P.op("pe", lambda e: e.matmul(pgB[:], cm["rw_M1s"][:], LWt[:], start=True, stop=True),
             reads=[cm["rw_M1s"].b, LWt.b], writes=[pgB.b])
        P.op("act", lambda e: e.activation(Ee[:], pgB[:], AF.Exp), reads=[pgB.b, Ee.b], writes=[Ee.b])
        P.op("pool", lambda e: e.tensor_tensor(Ab[:], Aa[:], Ee[:], ALU.mult), reads=[Aa.b, Ee.b], writes=[Ab.b])
        pg2 = nextbank()
        P.op("pe", lambda e: e.matmul(pg2[:], cm["rw_M2"][:], LWt[:], start=True, stop=True),
             reads=[cm["rw_M2"].b, LWt.b], writes=[pg2.b])
        P.op("act", lambda e: e.activation(Ee[:], pg2[:], AF.Exp), reads=[pg2.b, Ee.b], writes=[Ee.b])
        P.op("pool", lambda e: e.tensor_tensor(c["Bg"][:], Bf[:], Ee[:], ALU.mult),
             reads=[Bf.b, Ee.b], writes=[c["Bg"].b])
        P.op("dve", lambda e: e.tensor_tensor(c["Kg"][:], Kf[:], Ee[:], ALU.mult),
             reads=[Kf.b, Ee.b], writes=[c["Kg"].b])
        pg3 = nextbank()
        for m in range(4):
            P.op("pe", lambda e, m=m: e.matmul(pg3[:, m * 2:m * 2 + 2], LWt[:, m * 128:(m + 1) * 128],
                                               cm["rw_cind"][:], start=True, stop=True),
                 reads=[LWt.b, cm["rw_cind"].b], writes=[pg3.b])
        P.op("act", lambda e: e.activation(c["gC"][:], pg3[:, 0:8].rearrange("p (m c) -> p m c", c=2), AF.Exp),
             reads=[pg3.b], writes=[c["gC"].b])
        yield
        for (src0, src1, dstT) in ((Ab, Rb, c["ARt"]), (Bt_, Kt_, BKt)):
            for m in range(4):
                for j_, src in enumerate((src0, src1)):
                    P.op("pe", lambda e, j_=j_, src=src, m=m: e.transpose(
                        PTb[:, (m * 2 + j_) * 128:(m * 2 + j_ + 1) * 128], src[:, m * 128:(m + 1) * 128], identb[:]),
                        reads=[src.b, identb.b], writes=[PTb.b])
            d_ap = dstT[:].rearrange("p m a t -> p (m a t)")
            if dstT is BKt:
                P.op("act", lambda e, d_ap=d_ap: e.copy(d_ap, PTb[:]), reads=[PTb.b], writes=[dstT.b])
            else:
                P.op("dve", lambda e, d_ap=d_ap: e.tensor_copy(d_ap, PTb[:]), reads=[PTb.b], writes=[dstT.b])
            yield
        for hh in range(8):
            m, h2 = hh // 2, hh % 2
            hs = slice(h2 * 64, (h2 + 1) * 64)
            pg5 = nextbank()
            arv = c["ARt"][hs, m, :, :].rearrange("p a t -> p (a t)")
            bkv = BKt[hs, m, :, :].rearrange("p a t -> p (a t)")
            for j_ in range(2):
                P.op("pe", lambda e, pg5=pg5, j_=j_, hs=hs, m=m, arv=arv: e.matmul(
                    pg5[:, j_ * 256:(j_ + 1) * 256], BKt[hs, m, j_, :], arv, start=True, stop=True),
                    reads=[BKt.b, c["ARt"].b], writes=[pg5.b], pos=(1 if h2 else None))
            P.op("dve", lambda e, pg5=pg5, hh=hh: e.tensor_tensor(
                c["Nall"][:, hh, :, :].rearrange("p a t -> p (a t)"), pg5[:], cm["rw_mask4"][:], ALU.mult),
                reads=[pg5.b, cm["rw_mask4"].b], writes=[c["Nall"].b])
            pg6 = nextbank()
            P.op("pe", lambda e, pg6=pg6, hs=hs, m=m, bkv=bkv: e.matmul(
                pg6[:, 0:256], c["ARt"][hs, m, 0, :], bkv, start=True, stop=True),
                reads=[BKt.b, c["ARt"].b], writes=[pg6.b], pos=(1 if h2 else None))
            P.op("dve", lambda e, pg6=pg6, hh=hh: e.tensor_tensor(
                NTt[:, hh, :, :].rearrange("p a t -> p (a t)"), pg6[:, 0:256], cm["rw_maskL2"][:], ALU.mult),
                reads=[pg6.b, cm["rw_maskL2"].b], writes=[NTt.b])
            if hh % 2 == 1:
                yield
        P.op("pool", lambda e: e.tensor_tensor(Th[:], c["Nall"][:, :, 0, :],
                                               identb[:].unsqueeze(1).to_broadcast([128, 8, 128]), ALU.add),
             reads=[c["Nall"].b, identb.b], writes=[Th.b])
        for k in range(1, 6):
            pp = k % 2
            for pr in range(4):
                bank = nextbank()
                for hl in range(2):
                    hh = pr * 2 + hl
                    if k == 1:
                        Xp, Xtp = c["Nall"][:, hh, 0, :], NTt[:, hh, 0, :]
                        rd = [c["Nall"].b, NTt.b]
                    else:
                        Xp, Xtp = XX[:, 1 - pp, hh, 0, :], XX[:, 1 - pp, hh, 1, :]
                        rd = [XX.b]
                    P.op("pe", lambda e, bank=bank, hl=hl, Xp=Xp, Xtp=Xtp: e.matmul(
                        bank[:, hl * 256:hl * 256 + 128], Xtp, Xp, start=True, stop=True),
                        reads=rd, writes=[bank.b])
                    P.op("pe", lambda e, bank=bank, hl=hl, Xp=Xp, Xtp=Xtp: e.matmul(
                        bank[:, hl * 256 + 128:hl * 256 + 256], Xp, Xtp, start=True, stop=True),
                        reads=rd, writes=[bank.b])
                d_ap = XX[:, pp, pr * 2:pr * 2 + 2, :, :].rearrange("p h a t -> p (h a t)")
                P.op("act", lambda e, d_ap=d_ap, bank=bank: e.copy(d_ap, bank[:]), reads=[bank.b], writes=[XX.b])
            yield
            for pq_ in range(2):
                bank = nextbank()
                for hl in range(4):
                    hh = pq_ * 4 + hl
                    P.op("pe", lambda e, bank=bank, hl=hl, hh=hh, pp=pp: e.matmul(
                        bank[:, hl * 128:(hl + 1) * 128], XX[:, pp, hh, 1, :], Th[:, hh, :], start=True, stop=True),
                        reads=[XX.b, Th.b], writes=[bank.b])
                t_ap = Th[:, pq_ * 4:pq_ * 4 + 4, :].rearrange("p h t -> p (h t)")
                P.op("dve", lambda e, t_ap=t_ap, bank=bank: e.tensor_tensor(t_ap, bank[:], t_ap, ALU.add),
                     reads=[bank.b, Th.b], writes=[Th.b])
            yield
        pq = nextbank()
        for hh in range(8):
            m, h2 = hh // 2, hh % 2
            hs = slice(h2 * 64, (h2 + 1) * 64)
            P.op("pe", lambda e, hs=hs, m=m, hh=hh: e.matmul(
                pq[hs, m * 128:(m + 1) * 128], Ab[:, hh * 64:(hh + 1) * 64], Th[:, hh, :], start=True, stop=True),
                reads=[Ab.b, Th.b], writes=[pq.b], pos=(1 if h2 else None))
        pqv = pq[:].rearrange("p (m t) -> p m t", t=128)
        P.op("act", lambda e: e.copy(c["QR"][:, :, 0, 0:64], pqv[:, :, 0:64]),
             reads=[pq.b], writes=[c["QR"].b])
        P.op("dve", lambda e: e.tensor_copy(c["QR"][:, :, 1, 64:128], pqv[:, :, 64:128]),
             reads=[pq.b], writes=[c["QR"].b])
        P.op("pool", lambda e: e.tensor_copy(c["QR"][:, :, 0, 64:128], c["ARt"][:, :, 1, 0:64]),
             reads=[c["ARt"].b], writes=[c["QR"].b])
        P.op("pool", lambda e: e.tensor_copy(c["QR"][:, :, 1, 0:64], c["ARt"][:, :, 1, 64:128]),
             reads=[c["ARt"].b], writes=[c["QR"].b])
        for half in range(2):
            pp_ = nextbank()
            for hl in range(4):
                hh = half * 4 + hl
                P.op("pe", lambda e, pp_=pp_, hl=hl, hh=hh: e.matmul(
                    pp_[:, hl * 128:(hl + 1) * 128], NTt[:, hh, 1, :], Th[:, hh, :], start=True, stop=True),
                    reads=[NTt.b, Th.b], writes=[pp_.b])
            P.op("dve", lambda e, pp_=pp_, half=half: e.tensor_copy(
                c["Pall"][:, half * 4:half * 4 + 4, :].rearrange("p h t -> p (h t)"), pp_[:]),
                reads=[pp_.b], writes=[c["Pall"].b])
        yield
        puv = nextbank()
        for hh in range(8):
            P.op("pe", lambda e, hh=hh: e.matmul(
                puv[:, hh * 64:(hh + 1) * 64], c["Pall"][:, hh, :], c["Vv"][:, hh * 64:(hh + 1) * 64],
                start=True, stop=True),
                reads=[c["Pall"].b, c["Vv"].b], writes=[puv.b])
        P.op("act", lambda e: e.copy(c["UV"][:], puv[:]), reads=[puv.b], writes=[c["UV"].b])
        pyv = nextbank()
        for hh in range(8):
            P.op("pe", lambda e, hh=hh: e.matmul(
                pyv[:, hh * 64:(hh + 1) * 64], c["Nall"][:, hh, 3, :], c["Vv"][:, hh * 64:(hh + 1) * 64],
                start=True, stop=True),
                reads=[c["Nall"].b, c["Vv"].b], writes=[pyv.b])
        P.op("dve", lambda e: e.tensor_copy(c["YVn"][:], pyv[:]), reads=[pyv.b], writes=[c["YVn"].b])
        P.dma("sp", c["YVs"][64:128, :], c["YVn"][0:64, :], reads=[c["YVn"].b], writes=[c["YVs"].b])
        P.dma("act", c["YVs"][0:64, :], c["YVn"][64:128, :], reads=[c["YVn"].b], writes=[c["YVs"].b])
        yield

    def gen_seq(u, b, hg, tt):
        c = per[u % NU]
        ST = STs[b][hg]
        cols = slice(hg * 512, (hg + 1) * 512)
        rows = slice(tt * 128, (tt + 1) * 128)
        for cc in range(2):
            cs = slice(cc * 64, (cc + 1) * 64)
            P.op("pool", lambda e: e.tensor_copy(STb[:], ST[:]), reads=[ST.b], writes=[STb.b])
            for hh in range(8):
                m, h2 = hh // 2, hh % 2
                hs = slice(h2 * 64, (h2 + 1) * 64)
                hc = slice(hh * 64, (hh + 1) * 64)
                P.op("pe", lambda e, hs=hs, hc=hc, m=m, hh=hh, cc=cc: e.matmul(
                    PU_[:, hc], c["QR"][hs, m, cc, :], STb[hs, m, :], start=(hh == 0), stop=False),
                    reads=[c["QR"].b, STb.b], writes=[PU_.b], pos=(1 if h2 else None))
            P.op("dve", lambda e, cs=cs: e.tensor_tensor(Usb[cs, :], PU_[cs, :], c["UV"][cs, :], ALU.add),
                 reads=[PU_.b, c["UV"].b], writes=[Usb.b])
            yield
            ocs = slice((1 - cc) * 64, (2 - cc) * 64)
            for hh in range(8):
                m, h2 = hh // 2, hh % 2
                hs = slice(h2 * 64, (h2 + 1) * 64)
                hc = slice(hh * 64, (hh + 1) * 64)
                P.op("pe", lambda e, cs=cs, ocs=ocs, hc=hc, hh=hh: e.matmul(
                    PY_[ocs, hc], c["Nall"][cs, hh, 1, cs], Usb[cs, hc], start=False, stop=True),
                    reads=[c["Nall"].b, Usb.b], writes=[PY_.b], pos=1)
            P.op("dve", lambda e, ocs=ocs: e.tensor_tensor(c["Yt"][ocs, :], PY_[ocs, :], c["YVs"][ocs, :], ALU.add),
                 reads=[PY_.b, c["YVs"].b], writes=[c["Yt"].b])
            for hh in range(8):
                m, h2 = hh // 2, hh % 2
                hs = slice(h2 * 64, (h2 + 1) * 64)
                hc = slice(hh * 64, (hh + 1) * 64)
                P.op("pe", lambda e, cs=cs, hs=hs, hc=hc, m=m, hh=hh: e.matmul(
                    PS_[hs, m * 64:(m + 1) * 64], c["Bg"][cs, hc], Usb[cs, hc], start=(hh < 2), stop=False),
                    reads=[c["Bg"].b, Usb.b], writes=[PS_.b], pos=(1 if (h2 or cc) else None))
                P.op("pe", lambda e, cs=cs, hs=hs, hc=hc, m=m, hh=hh: e.matmul(
                    PS_[hs, m * 64:(m + 1) * 64], c["Kg"][cs, hc], c["Vv"][cs, hc], start=False, stop=True),
                    reads=[c["Kg"].b, c["Vv"].b], writes=[PS_.b], pos=(1 if (h2 or cc) else None))
            P.op("pool", lambda e, cc=cc: e.tensor_tensor(
                tmpS[:], ST[:], c["gC"][:, :, cc].unsqueeze(2).to_broadcast([128, 4, 64]), ALU.mult),
                reads=[ST.b, c["gC"].b], writes=[tmpS.b])
            P.op("dve", lambda e: e.tensor_tensor(
                ST[:].rearrange("p m i -> p (m i)"), PS_[:, 0:256], tmpS[:].rearrange("p m i -> p (m i)"), ALU.add),
                reads=[PS_.b, tmpS.b], writes=[ST.b])
            yield
        P.dma("sp", scr["Y"].t[b, tt * 128:tt * 128 + 64, cols], c["Yt"][64:128, :],
              reads=[c["Yt"].b], writes=[scr["Y"].b])
        P.dma("act", scr["Y"].t[b, tt * 128 + 64:tt * 128 + 128, cols], c["Yt"][0:64, :],
              reads=[c["Yt"].b], writes=[scr["Y"].b])
        yield

    units = [(b, hg, tt) for tt in range(NT) for b in range(NB) for hg in range(2)]
    n_u = len(units)
    for u in range(n_u + 1):
        streams = []
        if u >= 1:
            streams.append(gen_seq(u - 1, *units[u - 1]))
        if u < n_u:
            streams.append(gen_pre(u, *units[u]))
        while streams:
            for s_ in list(streams):
                try:
                    next(s_)
                except StopIteration:
                    streams.remove(s_)
    kb.pop()

    kb.push()
    lnw = kb.sb("rw_lnw", [128, 1024], F32)
    lnb = kb.sb("rw_lnb", [128, 1024], F32)
    bc_row(lnw, W["rwkv_ln_w"][0:1, :])
    bc_row(lnb, W["rwkv_ln_b"][0:1, :])
    yt = kb.sb("rc_y", [128, 1024], F32)
    y2 = kb.sb("rc_y2", [128, 1024], F32)
    bon = kb.sb("rc_bon", [128, 1024], F32)
    sgc = kb.sb("rc_sg", [128, 1024], F32)
    smc = kb.sb("rc_sm", [128, 32], F32)
    zTt = kb.sb("rc_zT", [128, 8, 128], BF16)
    for b in range(NB):
        for tt in range(NT):
            P.dma("sp", yt[:], tok_view("Y", b, tt), reads=[scr["Y"].b], writes=[yt.b])
            P.dma("act", bon[:], scr["BON"].t[b, tt * 128:(tt + 1) * 128, :], reads=[scr["BON"].b], writes=[bon.b])
            P.dma("act", sgc[:], scr["SG"].t[b, tt * 128:(tt + 1) * 128, :], reads=[scr["SG"].b], writes=[sgc.b])
            P.op("dve", lambda e: e.tensor_reduce(smc[:, 0:16], hd(yt[:]), AX.X, ALU.add), reads=[yt.b], writes=[smc.b])
            P.op("dve", lambda e: e.tensor_scalar(smc[:, 0:16], smc[:, 0:16], -1.0 / 64.0, None, ALU.mult),
                 reads=[smc.b], writes=[smc.b])
            P.op("dve", lambda e: e.tensor_tensor(hd(yt[:]), hd(yt[:]), bcj(smc[:, 0:16]), ALU.add),
                 reads=[yt.b, smc.b], writes=[yt.b])
            P.op("act", lambda e: e.activation(y2[:], yt[:], AF.Square), reads=[yt.b], writes=[y2.b])
            P.op("dve", lambda e: e.tensor_reduce(smc[:, 16:32], hd(y2[:]), AX.X, ALU.add), reads=[y2.b], writes=[smc.b])
            P.op("dve", lambda e: e.tensor_scalar(smc[:, 16:32], smc[:, 16:32], 1.0 / 64.0, 64e-5, ALU.mult, ALU.add),
                 reads=[smc.b], writes=[smc.b])
            P.op("act", lambda e: e.activation(smc[:, 16:32], smc[:, 16:32], AF.Sqrt), reads=[smc.b], writes=[smc.b])
            P.op("dve", lambda e: e.reciprocal(smc[:, 16:32], smc[:, 16:32]), reads=[smc.b], writes=[smc.b])
            P.op("dve", lambda e: e.tensor_tensor(hd(yt[:]), hd(yt[:]), bcj(smc[:, 16:32]), ALU.mult),
                 reads=[yt.b, smc.b], writes=[yt.b])
            P.op("pool", lambda e: e.tensor_tensor(yt[:], yt[:], lnw[:], ALU.mult), reads=[yt.b, lnw.b], writes=[yt.b])
            P.op("pool", lambda e: e.tensor_tensor(yt[:], yt[:], lnb[:], ALU.add), reads=[yt.b, lnb.b], writes=[yt.b])
            P.op("dve", lambda e: e.tensor_tensor(yt[:], yt[:], bon[:], ALU.add), reads=[yt.b, bon.b], writes=[yt.b])
            P.op("dve", lambda e: e.tensor_tensor(yt[:], yt[:], sgc[:], ALU.mult), reads=[yt.b, sgc.b], writes=[yt.b])
            for c in range(8):
                pt = ps[4 + c // 4]
                P.op("pe", lambda e, pt=pt, c=c: e.transpose(
                    pt[:, (c % 4) * 128:(c % 4 + 1) * 128], yt[:, c * 128:(c + 1) * 128], ident[:]),
                    reads=[yt.b, ident.b], writes=[pt.b])
            P.op("act", lambda e: e.copy(zTt[:, 0:4, :], ps[4][:].rearrange("p (c t) -> p c t", t=128)),
                 reads=[ps[4].b], writes=[zTt.b])
            P.op("dve", lambda e: e.tensor_copy(zTt[:, 4:8, :], ps[5][:].rearrange("p (c t) -> p c t", t=128)),
                 reads=[ps[5].b], writes=[zTt.b])
            back_tile(kb, g, b, tt, _Shift(zTt, tt * 128), [zTt.b], wout, x_src, x_dst)
    kb.pop()
    kb.pop()


class _W4:
    def __init__(self, t, n):
        self.t, self.n, self.b = t, n, t.b

    def __getitem__(self, k):
        a, kc, sl = k
        return self.t[a, self.n, kc, sl]


LAYER_FNS[2] = layer_rwkv2


def rwkv_consts():
    c = {}
    tp = np.arange(128)[:, None]
    t = np.arange(128)[None, :]
    same = (tp // 64) == (t // 64)
    c["rw_M1"] = (same & (tp <= t)).astype(np.float32)
    c["rw_M1s"] = (same & (tp < t)).astype(np.float32)
    c["rw_M2"] = (same & (tp > t)).astype(np.float32)
    strict = (same & (tp < t)).astype(np.float32)
    incl = (same & (tp <= t)).astype(np.float32)
    c["rw_mask4"] = np.concatenate([strict, incl, strict, incl], axis=1)
    low = (same & (t < tp)).astype(np.float32)
    c["rw_maskL2"] = np.concatenate([low, low], axis=1)
    c["rw_cind"] = np.stack([(np.arange(128) < 64), (np.arange(128) >= 64)], axis=1).astype(np.float32)
    return c


def kernel(**inputs):
    n_cores = 8
    x = np.ascontiguousarray(np.asarray(inputs["x"], dtype=np.float32))
    c = np.ascontiguousarray(np.asarray(inputs["c"], dtype=np.float32))
    wsh = {k: tuple(np.asarray(v).shape) for k, v in inputs.items()}
    wsh["c"] = (NB, D)
    consts = make_consts()
    nc = build([0, 1, 2, 3], wsh, consts)
    shared = {k: np.ascontiguousarray(np.asarray(v, dtype=np.float32))
              for k, v in inputs.items() if k not in ("x", "c")}
    for k, v in consts.items():
        shared["k_" + k] = v
    in_maps = []
    for i in range(n_cores):
        m = dict(shared)
        m["x"] = np.ascontiguousarray(x[i * NB:(i + 1) * NB])
        m["c"] = np.ascontiguousarray(c[i * NB:(i + 1) * NB])
        in_maps.append(m)
    res = run_bass_kernel_spmd(nc, in_maps, core_ids=list(range(n_cores)))
    out = np.concatenate([np.asarray(r["y"]) for r in res.results], axis=0)
    return out.astype(np.float32)
```

```python
from contextlib import ExitStack
import numpy as np
import concourse.bass as bass
import concourse.mybir as mybir
from concourse.bass_utils import run_bass_kernel_spmd

F32 = mybir.dt.float32
BF16 = mybir.dt.bfloat16
AF = mybir.ActivationFunctionType
ALU = mybir.AluOpType
AX = mybir.AxisListType

SAME_ENGINE_SYNC = True
LAZY_SIGNAL = ("pe",)
MAX_PENDING = 8
NO_SELF_SYNC = ("pe",)
N_DMA_SEMS = 8


class Buf:
    __slots__ = ("name", "w", "r")

    def __init__(self, name):
        self.name = name
        self.w = None
        self.r = {}


class Prog:
    ENG = ("pe", "act", "dve", "pool", "sp")

    def __init__(self, nc):
        self.nc = nc
        self.ops = {e: [] for e in self.ENG}
        self.count = {e: 0 for e in self.ENG}
        self.known = {e: {} for e in self.ENG}
        self.sems = {}
        self.dma_n = {e: 0 for e in self.ENG}
        self._stack = []
        self._last_pos = None
        self.recs = {e: [] for e in self.ENG}

    def alloc_sems(self, stack):
        for e in self.ENG:
            self.sems[("e", e)] = stack.enter_context(self.nc.semaphore("s_" + e))
        for e in ("sp", "act", "pool"):
            for i in range(N_DMA_SEMS):
                self.sems[("d", e, i)] = stack.enter_context(
                    self.nc.semaphore("d_%s%d" % (e, i)))

    def _deps(self, eng, reads, writes):
        need = {}
        def add(tok):
            if tok is None:
                return
            k, v = tok
            if need.get(k, 0) < v:
                need[k] = v
        for b in reads:
            add(b.w)
        for b in writes:
            add(b.w)
            for k, v in b.r.items():
                add((k, v))
        waits = []
        kn = self.known[eng]
        for k, v in need.items():
            if k == ("e", eng) and (not SAME_ENGINE_SYNC or eng in NO_SELF_SYNC):
                continue
            if kn.get(k, 0) < v:
                kn[k] = v
                waits.append((k, v))
                if k[0] == "e":
                    self.recs[k[1]][v - 1]["signal"] = True
        return waits

    def _commit(self, tok, reads, writes):
        k, v = tok
        for b in writes:
            b.w = tok
            b.r = {}
        for b in reads:
            if b.r.get(k, 0) < v:
                b.r[k] = v

    def op(self, eng, fn, reads=(), writes=(), pos=None):
        waits = self._deps(eng, reads, writes)
        if eng == "pe":
            if (pos is not None or self._last_pos is not None) and self.count["pe"] > 0:
                k, v = ("e", "pe"), self.count["pe"]
                if self.known["pe"].get(k, 0) < v:
                    self.known["pe"][k] = v
                    waits.append((k, v))
                    self.recs["pe"][v - 1]["signal"] = True
            self._last_pos = pos
        self.count[eng] += 1
        tok = (("e", eng), self.count[eng])
        rec = {"fn": fn, "waits": waits, "signal": eng not in LAZY_SIGNAL}
        self.ops[eng].append(rec)
        self.recs[eng].append(rec)
        self._commit(tok, reads, writes)

    def dma(self, q, out, in_, reads=(), writes=(), **kw):
        n = self.dma_n[q]
        self.dma_n[q] += 1
        slot = n % N_DMA_SEMS
        key = ("d", q, slot)
        val = 16 * (n // N_DMA_SEMS + 1)
        waits = self._deps(q, reads, writes)
        if val > 16:
            kn = self.known[q]
            if kn.get(key, 0) < val - 16:
                kn[key] = val - 16
                waits.append((key, val - 16))
        sems = self.sems

        def emit(e, waits=waits, key=key, out=out, in_=in_, kw=kw):
            for k, v in waits:
                e.wait_ge(sems[k], v)
            e.dma_start(out=out, in_=in_, **kw).then_inc(sems[key], 16)
        self.ops[q].append(emit)
        self._commit((key, val), reads, writes)

    def finish(self, final_bufs):
        waits = self._deps("sp", final_bufs, [])
        sems = self.sems

        def emit(e, waits=waits):
            for k, v in waits:
                e.wait_ge(sems[k], v)
        self.ops["sp"].append(emit)

    def _run(self, eng, e):
        sems = self.sems
        pending = 0
        n = len(self.ops[eng])
        last_rec = None
        for it in self.ops[eng]:
            if isinstance(it, dict):
                last_rec = it
        for it in self.ops[eng]:
            if not isinstance(it, dict):
                it(e)
                continue
            for k, v in it["waits"]:
                e.wait_ge(sems[k], v)
            ins = it["fn"](e)
            pending += 1
            if it["signal"] or it is last_rec or pending >= MAX_PENDING:
                ins.then_inc(sems[("e", eng)], pending)
                pending = 0

    def emit(self):
        nc = self.nc
        with nc.Block() as block:
            @block.tensor
            def _(e):
                self._run("pe", e)

            @block.scalar
            def _(e):
                self._run("act", e)

            @block.vector
            def _(e):
                self._run("dve", e)

            @block.gpsimd
            def _(e):
                self._run("pool", e)

            @block.sync
            def _(e):
                self._run("sp", e)


def _barrier(self):
    targets = {}
    for e in self.ENG:
        if self.count[e]:
            targets[("e", e)] = self.count[e]
            self.recs[e][self.count[e] - 1]["signal"] = True
    for q in ("sp", "act", "pool"):
        n = self.dma_n[q]
        for s in range(min(n, N_DMA_SEMS)):
            last = ((n - 1 - s) // N_DMA_SEMS) * N_DMA_SEMS + s
            targets[("d", q, s)] = 16 * (last // N_DMA_SEMS + 1)
    sems = self.sems
    for e in self.ENG:
        waits = []
        kn = self.known[e]
        for k, v in targets.items():
            if kn.get(k, 0) < v:
                kn[k] = v
                waits.append((k, v))

        def emit(eo, waits=waits):
            for k, v in waits:
                eo.wait_ge(sems[k], v)
        self.ops[e].append(emit)


Prog.barrier = _barrier


DEBUG_OUT = False
S = 2048
D = 1024
NB = 2
NT = S // 128
EPS = 1e-6


class T:
    def __init__(self, t, name, nslots=0):
        self.t = t
        self.b = Buf(name)
        self.bs = [Buf("%s_%d" % (name, i)) for i in range(nslots)]

    def __getitem__(self, k):
        return self.t[k]


class KB:
    def __init__(self, nc, P, st):
        self.nc, self.P, self.st = nc, P, st
        self.scopes = [st]
        self.rr = 0
        self.n = 0

    def push(self):
        s = ExitStack()
        self.scopes.append(s)
        return s

    def pop(self):
        self.P.barrier()
        s = self.scopes.pop()
        s.close()

    def sb(self, name, shape, dt=F32, nslots=0):
        self.n += 1
        nm = "%s_%d" % (name, self.n)
        t = self.scopes[-1].enter_context(self.nc.sbuf_tensor(nm, list(shape), dt))
        return T(t, nm, nslots)

    def psum(self, name, shape, dt=F32):
        t = self.scopes[-1].enter_context(self.nc.psum_tensor(name, list(shape), dt))
        return T(t, name)

    def dram(self, name, shape, dt=F32):
        t = self.nc.dram_tensor(name, list(shape), dt, kind=("ExternalOutput" if DEBUG_OUT else "Internal"))
        o = T(t.ap(), name)
        return o


def bufs(objs):
    out = []
    for o in objs:
        out.append(o.b if isinstance(o, T) else o)
    return out


def build_common(kb, consts):
    nc, P = kb.nc, kb.P
    g = {}
    g["ident"] = kb.sb("ident", [128, 128], F32)
    P.dma("sp", g["ident"][:], consts["ident"][:, :], writes=[g["ident"].b])
    g["identb"] = kb.sb("identb", [128, 128], BF16)
    P.op("dve", lambda e: e.tensor_copy(g["identb"][:], g["ident"][:]),
         reads=[g["ident"].b], writes=[g["identb"].b])
    g["ones"] = kb.sb("ones", [128, 128], F32)
    P.op("dve", lambda e: e.memset(g["ones"][:], 1.0), writes=[g["ones"].b])
    g["ps"] = [kb.psum("ps%d" % i, [128, 512], F32) for i in range(6)]
    g["stage_n"] = 0
    g["xt"] = [kb.sb("xt", [128, 1024], F32) for i in range(2)]
    g["xn"] = [kb.sb("xn", [128, 1024], F32) for i in range(1)]
    g["junk"] = kb.sb("junk", [128, 1024], F32)
    g["st"] = [kb.sb("stt", [128, 4], F32) for i in range(2)]
    g["gatebc"] = [kb.sb("gatebc", [128, 1024], F32) for b in range(NB)]
    g["AB"] = kb.sb("AB", [128, NB, 2, 8], F32)
    g["cT"] = kb.sb("cT", [128, 8, NB], F32)
    g["gainT"] = kb.sb("gainT", [128, 8], F32)
    g["mbT"] = kb.sb("mbT", [128, 16], F32)
    g["xo"] = [kb.sb("xo", [128, 1024], F32) for i in range(2)]
    g["cnt"] = 0
    return g


def begin_load(kb, g):
    kb.push()
    g["stage"] = [kb.sb("stage", [128, 8, 256], F32) for i in range(2)]
    g["crep"] = kb.sb("crep", [128, 8, 128], F32)
    g["mbrow"] = kb.sb("mbrow", [1, 256], F32)


def end_load(kb, g):
    kb.pop()


def load_w(kb, g, dst, dcol0, src, ncols, krows=1024):
    P = kb.P
    kc = krows // 128
    c0 = 0
    engs = ("dve", "pool")
    while c0 < ncols:
        n = min(256, ncols - c0)
        stg = g["stage"][g["stage_n"] % 2]
        g["stage_n"] += 1
        q = ("sp", "act")[g["stage_n"] % 2]
        P.dma(q, stg[:, 0:kc, 0:n], src[:, c0:c0 + n].rearrange("(k p) n -> p k n", p=128),
              writes=[stg.b])
        eng = engs[g["stage_n"] % 2]
        d_ap = dst[:, 0:kc, dcol0 + c0:dcol0 + c0 + n]
        s_ap = stg[:, 0:kc, 0:n]
        P.op(eng, lambda e, d_ap=d_ap, s_ap=s_ap: e.tensor_copy(d_ap, s_ap),
             reads=[stg.b], writes=[dst.b])
        c0 += n


def compute_mod(kb, g, layer, W):
    nc, P = kb.nc, kb.P
    ps = g["ps"]
    crep, mbrow = g["crep"], g["mbrow"]
    if not g.get("c_done"):
        g["c_done"] = True
        for b in range(NB):
            P.dma("sp", g["cT"][:, :, b], W["c"][b].rearrange("(k p) -> p k", p=128),
                  writes=[g["cT"].b], allow_slow_non_contiguous=True)
        P.op("act", lambda e: e.activation(g["cT"][:], g["cT"][:], AF.Silu),
             reads=[g["cT"].b], writes=[g["cT"].b])
    P.dma("sp", g["gainT"][:], W["ln_gain"][layer].rearrange("(k p) -> p k", p=128),
          writes=[g["gainT"].b], allow_slow_non_contiguous=True)
    P.dma("sp", g["mbT"][:], W["mod_b"][layer, 0:2048].rearrange("(k p) -> p k", p=128),
          writes=[g["mbT"].b], allow_slow_non_contiguous=True)
    pf = ps[2]
    for nb in range(12):
        stg = g["stage"][g["stage_n"] % 2]
        g["stage_n"] += 1
        P.dma("sp", stg[:], W["mod_w"][layer][:, nb * 256:(nb + 1) * 256].rearrange(
            "(k p) n -> p k n", p=128), writes=[stg.b])
        if nb < 8:
            for jj in range(2):
                j = nb * 2 + jj
                for k in range(8):
                    P.op("pe", lambda e, j=j, jj=jj, k=k, stg=stg: e.matmul(
                        pf[:, j * 2:j * 2 + 2], stg[:, k, jj * 128:(jj + 1) * 128], g["cT"][:, k, :],
                        start=(k == 0), stop=(k == 7)),
                        reads=[g["cT"].b, stg.b], writes=[pf.b])
        else:
            P.dma("act", mbrow[:], W["mod_b"][layer:layer + 1, nb * 256:(nb + 1) * 256],
                  writes=[mbrow.b])
            for b in range(NB):
                pt = ps[b]
                for k in range(8):
                    P.op("dve", lambda e, b=b, k=k: e.tensor_copy(
                        crep[:, k, :], g["cT"][:, k, b:b + 1].to_broadcast([128, 128])),
                        reads=[g["cT"].b], writes=[crep.b])
                for k in range(8):
                    P.op("pe", lambda e, pt=pt, k=k, stg=stg: e.matmul(
                        pt[:, 0:256], crep[:, k, :], stg[:, k, :], start=(k == 0), stop=False),
                        reads=[crep.b, stg.b], writes=[pt.b])
                P.op("pe", lambda e, pt=pt: e.matmul(
                    pt[:, 0:256], g["ones"][0:1, :], mbrow[0:1, :], start=False, stop=True),
                    reads=[g["ones"].b, mbrow.b], writes=[pt.b])
                P.op("act", lambda e, pt=pt, b=b, nb=nb: e.copy(
                    g["gatebc"][b][:, (nb - 8) * 256:(nb - 7) * 256], pt[:, 0:256]),
                    reads=[pt.b], writes=[g["gatebc"][b].b])
    pfv = pf[:, 0:32].rearrange("p (j b) -> p j b", b=2)
    for b in range(NB):
        P.op("dve", lambda e, b=b: e.tensor_tensor(
            g["AB"][:, b, 1, :], pfv[:, 0:8, b], g["mbT"][:, 0:8], ALU.add),
            reads=[pf.b, g["mbT"].b], writes=[g["AB"].b])
        P.op("dve", lambda e, b=b: e.tensor_tensor(
            g["AB"][:, b, 0, :], pfv[:, 8:16, b], g["mbT"][:, 8:16], ALU.add),
            reads=[pf.b, g["mbT"].b], writes=[g["AB"].b])
        P.op("dve", lambda e, b=b: e.scalar_tensor_tensor(
            g["AB"][:, b, 0, :], g["AB"][:, b, 0, :], 1.0, g["gainT"][:], ALU.add, ALU.mult),
            reads=[g["AB"].b, g["gainT"].b], writes=[g["AB"].b])


def front_tile(kb, g, b, tt, x_src, hT, hT_buf, pbanks=None, xi=None):
    nc, P = kb.nc, kb.P
    i = g["cnt"] % 2
    g["cnt"] += 1
    if xi is not None:
        i = xi
    xt, xn, stt = g["xt"][i], g["xn"][0], g["st"][i]
    P.dma("sp", xt[:], x_src[b, tt * 128:(tt + 1) * 128, :], reads=[x_src.b], writes=[xt.b])
    P.op("dve", lambda e: e.memset(stt[:], 0.0), writes=[stt.b])
    P.op("act", lambda e: e.activation(g["junk"][:], xt[:], AF.Square, accum_out=stt[:, 0:1]),
         reads=[xt.b, stt.b], writes=[g["junk"].b, stt.b])
    P.op("dve", lambda e: e.tensor_scalar(stt[:, 1:2], stt[:, 0:1], 1.0 / D, EPS, ALU.mult, ALU.add),
         reads=[stt.b], writes=[stt.b])
    P.op("act", lambda e: e.activation(stt[:, 1:2], stt[:, 1:2], AF.Sqrt),
         reads=[stt.b], writes=[stt.b])
    P.op("dve", lambda e: e.reciprocal(stt[:, 2:3], stt[:, 1:2]),
         reads=[stt.b], writes=[stt.b])
    P.op("dve", lambda e: e.tensor_scalar(xn[:], xt[:], stt[:, 2:3], None, ALU.mult),
         reads=[xt.b, stt.b], writes=[xn.b])
    pa, pb = pbanks if pbanks is not None else (g["ps"][0], g["ps"][1])
    for c in range(8):
        pt = pa if c < 4 else pb
        P.op("pe", lambda e, pt=pt, c=c: e.transpose(
            pt[:, (c % 4) * 128:(c % 4 + 1) * 128], xn[:, c * 128:(c + 1) * 128], g["ident"][:]),
            reads=[xn.b, g["ident"].b], writes=[pt.b])
    for c in range(8):
        pt = pa if c < 4 else pb
        src = pt[:, (c % 4) * 128:(c % 4 + 1) * 128]
        dst = hT[:, c, tt * 128:(tt + 1) * 128]
        A = g["AB"][:, b, 0, c:c + 1]
        Bv = g["AB"][:, b, 1, c:c + 1]
        if c % 2 == 0:
            P.op("act", lambda e, src=src, dst=dst, A=A, Bv=Bv: e.activation(
                dst, src, AF.Identity, bias=Bv, scale=A),
                reads=[pt.b, g["AB"].b], writes=[hT_buf])
        else:
            P.op("dve", lambda e, src=src, dst=dst, A=A, Bv=Bv: e.tensor_scalar(
                dst, src, A, Bv, ALU.mult, ALU.add),
                reads=[pt.b, g["AB"].b], writes=[hT_buf])


def back_tile(kb, g, b, tt, zT, zT_bufs, wout, x_src, x_dst, pbanks=None, xi=None):
    nc, P = kb.nc, kb.P
    i = g["cnt"] % 2
    g["cnt"] += 1
    if xi is not None:
        i = xi
    xt, xo = g["xt"][i], g["xo"][i]
    P.dma("act", xt[:], x_src[b, tt * 128:(tt + 1) * 128, :], reads=[x_src.b], writes=[xt.b])
    for half in range(2):
        pt = g["ps"][2 + half] if pbanks is None else pbanks[half]
        for c in range(8):
            P.op("pe", lambda e, pt=pt, c=c, half=half: e.matmul(
                pt[:], zT[:, c, tt * 128:(tt + 1) * 128], wout[:, c, half * 512:(half + 1) * 512],
                start=(c == 0), stop=(c == 7)),
                reads=list(zT_bufs) + [wout.b], writes=[pt.b])
        P.op("dve", lambda e, pt=pt, half=half: e.tensor_tensor(
            xo[:, half * 512:(half + 1) * 512], pt[:],
            g["gatebc"][b][:, half * 512:(half + 1) * 512], ALU.mult),
            reads=[pt.b, g["gatebc"][b].b], writes=[xo.b])
    P.op("pool", lambda e: e.tensor_tensor(xo[:], xo[:], xt[:], ALU.add),
         reads=[xo.b, xt.b], writes=[xo.b])
    P.dma("sp", x_dst[b, tt * 128:(tt + 1) * 128, :], xo[:], reads=[xo.b], writes=[x_dst.b])


def layer_lru(kb, g, W, x_src, x_dst):
    nc, P = kb.nc, kb.P
    kb.push()
    win = kb.sb("lru_win", [128, 8, 2048], BF16)
    wout = kb.sb("lru_wout", [128, 8, 1024], BF16)
    gwb = [kb.sb("lru_gwb", [128, 8, 128], BF16) for _ in range(2)]
    vec = kb.sb("lru_vec", [128, 10, 8], F32)
    begin_load(kb, g)
    compute_mod(kb, g, 1, W)
    load_w(kb, g, win, 0, W["lru_w_in"][0], 2048)
    load_w(kb, g, wout, 0, W["lru_w_out"][0], 1024)
    for i, nm in enumerate(("lru_gate_a_w", "lru_gate_x_w")):
        stg = g["stage"][g["stage_n"] % 2]
        g["stage_n"] += 1
        P.op("pool", lambda e, stg=stg: e.memset(stg[:], 0.0), writes=[stg.b])
        src = W[nm][0]
        for hh in range(2):
            P.dma("sp", stg[hh * 64:(hh + 1) * 64, :, hh * 64:(hh + 1) * 64],
                  src.rearrange("(j n2) c d -> n2 c j d", n2=2)[hh],
                  writes=[stg.b])
        P.op("dve", lambda e, i=i, stg=stg: e.tensor_copy(gwb[i][:], stg[:, :, 0:128]),
             reads=[stg.b], writes=[gwb[i].b])
    names = ["lru_conv_b", "lru_gate_a_b", "lru_gate_x_b", "lru_lambda"]
    for i, nm in enumerate(names):
        P.dma("sp", vec[:, i, :], W[nm][0].rearrange("(k p) -> p k", p=128),
              writes=[vec.b], allow_slow_non_contiguous=True)
    for j in range(4):
        P.dma("sp", vec[:, 4 + j, :], W["lru_conv_w"][0, j].rearrange("(k p) -> p k", p=128),
              writes=[vec.b], allow_slow_non_contiguous=True)
    P.op("act", lambda e: e.activation(vec[:, 8, :], vec[:, 3, :], AF.Exp, scale=-1.0),
         reads=[vec.b], writes=[vec.b])
    P.op("act", lambda e: e.activation(vec[:, 8, :], vec[:, 8, :], AF.Ln, bias=1.0),
         reads=[vec.b], writes=[vec.b])
    P.op("dve", lambda e: e.tensor_scalar(vec[:, 8, :], vec[:, 8, :], -8.0, None, ALU.mult),
         reads=[vec.b], writes=[vec.b])

    end_load(kb, g)
    TH = 1024
    hT = kb.sb("hT", [128, 8, S], BF16)
    zT = kb.sb("zT", [128, 8, S], BF16)
    sets = [dict(upad=kb.sb("upad", [128, 3 + TH], F32), uc=kb.sb("uc", [128, TH], F32),
                 ucb=kb.sb("ucb", [128, TH], BF16), rr=kb.sb("rr", [128, TH], F32),
                 ii=kb.sb("ii", [128, TH], F32), aa=kb.sb("aa", [128, TH], F32),
                 bb=kb.sb("bb", [128, TH], F32)) for _ in range(2)]
    ps = g["ps"]
    pbank = [0]

    def nb():
        pbank[0] += 1
        return ps[2 + pbank[0] % 4]

    def half(b, j, hf, cur, prev):
        upad, uc, ucb, rr, ii, aa, bb = (cur[k] for k in ("upad", "uc", "ucb", "rr", "ii", "aa", "bb"))
        t0 = hf * TH
        if hf == 0:
            P.op("dve", lambda e: e.memset(upad[:, 0:3], 0.0), writes=[upad.b])
        else:
            P.op("dve", lambda e: e.tensor_copy(upad[:, 0:3], prev["upad"][:, TH:TH + 3]),
                 reads=[prev["upad"].b], writes=[upad.b])
        for q in range(TH // 512):
            pt = nb()
            for c in range(8):
                P.op("pe", lambda e, pt=pt, c=c, q=q: e.matmul(
                    pt[:], win[:, c, j * 128:(j + 1) * 128], hT[:, c, t0 + q * 512:t0 + (q + 1) * 512],
                    start=(c == 0), stop=(c == 7)),
                    reads=[win.b, hT.b], writes=[pt.b])
            P.op("act", lambda e, pt=pt, q=q: e.copy(upad[:, 3 + q * 512:3 + (q + 1) * 512], pt[:]),
                 reads=[pt.b], writes=[upad.b])
        for q in range(TH // 512):
            pt = nb()
            for c in range(8):
                P.op("pe", lambda e, pt=pt, c=c, q=q: e.matmul(
                    pt[:], win[:, c, 1024 + j * 128:1024 + (j + 1) * 128],
                    hT[:, c, t0 + q * 512:t0 + (q + 1) * 512], start=(c == 0), stop=(c == 7)),
                    reads=[win.b, hT.b], writes=[pt.b])
            P.op("act", lambda e, pt=pt, q=q: e.activation(
                bb[:, q * 512:(q + 1) * 512], pt[:], AF.Silu),
                reads=[pt.b], writes=[bb.b])
        P.op("dve", lambda e: e.tensor_scalar(
            uc[:], upad[:, 0:TH], vec[:, 4, j:j + 1], vec[:, 0, j:j + 1], ALU.mult, ALU.add),
            reads=[upad.b, vec.b], writes=[uc.b])
        for k in range(1, 4):
            P.op("dve", lambda e, k=k: e.scalar_tensor_tensor(
                uc[:], upad[:, k:k + TH], vec[:, 4 + k, j:j + 1], uc[:], ALU.mult, ALU.add),
                reads=[upad.b, vec.b, uc.b], writes=[uc.b])
        P.op("pool", lambda e: e.tensor_copy(ucb[:], uc[:]), reads=[uc.b], writes=[ucb.b])
        for gi, dst, bi in ((0, aa, 1), (1, ii, 2)):
            for q in range(TH // 512):
                pt = nb()
                P.op("pe", lambda e, pt=pt, q=q, gi=gi: e.matmul(
                    pt[:], gwb[gi][:, j, :], ucb[:, q * 512:(q + 1) * 512], start=True, stop=True),
                    reads=[gwb[gi].b, ucb.b], writes=[pt.b])
                P.op("act", lambda e, pt=pt, q=q, dst=dst, bi=bi: e.activation(
                    dst[:, q * 512:(q + 1) * 512], pt[:], AF.Sigmoid, bias=vec[:, bi, j:j + 1]),
                    reads=[pt.b, vec.b], writes=[dst.b])
        P.op("act", lambda e: e.activation(aa[:], aa[:], AF.Exp, scale=vec[:, 8, j:j + 1]),
             reads=[aa.b, vec.b], writes=[aa.b])
        P.op("pool", lambda e: e.tensor_tensor(ii[:], ii[:], uc[:], ALU.mult),
             reads=[ii.b, uc.b], writes=[ii.b])
        P.op("dve", lambda e: e.tensor_tensor(rr[:], aa[:], aa[:], ALU.mult),
             reads=[aa.b], writes=[rr.b])
        P.op("act", lambda e: e.activation(rr[:], rr[:], AF.Sqrt, bias=1.0, scale=-1.0),
             reads=[rr.b], writes=[rr.b])
        P.op("dve", lambda e: e.tensor_tensor(ii[:], rr[:], ii[:], ALU.mult),
             reads=[rr.b, ii.b], writes=[ii.b])
        if hf == 0:
            P.op("dve", lambda e: e.tensor_tensor_scan(rr[:], aa[:], ii[:], 0.0, ALU.mult, ALU.add),
                 reads=[aa.b, ii.b], writes=[rr.b])
        else:
            P.op("dve", lambda e: e.tensor_tensor_scan(rr[:], aa[:], ii[:], prev["rr"][:, TH - 1:TH], ALU.mult, ALU.add),
                 reads=[aa.b, ii.b, prev["rr"].b], writes=[rr.b])
        P.op("pool", lambda e: e.tensor_tensor(zT[:, j, t0:t0 + TH], rr[:], bb[:], ALU.mult),
             reads=[rr.b, bb.b], writes=[zT.b])

    cnt = 0
    for b in range(NB):
        for tt in range(NT):
            front_tile(kb, g, b, tt, x_src, hT, hT.b)
        for j in range(8):
            for hf in range(S // TH):
                half(b, j, hf, sets[cnt % 2], sets[(cnt - 1) % 2])
                cnt += 1
        for tt in range(NT):
            back_tile(kb, g, b, tt, zT, [zT.b], wout, x_src, x_dst)
    kb.pop()


WSHAPES = None


def make_consts():
    c = {}
    c["ident"] = np.eye(128, dtype=np.float32)
    c.update(gla_consts())
    c.update(dsa_consts())
    c.update(rwkv_consts())
    return c


def build(layers, wshapes, consts_np):
    nc = bass.Bass("TRN2", target_bir_lowering=False)
    W = {}
    for k, shp in wshapes.items():
        if k == "x":
            continue
        W[k] = nc.dram_tensor(k, list(shp), F32, kind="ExternalInput").ap()
    consts = {k: nc.dram_tensor("k_" + k, list(v.shape), F32, kind="ExternalInput").ap()
              for k, v in consts_np.items()}
    x_in = T(nc.dram_tensor("x", [NB, S, D], F32, kind="ExternalInput").ap(), "x_in")
    y_out = T(nc.dram_tensor("y", [NB, S, D], F32, kind="ExternalOutput").ap(), "y_out")
    with ExitStack() as st:
        P = Prog(nc)
        P.alloc_sems(st)
        kb = KB(nc, P, st)
        g = build_common(kb, consts)
        g["consts"] = consts
        scr = [kb.dram("xs%d" % i, [NB, S, D]) for i in range(2)]
        fns = {0: None, 1: layer_lru, 2: None, 3: None}
        fns.update(LAYER_FNS)
        src = x_in
        for li, layer in enumerate(layers):
            dst = y_out if li == len(layers) - 1 else scr[li % 2]
            fns[layer](kb, g, W, src, dst)
            src = dst
        P.finish([y_out.b])
        P.emit()
    return nc


LAYER_FNS = {}


def layer_gla(kb, g, W, x_src, x_dst):
    nc, P = kb.nc, kb.P
    C = g["consts"]
    kb.push()
    win = kb.sb("gla_win", [128, 8, 3088], BF16)
    wout = kb.sb("gla_wout", [128, 8, 1024], BF16)
    begin_load(kb, g)
    compute_mod(kb, g, 3, W)
    load_w(kb, g, win, 0, W["gla_w_in"][0], 3088)
    load_w(kb, g, wout, 0, W["gla_w_out"][0], 1024)
    end_load(kb, g)
    aw2 = kb.sb("aw2", [16, 512], F32)
    P.dma("sp", aw2[:], W["gla_alpha_w2"][0], writes=[aw2.b])
    nab = kb.sb("nab", [128, 4], F32)
    P.dma("sp", nab[:], W["gla_alpha_b"][0].rearrange("(k p) -> p k", p=128),
          writes=[nab.b], allow_slow_non_contiguous=True)
    P.op("dve", lambda e: e.tensor_scalar(nab[:], nab[:], -1.0, None, ALU.mult),
         reads=[nab.b], writes=[nab.b])
    gbc = kb.sb("gbc", [128, 256], F32)
    P.dma("sp", gbc[:], W["gla_norm_gain"][0:1, :].to_broadcast([128, 256]), writes=[gbc.b])
    smask = kb.sb("smask", [128, 512], F32)
    P.dma("sp", smask[:], C["gla_scanmask"][:, :], writes=[smask.b])
    mbd = kb.sb("mbd", [128, 128], F32)
    P.dma("sp", mbd[:], C["gla_mbd"][:, :], writes=[mbd.b])
    m2 = kb.sb("m2", [128, 128], F32)
    P.dma("sp", m2[:], C["gla_m2"][:, :], writes=[m2.b])

    TB = 512
    hT = kb.sb("hTb", [128, 8, TB], BF16)
    zT = kb.sb("zTb", [128, 8, TB], BF16)
    alow = kb.sb("alow", [16, TB], F32)
    laT = kb.sb("laT", [128, TB], F32)
    cumT = kb.sb("cumT", [128, TB], F32)
    eq = kb.sb("eq", [128, TB], F32)
    ek = kb.sb("ek", [128, TB], F32)
    qdT = kb.sb("qdT", [128, 4, TB], F32)
    kiT = kb.sb("kiT", [128, 4, TB], F32)
    el = kb.sb("el", [128, 4, 8], F32)
    latok = kb.sb("latok", [128, 4, 512], F32)
    edl = kb.sb("edl", [128, 512], F32)
    kend = kb.sb("kend", [128, 512], F32)
    vtok = kb.sb("vtok", [128, 1024], F32)
    sg = kb.sb("sg", [128, 1024], F32)
    attm4 = kb.sb("attm4", [128, 4, 128], F32)
    Sst = kb.sb("Sst", [128, 4, 256], F32)
    zz = kb.sb("zz", [128, 1024], F32)
    ss = kb.sb("ss", [128, 8], F32)
    ps = g["ps"]
    scale_q = 128.0 ** -0.5
    for b in range(NB):
        P.op("dve", lambda e: e.memset(Sst[:], 0.0), writes=[Sst.b])
        for tb in range(S // TB):
            for tl in range(4):
                front_tile(kb, g, b, tb * 4 + tl, x_src, _Shift(hT, tb * TB), hT.b)
            for c in range(8):
                P.op("pe", lambda e, c=c: e.matmul(ps[4][0:16, :], win[:, c, 3072:3088], hT[:, c, :],
                                                   start=(c == 0), stop=(c == 7)),
                     reads=[win.b, hT.b], writes=[ps[4].b], pos="M16")
            P.op("act", lambda e: e.copy(alow[:], ps[4][0:16, :]), reads=[ps[4].b], writes=[alow.b])
            for h in range(4):
                P.op("pe", lambda e, h=h: e.matmul(ps[4][:], aw2[:, h * 128:(h + 1) * 128], alow[:],
                                                   start=True, stop=True),
                     reads=[aw2.b, alow.b], writes=[ps[4].b], pos="K16")
                P.op("act", lambda e, h=h: e.activation(laT[:], ps[4][:], AF.Exp, bias=nab[:, h:h + 1], scale=-1.0),
                     reads=[ps[4].b, nab.b], writes=[laT.b])
                P.op("act", lambda e: e.activation(laT[:], laT[:], AF.Ln, bias=1.0),
                     reads=[laT.b], writes=[laT.b])
                P.op("dve", lambda e: e.tensor_scalar(laT[:], laT[:], -1.0 / 16.0, None, ALU.mult),
                     reads=[laT.b], writes=[laT.b])
                P.op("dve", lambda e: e.tensor_tensor_scan(cumT[:], smask[:], laT[:], 0.0, ALU.mult, ALU.add),
                     reads=[smask.b, laT.b], writes=[cumT.b])
                P.op("act", lambda e: e.activation(eq[:], cumT[:], AF.Exp),
                     reads=[cumT.b], writes=[eq.b])
                P.op("act", lambda e: e.activation(ek[:], cumT[:], AF.Exp, scale=-1.0),
                     reads=[cumT.b], writes=[ek.b])
                for c in range(8):
                    P.op("pe", lambda e, c=c, h=h: e.matmul(ps[5][:], win[:, c, h * 128:(h + 1) * 128], hT[:, c, :],
                                                            start=(c == 0), stop=(c == 7)),
                         reads=[win.b, hT.b], writes=[ps[5].b])
                P.op("dve", lambda e, h=h: e.scalar_tensor_tensor(qdT[:, h, :], ps[5][:], scale_q, eq[:],
                                                                  ALU.mult, ALU.mult),
                     reads=[ps[5].b, eq.b], writes=[qdT.b])
                for c in range(8):
                    P.op("pe", lambda e, c=c, h=h: e.matmul(ps[5][:], win[:, c, 512 + h * 128:512 + (h + 1) * 128],
                                                            hT[:, c, :], start=(c == 0), stop=(c == 7)),
                         reads=[win.b, hT.b], writes=[ps[5].b])
                P.op("dve", lambda e, h=h: e.tensor_tensor(kiT[:, h, :], ps[5][:], ek[:], ALU.mult),
                     reads=[ps[5].b, ek.b], writes=[kiT.b])
                P.op("dve", lambda e, h=h: e.tensor_copy(
                    el[:, h, :], eq[:].rearrange("p (n c) -> p n c", c=64)[:, :, 63]),
                    reads=[eq.b], writes=[el.b])
                for tl in range(4):
                    P.op("pe", lambda e, tl=tl: e.transpose(ps[2][:, tl * 128:(tl + 1) * 128],
                                                            laT[:, tl * 128:(tl + 1) * 128], g["ident"][:]),
                         reads=[laT.b, g["ident"].b], writes=[ps[2].b])
                P.op("act", lambda e, h=h: e.copy(
                    latok[:, :, h * 128:(h + 1) * 128], ps[2][:].rearrange("p (t f) -> p t f", f=128)),
                    reads=[ps[2].b], writes=[latok.b])
            for tl in range(4):
                tsl = slice(tl * 128, (tl + 1) * 128)
                P.op("pe", lambda e, tl=tl: e.matmul(ps[2][:], m2[:], latok[:, tl, :], start=True, stop=True),
                     reads=[m2.b, latok.b], writes=[ps[2].b])
                P.op("act", lambda e: e.activation(edl[:], ps[2][:], AF.Exp), reads=[ps[2].b], writes=[edl.b])
                for c in range(8):
                    P.op("pe", lambda e, c=c, tsl=tsl: e.matmul(ps[3][:], hT[:, c, tsl], win[:, c, 512:1024],
                                                                start=(c == 0), stop=(c == 7)),
                         reads=[win.b, hT.b], writes=[ps[3].b])
                P.op("dve", lambda e: e.tensor_tensor(kend[:], ps[3][:], edl[:], ALU.mult),
                     reads=[ps[3].b, edl.b], writes=[kend.b])
                for half in range(2):
                    pt = ps[4 + half]
                    for c in range(8):
                        P.op("pe", lambda e, c=c, tsl=tsl, pt=pt, half=half: e.matmul(
                            pt[:], hT[:, c, tsl], win[:, c, 1024 + half * 512:1024 + (half + 1) * 512],
                            start=(c == 0), stop=(c == 7)),
                            reads=[win.b, hT.b], writes=[pt.b])
                    P.op("act", lambda e, pt=pt, half=half: e.copy(vtok[:, half * 512:(half + 1) * 512], pt[:]),
                         reads=[pt.b], writes=[vtok.b])
                for half in range(2):
                    pt = ps[4 + half]
                    for c in range(8):
                        P.op("pe", lambda e, c=c, tsl=tsl, pt=pt, half=half: e.matmul(
                            pt[:], hT[:, c, tsl], win[:, c, 2048 + half * 512:2048 + (half + 1) * 512],
                            start=(c == 0), stop=(c == 7)),
                            reads=[win.b, hT.b], writes=[pt.b])
                    P.op("act", lambda e, pt=pt, half=half: e.activation(
                        sg[:, half * 512:(half + 1) * 512], pt[:], AF.Silu),
                        reads=[pt.b], writes=[sg.b])
                hp = [(ps[h // 2], slice((h % 2) * 256, (h % 2 + 1) * 256), slice(h * 256, (h + 1) * 256))
                      for h in range(4)]
                for h in range(4):
                    P.op("pe", lambda e, h=h, tsl=tsl: e.matmul(ps[2][:, h * 128:(h + 1) * 128], kiT[:, h, tsl],
                                                                qdT[:, h, tsl], start=True, stop=True),
                         reads=[kiT.b, qdT.b], writes=[ps[2].b])
                P.op("dve", lambda e: e.tensor_tensor(
                    attm4[:], ps[2][:].rearrange("p (h t) -> p h t", t=128),
                    mbd[:].unsqueeze(1).to_broadcast([128, 4, 128]), ALU.mult),
                    reads=[ps[2].b, mbd.b], writes=[attm4.b])
                for h in range(4):
                    po, pc, vsl = hp[h]
                    P.op("pe", lambda e, po=po, pc=pc, vsl=vsl, h=h: e.matmul(po[:, pc], attm4[:, h, :], vtok[:, vsl],
                                                                              start=(h % 2 == 0), stop=False),
                         reads=[attm4.b, vtok.b], writes=[po.b])
                for cc in range(2):
                    n = tl * 2 + cc
                    psl = slice(cc * 64, (cc + 1) * 64)
                    qsl = slice(tl * 128 + cc * 64, tl * 128 + (cc + 1) * 64)
                    for h in range(4):
                        po, pc, vsl = hp[h]
                        P.op("pe", lambda e, po=po, pc=pc, psl=psl, qsl=qsl, h=h: e.matmul(
                            po[psl, pc], qdT[:, h, qsl], Sst[:, h, :], start=False, stop=True),
                            reads=[qdT.b, Sst.b], writes=[po.b], pos=("T1" if cc else "T0"))
                    for h in range(4):
                        po, pc, vsl = hp[h]
                        kvb = ps[3] if h < 2 else ps[5]
                        kc = slice((h % 2) * 256, (h % 2 + 1) * 256)
                        P.op("pe", lambda e, psl=psl, h=h, vsl=vsl, kvb=kvb, kc=kc: e.matmul(
                            kvb[:, kc], kend[psl, h * 128:(h + 1) * 128], vtok[psl, vsl],
                            start=True, stop=True),
                            reads=[kend.b, vtok.b], writes=[kvb.b], pos=("R1" if cc else "R0"))
                    for h in range(4):
                        kvb = ps[3] if h < 2 else ps[5]
                        kc = slice((h % 2) * 256, (h % 2 + 1) * 256)
                        P.op("dve", lambda e, h=h, n=n, kvb=kvb, kc=kc: e.scalar_tensor_tensor(
                            Sst[:, h, :], Sst[:, h, :], el[:, h, n:n + 1], kvb[:, kc], ALU.mult, ALU.add),
                            reads=[Sst.b, el.b, kvb.b], writes=[Sst.b])
                P.op("dve", lambda e: e.memset(ss[:], 0.0), writes=[ss.b])
                for h in range(4):
                    po = ps[h // 2]
                    pc = slice((h % 2) * 256, (h % 2 + 1) * 256)
                    P.op("act", lambda e, po=po, pc=pc, h=h: e.activation(
                        g["junk"][:, 0:256], po[:, pc], AF.Square, accum_out=ss[:, h:h + 1]),
                        reads=[po.b, ss.b], writes=[g["junk"].b, ss.b])
                P.op("dve", lambda e: e.tensor_scalar(ss[:, 4:8], ss[:, 0:4], 1.0 / 256.0, EPS, ALU.mult, ALU.add),
                     reads=[ss.b], writes=[ss.b])
                P.op("act", lambda e: e.activation(ss[:, 4:8], ss[:, 4:8], AF.Sqrt), reads=[ss.b], writes=[ss.b])
                P.op("dve", lambda e: e.reciprocal(ss[:, 4:8], ss[:, 4:8]), reads=[ss.b], writes=[ss.b])
                for h in range(4):
                    po = ps[h // 2]
                    pc = slice((h % 2) * 256, (h % 2 + 1) * 256)
                    vsl = slice(h * 256, (h + 1) * 256)
                    P.op("dve", lambda e, po=po, pc=pc, vsl=vsl, h=h: e.scalar_tensor_tensor(
                        zz[:, vsl], po[:, pc], ss[:, 4 + h:5 + h], gbc[:], ALU.mult, ALU.mult),
                        reads=[po.b, ss.b, gbc.b], writes=[zz.b])
                P.op("pool", lambda e: e.tensor_tensor(zz[:], zz[:], sg[:], ALU.mult),
                     reads=[zz.b, sg.b], writes=[zz.b])
                for c in range(8):
                    pt = ps[4 + c // 4]
                    P.op("pe", lambda e, pt=pt, c=c: e.transpose(
                        pt[:, (c % 4) * 128:(c % 4 + 1) * 128], zz[:, c * 128:(c + 1) * 128], g["ident"][:]),
                        reads=[zz.b, g["ident"].b], writes=[pt.b])
                for half in range(2):
                    pt = ps[4 + half]
                    eng = ("act", "dve")[half]
                    if half == 0:
                        P.op("act", lambda e, pt=pt, tsl=tsl: e.copy(
                            zT[:, 0:4, tsl], pt[:].rearrange("p (c t) -> p c t", t=128)),
                            reads=[pt.b], writes=[zT.b])
                    else:
                        P.op("dve", lambda e, pt=pt, tsl=tsl: e.tensor_copy(
                            zT[:, 4:8, tsl], pt[:].rearrange("p (c t) -> p c t", t=128)),
                            reads=[pt.b], writes=[zT.b])
            for tl in range(4):
                back_tile(kb, g, b, tb * 4 + tl, _Shift(zT, tb * TB), [zT.b], wout, x_src, x_dst)
    kb.pop()


class _Shift:
    def __init__(self, t, t0):
        self.t, self.t0 = t, t0

    def __getitem__(self, k):
        a, c, sl = k
        return self.t[a, c, sl.start - self.t0:sl.stop - self.t0]


def gla_consts():
    c = {}
    t = np.arange(512)
    c["gla_scanmask"] = np.tile((t % 64 != 0).astype(np.float32)[None, :], (128, 1))
    j = np.arange(128)[:, None]
    i = np.arange(128)[None, :]
    same = (j // 64) == (i // 64)
    c["gla_mbd"] = (same & (j <= i)).astype(np.float32)
    c["gla_m2"] = (same & (j > i)).astype(np.float32)
    return c


LAYER_FNS[3] = layer_gla


NEG = -1.0e30


def _rope(P, eng2, x, cos, sin, out_list, tmp1, tmp2, nh, half, rd, wr):
    xv = x[:, 0:nh * 2 * half].rearrange("p (h two d) -> p h two d", two=2, d=half)
    x1, x2 = xv[:, :, 0, :], xv[:, :, 1, :]
    cb = cos.unsqueeze(1).to_broadcast([128, nh, half])
    sb_ = sin.unsqueeze(1).to_broadcast([128, nh, half])
    t1 = tmp1[:, 0:nh * half].rearrange("p (h d) -> p h d", d=half)
    t2 = tmp2[:, 0:nh * half].rearrange("p (h d) -> p h d", d=half)
    P.op("dve", lambda e: e.tensor_tensor(t1, x1, cb, ALU.mult), reads=rd, writes=[tmp1.b])
    P.op(eng2, lambda e: e.tensor_tensor(t2, x2, sb_, ALU.mult), reads=rd, writes=[tmp2.b])
    for o in out_list:
        P.op("dve", lambda e, o=o: e.tensor_tensor(o[0], t1, t2, ALU.subtract),
             reads=[tmp1.b, tmp2.b], writes=wr)
    P.op("dve", lambda e: e.tensor_tensor(t1, x2, cb, ALU.mult), reads=rd + [tmp1.b], writes=[tmp1.b])
    P.op(eng2, lambda e: e.tensor_tensor(t2, x1, sb_, ALU.mult), reads=rd + [tmp2.b], writes=[tmp2.b])
    for o in out_list:
        P.op("dve", lambda e, o=o: e.tensor_tensor(o[1], t1, t2, ALU.add),
             reads=[tmp1.b, tmp2.b], writes=wr)


def layer_dsa(kb, g, W, x_src, x_dst):
    nc, P = kb.nc, kb.P
    C = g["consts"]
    kb.push()
    win = kb.sb("dsa_win", [128, 8, 3720], BF16)
    wout = kb.sb("dsa_wout", [128, 8, 1024], BF16)
    begin_load(kb, g)
    compute_mod(kb, g, 0, W)
    load_w(kb, g, win, 0, W["dsa_w_in"][0], 3720)
    load_w(kb, g, wout, 0, W["dsa_w_out"][0], 1024)
    end_load(kb, g)
    ps = g["ps"]
    psb = [kb.psum("psb%d" % i, [128, 1024], BF16) for i in range(2)]
    ident, identb = g["ident"], g["identb"]
    cs32 = kb.sb("cs32", [128, 2, NT, 32], F32)
    cs64 = kb.sb("cs64", [128, 2, NT, 64], F32)
    for i, nm in enumerate(("cos32", "sin32")):
        P.dma("sp", cs32[:, i], C[nm].rearrange("(tt p) f -> p tt f", p=128), writes=[cs32.b])
    for i, nm in enumerate(("cos64", "sin64")):
        P.dma("sp", cs64[:, i], C[nm].rearrange("(tt p) f -> p tt f", p=128), writes=[cs64.b])
    cmask = kb.sb("cmask", [128, 128], F32)
    P.dma("sp", cmask[:], C["causal"][:, :], writes=[cmask.b])
    gq = kb.sb("gq", [128, 2, 64], F32)
    P.dma("sp", gq[:, 0, :], W["dsa_q_gain"][0:1, :].to_broadcast([128, 64]), writes=[gq.b])
    P.dma("sp", gq[:, 1, :], W["dsa_k_gain"][0:1, :].to_broadcast([128, 64]), writes=[gq.b])
    kT2 = kb.sb("kT2", [128, 4, S], BF16)
    v1 = kb.sb("v1", [128, NT, 4, 65], BF16)
    kiT = kb.sb("kiT", [128, S], BF16)
    hTt = kb.sb("hTt", [128, 8, 128], BF16)
    big = kb.sb("big", [128, S], F32)
    qf = _Off(big, 0)
    tmpa = _Off(big, 1024)
    tmpb = kb.sb("tmpb", [128, 512], F32)
    tmpc = kb.sb("tmpc", [128, 512], F32)
    qb = kb.sb("qb", [128, 1024], BF16)
    kb2 = kb.sb("kb2", [128, 4, 2, 64], BF16)
    qTt = kb.sb("qTt", [128, 8, 128], BF16)
    qiTt = kb.sb("qiTt", [128, 8, 128], BF16)
    sgt = kb.sb("sgt", [128, 1024], BF16)
    kf = kb.sb("kf", [128, 512], F32)
    wkf = kb.sb("wkf", [128, 136], F32)
    kib = kb.sb("kib", [128, 128], BF16)
    sm = kb.sb("sm", [128, 64], F32)
    score = kb.sb("score", [128, S], F32)
    work = big
    rl = [kb.sb("rl", [128, 512], F32) for _ in range(2)]
    maskb = kb.sb("maskb", [128, S], BF16)
    maskT = kb.sb("maskT", [128, NT, 128], BF16)
    eb = [kb.sb("eb", [128, 512], BF16) for _ in range(2)]
    pT = [kb.sb("pT", [128, 512], BF16) for _ in range(2)]
    m8 = kb.sb("m8", [128, 8], F32)
    thr = kb.sb("thr", [128, 2], F32)
    zz = _Off(big, 0)
    zTt = kb.sb("zTt", [128, 8, 128], BF16)
    rden = kb.sb("rden", [128, 16], F32)
    IDXS = 1024.0 ** -0.5
    P.op("pool", lambda e: e.memset(v1[:], 1.0), writes=[v1.b])

    def rms_heads(x, nh, gi, out):
        xv = x[:, 0:nh * 64].rearrange("p (h d) -> p h d", d=64)
        P.op("act", lambda e: e.activation(tmpa[:, 0:nh * 64], x[:, 0:nh * 64], AF.Square),
             reads=[x.b], writes=[tmpa.b])
        P.op("dve", lambda e: e.tensor_reduce(sm[:, 0:nh], tmpa[:, 0:nh * 64].rearrange("p (h d) -> p h d", d=64),
                                              AX.X, ALU.add),
             reads=[tmpa.b], writes=[sm.b])
        P.op("dve", lambda e: e.tensor_scalar(sm[:, 0:nh], sm[:, 0:nh], 1.0 / 64.0, EPS, ALU.mult, ALU.add),
             reads=[sm.b], writes=[sm.b])
        P.op("act", lambda e: e.activation(sm[:, 0:nh], sm[:, 0:nh], AF.Sqrt), reads=[sm.b], writes=[sm.b])
        P.op("dve", lambda e: e.reciprocal(sm[:, 0:nh], sm[:, 0:nh]), reads=[sm.b], writes=[sm.b])
        ov = out[:, 0:nh * 64].rearrange("p (h d) -> p h d", d=64)
        P.op("dve", lambda e: e.tensor_tensor(ov, xv, sm[:, 0:nh].unsqueeze(2).to_broadcast([128, nh, 64]), ALU.mult),
             reads=[x.b, sm.b], writes=[out.b])
        P.op("pool", lambda e: e.tensor_tensor(ov, ov, gq[:, gi, :].unsqueeze(1).to_broadcast([128, nh, 64]), ALU.mult),
             reads=[out.b, gq.b], writes=[out.b])

    for b in range(NB):
        for tt in range(NT):
            tsl = slice(tt * 128, (tt + 1) * 128)
            L = (tt + 1) * 128
            front_tile(kb, g, b, tt, x_src, _Shift(hTt, tt * 128), hTt.b)

            def proj(pt, c0, n):
                for c in range(8):
                    P.op("pe", lambda e, c=c: e.matmul(pt[:, 0:n], hTt[:, c, :], win[:, c, c0:c0 + n],
                                                       start=(c == 0), stop=(c == 7)),
                         reads=[hTt.b, win.b], writes=[pt.b])
            for half in range(2):
                proj(ps[2 + half], half * 512, 512)
                P.op("act", lambda e, half=half: e.copy(qf[:, half * 512:(half + 1) * 512], ps[2 + half][:]),
                     reads=[ps[2 + half].b], writes=[qf.b])
            rms_heads(qf, 16, 0, qf)
            qbv = qb[:].rearrange("p (h two d) -> p h two d", two=2, d=32)
            _rope(P, "pool", qf, cs32[:, 0, tt, :], cs32[:, 1, tt, :], [(qbv[:, :, 0, :], qbv[:, :, 1, :])],
                  tmpb, tmpc, 16, 32, [qf.b, cs32.b], [qb.b])
            for c in range(8):
                P.op("pe", lambda e, c=c: e.transpose(psb[0][:, c * 128:(c + 1) * 128], qb[:, c * 128:(c + 1) * 128],
                                                      identb[:]),
                     reads=[qb.b, identb.b], writes=[psb[0].b])
            P.op("act", lambda e: e.copy(qTt[:], psb[0][:].rearrange("p (c t) -> p c t", t=128)),
                 reads=[psb[0].b], writes=[qTt.b])
            proj(ps[4], 1024, 512)
            P.op("act", lambda e: e.copy(kf[:], ps[4][:]), reads=[ps[4].b], writes=[kf.b])
            P.op("dve", lambda e, tt=tt: e.tensor_copy(
                v1[:, tt, :, 0:64], kf[:, 256:512].rearrange("p (g d) -> p g d", d=64)),
                reads=[kf.b], writes=[v1.b])
            rms_heads(kf, 4, 1, kf)
            k2v = kb2[:].rearrange("p g r (two d) -> p g r two d", two=2)
            _rope(P, "pool", kf, cs32[:, 0, tt, :], cs32[:, 1, tt, :],
                  [(k2v[:, :, 0, 0, :], k2v[:, :, 0, 1, :]), (k2v[:, :, 1, 0, :], k2v[:, :, 1, 1, :])],
                  tmpb, tmpc, 4, 32, [kf.b, cs32.b], [kb2.b])
            for gg in range(4):
                P.op("pe", lambda e, gg=gg: e.transpose(
                    psb[1][:, gg * 128:(gg + 1) * 128], kb2[:, gg].rearrange("p r d -> p (r d)"), identb[:]),
                    reads=[kb2.b, identb.b], writes=[psb[1].b])
            P.op("act", lambda e, tsl=tsl: e.copy(kT2[:, :, tsl], psb[1][:, 0:512].rearrange("p (g t) -> p g t", t=128)),
                 reads=[psb[1].b], writes=[kT2.b])
            for half in range(2):
                proj(ps[2 + half], 1536 + half * 512, 512)
                P.op("act", lambda e, half=half: e.activation(sgt[:, half * 512:(half + 1) * 512], ps[2 + half][:], AF.Silu),
                     reads=[ps[2 + half].b], writes=[sgt.b])
            for half in range(2):
                proj(ps[4 + half], 2560 + half * 512, 512)
                P.op("act", lambda e, half=half: e.copy(qf[:, half * 512:(half + 1) * 512], ps[4 + half][:]),
                     reads=[ps[4 + half].b], writes=[qf.b])
            qbv2 = qb[:].rearrange("p (h two d) -> p h two d", two=2, d=64)
            _rope(P, "pool", qf, cs64[:, 0, tt, :], cs64[:, 1, tt, :], [(qbv2[:, :, 0, :], qbv2[:, :, 1, :])],
                  tmpb, tmpc, 8, 64, [qf.b, cs64.b], [qb.b])
            for c in range(8):
                P.op("pe", lambda e, c=c: e.transpose(psb[0][:, c * 128:(c + 1) * 128], qb[:, c * 128:(c + 1) * 128],
                                                      identb[:]),
                     reads=[qb.b, identb.b], writes=[psb[0].b])
            P.op("act", lambda e: e.copy(qiTt[:], psb[0][:].rearrange("p (c t) -> p c t", t=128)),
                 reads=[psb[0].b], writes=[qiTt.b])
            proj(ps[4], 3584, 136)
            P.op("act", lambda e: e.copy(wkf[:], ps[4][:, 0:136]), reads=[ps[4].b], writes=[wkf.b])
            P.op("dve", lambda e: e.tensor_scalar(wkf[:, 0:8], wkf[:, 0:8], IDXS, None, ALU.mult),
                 reads=[wkf.b], writes=[wkf.b])
            kiv = kib[:].rearrange("p (h two d) -> p h two d", two=2, d=64)
            _rope(P, "pool", _Off(wkf, 8), cs64[:, 0, tt, :], cs64[:, 1, tt, :], [(kiv[:, :, 0, :], kiv[:, :, 1, :])],
                  tmpb, tmpc, 1, 64, [wkf.b, cs64.b], [kib.b])
            P.op("pe", lambda e: e.transpose(psb[1][:, 0:128], kib[:], identb[:]),
                 reads=[kib.b, identb.b], writes=[psb[1].b])
            P.op("act", lambda e, tsl=tsl: e.copy(kiT[:, tsl], psb[1][:, 0:128]), reads=[psb[1].b], writes=[kiT.b])
            n_it = 0
            for k0 in range(0, L, 512):
                w = min(512, L - k0)
                for h in range(8):
                    pt = ps[2 + n_it % 2]
                    r_ = rl[n_it % 2]
                    n_it += 1
                    P.op("pe", lambda e, pt=pt, h=h, k0=k0, w=w: e.matmul(
                        pt[:, 0:w], qiTt[:, h, :], kiT[:, k0:k0 + w], start=True, stop=True),
                        reads=[qiTt.b, kiT.b], writes=[pt.b])
                    P.op("act", lambda e, pt=pt, r_=r_, w=w: e.activation(r_[:, 0:w], pt[:, 0:w], AF.Relu),
                         reads=[pt.b], writes=[r_.b])
                    if h == 0:
                        P.op("dve", lambda e, r_=r_, k0=k0, w=w, h=h: e.tensor_scalar(
                            score[:, k0:k0 + w], r_[:, 0:w], wkf[:, h:h + 1], None, ALU.mult),
                            reads=[r_.b, wkf.b], writes=[score.b])
                    else:
                        P.op("dve", lambda e, r_=r_, k0=k0, w=w, h=h: e.scalar_tensor_tensor(
                            score[:, k0:k0 + w], r_[:, 0:w], wkf[:, h:h + 1], score[:, k0:k0 + w],
                            ALU.mult, ALU.add),
                            reads=[r_.b, wkf.b, score.b], writes=[score.b])
            P.op("dve", lambda e, tsl=tsl: e.tensor_tensor(score[:, tsl], score[:, tsl], cmask[:], ALU.add),
                 reads=[score.b, cmask.b], writes=[score.b])
            if tt >= 2:
                P.op("pool", lambda e, L=L: e.tensor_copy(work[:, 0:L], score[:, 0:L]),
                     reads=[score.b], writes=[work.b])
                for it in range(32):
                    P.op("dve", lambda e, L=L: e.max(m8[:], work[:, 0:L]), reads=[work.b], writes=[m8.b])
                    if it < 31:
                        P.op("dve", lambda e, L=L: e.match_replace(work[:, 0:L], m8[:], work[:, 0:L], NEG),
                             reads=[work.b, m8.b], writes=[work.b])
                P.op("dve", lambda e: e.tensor_reduce(thr[:, 0:1], m8[:], AX.X, ALU.min),
                     reads=[m8.b], writes=[thr.b])
                P.op("dve", lambda e: e.tensor_scalar(thr[:, 0:1], thr[:, 0:1], -1.0e29, None, ALU.max),
                     reads=[thr.b], writes=[thr.b])
            else:
                P.op("dve", lambda e: e.memset(thr[:, 0:1], -1.0e29), writes=[thr.b])
            P.op("dve", lambda e, L=L: e.tensor_scalar(maskb[:, 0:L], score[:, 0:L], thr[:, 0:1], None, ALU.is_ge),
                 reads=[score.b, thr.b], writes=[maskb.b])
            for k0 in range(0, tt + 1, 8):
                nk = min(8, tt + 1 - k0)
                for kk in range(nk):
                    kbk = k0 + kk
                    P.op("pe", lambda e, kk=kk, kbk=kbk: e.transpose(
                        psb[0][:, kk * 128:(kk + 1) * 128], maskb[:, kbk * 128:(kbk + 1) * 128], identb[:]),
                        reads=[maskb.b, identb.b], writes=[psb[0].b])
                P.op("act", lambda e, k0=k0, nk=nk: e.copy(
                    maskT[:, k0:k0 + nk, :], psb[0][:, 0:nk * 128].rearrange("p (k t) -> p k t", t=128)),
                    reads=[psb[0].b], writes=[maskT.b])
            n_it = 0
            for gg in range(4):
                po = ps[gg]
                for kbk in range(tt + 1):
                    ksl = slice(kbk * 128, (kbk + 1) * 128)
                    pt = ps[4 + n_it % 2]
                    e_ = eb[n_it % 2]
                    p_ = pT[n_it % 2]
                    n_it += 1
                    for par in range(2):
                        hs = slice(par * 64, (par + 1) * 64)
                        P.op("pe", lambda e, pt=pt, par=par, hs=hs, gg=gg, ksl=ksl: e.matmul(
                            pt[:, par * 256:(par + 1) * 256], kT2[hs, gg, ksl], qTt[hs, 2 * gg:2 * gg + 2, :],
                            start=True, stop=True),
                            reads=[kT2.b, qTt.b], writes=[pt.b], pos=(1 if par else None))
                    P.op("act", lambda e, pt=pt, e_=e_: e.activation(e_[:], pt[:], AF.Exp, scale=0.125),
                         reads=[pt.b], writes=[e_.b])
                    P.op("dve", lambda e, e_=e_, p_=p_, kbk=kbk: e.tensor_tensor(
                        p_[:].rearrange("p (a t) -> p a t", t=128), e_[:].rearrange("p (a t) -> p a t", t=128),
                        maskT[:, kbk, :].unsqueeze(1).to_broadcast([128, 4, 128]), ALU.mult),
                        reads=[e_.b, maskT.b], writes=[p_.b])
                    for blk in range(4):
                        c0 = blk * 65
                        P.op("pe", lambda e, po=po, c0=c0, p_=p_, blk=blk, tt=tt, kbk=kbk, gg=gg: e.matmul(
                            po[:, c0:c0 + 65], p_[:, blk * 128:(blk + 1) * 128], v1[:, kbk, gg, :],
                            start=(kbk == 0 and blk == 0), stop=(kbk == tt)),
                            reads=[p_.b, v1.b], writes=[po.b])
            for gg in range(4):
                po = ps[gg]
                pv = po[:, 0:260].rearrange("p (k d) -> p k d", d=65)
                P.op("dve", lambda e, pv=pv, gg=gg: e.reciprocal(rden[:, gg * 4:(gg + 1) * 4], pv[:, :, 64]),
                     reads=[po.b], writes=[rden.b])
                zv = zz[:, gg * 256:(gg + 1) * 256].rearrange("p (cp par d) -> p par cp d", cp=2, par=2)
                for par in range(2):
                    P.op("dve", lambda e, pv=pv, zv=zv, par=par, gg=gg: e.tensor_tensor(
                        zv[:, par], pv[:, par * 2:par * 2 + 2, 0:64],
                        rden[:, gg * 4 + par * 2:gg * 4 + par * 2 + 2].unsqueeze(2).to_broadcast([128, 2, 64]),
                        ALU.mult),
                        reads=[po.b, rden.b], writes=[zz.b])
            P.op("pool", lambda e: e.tensor_tensor(zz[:, 0:1024], zz[:, 0:1024], sgt[:], ALU.mult),
                 reads=[zz.b, sgt.b], writes=[zz.b])
            for c in range(8):
                pt = ps[4 + c // 4]
                P.op("pe", lambda e, pt=pt, c=c: e.transpose(
                    pt[:, (c % 4) * 128:(c % 4 + 1) * 128], zz[:, c * 128:(c + 1) * 128], ident[:]),
                    reads=[zz.b, ident.b], writes=[pt.b])
            P.op("act", lambda e: e.copy(zTt[:, 0:4, :], ps[4][:].rearrange("p (c t) -> p c t", t=128)),
                 reads=[ps[4].b], writes=[zTt.b])
            P.op("dve", lambda e: e.tensor_copy(zTt[:, 4:8, :], ps[5][:].rearrange("p (c t) -> p c t", t=128)),
                 reads=[ps[5].b], writes=[zTt.b])
            back_tile(kb, g, b, tt, _Shift(zTt, tt * 128), [zTt.b], wout, x_src, x_dst)
    kb.pop()


def layer_dsa2(kb, g, W, x_src, x_dst):
    nc, P = kb.nc, kb.P
    C = g["consts"]
    kb.push()
    win = kb.sb("dsa_win", [128, 8, 3720], BF16)
    wout = kb.sb("dsa_wout", [128, 8, 1024], BF16)
    begin_load(kb, g)
    compute_mod(kb, g, 0, W)
    load_w(kb, g, win, 0, W["dsa_w_in"][0], 3720)
    load_w(kb, g, wout, 0, W["dsa_w_out"][0], 1024)
    end_load(kb, g)
    ps = g["ps"]
    psb = [kb.psum("psb%d" % i, [128, 1024], BF16) for i in range(2)]
    ident, identb = g["ident"], g["identb"]
    cs32 = kb.sb("cs32", [128, 2, 32], F32)
    cs64 = kb.sb("cs64", [128, 2, 64], F32)
    cmask = kb.sb("cmask", [128, 128], F32)
    P.dma("sp", cmask[:], C["causal"][:, :], writes=[cmask.b])
    gq = kb.sb("gq", [128, 2, 64], F32)
    P.dma("sp", gq[:, 0, :], W["dsa_q_gain"][0:1, :].to_broadcast([128, 64]), writes=[gq.b])
    P.dma("sp", gq[:, 1, :], W["dsa_k_gain"][0:1, :].to_broadcast([128, 64]), writes=[gq.b])
    kT2 = kb.sb("kT2", [128, 4, S], BF16)
    v1 = kb.sb("v1", [128, NT, 4, 65], BF16)
    kiT = kb.sb("kiT", [128, S], BF16)
    kT2b = [Buf("kT2_%d" % i) for i in range(NT)]
    v1b = [Buf("v1_%d" % i) for i in range(NT)]
    kiTb = [Buf("kiT_%d" % i) for i in range(NT)]
    hTt = kb.sb("hTt", [128, 8, 128], BF16)
    big = kb.sb("big", [128, S], F32)
    qf = _Off(big, 0)
    tmpa = _Off(big, 1024)
    work = big
    tmpb = kb.sb("tmpb", [128, 512], F32)
    tmpc = kb.sb("tmpc", [128, 512], F32)
    qb = kb.sb("qb", [128, 1024], BF16)
    kb2 = kb.sb("kb2", [128, 4, 2, 64], BF16)
    qiTt = kb.sb("qiTt", [128, 8, 128], BF16)
    kf = kb.sb("kf", [128, 512], F32)
    wkf = kb.sb("wkf", [128, 136], F32)
    kib = kb.sb("kib", [128, 128], BF16)
    sm = kb.sb("sm", [128, 64], F32)
    score = kb.sb("score", [128, S], F32)
    rl = [kb.sb("rl", [128, 512], F32) for _ in range(2)]
    maskb = kb.sb("maskb", [128, S], BF16)
    m8 = kb.sb("m8", [128, 8], F32)
    thr = kb.sb("thr", [128, 2], F32)
    qTt = [kb.sb("qTz", [128, 16, 128], BF16) for _ in range(2)]
    for i_ in range(2):
        P.op("pool", lambda e, i_=i_: e.memset(qTt[i_][:], 0.0), writes=[qTt[i_].b])
    sgt = [kb.sb("sgt", [128, 1024], BF16) for _ in range(2)]
    maskT = [kb.sb("maskT", [128, NT, 128], BF16) for _ in range(2)]
    eb = [kb.sb("eb", [128, 512], BF16) for _ in range(2)]
    pT = [kb.sb("pT", [128, 512], BF16) for _ in range(2)]
    zz = kb.sb("zzd", [128, 1024], F32)
    zTt = kb.sb("zTt", [128, 8, 128], BF16)
    rden = kb.sb("rden", [128, 16], F32)
    IDXS = 1024.0 ** -0.5
    P.op("pool", lambda e: e.memset(v1[:], 1.0), writes=v1b)
    S1B = (ps[4], ps[5])

    def rms_heads(x, nh, gi, out):
        xv = x[:, 0:nh * 64].rearrange("p (h d) -> p h d", d=64)
        P.op("act", lambda e: e.activation(tmpa[:, 0:nh * 64], x[:, 0:nh * 64], AF.Square),
             reads=[x.b], writes=[tmpa.b])
        P.op("dve", lambda e: e.tensor_reduce(sm[:, 0:nh], tmpa[:, 0:nh * 64].rearrange("p (h d) -> p h d", d=64),
                                              AX.X, ALU.add),
             reads=[tmpa.b], writes=[sm.b])
        P.op("dve", lambda e: e.tensor_scalar(sm[:, 0:nh], sm[:, 0:nh], 1.0 / 64.0, EPS, ALU.mult, ALU.add),
             reads=[sm.b], writes=[sm.b])
        P.op("act", lambda e: e.activation(sm[:, 0:nh], sm[:, 0:nh], AF.Sqrt), reads=[sm.b], writes=[sm.b])
        P.op("dve", lambda e: e.reciprocal(sm[:, 0:nh], sm[:, 0:nh]), reads=[sm.b], writes=[sm.b])
        ov = out[:, 0:nh * 64].rearrange("p (h d) -> p h d", d=64)
        P.op("dve", lambda e: e.tensor_tensor(ov, xv, sm[:, 0:nh].unsqueeze(2).to_broadcast([128, nh, 64]), ALU.mult),
             reads=[x.b, sm.b], writes=[out.b])
        P.op("pool", lambda e: e.tensor_tensor(ov, ov, gq[:, gi, :].unsqueeze(1).to_broadcast([128, nh, 64]), ALU.mult),
             reads=[out.b, gq.b], writes=[out.b])

    def stage1(b, tt):
        hs_ = tt % 2
        qTt_c, sgt_c, maskT_c = qTt[hs_], sgt[hs_], maskT[hs_]
        tsl = slice(tt * 128, (tt + 1) * 128)
        L = (tt + 1) * 128
        for i, nm in enumerate(("cos32", "sin32")):
            P.dma("act", cs32[:, i, :], C[nm][tt * 128:(tt + 1) * 128, :], writes=[cs32.b])
        for i, nm in enumerate(("cos64", "sin64")):
            P.dma("act", cs64[:, i, :], C[nm][tt * 128:(tt + 1) * 128, :], writes=[cs64.b])
        front_tile(kb, g, b, tt, x_src, _Shift(hTt, tt * 128), hTt.b, pbanks=S1B, xi=0)
        yield

        def proj(pt, c0, n):
            for c in range(8):
                P.op("pe", lambda e, c=c: e.matmul(pt[:, 0:n], hTt[:, c, :], win[:, c, c0:c0 + n],
                                                   start=(c == 0), stop=(c == 7)),
                     reads=[hTt.b, win.b], writes=[pt.b])
        for half in range(2):
            proj(S1B[half], half * 512, 512)
            P.op("act", lambda e, half=half: e.copy(qf[:, half * 512:(half + 1) * 512], S1B[half][:]),
                 reads=[S1B[half].b], writes=[qf.b])
        rms_heads(qf, 16, 0, qf)
        qbv = qb[:].rearrange("p (h two d) -> p h two d", two=2, d=32)
        _rope(P, "pool", qf, cs32[:, 0, :], cs32[:, 1, :], [(qbv[:, :, 0, :], qbv[:, :, 1, :])],
              tmpb, tmpc, 16, 32, [qf.b, cs32.b], [qb.b])
        for c in range(8):
            P.op("pe", lambda e, c=c: e.transpose(psb[0][:, c * 128:(c + 1) * 128], qb[:, c * 128:(c + 1) * 128],
                                                  identb[:]),
                 reads=[qb.b, identb.b], writes=[psb[0].b])
        for h2 in range(2):
            hsl = slice(h2 * 64, (h2 + 1) * 64)
            P.op(("act", "dve")[h2], lambda e, h2=h2, hsl=hsl: (e.copy if h2 == 0 else e.tensor_copy)(
                qTt_c[hsl, :, :].rearrange("p (c two) t -> p c two t", two=2)[:, :, h2, :],
                psb[0][hsl, :].rearrange("p (c t) -> p c t", t=128)),
                reads=[psb[0].b], writes=[qTt_c.b])
        yield
        proj(S1B[0], 1024, 512)
        P.op("act", lambda e: e.copy(kf[:], S1B[0][:]), reads=[S1B[0].b], writes=[kf.b])
        P.op("dve", lambda e: e.tensor_copy(
            v1[:, tt, :, 0:64], kf[:, 256:512].rearrange("p (g d) -> p g d", d=64)),
            reads=[kf.b], writes=[v1b[tt]])
        rms_heads(kf, 4, 1, kf)
        k2v = kb2[:].rearrange("p g r (two d) -> p g r two d", two=2)
        _rope(P, "pool", kf, cs32[:, 0, :], cs32[:, 1, :],
              [(k2v[:, :, 0, 0, :], k2v[:, :, 0, 1, :]), (k2v[:, :, 1, 0, :], k2v[:, :, 1, 1, :])],
              tmpb, tmpc, 4, 32, [kf.b, cs32.b], [kb2.b])
        for gg in range(4):
            P.op("pe", lambda e, gg=gg: e.transpose(
                psb[1][:, gg * 128:(gg + 1) * 128], kb2[:, gg].rearrange("p r d -> p (r d)"), identb[:]),
                reads=[kb2.b, identb.b], writes=[psb[1].b])
        P.op("act", lambda e: e.copy(kT2[:, :, tsl], psb[1][:, 0:512].rearrange("p (g t) -> p g t", t=128)),
             reads=[psb[1].b], writes=[kT2b[tt]])
        yield
        for half in range(2):
            proj(S1B[half], 1536 + half * 512, 512)
            P.op("act", lambda e, half=half: e.activation(sgt_c[:, half * 512:(half + 1) * 512], S1B[half][:], AF.Silu),
                 reads=[S1B[half].b], writes=[sgt_c.b])
        yield
        for half in range(2):
            proj(S1B[half], 2560 + half * 512, 512)
            P.op("act", lambda e, half=half: e.copy(qf[:, half * 512:(half + 1) * 512], S1B[half][:]),
                 reads=[S1B[half].b], writes=[qf.b])
        qbv2 = qb[:].rearrange("p (h two d) -> p h two d", two=2, d=64)
        _rope(P, "pool", qf, cs64[:, 0, :], cs64[:, 1, :], [(qbv2[:, :, 0, :], qbv2[:, :, 1, :])],
              tmpb, tmpc, 8, 64, [qf.b, cs64.b], [qb.b])
        for c in range(8):
            P.op("pe", lambda e, c=c: e.transpose(psb[0][:, c * 128:(c + 1) * 128], qb[:, c * 128:(c + 1) * 128],
                                                  identb[:]),
                 reads=[qb.b, identb.b], writes=[psb[0].b])
        P.op("act", lambda e: e.copy(qiTt[:], psb[0][:].rearrange("p (c t) -> p c t", t=128)),
             reads=[psb[0].b], writes=[qiTt.b])
        yield
        proj(S1B[0], 3584, 136)
        P.op("act", lambda e: e.copy(wkf[:], S1B[0][:, 0:136]), reads=[S1B[0].b], writes=[wkf.b])
        P.op("dve", lambda e: e.tensor_scalar(wkf[:, 0:8], wkf[:, 0:8], IDXS, None, ALU.mult),
             reads=[wkf.b], writes=[wkf.b])
        kiv = kib[:].rearrange("p (h two d) -> p h two d", two=2, d=64)
        _rope(P, "pool", _Off(wkf, 8), cs64[:, 0, :], cs64[:, 1, :], [(kiv[:, :, 0, :], kiv[:, :, 1, :])],
              tmpb, tmpc, 1, 64, [wkf.b, cs64.b], [kib.b])
        P.op("pe", lambda e: e.transpose(psb[1][:, 0:128], kib[:], identb[:]),
             reads=[kib.b, identb.b], writes=[psb[1].b])
        P.op("act", lambda e: e.copy(kiT[:, tsl], psb[1][:, 0:128]), reads=[psb[1].b], writes=[kiTb[tt]])
        yield
        n_it = 0
        for k0 in range(0, L, 512):
            w = min(512, L - k0)
            kbufs = kiTb[k0 // 128:(k0 + w) // 128]
            for h in range(8):
                pt = S1B[n_it % 2]
                r_ = rl[n_it % 2]
                n_it += 1
                P.op("pe", lambda e, pt=pt, h=h, k0=k0, w=w: e.matmul(
                    pt[:, 0:w], qiTt[:, h, :], kiT[:, k0:k0 + w], start=True, stop=True),
                    reads=[qiTt.b] + kbufs, writes=[pt.b])
                P.op("act", lambda e, pt=pt, r_=r_, w=w: e.activation(r_[:, 0:w], pt[:, 0:w], AF.Relu),
                     reads=[pt.b], writes=[r_.b])
                if h == 0:
                    P.op("dve", lambda e, r_=r_, k0=k0, w=w, h=h: e.tensor_scalar(
                        score[:, k0:k0 + w], r_[:, 0:w], wkf[:, h:h + 1], None, ALU.mult),
                        reads=[r_.b, wkf.b], writes=[score.b])
                else:
                    P.op("dve", lambda e, r_=r_, k0=k0, w=w, h=h: e.scalar_tensor_tensor(
                        score[:, k0:k0 + w], r_[:, 0:w], wkf[:, h:h + 1], score[:, k0:k0 + w],
                        ALU.mult, ALU.add),
                        reads=[r_.b, wkf.b, score.b], writes=[score.b])
            yield
        P.op("dve", lambda e: e.tensor_tensor(score[:, tsl], score[:, tsl], cmask[:], ALU.add),
             reads=[score.b, cmask.b], writes=[score.b])
        if tt >= 2:
            P.op("pool", lambda e: e.tensor_copy(work[:, 0:L], score[:, 0:L]),
                 reads=[score.b], writes=[work.b])
            for it in range(32):
                P.op("dve", lambda e: e.max(m8[:], work[:, 0:L]), reads=[work.b], writes=[m8.b])
                if it < 31:
                    P.op("dve", lambda e: e.match_replace(work[:, 0:L], m8[:], work[:, 0:L], NEG),
                         reads=[work.b, m8.b], writes=[work.b])
                yield
            P.op("dve", lambda e: e.tensor_reduce(thr[:, 0:1], m8[:], AX.X, ALU.min),
                 reads=[m8.b], writes=[thr.b])
            P.op("dve", lambda e: e.tensor_scalar(thr[:, 0:1], thr[:, 0:1], -1.0e29, None, ALU.max),
                 reads=[thr.b], writes=[thr.b])
        else:
            P.op("dve", lambda e: e.memset(thr[:, 0:1], -1.0e29), writes=[thr.b])
        P.op("dve", lambda e: e.tensor_scalar(maskb[:, 0:L], score[:, 0:L], thr[:, 0:1], None, ALU.is_ge),
             reads=[score.b, thr.b], writes=[maskb.b])
        for k0 in range(0, tt + 1, 8):
            nk = min(8, tt + 1 - k0)
            for kk in range(nk):
                kbk = k0 + kk
                P.op("pe", lambda e, kk=kk, kbk=kbk: e.transpose(
                    psb[0][:, kk * 128:(kk + 1) * 128], maskb[:, kbk * 128:(kbk + 1) * 128], identb[:]),
                    reads=[maskb.b, identb.b], writes=[psb[0].b])
            P.op("act", lambda e, k0=k0, nk=nk: e.copy(
                maskT_c[:, k0:k0 + nk, :], psb[0][:, 0:nk * 128].rearrange("p (k t) -> p k t", t=128)),
                reads=[psb[0].b], writes=[maskT_c.b])
        yield

    def stage2(b, tt):
        hs_ = tt % 2
        qTt_c, sgt_c, maskT_c = qTt[hs_], sgt[hs_], maskT[hs_]
        n_it = 0
        for gg in range(4):
            for kbk in range(tt + 1):
                ksl = slice(kbk * 128, (kbk + 1) * 128)
                pt = ps[3]
                e_ = eb[n_it % 2]
                p_ = pT[n_it % 2]
                n_it += 1
                P.op("pe", lambda e, gg=gg, ksl=ksl: e.matmul(
                    pt[:], kT2[:, gg, ksl], qTt_c[:, 4 * gg:4 * gg + 4, :], start=True, stop=True),
                    reads=[kT2b[kbk], qTt_c.b], writes=[pt.b])
                P.op("act", lambda e, e_=e_: e.activation(e_[:], pt[:], AF.Exp, scale=0.125),
                     reads=[pt.b], writes=[e_.b])
                P.op("dve", lambda e, e_=e_, p_=p_, kbk=kbk: e.tensor_tensor(
                    p_[:].rearrange("p (a t) -> p a t", t=128), e_[:].rearrange("p (a t) -> p a t", t=128),
                    maskT_c[:, kbk, :].unsqueeze(1).to_broadcast([128, 4, 128]), ALU.mult),
                    reads=[e_.b, maskT_c.b], writes=[p_.b])
                for blk in range(4):
                    hidx = gg * 4 + blk
                    po = ps[hidx // 7]
                    c0 = (hidx % 7) * 65
                    P.op("pe", lambda e, po=po, c0=c0, p_=p_, blk=blk, kbk=kbk, gg=gg, hidx=hidx: e.matmul(
                        po[:, c0:c0 + 65], p_[:, blk * 128:(blk + 1) * 128], v1[:, kbk, gg, :],
                        start=(kbk == 0 and hidx % 7 == 0), stop=(kbk == tt)),
                        reads=[p_.b, v1b[kbk]], writes=[po.b])
                yield
        for bk in range(3):
            nh = 7 if bk < 2 else 2
            pv = ps[bk][:, 0:nh * 65].rearrange("p (k d) -> p k d", d=65)
            P.op("dve", lambda e, pv=pv, bk=bk, nh=nh: e.reciprocal(rden[:, bk * 7:bk * 7 + nh], pv[:, :, 64]),
                 reads=[ps[bk].b], writes=[rden.b])
        for hidx in range(16):
            gg, blk = hidx // 4, hidx % 4
            head = hidx
            po = ps[hidx // 7]
            c0 = (hidx % 7) * 65
            P.op("dve", lambda e, po=po, c0=c0, head=head, hidx=hidx: e.tensor_scalar(
                zz[:, head * 64:(head + 1) * 64], po[:, c0:c0 + 64], rden[:, hidx:hidx + 1], None, ALU.mult),
                reads=[po.b, rden.b], writes=[zz.b])
        P.op("pool", lambda e: e.tensor_tensor(zz[:], zz[:], sgt_c[:], ALU.mult),
             reads=[zz.b, sgt_c.b], writes=[zz.b])
        yield
        for c in range(8):
            pt2 = ps[c // 4]
            P.op("pe", lambda e, pt2=pt2, c=c: e.transpose(
                pt2[:, (c % 4) * 128:(c % 4 + 1) * 128], zz[:, c * 128:(c + 1) * 128], ident[:]),
                reads=[zz.b, ident.b], writes=[pt2.b])
        P.op("act", lambda e: e.copy(zTt[:, 0:4, :], ps[0][:].rearrange("p (c t) -> p c t", t=128)),
             reads=[ps[0].b], writes=[zTt.b])
        P.op("dve", lambda e: e.tensor_copy(zTt[:, 4:8, :], ps[1][:].rearrange("p (c t) -> p c t", t=128)),
             reads=[ps[1].b], writes=[zTt.b])
        back_tile(kb, g, b, tt, _Shift(zTt, tt * 128), [zTt.b], wout, x_src, x_dst, pbanks=(ps[2], ps[3]), xi=1)
        yield

    def n_seg1(tt):
        return 7 + (tt + 4) // 4 + (32 if tt >= 2 else 0) + 1

    def n_seg2(tt):
        return 4 * (tt + 1) + 2

    for b in range(NB):
        for step in range(NT + 1):
            s2 = stage2(b, step - 1) if step >= 1 else None
            s1 = stage1(b, step) if step < NT else None
            if s1 is None or s2 is None:
                for s_ in (s1, s2):
                    if s_ is not None:
                        for _ in s_:
                            pass
                continue
            n1, n2 = n_seg1(step), n_seg2(step - 1)
            a1 = a2 = 0.0
            d1 = d2 = False
            while not (d1 and d2):
                if not d1 and (d2 or a1 / n1 <= a2 / n2):
                    try:
                        next(s1)
                        a1 += 1
                    except StopIteration:
                        d1 = True
                else:
                    try:
                        next(s2)
                        a2 += 1
                    except StopIteration:
                        d2 = True
    kb.pop()


class _Off:
    def __init__(self, t, off):
        self.t, self.off, self.b = t, off, t.b

    def __getitem__(self, k):
        a, sl = k
        return self.t[a, sl.start + self.off:sl.stop + self.off]


def dsa_consts():
    c = {}
    pos = np.arange(S, dtype=np.float32)[:, None]
    for half, nm in ((32, "32"), (64, "64")):
        inv = (10000.0 ** (-np.arange(half, dtype=np.float32) / half)).astype(np.float32)
        ang = (pos * inv[None, :]).astype(np.float32)
        c["cos" + nm] = np.cos(ang).astype(np.float32)
        c["sin" + nm] = np.sin(ang).astype(np.float32)
    t = np.arange(128)[:, None]
    s_ = np.arange(128)[None, :]
    c["causal"] = np.where(s_ <= t, 0.0, NEG).astype(np.float32)
    return c


LAYER_FNS[0] = layer_dsa2


def layer_rwkv(kb, g, W, x_src, x_dst):
    nc, P = kb.nc, kb.P
    ps = g["ps"]
    ident = g["ident"]
    kb.push()
    wout = kb.sb("rw_wout", [128, 8, 1024], BF16)
    names = ["R", "Wd", "K", "A", "B", "V", "Y"]
    if "rw_scr" not in g:
        g["rw_scr"] = {n: kb.dram("rw_" + n, [NB, 16, S, 64]) for n in names}
        g["rw_scr"]["SG"] = kb.dram("rw_SG", [NB, S, 1024])
        if DEBUG_OUT:
            for nm in ("D1", "D2", "D3"):
                g["rw_scr"][nm] = kb.dram("rw_" + nm, [NB, S, 1024])
        g["rw_scr"]["BON"] = kb.dram("rw_BON", [NB, S, 1024])
    scr = g["rw_scr"]

    def tok_view(n, b, tt):
        return scr[n].t[b].rearrange("h t j -> t h j")[tt * 128:(tt + 1) * 128]

    def bc_row(dst, src_row):
        P.dma("sp", dst[:], src_row.to_broadcast([128, 1024]), writes=[dst.b])

    kb.push()
    win = kb.sb("rw_win", [128, 4, 8, 1024], BF16)
    w1a1 = kb.sb("rw_w1a1", [128, 8, 128], BF16)
    w2e = [kb.sb("rw_w2e", [65, 1024], BF16) for _ in range(2)]
    muT = kb.sb("rw_mu", [128, 6, 8], F32)
    kkb = kb.sb("rw_kk", [128, 1024], F32)
    kab = kb.sb("rw_ka", [128, 1024], F32)
    rkb = kb.sb("rw_rk", [128, 1024], F32)
    begin_load(kb, g)
    compute_mod(kb, g, 2, W)
    for n in range(4):
        load_w(kb, g, _W4(win, n), 0, W["rwkv_w_in"][0, n], 1024)
    load_w(kb, g, wout, 0, W["rwkv_w_out"][0], 1024)
    load_w(kb, g, w1a1, 0, W["rwkv_w1"][0], 64)
    load_w(kb, g, w1a1, 64, W["rwkv_a1"][0], 64)
    for i, (m2_, m0_) in enumerate((("rwkv_w2", "rwkv_w0"), ("rwkv_a2", "rwkv_a0"))):
        stg = g["stage"][g["stage_n"] % 2]
        g["stage_n"] += 1
        sv = stg[:].rearrange("p k n -> p (k n)")
        P.dma("sp", sv[0:64, 0:1024], W[m2_][0], writes=[stg.b])
        P.dma("sp", sv[64:65, 0:1024], W[m0_][0:1, :], writes=[stg.b])
        P.op("dve", lambda e, i=i, sv=sv: e.tensor_copy(w2e[i][:], sv[0:65, 0:1024]),
             reads=[stg.b], writes=[w2e[i].b])
    end_load(kb, g)
    for n in range(6):
        P.dma("sp", muT[:, n, :], W["rwkv_mu"][0, n].rearrange("(k p) -> p k", p=128),
              writes=[muT.b], allow_slow_non_contiguous=True)
    bc_row(kkb, W["rwkv_k_k"][0:1, :])
    bc_row(kab, W["rwkv_k_a"][0:1, :])
    bc_row(rkb, W["rwkv_r_k"][0].rearrange("h j -> (h j)").unsqueeze(0))
    TB = 256
    NTL = TB // 128
    hTb = kb.sb("rw_hT", [128, 8, 1 + TB], BF16)
    dT = kb.sb("rw_dT", [128, 8, TB], BF16)
    xsT = kb.sb("rw_xsT", [128, 8, TB], BF16)
    Rt = kb.sb("rw_R", [128, NTL, 1024], F32)
    pad_ = kb.sb("rw_pad", [128, 256], F32)
    Kt = kb.sb("rw_K", [128, NTL, 1024], F32)
    Vt = kb.sb("rw_V", [128, NTL, 1024], F32)
    SGt = kb.sb("rw_SG", [128, NTL, 1024], F32)
    Wdt = kb.sb("rw_Wd", [128, 1024], F32)
    Ast = kb.sb("rw_As", [128, 1024], F32)
    t1e = [kb.sb("rw_t1e", [65, TB], BF16) for _ in range(2)]
    tm1 = kb.sb("rw_tm1", [128, 1024], F32)
    tm2 = kb.sb("rw_tm2", [128, 1024], F32)
    sm = kb.sb("rw_sm", [128, 32], F32)
    for i in range(2):
        P.op("dve", lambda e, i=i: e.memset(t1e[i][:], 1.0), writes=[t1e[i].b])
    hd = lambda ap: ap.rearrange("p (h j) -> p h j", j=64)
    bcj = lambda ap: ap.unsqueeze(2).to_broadcast([128, 16, 64])
    for b in range(NB):
        P.op("dve", lambda e: e.memset(hTb[:, :, 0:1], 0.0), writes=[hTb.b])
        for tb in range(S // TB):
            for tl in range(NTL):
                front_tile(kb, g, b, tb * NTL + tl, x_src, _Shift(hTb, tb * TB - 1), hTb.b)
            P.op("dve", lambda e: e.tensor_tensor(dT[:], hTb[:, :, 0:TB], hTb[:, :, 1:TB + 1], ALU.subtract),
                 reads=[hTb.b], writes=[dT.b])

            def make_xs(n):
                for c in range(8):
                    P.op("dve", lambda e, c=c, n=n: e.scalar_tensor_tensor(
                        xsT[:, c, :], dT[:, c, :], muT[:, n, c:c + 1], hTb[:, c, 1:TB + 1], ALU.mult, ALU.add),
                        reads=[dT.b, muT.b, hTb.b], writes=[xsT.b])
            for n, dst in ((0, Rt), (1, Kt), (2, Vt), (3, SGt)):
                make_xs(n)
                for tl in range(NTL):
                    for half in range(2):
                        pt = ps[2 + half + 2 * (n % 2)]
                        for c in range(8):
                            P.op("pe", lambda e, pt=pt, c=c, tl=tl, half=half, n=n: e.matmul(
                                pt[:], xsT[:, c, tl * 128:(tl + 1) * 128], win[:, n, c, half * 512:(half + 1) * 512],
                                start=(c == 0), stop=(c == 7)),
                                reads=[xsT.b, win.b], writes=[pt.b])
                        if n == 3:
                            P.op("act", lambda e, pt=pt, tl=tl, half=half, dst=dst: e.activation(
                                dst[:, tl, half * 512:(half + 1) * 512], pt[:], AF.Silu),
                                reads=[pt.b], writes=[dst.b])
                        else:
                            P.op("act", lambda e, pt=pt, tl=tl, half=half, dst=dst: e.copy(
                                dst[:, tl, half * 512:(half + 1) * 512], pt[:]),
                                reads=[pt.b], writes=[dst.b])
            for i, n in ((0, 4), (1, 5)):
                make_xs(n)
                for c in range(8):
                    P.op("pe", lambda e, c=c, i=i: e.matmul(
                        ps[4][0:64, 0:TB], w1a1[:, c, i * 64:(i + 1) * 64], xsT[:, c, :],
                        start=(c == 0), stop=(c == 7)),
                        reads=[w1a1.b, xsT.b], writes=[ps[4].b], pos="T0")
                if i == 0:
                    P.op("act", lambda e, i=i: e.activation(t1e[i][0:64, :], ps[4][0:64, 0:TB], AF.Tanh),
                         reads=[ps[4].b], writes=[t1e[i].b])
                else:
                    P.op("act", lambda e, i=i: e.copy(t1e[i][0:64, :], ps[4][0:64, 0:TB]),
                         reads=[ps[4].b], writes=[t1e[i].b])
            for tl in range(NTL):
                tt = tb * NTL + tl
                R_, K_, V_ = Rt[:, tl, :], Kt[:, tl, :], Vt[:, tl, :]
                for i, dst in ((0, Wdt), (1, Ast)):
                    for half in range(2):
                        pt = ps[2 + half]
                        P.op("pe", lambda e, pt=pt, i=i, tl=tl, half=half: e.matmul(
                            pt[:], t1e[i][:, tl * 128:(tl + 1) * 128], w2e[i][:, half * 512:(half + 1) * 512],
                            start=True, stop=True),
                            reads=[t1e[i].b, w2e[i].b], writes=[pt.b], pos="K65")
                        P.op("act", lambda e, pt=pt, half=half, dst=dst: e.activation(
                            dst[:, half * 512:(half + 1) * 512], pt[:], AF.Sigmoid),
                            reads=[pt.b], writes=[dst.b])
                P.op("act", lambda e: e.activation(Wdt[:], Wdt[:], AF.Exp, scale=-0.6065306597126334),
                     reads=[Wdt.b], writes=[Wdt.b])
                if DEBUG_OUT:
                    P.dma("sp", scr["D1"].t[b, tt * 128:(tt + 1) * 128, :], K_, reads=[Kt.b], writes=[scr["D1"].b])
                    P.dma("sp", scr["D2"].t[b, tt * 128:(tt + 1) * 128, :], Ast[:], reads=[Ast.b], writes=[scr["D2"].b])
                    P.dma("sp", scr["D3"].t[b, tt * 128:(tt + 1) * 128, :], kkb[:], reads=[kkb.b], writes=[scr["D3"].b])
                P.op("dve", lambda e, K_=K_: e.tensor_tensor(tm1[:], K_, kkb[:], ALU.mult),
                     reads=[Kt.b, kkb.b], writes=[tm1.b])
                P.op("act", lambda e: e.activation(tm2[:], tm1[:], AF.Square), reads=[tm1.b], writes=[tm2.b])
                P.op("dve", lambda e: e.tensor_reduce(sm[:, 0:16], hd(tm2[:]), AX.X, ALU.add),
                     reads=[tm2.b], writes=[sm.b])
                P.op("act", lambda e: e.activation(sm[:, 0:16], sm[:, 0:16], AF.Sqrt), reads=[sm.b], writes=[sm.b])
                P.op("dve", lambda e: e.tensor_scalar(sm[:, 0:16], sm[:, 0:16], 1e-12, None, ALU.max),
                     reads=[sm.b], writes=[sm.b])
                P.op("dve", lambda e: e.reciprocal(sm[:, 0:16], sm[:, 0:16]), reads=[sm.b], writes=[sm.b])
                P.op("dve", lambda e: e.tensor_tensor(hd(tm1[:]), hd(tm1[:]), bcj(sm[:, 0:16]), ALU.mult),
                     reads=[tm1.b, sm.b], writes=[tm1.b])
                P.op("pool", lambda e: e.tensor_tensor(tm2[:], tm1[:], Ast[:], ALU.mult),
                     reads=[tm1.b, Ast.b], writes=[tm2.b])
                P.dma("sp", tok_view("B", b, tt), hd(tm2[:]), reads=[tm2.b], writes=[scr["B"].b])
                P.op("dve", lambda e: e.tensor_scalar(tm1[:], tm1[:], -1.0, None, ALU.mult),
                     reads=[tm1.b], writes=[tm1.b])
                P.dma("sp", tok_view("A", b, tt), hd(tm1[:]), reads=[tm1.b], writes=[scr["A"].b])
                P.dma("act", tok_view("Wd", b, tt), hd(Wdt[:]), reads=[Wdt.b], writes=[scr["Wd"].b])
                P.dma("act", tok_view("R", b, tt), hd(R_), reads=[Rt.b], writes=[scr["R"].b])
                P.dma("act", tok_view("V", b, tt), hd(V_), reads=[Vt.b], writes=[scr["V"].b])
                P.op("dve", lambda e: e.scalar_tensor_tensor(tm2[:], Ast[:], -1.0, kab[:], ALU.add, ALU.mult),
                     reads=[Ast.b, kab.b, tm2.b], writes=[tm2.b])
                P.op("dve", lambda e, K_=K_: e.scalar_tensor_tensor(tm2[:], tm2[:], 1.0, K_, ALU.add, ALU.mult),
                     reads=[tm2.b, Kt.b], writes=[tm2.b])
                P.dma("sp", tok_view("K", b, tt), hd(tm2[:]), reads=[tm2.b], writes=[scr["K"].b])
                P.op("pool", lambda e, R_=R_: e.tensor_tensor(tm1[:], tm2[:], R_, ALU.mult),
                     reads=[tm2.b, Rt.b, tm1.b], writes=[tm1.b])
                P.op("pool", lambda e: e.tensor_tensor(tm1[:], tm1[:], rkb[:], ALU.mult),
                     reads=[tm1.b, rkb.b], writes=[tm1.b])
                P.op("dve", lambda e: e.tensor_reduce(sm[:, 16:32], hd(tm1[:]), AX.X, ALU.add),
                     reads=[tm1.b], writes=[sm.b])
                P.op("dve", lambda e, V_=V_: e.tensor_tensor(hd(tm1[:]), hd(V_), bcj(sm[:, 16:32]), ALU.mult),
                     reads=[Vt.b, sm.b, tm1.b], writes=[tm1.b])
                P.dma("sp", scr["BON"].t[b, tt * 128:(tt + 1) * 128, :], tm1[:], reads=[tm1.b], writes=[scr["BON"].b])
                P.dma("act", scr["SG"].t[b, tt * 128:(tt + 1) * 128, :], SGt[:, tl, :], reads=[SGt.b], writes=[scr["SG"].b])
            P.op("dve", lambda e: e.tensor_copy(hTb[:, :, 0:1], hTb[:, :, TB:TB + 1]),
                 reads=[hTb.b], writes=[hTb.b])
    kb.pop()

    kb.push()
    TC = 32
    blk = {n: [kb.sb("rb_" + n, [128, TC, 64], F32) for _ in range(2)] for n in ("Wd", "A", "B", "K", "R")}
    Vb = [kb.sb("rb_V", [128, TC, 16], F32) for _ in range(2)]
    Yb = [kb.sb("rb_Y", [128, TC, 16], F32) for _ in range(2)]
    St = [kb.sb("rb_S", [128, 16, 64], F32) for _ in range(2)]
    t1 = kb.sb("rb_t1", [128, 16, 64], F32)
    t2 = kb.sb("rb_t2", [128, 16, 64], F32)
    t3 = kb.sb("rb_t3", [128, 16, 64], F32)
    t4 = kb.sb("rb_t4", [128, 16, 64], F32)
    sa = kb.sb("rb_sa", [128, 16], F32)
    P.op("dve", lambda e: e.memset(St[0][:], 0.0), writes=[St[0].b])
    qs = ("sp", "act")
    step = 0
    for tbk in range(S // TC):
        t0 = tbk * TC
        i2 = tbk % 2
        nq = 0
        for n in ("Wd", "A", "B", "K", "R"):
            src = scr[n].t.rearrange("b h t j -> (b h) t j")[:, t0:t0 + TC, :]
            for iq in range(4):
                P.dma(qs[nq % 2], blk[n][i2][iq * 32:(iq + 1) * 32, :, :], src,
                      reads=[scr[n].b], writes=[blk[n][i2].b])
                nq += 1
        for iq in range(4):
            src = scr["V"].t.rearrange("b h t j -> (b h) t j")[:, t0:t0 + TC, iq * 16:(iq + 1) * 16]
            P.dma(qs[nq % 2], Vb[i2][iq * 32:(iq + 1) * 32, :, :], src, reads=[scr["V"].b], writes=[Vb[i2].b])
            nq += 1
        for t in range(TC):
            So, Sn = St[step % 2], St[(step + 1) % 2]
            step += 1
            bcr = lambda n, t=t: blk[n][i2][:, t, :].unsqueeze(1).to_broadcast([128, 16, 64])
            a_bc, w_bc, b_bc, k_bc, r_bc = bcr("A"), bcr("Wd"), bcr("B"), bcr("K"), bcr("R")
            v_bc = Vb[i2][:, t, :].unsqueeze(2).to_broadcast([128, 16, 64])
            P.op("dve", lambda e, So=So, a_bc=a_bc: e.tensor_tensor(t1[:], So[:], a_bc, ALU.mult),
                 reads=[So.b, blk["A"][i2].b], writes=[t1.b])
            P.op("dve", lambda e: e.tensor_reduce(sa[:], t1[:], AX.X, ALU.add), reads=[t1.b], writes=[sa.b])
            P.op("pool", lambda e, So=So, w_bc=w_bc: e.tensor_tensor(t2[:], So[:], w_bc, ALU.mult),
                 reads=[So.b, blk["Wd"][i2].b], writes=[t2.b])
            P.op("pool", lambda e, v_bc=v_bc, k_bc=k_bc: e.tensor_tensor(t3[:], v_bc, k_bc, ALU.mult),
                 reads=[Vb[i2].b, blk["K"][i2].b], writes=[t3.b])
            P.op("dve", lambda e, b_bc=b_bc: e.tensor_tensor(
                t1[:], sa[:].unsqueeze(2).to_broadcast([128, 16, 64]), b_bc, ALU.mult),
                reads=[sa.b, blk["B"][i2].b], writes=[t1.b])
            P.op("dve", lambda e: e.tensor_tensor(t1[:], t1[:], t3[:], ALU.add),
                 reads=[t1.b, t3.b], writes=[t1.b])
            P.op("dve", lambda e, Sn=Sn: e.tensor_tensor(Sn[:], t1[:], t2[:], ALU.add),
                 reads=[t1.b, t2.b], writes=[Sn.b])
            P.op("pool", lambda e, Sn=Sn, r_bc=r_bc: e.tensor_tensor(t4[:], Sn[:], r_bc, ALU.mult),
                 reads=[Sn.b, blk["R"][i2].b], writes=[t4.b])
            y_ap = Yb[i2][:, t, :]
            P.op("dve", lambda e, y_ap=y_ap: e.tensor_reduce(y_ap, t4[:], AX.X, ALU.add),
                 reads=[t4.b], writes=[Yb[i2].b])
        for iq in range(4):
            dst = scr["Y"].t.rearrange("b h t j -> (b h) t j")[:, t0:t0 + TC, iq * 16:(iq + 1) * 16]
            P.dma(qs[iq % 2], dst, Yb[i2][iq * 32:(iq + 1) * 32, :, :], reads=[Yb[i2].b], writes=[scr["Y"].b])
    kb.pop()

    kb.push()
    lnw = kb.sb("rw_lnw", [128, 1024], F32)
    lnb = kb.sb("rw_lnb", [128, 1024], F32)
    bc_row(lnw, W["rwkv_ln_w"][0:1, :])
    bc_row(lnb, W["rwkv_ln_b"][0:1, :])
    yt = kb.sb("rc_y", [128, 1024], F32)
    y2 = kb.sb("rc_y2", [128, 1024], F32)
    bon = kb.sb("rc_bon", [128, 1024], F32)
    sgc = kb.sb("rc_sg", [128, 1024], F32)
    smc = kb.sb("rc_sm", [128, 32], F32)
    zTt = kb.sb("rc_zT", [128, 8, 128], BF16)
    for b in range(NB):
        for tt in range(NT):
            P.dma("sp", hd(yt[:]), tok_view("Y", b, tt), reads=[scr["Y"].b], writes=[yt.b])
            P.dma("act", bon[:], scr["BON"].t[b, tt * 128:(tt + 1) * 128, :], reads=[scr["BON"].b], writes=[bon.b])
            P.dma("act", sgc[:], scr["SG"].t[b, tt * 128:(tt + 1) * 128, :], reads=[scr["SG"].b], writes=[sgc.b])
            P.op("dve", lambda e: e.tensor_reduce(smc[:, 0:16], hd(yt[:]), AX.X, ALU.add), reads=[yt.b], writes=[smc.b])
            P.op("dve", lambda e: e.tensor_scalar(smc[:, 0:16], smc[:, 0:16], -1.0 / 64.0, None, ALU.mult),
                 reads=[smc.b], writes=[smc.b])
            P.op("dve", lambda e: e.tensor_tensor(hd(yt[:]), hd(yt[:]), bcj(smc[:, 0:16]), ALU.add),
                 reads=[yt.b, smc.b], writes=[yt.b])
            P.op("act", lambda e: e.activation(y2[:], yt[:], AF.Square), reads=[yt.b], writes=[y2.b])
            P.op("dve", lambda e: e.tensor_reduce(smc[:, 16:32], hd(y2[:]), AX.X, ALU.add), reads=[y2.b], writes=[smc.b])
            P.op("dve", lambda e: e.tensor_scalar(smc[:, 16:32], smc[:, 16:32], 1.0 / 64.0, 64e-5, ALU.mult, ALU.add),
                 reads=[smc.b], writes=[smc.b])
            P.op("act", lambda e: e.activation(smc[:, 16:32], smc[:, 16:32], AF.Sqrt), reads=[smc.b], writes=[smc.b])
            P.op("dve", lambda e: e.reciprocal(smc[:, 16:32], smc[:, 16:32]), reads=[smc.b], writes=[smc.b])
            P.op("dve", lambda e: e.tensor_tensor(hd(yt[:]), hd(yt[:]), bcj(smc[:, 16:32]), ALU.mult),
                 reads=[yt.b, smc.b], writes=[yt.b])
            P.op("pool", lambda e: e.tensor_tensor(yt[:], yt[:], lnw[:], ALU.mult), reads=[yt.b, lnw.b], writes=[yt.b])
            P.op("pool", lambda e: e.tensor_tensor(yt[:], yt[:], lnb[:], ALU.add), reads=[yt.b, lnb.b], writes=[yt.b])
            P.op("dve", lambda e: e.tensor_tensor(yt[:], yt[:], bon[:], ALU.add), reads=[yt.b, bon.b], writes=[yt.b])
            P.op("dve", lambda e: e.tensor_tensor(yt[:], yt[:], sgc[:], ALU.mult), reads=[yt.b, sgc.b], writes=[yt.b])
            for c in range(8):
                pt = ps[4 + c // 4]
                P.op("pe", lambda e, pt=pt, c=c: e.transpose(
                    pt[:, (c % 4) * 128:(c % 4 + 1) * 128], yt[:, c * 128:(c + 1) * 128], ident[:]),
                    reads=[yt.b, ident.b], writes=[pt.b])
            P.op("act", lambda e: e.copy(zTt[:, 0:4, :], ps[4][:].rearrange("p (c t) -> p c t", t=128)),
                 reads=[ps[4].b], writes=[zTt.b])
            P.op("dve", lambda e: e.tensor_copy(zTt[:, 4:8, :], ps[5][:].rearrange("p (c t) -> p c t", t=128)),
                 reads=[ps[5].b], writes=[zTt.b])
            back_tile(kb, g, b, tt, _Shift(zTt, tt * 128), [zTt.b], wout, x_src, x_dst)
    kb.pop()
    kb.pop()


def layer_rwkv2(kb, g, W, x_src, x_dst):
    nc, P = kb.nc, kb.P
    ps = g["ps"]
    ident = g["ident"]
    kb.push()
    wout = kb.sb("rw_wout", [128, 8, 1024], BF16)
    names = ["R", "Wd", "K", "A", "B", "V", "Y"]
    if "rw_scr" not in g:
        g["rw_scr"] = {n: kb.dram("rw_" + n, [NB, S, 1024]) for n in names}
        g["rw_scr"]["SG"] = kb.dram("rw_SG", [NB, S, 1024])
        if DEBUG_OUT:
            for nm in ("D1", "D2", "D3"):
                g["rw_scr"][nm] = kb.dram("rw_" + nm, [NB, S, 1024])
        g["rw_scr"]["BON"] = kb.dram("rw_BON", [NB, S, 1024])
    scr = g["rw_scr"]

    def tok_view(n, b, tt):
        return scr[n].t[b, tt * 128:(tt + 1) * 128, :]

    def bc_row(dst, src_row):
        P.dma("sp", dst[:], src_row.to_broadcast([128, 1024]), writes=[dst.b])

    kb.push()
    win = kb.sb("rw_win", [128, 4, 8, 1024], BF16)
    w1a1 = kb.sb("rw_w1a1", [128, 8, 128], BF16)
    w2e = [kb.sb("rw_w2e", [65, 1024], BF16) for _ in range(2)]
    muT = kb.sb("rw_mu", [128, 6, 8], F32)
    kkb = kb.sb("rw_kk", [128, 1024], F32)
    kab = kb.sb("rw_ka", [128, 1024], F32)
    rkb = kb.sb("rw_rk", [128, 1024], F32)
    begin_load(kb, g)
    compute_mod(kb, g, 2, W)
    for n in range(4):
        load_w(kb, g, _W4(win, n), 0, W["rwkv_w_in"][0, n], 1024)
    load_w(kb, g, wout, 0, W["rwkv_w_out"][0], 1024)
    load_w(kb, g, w1a1, 0, W["rwkv_w1"][0], 64)
    load_w(kb, g, w1a1, 64, W["rwkv_a1"][0], 64)
    for i, (m2_, m0_) in enumerate((("rwkv_w2", "rwkv_w0"), ("rwkv_a2", "rwkv_a0"))):
        stg = g["stage"][g["stage_n"] % 2]
        g["stage_n"] += 1
        sv = stg[:].rearrange("p k n -> p (k n)")
        P.dma("sp", sv[0:64, 0:1024], W[m2_][0], writes=[stg.b])
        P.dma("sp", sv[64:65, 0:1024], W[m0_][0:1, :], writes=[stg.b])
        P.op("dve", lambda e, i=i, sv=sv: e.tensor_copy(w2e[i][:], sv[0:65, 0:1024]),
             reads=[stg.b], writes=[w2e[i].b])
    end_load(kb, g)
    for n in range(6):
        P.dma("sp", muT[:, n, :], W["rwkv_mu"][0, n].rearrange("(k p) -> p k", p=128),
              writes=[muT.b], allow_slow_non_contiguous=True)
    bc_row(kkb, W["rwkv_k_k"][0:1, :])
    bc_row(kab, W["rwkv_k_a"][0:1, :])
    bc_row(rkb, W["rwkv_r_k"][0].rearrange("h j -> (h j)").unsqueeze(0))
    TB = 256
    NTL = TB // 128
    hTb = kb.sb("rw_hT", [128, 8, 1 + TB], BF16)
    dT = kb.sb("rw_dT", [128, 8, TB], BF16)
    xsTs = [kb.sb("rw_xsT", [128, 8, TB], BF16) for _ in range(2)]
    Rt = kb.sb("rw_R", [128, NTL, 1024], F32)
    pad_ = kb.sb("rw_pad", [128, 256], F32)
    Kt = kb.sb("rw_K", [128, NTL, 1024], F32)
    Vt = kb.sb("rw_V", [128, NTL, 1024], F32)
    SGt = kb.sb("rw_SG", [128, NTL, 1024], F32)
    Wdt = kb.sb("rw_Wd", [128, 1024], F32)
    Ast = kb.sb("rw_As", [128, 1024], F32)
    t1e = [kb.sb("rw_t1e", [65, TB], BF16) for _ in range(2)]
    tm1 = kb.sb("rw_tm1", [128, 1024], F32)
    tm2 = kb.sb("rw_tm2", [128, 1024], F32)
    sm = kb.sb("rw_sm", [128, 32], F32)
    for i in range(2):
        P.op("dve", lambda e, i=i: e.memset(t1e[i][:], 1.0), writes=[t1e[i].b])
    hd = lambda ap: ap.rearrange("p (h j) -> p h j", j=64)
    bcj = lambda ap: ap.unsqueeze(2).to_broadcast([128, 16, 64])
    for b in range(NB):
        P.op("dve", lambda e: e.memset(hTb[:, :, 0:1], 0.0), writes=[hTb.b])
        for tb in range(S // TB):
            for tl in range(NTL):
                front_tile(kb, g, b, tb * NTL + tl, x_src, _Shift(hTb, tb * TB - 1), hTb.b)
            P.op("dve", lambda e: e.tensor_tensor(dT[:], hTb[:, :, 0:TB], hTb[:, :, 1:TB + 1], ALU.subtract),
                 reads=[hTb.b], writes=[dT.b])

            def make_xs(n):
                xsT = xsTs[n % 2]
                for c in range(8):
                    P.op("dve", lambda e, c=c, n=n: e.scalar_tensor_tensor(
                        xsT[:, c, :], dT[:, c, :], muT[:, n, c:c + 1], hTb[:, c, 1:TB + 1], ALU.mult, ALU.add),
                        reads=[dT.b, muT.b, hTb.b], writes=[xsT.b])
            for n, dst in ((0, Rt), (1, Kt), (2, Vt), (3, SGt)):
                make_xs(n)
                xsT = xsTs[n % 2]
                for tl in range(NTL):
                    for half in range(2):
                        pt = ps[2 + half + 2 * (n % 2)]
                        for c in range(8):
                            P.op("pe", lambda e, pt=pt, c=c, tl=tl, half=half, n=n, xsT=xsT: e.matmul(
                                pt[:], xsT[:, c, tl * 128:(tl + 1) * 128], win[:, n, c, half * 512:(half + 1) * 512],
                                start=(c == 0), stop=(c == 7)),
                                reads=[xsT.b, win.b], writes=[pt.b])
                        if n == 3:
                            P.op("act", lambda e, pt=pt, tl=tl, half=half, dst=dst: e.activation(
                                dst[:, tl, half * 512:(half + 1) * 512], pt[:], AF.Silu),
                                reads=[pt.b], writes=[dst.b])
                        else:
                            P.op("act", lambda e, pt=pt, tl=tl, half=half, dst=dst: e.copy(
                                dst[:, tl, half * 512:(half + 1) * 512], pt[:]),
                                reads=[pt.b], writes=[dst.b])
            for i, n in ((0, 4), (1, 5)):
                make_xs(n)
                xsT = xsTs[n % 2]
                for c in range(8):
                    P.op("pe", lambda e, c=c, i=i, xsT=xsT: e.matmul(
                        ps[4][0:64, 0:TB], w1a1[:, c, i * 64:(i + 1) * 64], xsT[:, c, :],
                        start=(c == 0), stop=(c == 7)),
                        reads=[w1a1.b, xsT.b], writes=[ps[4].b], pos="T0")
                if i == 0:
                    P.op("act", lambda e, i=i: e.activation(t1e[i][0:64, :], ps[4][0:64, 0:TB], AF.Tanh),
                         reads=[ps[4].b], writes=[t1e[i].b])
                else:
                    P.op("act", lambda e, i=i: e.copy(t1e[i][0:64, :], ps[4][0:64, 0:TB]),
                         reads=[ps[4].b], writes=[t1e[i].b])
            for tl in range(NTL):
                tt = tb * NTL + tl
                R_, K_, V_ = Rt[:, tl, :], Kt[:, tl, :], Vt[:, tl, :]
                for i, dst in ((0, Wdt), (1, Ast)):
                    for half in range(2):
                        pt = ps[2 + half]
                        P.op("pe", lambda e, pt=pt, i=i, tl=tl, half=half: e.matmul(
                            pt[:], t1e[i][:, tl * 128:(tl + 1) * 128], w2e[i][:, half * 512:(half + 1) * 512],
                            start=True, stop=True),
                            reads=[t1e[i].b, w2e[i].b], writes=[pt.b], pos="K65")
                        P.op("act", lambda e, pt=pt, half=half, dst=dst: e.activation(
                            dst[:, half * 512:(half + 1) * 512], pt[:], AF.Sigmoid),
                            reads=[pt.b], writes=[dst.b])
                P.op("dve", lambda e: e.tensor_scalar(Wdt[:], Wdt[:], -0.6065306597126334, None, ALU.mult),
                     reads=[Wdt.b], writes=[Wdt.b])
                if DEBUG_OUT:
                    P.dma("sp", scr["D1"].t[b, tt * 128:(tt + 1) * 128, :], K_, reads=[Kt.b], writes=[scr["D1"].b])
                    P.dma("sp", scr["D2"].t[b, tt * 128:(tt + 1) * 128, :], Ast[:], reads=[Ast.b], writes=[scr["D2"].b])
                    P.dma("sp", scr["D3"].t[b, tt * 128:(tt + 1) * 128, :], kkb[:], reads=[kkb.b], writes=[scr["D3"].b])
                P.op("dve", lambda e, K_=K_: e.tensor_tensor(tm1[:], K_, kkb[:], ALU.mult),
                     reads=[Kt.b, kkb.b], writes=[tm1.b])
                P.op("act", lambda e: e.activation(tm2[:], tm1[:], AF.Square), reads=[tm1.b], writes=[tm2.b])
                P.op("dve", lambda e: e.tensor_reduce(sm[:, 0:16], hd(tm2[:]), AX.X, ALU.add),
                     reads=[tm2.b], writes=[sm.b])
                P.op("act", lambda e: e.activation(sm[:, 0:16], sm[:, 0:16], AF.Sqrt), reads=[sm.b], writes=[sm.b])
                P.op("dve", lambda e: e.tensor_scalar(sm[:, 0:16], sm[:, 0:16], 1e-12, None, ALU.max),
                     reads=[sm.b], writes=[sm.b])
                P.op("dve", lambda e: e.reciprocal(sm[:, 0:16], sm[:, 0:16]), reads=[sm.b], writes=[sm.b])
                P.op("dve", lambda e: e.tensor_tensor(hd(tm1[:]), hd(tm1[:]), bcj(sm[:, 0:16]), ALU.mult),
                     reads=[tm1.b, sm.b], writes=[tm1.b])
                P.op("pool", lambda e: e.tensor_tensor(tm2[:], tm1[:], Ast[:], ALU.mult),
                     reads=[tm1.b, Ast.b], writes=[tm2.b])
                P.dma("sp", tok_view("B", b, tt), tm2[:], reads=[tm2.b], writes=[scr["B"].b])
                P.op("dve", lambda e: e.tensor_scalar(tm1[:], tm1[:], -1.0, None, ALU.mult),
                     reads=[tm1.b], writes=[tm1.b])
                P.dma("sp", tok_view("A", b, tt), tm1[:], reads=[tm1.b], writes=[scr["A"].b])
                P.dma("act", tok_view("Wd", b, tt), Wdt[:], reads=[Wdt.b], writes=[scr["Wd"].b])
                P.dma("act", tok_view("R", b, tt), R_, reads=[Rt.b], writes=[scr["R"].b])
                P.dma("act", tok_view("V", b, tt), V_, reads=[Vt.b], writes=[scr["V"].b])
                P.op("dve", lambda e: e.scalar_tensor_tensor(tm2[:], Ast[:], -1.0, kab[:], ALU.add, ALU.mult),
                     reads=[Ast.b, kab.b, tm2.b], writes=[tm2.b])
                P.op("dve", lambda e, K_=K_: e.scalar_tensor_tensor(tm2[:], tm2[:], 1.0, K_, ALU.add, ALU.mult),
                     reads=[tm2.b, Kt.b], writes=[tm2.b])
                P.dma("sp", tok_view("K", b, tt), tm2[:], reads=[tm2.b], writes=[scr["K"].b])
                P.op("pool", lambda e, R_=R_: e.tensor_tensor(tm1[:], tm2[:], R_, ALU.mult),
                     reads=[tm2.b, Rt.b, tm1.b], writes=[tm1.b])
                P.op("pool", lambda e: e.tensor_tensor(tm1[:], tm1[:], rkb[:], ALU.mult),
                     reads=[tm1.b, rkb.b], writes=[tm1.b])
                P.op("dve", lambda e: e.tensor_reduce(sm[:, 16:32], hd(tm1[:]), AX.X, ALU.add),
                     reads=[tm1.b], writes=[sm.b])
                P.op("dve", lambda e, V_=V_: e.tensor_tensor(hd(tm1[:]), hd(V_), bcj(sm[:, 16:32]), ALU.mult),
                     reads=[Vt.b, sm.b, tm1.b], writes=[tm1.b])
                P.dma("sp", scr["BON"].t[b, tt * 128:(tt + 1) * 128, :], tm1[:], reads=[tm1.b], writes=[scr["BON"].b])
                P.dma("act", scr["SG"].t[b, tt * 128:(tt + 1) * 128, :], SGt[:, tl, :], reads=[SGt.b], writes=[scr["SG"].b])
            P.op("dve", lambda e: e.tensor_copy(hTb[:, :, 0:1], hTb[:, :, TB:TB + 1]),
                 reads=[hTb.b], writes=[hTb.b])
    kb.pop()

    kb.push()
    C_ = g["consts"]
    psr = [kb.psum("rps%d" % i, [128, 512], F32) for i in range(1)] + g["ps"]
    PU_, PS_ = psr[0], psr[1]
    PY_ = PU_
    PI = psr[2:7]
    bank_ctr = [0]

    def nextbank():
        bank_ctr[0] += 1
        return PI[bank_ctr[0] % len(PI)]
    PTb = kb.psum("rpsb", [128, 1024], BF16)
    identb = g["identb"]
    cm = {}
    for nm, shp in (("rw_M1", [128, 128]), ("rw_M1s", [128, 128]), ("rw_M2", [128, 128]),
                    ("rw_mask4", [128, 512]), ("rw_maskL2", [128, 256]), ("rw_cind", [128, 2])):
        cm[nm] = kb.sb(nm, shp, F32)
        P.dma("sp", cm[nm][:], C_[nm][:, :], writes=[cm[nm].b])
    NU = 2
    per = [dict(
        Vv=kb.sb("c_V", [128, 512], BF16), Bg=kb.sb("c_Bg", [128, 512], BF16), Kg=kb.sb("c_Kg", [128, 512], BF16),
        ARt=kb.sb("c_ARt", [128, 4, 2, 128], BF16), Nall=kb.sb("c_Nall", [128, 8, 4, 128], BF16),
        QR=kb.sb("c_QR", [128, 4, 2, 128], BF16), UV=kb.sb("c_UV", [128, 512], F32),
        YVn=kb.sb("c_YVn", [128, 512], F32), YVs=kb.sb("c_YVs", [128, 512], F32),
        Pall=kb.sb("c_P", [128, 8, 128], BF16),
        gC=kb.sb("c_gC", [128, 4, 2], F32), Yt=kb.sb("c_Yt", [128, 512], F32)) for _ in range(NU)]
    Rr = kb.sb("c_R", [128, 512], F32)
    LWt = kb.sb("c_LW", [128, 512], F32)
    Aa = kb.sb("c_A", [128, 512], F32)
    Bf = kb.sb("c_Bf", [128, 512], F32)
    Kf = kb.sb("c_Kf", [128, 512], F32)
    Vf = kb.sb("c_Vf", [128, 512], F32)
    Rb = kb.sb("c_Rb", [128, 512], BF16)
    Ab = kb.sb("c_Ab", [128, 512], BF16)
    Ee = kb.sb("c_E", [128, 512], F32)
    Bt_ = kb.sb("c_Bt", [128, 512], BF16)
    Kt_ = kb.sb("c_Kt", [128, 512], BF16)
    BKt = kb.sb("c_BKt", [128, 4, 2, 128], BF16)
    NTt = kb.sb("c_NT", [128, 8, 2, 128], BF16)
    XX = kb.sb("c_XX", [128, 2, 8, 2, 128], BF16)
    Th = kb.sb("c_Th", [128, 8, 128], BF16)
    Usb = kb.sb("c_U", [128, 512], BF16)
    STb = kb.sb("c_STb", [128, 4, 64], BF16)
    tmpS = kb.sb("c_tmpS", [128, 4, 64], F32)
    STs = [[kb.sb("c_ST", [128, 4, 64], F32) for hg in range(2)] for b in range(NB)]
    for b in range(NB):
        for hg in range(2):
            P.op("pool", lambda e, t_=STs[b][hg]: e.memset(t_[:], 0.0), writes=[STs[b][hg].b])
    identf = ident

    def gen_pre(u, b, hg, tt):
        c = per[u % NU]
        cols = slice(hg * 512, (hg + 1) * 512)
        rows = slice(tt * 128, (tt + 1) * 128)
        ld = (("R", Rr), ("Wd", LWt), ("A", Aa), ("B", Bf), ("K", Kf), ("V", Vf))
        for i, (nm, dst) in enumerate(ld):
            P.dma(("sp", "act")[i % 2], dst[:], scr[nm].t[b, rows, cols], reads=[scr[nm].b], writes=[dst.b])
        pgA = nextbank()
        P.op("pe", lambda e: e.matmul(pgA[:], cm["rw_M1"][:], LWt[:], start=True, stop=True),
             reads=[cm["rw_M1"].b, LWt.b], writes=[pgA.b])
        P.op("act", lambda e: e.activation(Ee[:], pgA[:], AF.Exp), reads=[pgA.b], writes=[Ee.b])
        P.op("pool", lambda e: e.tensor_tensor(Rb[:], Rr[:], Ee[:], ALU.mult), reads=[Rr.b, Ee.b], writes=[Rb.b])
        P.op("pool", lambda e: e.tensor_copy(c["Vv"][:], Vf[:]), reads=[Vf.b], writes=[c["Vv"].b])
        P.op("act", lambda e: e.activation(Ee[:], pgA[:], AF.Exp, scale=-1.0), reads=[pgA.b, Ee.b], writes=[Ee.b])
        P.op("pool", lambda e: e.tensor_tensor(Bt_[:], Bf[:], Ee[:], ALU.mult),
             reads=[Bf.b, Ee.b], writes=[Bt_.b])
        P.op("dve", lambda e: e.tensor_tensor(Kt_[:], Kf[:], Ee[:], ALU.mult),
             reads=[Kf.b, Ee.b], writes=[Kt_.b])
        yield
        pgB = nextbank()
        P.op("pe", lambda e: e.matmul(pgB[:], cm["rw_M1s"][:], LWt[:], start=True, stop=True),
             reads=[cm["rw_M1s"].b, LWt.b], writes=[pgB.b])
        P.op("act", lambda e: e.activation(Ee[:], pgB[:], AF.Exp), reads=[pgB.b, Ee.b], writes=[Ee.b])
        P.op("pool", lambda e: e.tensor_tensor(Ab[:], Aa[:], Ee[:], ALU.mult), reads=[Aa.b, Ee.b], writes=[Ab.b])
        pg2 = nextbank()
        P.op("pe", lambda e: e.matmul(pg2[:], cm["rw_M2"][:], LWt[:], start=True, stop=True),
             reads=[cm["rw_M2"].b, LWt.b], writes=[pg2.b])
        P.op("act", lambda e: e.activation(Ee[:], pg2[:], AF.Exp), reads=[pg2.b, Ee.b], writes=[Ee.b])
        P.op("pool", lambda e: e.tensor_tensor(c["Bg"][:], Bf[:], Ee[:], ALU.mult),
             reads=[Bf.b, Ee.b], writes=[c["Bg"].b])
        P.op("dve", lambda e: e.tensor_tensor(c["Kg"][:], Kf[:], Ee[:], ALU.mult),
             reads=[Kf.b, Ee.b], writes=[c["Kg"].b])
        pg3 = nextbank()
        for m in range(4):
            P.op("pe", lambda e, m=m: e.matmul(pg3[:, m * 2:m * 2 + 2], LWt[:, m * 128:(m + 1) * 128],
                                               cm["rw_cind"][:], start=True, stop=True),
                 reads=[LWt.b, cm["rw_cind"].b], writes=[pg3.b])
        P.op("act", lambda e: e.activation(c["gC"][:], pg3[:, 0:8].rearrange("p (m c) -> p m c", c=2), AF.Exp),
             reads=[pg3.b], writes=[c["gC"].b])
        yield
        for (src0, src1, dstT) in ((Ab, Rb, c["ARt"]), (Bt_, Kt_, BKt)):
            for m in range(4):
                for j_, src in enumerate((src0, src1)):
                    P.op("pe", lambda e, j_=j_, src=src, m=m: e.transpose(
                        PTb[:, (m * 2 + j_) * 128:(m * 2 + j_ + 1) * 128], src[:, m * 128:(m + 1) * 128], identb[:]),
                        reads=[src.b, identb.b], writes=[PTb.b])
            d_ap = dstT[:].rearrange("p m a t -> p (m a t)")
            if dstT is BKt:
                P.op("act", lambda e, d_ap=d_ap: e.copy(d_ap, PTb[:]), reads=[PTb.b], writes=[dstT.b])
            else:
                P.op("dve", lambda e, d_ap=d_ap: e.tensor_copy(d_ap, PTb[:]), reads=[PTb.b], writes=[dstT.b])
            yield
        for hh in range(8):
            m, h2 = hh // 2, hh % 2
            hs = slice(h2 * 64, (h2 + 1) * 64)
            pg5 = nextbank()
            arv = c["ARt"][hs, m, :, :].rearrange("p a t -> p (a t)")
            bkv = BKt[hs, m, :, :].rearrange("p a t -> p (a t)")
            for j_ in range(2):
                P.op("pe", lambda e, pg5=pg5, j_=j_, hs=hs, m=m, arv=arv: e.matmul(
                    pg5[:, j_ * 256:(j_ + 1) * 256], BKt[hs, m, j_, :], arv, start=True, stop=True),
                    reads=[BKt.b, c["ARt"].b], writes=[pg5.b], pos=(1 if h2 else None))
            P.op("dve", lambda e, pg5=pg5, hh=hh: e.tensor_tensor(
                c["Nall"][:, hh, :, :].rearrange("p a t -> p (a t)"), pg5[:], cm["rw_mask4"][:], ALU.mult),
                reads=[pg5.b, cm["rw_mask4"].b], writes=[c["Nall"].b])
            pg6 = nextbank()
            P.op("pe", lambda e, pg6=pg6, hs=hs, m=m, bkv=bkv: e.matmul(
                pg6[:, 0:256], c["ARt"][hs, m, 0, :], bkv, start=True, stop=True),
                reads=[BKt.b, c["ARt"].b], writes=[pg6.b], pos=(1 if h2 else None))
            P.op("dve", lambda e, pg6=pg6, hh=hh: e.tensor_tensor(
                NTt[:, hh, :, :].rearrange("p a t -> p (a t)"), pg6[:, 0:256], cm["rw_maskL2"][:], ALU.mult),
                reads=[pg6.b, cm["rw_maskL2"].b], writes=[NTt.b])
            if hh % 2 == 1:
                yield
        P.op("pool", lambda e: e.tensor_tensor(Th[:], c["Nall"][:, :, 0, :],
                                               identb[:].unsqueeze(1).to_broadcast([128, 8, 128]), ALU.add),
             reads=[c["Nall"].b, identb.b], writes=[Th.b])
        for k in range(1, 6):
            pp = k % 2
            for pr in range(4):
                bank = nextbank()
                for hl in range(2):
                    hh = pr * 2 + hl
                    if k == 1:
                        Xp, Xtp = c["Nall"][:, hh, 0, :], NTt[:, hh, 0, :]
                        rd = [c["Nall"].b, NTt.b]
                    else:
                        Xp, Xtp = XX[:, 1 - pp, hh, 0, :], XX[:, 1 - pp, hh, 1, :]
                        rd = [XX.b]
                    P.op("pe", lambda e, bank=bank, hl=hl, Xp=Xp, Xtp=Xtp: e.matmul(
                        bank[:, hl * 256:hl * 256 + 128], Xtp, Xp, start=True, stop=True),
                        reads=rd, writes=[bank.b])
                    P.op("pe", lambda e, bank=bank, hl=hl, Xp=Xp, Xtp=Xtp: e.matmul(
                        bank[:, hl * 256 + 128:hl * 256 + 256], Xp, Xtp, start=True, stop=True),
                        reads=rd, writes=[bank.b])
                d_ap = XX[:, pp, pr * 2:pr * 2 + 2, :, :].rearrange("p h a t -> p (h a t)")
                P.op("act", lambda e, d_ap=d_ap, bank=bank: e.copy(d_ap, bank[:]), reads=[bank.b], writes=[XX.b])
            yield
            for pq_ in range(2):
                bank = nextbank()
                for hl in range(4):
                    hh = pq_ * 4 + hl
                    P.op("pe", lambda e, bank=bank, hl=hl, hh=hh, pp=pp: e.matmul(
                        bank[:, hl * 128:(hl + 1) * 128], XX[:, pp, hh, 1, :], Th[:, hh, :], start=True, stop=True),
                        reads=[XX.b, Th.b], writes=[bank.b])
                t_ap = Th[:, pq_ * 4:pq_ * 4 + 4, :].rearrange("p h t -> p (h t)")
                P.op("dve", lambda e, t_ap=t_ap, bank=bank: e.tensor_tensor(t_ap, bank[:], t_ap, ALU.add),
                     reads=[bank.b, Th.b], writes=[Th.b])
            yield
        pq = nextbank()
        for hh in range(8):
            m, h2 = hh // 2, hh % 2
            hs = slice(h2 * 64, (h2 + 1) * 64)
            P.op("pe", lambda e, hs=hs, m=m, hh=hh: e.matmul(
                pq[hs, m * 128:(m + 1) * 128], Ab[:, hh * 64:(hh + 1) * 64], Th[:, hh, :], start=True, stop=True),
                reads=[Ab.b, Th.b], writes=[pq.b], pos=(1 if h2 else None))
        pqv = pq[:].rearrange("p (m t) -> p m t", t=128)
        P.op("act", lambda e: e.copy(c["QR"][:, :, 0, 0:64], pqv[:, :, 0:64]),
             reads=[pq.b], writes=[c["QR"].b])
        P.op("dve", lambda e: e.tensor_copy(c["QR"][:, :, 1, 64:128], pqv[:, :, 64:128]),
             reads=[pq.b], writes=[c["QR"].b])
        P.op("pool", lambda e: e.tensor_copy(c["QR"][:, :, 0, 64:128], c["ARt"][:, :, 1, 0:64]),
             reads=[c["ARt"].b], writes=[c["QR"].b])
        P.op("pool", lambda e: e.tensor_copy(c["QR"][:, :, 1, 0:64], c["ARt"][:, :, 1, 64:128]),
             reads=[c["ARt"].b], writes=[c["QR"].b])
        for half in range(2):
            pp_ = nextbank()
            for hl in range(4):
                hh = half * 4 + hl
                P.op("pe", lambda e, pp_=pp_, hl=hl, hh=hh: e.matmul(
                    pp_[:, hl * 128:(hl + 1) * 128], NTt[:, hh, 1, :], Th[:, hh, :], start=True, stop=True),
                    reads=[NTt.b, Th.b], writes=[pp_.b])
            P.op("dve", lambda e, pp_=pp_, half=half: e.tensor_copy(
                c["Pall"][:, half * 4:half * 4 + 4, :].rearrange("p h t -> p (h t)"), pp_[:]),
                reads=[pp_.b], writes=[c["Pall"].b])
        yield
        puv = nextbank()
        for hh in range(8):
            P.op("pe", lambda e, hh=hh: e.matmul(
                puv[:, hh * 64:(hh + 1) * 64], c["Pall"][:, hh, :], c["Vv"][:, hh * 64:(hh + 1) * 64],
                start=True, stop=True),
                reads=[c["Pall"].b, c["Vv"].b], writes=[puv.b])
        P.op("act", lambda e: e.copy(c["UV"][:], puv[:]), reads=[puv.b], writes=[c["UV"].b])
        pyv = nextbank()
        for hh in range(8):
            P.op("pe", lambda e, hh=hh: e.matmul(
                pyv[:, hh * 64:(hh + 1) * 64], c["Nall"][:, hh, 3, :], c["Vv"][:, hh * 64:(hh + 1) * 64],
                start=True, stop=True),
                reads=[c["Nall"].b, c["Vv"].b], writes=[pyv.b])
        P.op("dve", lambda e: e.tensor_copy(c["YVn"][:], pyv[:]), reads=[pyv.b], writes=[c["YVn"].b])
        P.dma("sp", c["YVs"][64:128, :], c["YVn"][0:64, :], reads=[c["YVn"].b], writes=[c["YVs"].b])
        P.dma("act", c["YVs"][0:64, :], c["YVn"][64:128, :], reads=[c["YVn"].b], writes=[c["YVs"].b])
        yield

    def gen_seq(u, b, hg, tt):
        c = per[u % NU]
        ST = STs[b][hg]
        cols = slice(hg * 512, (hg + 1) * 512)
        rows = slice(tt * 128, (tt + 1) * 128)
        for cc in range(2):
            cs = slice(cc * 64, (cc + 1) * 64)
            P.op("pool", lambda e: e.tensor_copy(STb[:], ST[:]), reads=[ST.b], writes=[STb.b])
            for hh in range(8):
                m, h2 = hh // 2, hh % 2
                hs = slice(h2 * 64, (h2 + 1) * 64)
                hc = slice(hh * 64, (hh + 1) * 64)
                P.op("pe", lambda e, hs=hs, hc=hc, m=m, hh=hh, cc=cc: e.matmul(
                    PU_[:, hc], c["QR"][hs, m, cc, :], STb[hs, m, :], start=(hh == 0), stop=False),
                    reads=[c["QR"].b, STb.b], writes=[PU_.b], pos=(1 if h2 else None))
            P.op("dve", lambda e, cs=cs: e.tensor_tensor(Usb[cs, :], PU_[cs, :], c["UV"][cs, :], ALU.add),
                 reads=[PU_.b, c["UV"].b], writes=[Usb.b])
            yield
            ocs = slice((1 - cc) * 64, (2 - cc) * 64)
            for hh in range(8):
                m, h2 = hh // 2, hh % 2
                hs = slice(h2 * 64, (h2 + 1) * 64)
                hc = slice(hh * 64, (hh + 1) * 64)
                P.op("pe", lambda e, cs=cs, ocs=ocs, hc=hc, hh=hh: e.matmul(
                    PY_[ocs, hc], c["Nall"][cs, hh, 1, cs], Usb[cs, hc], start=False, stop=True),
                    reads=[c["Nall"].b, Usb.b], writes=[PY_.b], pos=1)
            P.op("dve", lambda e, ocs=ocs: e.tensor_tensor(c["Yt"][ocs, :], PY_[ocs, :], c["YVs"][ocs, :], ALU.add),
                 reads=[PY_.b, c["YVs"].b], writes=[c["Yt"].b])
            for hh in range(8):
                m, h2 = hh // 2, hh % 2
                hs = slice(h2 * 64, (h2 + 1) * 64)
                hc = slice(hh * 64, (hh + 1) * 64)
                P.op("pe", lambda e, cs=cs, hs=hs, hc=hc, m=m, hh=hh: e.matmul(
                    PS_[hs, m * 64:(m + 1) * 64], c["Bg"][cs, hc], Usb[cs, hc], start=(hh < 2), stop=False),
                    reads=[c["Bg"].b, Usb.b], writes=[PS_.b], pos=(1 if (h2 or cc) else None))
                P.op("pe", lambda e, cs=cs, hs=hs, hc=hc, m=m, hh=hh: e.matmul(
                    PS_[hs, m * 64:(m + 1) * 64], c["Kg"][cs, hc], c["Vv"][cs, hc], start=False, stop=True),
                    reads=[c["Kg"].b, c["Vv"].b], writes=[PS_.b], pos=(1 if (h2 or cc) else None))
            P.op("pool", lambda e, cc=cc: e.tensor_tensor(
                tmpS[:], ST[:], c["gC"][:, :, cc].unsqueeze(2).to_broadcast([128, 4, 64]), ALU.mult),
                reads=[ST.b, c["gC"].b], writes=[tmpS.b])
            P.op("dve", lambda e: e.tensor_tensor(
                ST[:].rearrange("p m i -> p (m i)"), PS_[:, 0:256], tmpS[:].rearrange("p m i -> p (m i)"), ALU.add),
                reads=[PS_.b, tmpS.b], writes=[ST.b])
            yield
        P.dma("sp", scr["Y"].t[b, tt * 128:tt * 128 + 64, cols], c["Yt"][64:128, :],
              reads=[c["Yt"].b], writes=[scr["Y"].b])
        P.dma("act", scr["Y"].t[b, tt * 128 + 64:tt * 128 + 128, cols], c["Yt"][0:64, :],
              reads=[c["Yt"].b], writes=[scr["Y"].b])
        yield

    units = [(b, hg, tt) for tt in range(NT) for b in range(NB) for hg in range(2)]
    n_u = len(units)
    for u in range(n_u + 1):
        streams = []
        if u >= 1:
            streams.append(gen_seq(u - 1, *units[u - 1]))
        if u < n_u:
            streams.append(gen_pre(u, *units[u]))
        while streams:
            for s_ in list(streams):
                try:
                    next(s_)
                except StopIteration:
                    streams.remove(s_)
    kb.pop()

    kb.push()
    lnw = kb.sb("rw_lnw", [128, 1024], F32)
    lnb = kb.sb("rw_lnb", [128, 1024], F32)
    bc_row(lnw, W["rwkv_ln_w"][0:1, :])
    bc_row(lnb, W["rwkv_ln_b"][0:1, :])
    yt = kb.sb("rc_y", [128, 1024], F32)
    y2 = kb.sb("rc_y2", [128, 1024], F32)
    bon = kb.sb("rc_bon", [128, 1024], F32)
    sgc = kb.sb("rc_sg", [128, 1024], F32)
    smc = kb.sb("rc_sm", [128, 32], F32)
    zTt = kb.sb("rc_zT", [128, 8, 128], BF16)
    for b in range(NB):
        for tt in range(NT):
            P.dma("sp", yt[:], tok_view("Y", b, tt), reads=[scr["Y"].b], writes=[yt.b])
            P.dma("act", bon[:], scr["BON"].t[b, tt * 128:(tt + 1) * 128, :], reads=[scr["BON"].b], writes=[bon.b])
            P.dma("act", sgc[:], scr["SG"].t[b, tt * 128:(tt + 1) * 128, :], reads=[scr["SG"].b], writes=[sgc.b])
            P.op("dve", lambda e: e.tensor_reduce(smc[:, 0:16], hd(yt[:]), AX.X, ALU.add), reads=[yt.b], writes=[smc.b])
            P.op("dve", lambda e: e.tensor_scalar(smc[:, 0:16], smc[:, 0:16], -1.0 / 64.0, None, ALU.mult),
                 reads=[smc.b], writes=[smc.b])
            P.op("dve", lambda e: e.tensor_tensor(hd(yt[:]), hd(yt[:]), bcj(smc[:, 0:16]), ALU.add),
                 reads=[yt.b, smc.b], writes=[yt.b])
            P.op("act", lambda e: e.activation(y2[:], yt[:], AF.Square), reads=[yt.b], writes=[y2.b])
            P.op("dve", lambda e: e.tensor_reduce(smc[:, 16:32], hd(y2[:]), AX.X, ALU.add), reads=[y2.b], writes=[smc.b])
            P.op("dve", lambda e: e.tensor_scalar(smc[:, 16:32], smc[:, 16:32], 1.0 / 64.0, 64e-5, ALU.mult, ALU.add),
                 reads=[smc.b], writes=[smc.b])
            P.op("act", lambda e: e.activation(smc[:, 16:32], smc[:, 16:32], AF.Sqrt), reads=[smc.b], writes=[smc.b])
            P.op("dve", lambda e: e.reciprocal(smc[:, 16:32], smc[:, 16:32]), reads=[smc.b], writes=[smc.b])
            P.op("dve", lambda e: e.tensor_tensor(hd(yt[:]), hd(yt[:]), bcj(smc[:, 16:32]), ALU.mult),
                 reads=[yt.b, smc.b], writes=[yt.b])
            P.op("pool", lambda e: e.tensor_tensor(yt[:], yt[:], lnw[:], ALU.mult), reads=[yt.b, lnw.b], writes=[yt.b])
            P.op("pool", lambda e: e.tensor_tensor(yt[:], yt[:], lnb[:], ALU.add), reads=[yt.b, lnb.b], writes=[yt.b])
            P.op("dve", lambda e: e.tensor_tensor(yt[:], yt[:], bon[:], ALU.add), reads=[yt.b, bon.b], writes=[yt.b])
            P.op("dve", lambda e: e.tensor_tensor(yt[:], yt[:], sgc[:], ALU.mult), reads=[yt.b, sgc.b], writes=[yt.b])
            for c in range(8):
                pt = ps[4 + c // 4]
                P.op("pe", lambda e, pt=pt, c=c: e.transpose(
                    pt[:, (c % 4) * 128:(c % 4 + 1) * 128], yt[:, c * 128:(c + 1) * 128], ident[:]),
                    reads=[yt.b, ident.b], writes=[pt.b])
            P.op("act", lambda e: e.copy(zTt[:, 0:4, :], ps[4][:].rearrange("p (c t) -> p c t", t=128)),
                 reads=[ps[4].b], writes=[zTt.b])
            P.op("dve", lambda e: e.tensor_copy(zTt[:, 4:8, :], ps[5][:].rearrange("p (c t) -> p c t", t=128)),
                 reads=[ps[5].b], writes=[zTt.b])
            back_tile(kb, g, b, tt, _Shift(zTt, tt * 128), [zTt.b], wout, x_src, x_dst)
    kb.pop()
    kb.pop()


class _W4:
    def __init__(self, t, n):
        self.t, self.n, self.b = t, n, t.b

    def __getitem__(self, k):
        a, kc, sl = k
        return self.t[a, self.n, kc, sl]


LAYER_FNS[2] = layer_rwkv2


def rwkv_consts():
    c = {}
    tp = np.arange(128)[:, None]
    t = np.arange(128)[None, :]
    same = (tp // 64) == (t // 64)
    c["rw_M1"] = (same & (tp <= t)).astype(np.float32)
    c["rw_M1s"] = (same & (tp < t)).astype(np.float32)
    c["rw_M2"] = (same & (tp > t)).astype(np.float32)
    strict = (same & (tp < t)).astype(np.float32)
    incl = (same & (tp <= t)).astype(np.float32)
    c["rw_mask4"] = np.concatenate([strict, incl, strict, incl], axis=1)
    low = (same & (t < tp)).astype(np.float32)
    c["rw_maskL2"] = np.concatenate([low, low], axis=1)
    c["rw_cind"] = np.stack([(np.arange(128) < 64), (np.arange(128) >= 64)], axis=1).astype(np.float32)
    return c


def kernel(**inputs):
    n_cores = 8
    x = np.ascontiguousarray(np.asarray(inputs["x"], dtype=np.float32))
    c = np.ascontiguousarray(np.asarray(inputs["c"], dtype=np.float32))
    wsh = {k: tuple(np.asarray(v).shape) for k, v in inputs.items()}
    wsh["c"] = (NB, D)
    consts = make_consts()
    nc = build([0, 1, 2, 3], wsh, consts)
    shared = {k: np.ascontiguousarray(np.asarray(v, dtype=np.float32))
              for k, v in inputs.items() if k not in ("x", "c")}
    for k, v in consts.items():
        shared["k_" + k] = v
    in_maps = []
    for i in range(n_cores):
        m = dict(shared)
        m["x"] = np.ascontiguousarray(x[i * NB:(i + 1) * NB])
        m["c"] = np.ascontiguousarray(c[i * NB:(i + 1) * NB])
        in_maps.append(m)
    res = run_bass_kernel_spmd(nc, in_maps, core_ids=list(range(n_cores)))
    out = np.concatenate([np.asarray(r["y"]) for r in res.results], axis=0)
    return out.astype(np.float32)
```
